# Optimizing a Trainium2 kernel written in Bass

```python
import jax, jax.numpy as jnp
from jax import lax
import numpy as np

D_MODEL = 4096
BATCH = 4
SEQ = 2048
DEPTH = 2
DEC_BATCH = 32
DEC_SEQ = 1
PAST_LEN = 16384
PAGE_SIZE = 128

N_A = DEPTH // 2
N_B = DEPTH - N_A
RW_HEAD = 64
RW_HEADS = D_MODEL // RW_HEAD
D_DECAY_LORA = max(32, int(round(1.8 * D_MODEL ** 0.5 / 32)) * 32)
D_AAA_LORA = max(32, int(round(1.8 * D_MODEL ** 0.5 / 32)) * 32)
D_GATE_LORA = max(32, int(round(0.6 * D_MODEL ** 0.8 / 32)) * 32)
GN_EPS = 64e-5
N_SHIFT_MIX = 6
HEAD_DIM = 64
N_Q_HEADS = D_MODEL // HEAD_DIM
N_KV_HEADS = max(1, N_Q_HEADS // 8)
GQA_GROUP = N_Q_HEADS // N_KV_HEADS
WINDOW = 128
BLOCK = WINDOW
ATTN_SCALE = HEAD_DIM ** -0.5
D_FF = ((7 * D_MODEL // 2 + 127) // 128) * 128
CONV_W = 3
RMS_EPS = 1e-6

kernel_name = 'yoco_rwkv7_swa_sink_convffn_step'


def _rmsnorm(x, g):
    x32 = x.astype(jnp.float32)
    y = x32 * lax.rsqrt(jnp.mean(x32 * x32, axis=-1, keepdims=True) + RMS_EPS)
    return (y * g.astype(jnp.float32)).astype(x.dtype)


def _adaln(c, w, b, n):
    m = jax.nn.silu(c) @ w + b
    return jnp.split(m[:, None, :], n, axis=-1)


def _rwkv7_mix(h, shift_prev, S0, mix, w0, w1, w2, a0, a1, a2, g1, g2, k_k, k_a, r_k,
               w_r, w_k, w_v, w_o, lnx_w, lnx_b):
    B, T, D = h.shape
    H, N = RW_HEADS, RW_HEAD
    f32 = jnp.float32
    h_prev = jnp.concatenate([shift_prev[:, None, :].astype(h.dtype), h[:, :-1]], axis=1)
    dx = h_prev - h
    xr, xw, xk, xv, xa, xg = [h + dx * mix[i] for i in range(N_SHIFT_MIX)]
    r = (xr @ w_r).astype(f32)
    k = (xk @ w_k).astype(f32)
    v = (xv @ w_v).astype(f32)
    w_log = -jax.nn.softplus(-(w0 + jnp.tanh(xw @ w1) @ w2).astype(f32)) - 0.5
    decay = jnp.exp(-jnp.exp(w_log))
    a = jax.nn.sigmoid((a0 + (xa @ a1) @ a2).astype(f32))
    g = jax.nn.sigmoid(xg @ g1) @ g2
    kk = (k * k_k.astype(f32)).reshape(B, T, H, N)
    kk = kk / jnp.maximum(jnp.sqrt(jnp.sum(kk * kk, axis=-1, keepdims=True)), 1e-12)
    k = k * (1.0 + (a - 1.0) * k_a.astype(f32))
    r4, k4, v4 = r.reshape(B, T, H, N), k.reshape(B, T, H, N), v.reshape(B, T, H, N)
    d4, a4 = decay.reshape(B, T, H, N), a.reshape(B, T, H, N)
    tm = lambda t: jnp.swapaxes(t, 0, 1)
    xs = (tm(r4), tm(k4), tm(v4), tm(d4), tm(kk), tm(kk * a4))

    def step(S, inp):
        r_t, k_t, v_t, d_t, kk_t, b_t = inp
        s_kk = jnp.einsum('bhvk,bhk->bhv', S, kk_t)
        S = (S * d_t[:, :, None, :] - s_kk[..., None] * b_t[:, :, None, :]
             + v_t[..., None] * k_t[:, :, None, :])
        y_t = jnp.einsum('bhvk,bhk->bhv', S, r_t)
        return S, y_t

    S_T, y = lax.scan(step, S0.astype(f32), xs)
    y = jnp.swapaxes(y, 0, 1)
    mu = jnp.mean(y, axis=-1, keepdims=True)
    var = jnp.mean(jnp.square(y - mu), axis=-1, keepdims=True)
    y = (y - mu) * lax.rsqrt(var + GN_EPS) * lnx_w.astype(f32).reshape(H, N) + lnx_b.astype(f32).reshape(H, N)
    y = y + jnp.sum(r4 * k4 * r_k.astype(f32), axis=-1, keepdims=True) * v4
    out = (y.reshape(B, T, D).astype(h.dtype) * g) @ w_o
    return out, S_T.astype(S0.dtype), h[:, -1]


def _conv_ffn(h, buf, w_in, conv_w, conv_b, w_out):
    T = h.shape[1]
    gate, up = jnp.split(h @ w_in, 2, axis=-1)
    pad = jnp.concatenate([buf.astype(gate.dtype), gate], axis=1)
    conv = conv_b + sum(pad[:, j:j + T] * conv_w[j] for j in range(CONV_W))
    out = (jax.nn.silu(conv) * up) @ w_out
    return out, pad[:, -(CONV_W - 1):]


def _shared_kv(x, c, kv_norm_g, kv_mod_w, kv_mod_b, w_kv, k_norm_g):
    B, T, _ = x.shape
    sh, sc = _adaln(c, kv_mod_w, kv_mod_b, 2)
    hn = _rmsnorm(x, kv_norm_g) * (1 + sc) + sh
    k, v = jnp.split(hn @ w_kv, 2, axis=-1)
    k = _rmsnorm(k.reshape(B, T, N_KV_HEADS, HEAD_DIM), k_norm_g)
    v = v.reshape(B, T, N_KV_HEADS, HEAD_DIM)
    return k, v


def _sink_attend(q, k, v, mask, sinks):
    s = jnp.einsum('...qkgd,...skd->...kgqs', q, k, preferred_element_type=jnp.float32) * ATTN_SCALE
    s = jnp.where(mask, s, -jnp.inf)
    sink = sinks.astype(jnp.float32).reshape(N_KV_HEADS, GQA_GROUP, 1, 1)
    m = jnp.maximum(jnp.max(s, axis=-1, keepdims=True), sink)
    p = jnp.exp(s - m)
    p = p / (jnp.sum(p, axis=-1, keepdims=True) + jnp.exp(sink - m))
    return jnp.einsum('...kgqs,...skd->...qkgd', p.astype(v.dtype), v)


def _swa_banded(q, k, v, sinks):
    B, T = q.shape[:2]
    nb = T // BLOCK
    qb = q.reshape(B, nb, BLOCK, N_KV_HEADS, GQA_GROUP, HEAD_DIM)
    kb = k.reshape(B, nb, BLOCK, N_KV_HEADS, HEAD_DIM)
    vb = v.reshape(B, nb, BLOCK, N_KV_HEADS, HEAD_DIM)
    prev = lambda t: jnp.concatenate([jnp.zeros_like(t[:, :1]), t[:, :-1]], axis=1)
    k_band = jnp.concatenate([prev(kb), kb], axis=2)
    v_band = jnp.concatenate([prev(vb), vb], axis=2)
    blk = jnp.arange(nb)[:, None]
    qpos = blk * BLOCK + jnp.arange(BLOCK)[None]
    kpos = (blk - 1) * BLOCK + jnp.arange(2 * BLOCK)[None]
    rel = qpos[:, :, None] - kpos[:, None, :]
    mask = (rel >= 0) & (rel < WINDOW) & (kpos[:, None, :] >= 0)
    o = _sink_attend(qb, k_band, v_band, mask[:, None, None], sinks)
    return o.reshape(B, T, N_Q_HEADS * HEAD_DIM)


def _swa_cached(q, k_all, v_all, n_past, sinks):
    B, T = q.shape[:2]
    krel = jnp.arange(n_past + T) - n_past
    rel = jnp.arange(T)[:, None] - krel[None, :]
    mask = (rel >= 0) & (rel < WINDOW)
    o = _sink_attend(q, k_all, v_all, mask, sinks)
    return o.reshape(B, T, N_Q_HEADS * HEAD_DIM)


def _forward(x, c, wkv0, shift0, conv0, k_buf, v_buf, p):
    B, T, _ = x.shape
    new_wkv, new_shift, new_conv = [], [], []
    k_att = v_att = k_win = v_win = None
    for l in range(DEPTH):
        sh1, sc1, gt1, sh2, sc2, gt2 = _adaln(c, p['mod_w'][l], p['mod_b'][l], 6)
        h = _rmsnorm(x, p['ln1_g'][l]) * (1 + sc1) + sh1
        if l < N_A:
            o, S, s = _rwkv7_mix(h, shift0[l], wkv0[l], p['rwkv_mix'][l], p['rwkv_w0'][l], p['rwkv_w1'][l],
                                 p['rwkv_w2'][l], p['rwkv_a0'][l], p['rwkv_a1'][l], p['rwkv_a2'][l],
                                 p['rwkv_g1'][l], p['rwkv_g2'][l], p['rwkv_k_k'][l], p['rwkv_k_a'][l],
                                 p['rwkv_r_k'][l], p['rwkv_w_r'][l], p['rwkv_w_k'][l], p['rwkv_w_v'][l],
                                 p['rwkv_w_o'][l], p['rwkv_lnx_w'][l], p['rwkv_lnx_b'][l])
            new_wkv.append(S)
            new_shift.append(s)
        else:
            j = l - N_A
            q = (h @ p['attn_w_q'][j]).reshape(B, T, N_KV_HEADS, GQA_GROUP, HEAD_DIM)
            q = _rmsnorm(q, p['attn_q_norm_g'][j])
            if k_buf is None:
                attn = _swa_banded(q, k_att, v_att, p['attn_sinks'][j])
            else:
                attn = _swa_cached(q, k_att, v_att, k_buf.shape[1], p['attn_sinks'][j])
            o = attn @ p['attn_w_o'][j]
        x = x + gt1 * o
        h2 = _rmsnorm(x, p['ln2_g'][l]) * (1 + sc2) + sh2
        f, cb = _conv_ffn(h2, conv0[l], p['ffn_w_in'][l], p['ffn_conv_w'][l], p['ffn_conv_b'][l], p['ffn_w_out'][l])
        new_conv.append(cb)
        x = x + gt2 * f
        if l == N_A - 1:
            k_new, v_new = _shared_kv(x, c, p['kv_norm_g'], p['kv_mod_w'], p['kv_mod_b'], p['w_kv'], p['k_norm_g'])
            if k_buf is None:
                w = min(WINDOW, T)
                k_att, v_att = k_new, v_new
                k_win, v_win = k_new[:, T - w:], v_new[:, T - w:]
            else:
                n_buf = k_buf.shape[1]
                k_att = jnp.concatenate([k_buf.astype(k_new.dtype), k_new], axis=1)
                v_att = jnp.concatenate([v_buf.astype(v_new.dtype), v_new], axis=1)
                k_win, v_win = k_att[:, -n_buf:], v_att[:, -n_buf:]
    return x, jnp.stack(new_wkv), jnp.stack(new_shift), jnp.stack(new_conv), k_win, v_win


def setup_inputs(seed: int = 0) -> dict:
    key = jax.random.key(seed)
    ks = iter(jax.random.split(key, 64))
    f32 = jnp.float32
    nrm = lambda shape, scale: scale * jax.random.normal(next(ks), shape, f32)
    uni = lambda shape, lo, hi: jax.random.uniform(next(ks), shape, f32, lo, hi)
    D, H, N = D_MODEL, RW_HEADS, RW_HEAD
    inv = D ** -0.5
    W_BUF = min(WINDOW, PAST_LEN)
    QW = N_Q_HEADS * HEAD_DIM
    KVW = N_KV_HEADS * HEAD_DIM
    return dict(
        x_prompt=nrm((BATCH, SEQ, D), 1.0),
        x_sample=nrm((DEC_BATCH, DEC_SEQ, D), 1.0),
        c_prompt=nrm((BATCH, D), 1.0),
        c_sample=nrm((DEC_BATCH, D), 1.0),
        state_wkv=nrm((N_A, DEC_BATCH, H, N, N), 0.5),
        state_shift=nrm((N_A, DEC_BATCH, D), 1.0),
        state_conv=nrm((DEPTH, DEC_BATCH, CONV_W - 1, D_FF), 1.0),
        cache_k_win=nrm((DEC_BATCH, W_BUF, N_KV_HEADS, HEAD_DIM), 1.0),
        cache_v_win=nrm((DEC_BATCH, W_BUF, N_KV_HEADS, HEAD_DIM), 1.0),
        mod_w=nrm((DEPTH, D, 6 * D), 0.5 * inv),
        mod_b=nrm((DEPTH, 6 * D), 0.02),
        ln1_g=1.0 + nrm((DEPTH, D), 0.02),
        ln2_g=1.0 + nrm((DEPTH, D), 0.02),
        rwkv_mix=uni((N_A, N_SHIFT_MIX, D), 0.0, 1.0),
        rwkv_w0=uni((N_A, D), -6.0, 1.0),
        rwkv_w1=nrm((N_A, D, D_DECAY_LORA), inv),
        rwkv_w2=nrm((N_A, D_DECAY_LORA, D), 0.5 * D_DECAY_LORA ** -0.5),
        rwkv_a0=nrm((N_A, D), 0.1),
        rwkv_a1=nrm((N_A, D, D_AAA_LORA), inv),
        rwkv_a2=nrm((N_A, D_AAA_LORA, D), 0.5 * D_AAA_LORA ** -0.5),
        rwkv_g1=nrm((N_A, D, D_GATE_LORA), inv),
        rwkv_g2=nrm((N_A, D_GATE_LORA, D), D_GATE_LORA ** -0.5),
        rwkv_k_k=0.85 + nrm((N_A, D), 0.02),
        rwkv_k_a=1.0 + nrm((N_A, D), 0.02),
        rwkv_r_k=nrm((N_A, H, N), 0.1),
        rwkv_w_r=nrm((N_A, D, D), inv),
        rwkv_w_k=nrm((N_A, D, D), inv),
        rwkv_w_v=nrm((N_A, D, D), inv),
        rwkv_w_o=nrm((N_A, D, D), inv),
        rwkv_lnx_w=1.0 + nrm((N_A, D), 0.02),
        rwkv_lnx_b=nrm((N_A, D), 0.02),
        kv_norm_g=1.0 + nrm((D,), 0.02),
        kv_mod_w=nrm((D, 2 * D), 0.5 * inv),
        kv_mod_b=nrm((2 * D,), 0.02),
        w_kv=nrm((D, 2 * KVW), inv),
        k_norm_g=1.0 + nrm((HEAD_DIM,), 0.02),
        attn_w_q=nrm((N_B, D, QW), inv),
        attn_q_norm_g=1.0 + nrm((N_B, HEAD_DIM), 0.02),
        attn_sinks=nrm((N_B, N_Q_HEADS), 0.5),
        attn_w_o=nrm((N_B, QW, D), QW ** -0.5),
        ffn_w_in=nrm((DEPTH, D, 2 * D_FF), inv),
        ffn_conv_w=nrm((DEPTH, CONV_W, D_FF), CONV_W ** -0.5),
        ffn_conv_b=nrm((DEPTH, D_FF), 0.02),
        ffn_w_out=nrm((DEPTH, D_FF, D), D_FF ** -0.5),
    )


def reference(x_prompt, x_sample, c_prompt, c_sample, state_wkv, state_shift, state_conv,
              cache_k_win, cache_v_win, mod_w, mod_b, ln1_g, ln2_g,
              rwkv_mix, rwkv_w0, rwkv_w1, rwkv_w2, rwkv_a0, rwkv_a1, rwkv_a2, rwkv_g1, rwkv_g2,
              rwkv_k_k, rwkv_k_a, rwkv_r_k, rwkv_w_r, rwkv_w_k, rwkv_w_v, rwkv_w_o,
              rwkv_lnx_w, rwkv_lnx_b, kv_norm_g, kv_mod_w, kv_mod_b, w_kv, k_norm_g,
              attn_w_q, attn_q_norm_g, attn_sinks, attn_w_o,
              ffn_w_in, ffn_conv_w, ffn_conv_b, ffn_w_out):
    p = dict(mod_w=mod_w, mod_b=mod_b, ln1_g=ln1_g, ln2_g=ln2_g,
             rwkv_mix=rwkv_mix, rwkv_w0=rwkv_w0, rwkv_w1=rwkv_w1, rwkv_w2=rwkv_w2,
             rwkv_a0=rwkv_a0, rwkv_a1=rwkv_a1, rwkv_a2=rwkv_a2, rwkv_g1=rwkv_g1, rwkv_g2=rwkv_g2,
             rwkv_k_k=rwkv_k_k, rwkv_k_a=rwkv_k_a, rwkv_r_k=rwkv_r_k, rwkv_w_r=rwkv_w_r,
             rwkv_w_k=rwkv_w_k, rwkv_w_v=rwkv_w_v, rwkv_w_o=rwkv_w_o,
             rwkv_lnx_w=rwkv_lnx_w, rwkv_lnx_b=rwkv_lnx_b,
             kv_norm_g=kv_norm_g, kv_mod_w=kv_mod_w, kv_mod_b=kv_mod_b, w_kv=w_kv, k_norm_g=k_norm_g,
             attn_w_q=attn_w_q, attn_q_norm_g=attn_q_norm_g, attn_sinks=attn_sinks, attn_w_o=attn_w_o,
             ffn_w_in=ffn_w_in, ffn_conv_w=ffn_conv_w, ffn_conv_b=ffn_conv_b, ffn_w_out=ffn_w_out)
    B = x_prompt.shape[0]
    dt = x_prompt.dtype
    wkv0 = jnp.zeros((N_A, B, RW_HEADS, RW_HEAD, RW_HEAD), dt)
    shift0 = jnp.zeros((N_A, B, D_MODEL), dt)
    conv0 = jnp.zeros((DEPTH, B, CONV_W - 1, D_FF), dt)
    y_prompt, wkv_p, shift_p, conv_p, kwin_p, vwin_p = _forward(
        x_prompt, c_prompt, wkv0, shift0, conv0, None, None, p)
    y_sample, wkv_s, shift_s, conv_s, kwin_s, vwin_s = _forward(
        x_sample, c_sample, state_wkv, state_shift, state_conv, cache_k_win, cache_v_win, p)
    return (y_prompt, y_sample, wkv_p, wkv_s, shift_p, shift_s, conv_p, conv_s,
            kwin_p, kwin_s, vwin_p, vwin_s)
```

```python
import contextlib
import numpy as np
import concourse.bass as bass
import concourse.mybir as mybir
from concourse.bass_utils import run_bass_kernel_spmd

F32 = mybir.dt.float32
BF16 = mybir.dt.bfloat16
AF = mybir.ActivationFunctionType
ALU = mybir.AluOpType

REAL_CFG = dict(D=4096, T=2048, DFF=14336, DL=128, DA=128, DG=480, NS=4, WB=128)


class Prog:
    ENG = ("pe", "act", "dve", "pool", "sp")

    def __init__(self, nc, es):
        self.nc = nc
        self.h = dict(pe=nc.tensor, act=nc.scalar, dve=nc.vector, pool=nc.gpsimd, sp=nc.sync)
        self.sem = {e: es.enter_context(nc.semaphore("s_" + e)) for e in self.ENG}
        self.cnt = {e: 0 for e in self.ENG}
        self.q = {e: [] for e in self.ENG}
        self.waited = {e: {} for e in self.ENG}
        self.lastw = {}
        self.readers = {}
        self.semobj = {("c", e): self.sem[e] for e in self.ENG}
        self.dsem = {}
        for e in ("sp", "pool", "act"):
            self.dsem[e] = [es.enter_context(nc.semaphore("d_%s%d" % (e, i))) for i in range(12)]
            for i, s in enumerate(self.dsem[e]):
                self.semobj[("d", e, i)] = s
        self.dval = {k: 0 for k in self.semobj if k[0] == "d"}
        self.drr = {e: 0 for e in ("sp", "pool", "act")}
        self.nbank = 0

    def _deps(self, reads, writes):
        need = {}
        for r in reads:
            lw = self.lastw.get(r)
            if lw:
                need[lw[0]] = max(need.get(lw[0], 0), lw[1])
        for w in writes:
            lw = self.lastw.get(w)
            if lw:
                need[lw[0]] = max(need.get(lw[0], 0), lw[1])
            for k, v in self.readers.get(w, {}).items():
                need[k] = max(need.get(k, 0), v)
        return need

    def _emit_waits(self, eng, need):
        for k, v in need.items():
            if eng == "pe" and k == ("c", "pe"):
                continue
            if self.waited[eng].get(k, 0) < v:
                self.waited[eng][k] = v
                so = self.semobj[k]
                self.q[eng].append(lambda E, so=so, v=v: E.wait_ge(so, v))

    def _mark(self, tok, reads, writes):
        for w in writes:
            self.lastw[w] = tok
            self.readers[w] = {}
        for r in reads:
            self.readers.setdefault(r, {})
            d = self.readers[r]
            d[tok[0]] = max(d.get(tok[0], 0), tok[1])

    def op(self, eng, fn, reads=(), writes=()):
        bk_ = [r for r in reads if isinstance(r, tuple) and r and r[0] == "bank"]
        if bk_:
            writes = list(writes) + bk_
        need = self._deps(reads, writes)
        self._emit_waits(eng, need)
        self.cnt[eng] += 1
        idx = self.cnt[eng]
        so = self.sem[eng]
        self.q[eng].append(lambda E, fn=fn, so=so: fn(E).then_inc(so, 1))
        self.waited[eng][("c", eng)] = max(self.waited[eng].get(("c", eng), 0), 0)
        self._mark((("c", eng), idx), reads, writes)

    def dma(self, eng, out, in_, reads=(), writes=()):
        need = self._deps(reads, writes)
        i = self.drr[eng]
        self.drr[eng] = (i + 1) % len(self.dsem[eng])
        key = ("d", eng, i)
        if self.dval[key] > 0:
            need[key] = max(need.get(key, 0), self.dval[key])
        self._emit_waits(eng, need)
        self.dval[key] += 16
        so = self.semobj[key]
        self.q[eng].append(lambda E, so=so, out=out, in_=in_: E.dma_start(out=out, in_=in_).then_inc(so, 16))
        self._mark((key, self.dval[key]), reads, writes)

    def finish_waits(self, eng="sp"):
        need = {}
        for k, v in self.dval.items():
            if v:
                need[k] = v
        for e in self.ENG:
            if self.cnt[e] and e != eng:
                need[("c", e)] = self.cnt[e]
        self._emit_waits(eng, need)

    def barrier(self):
        for e in self.ENG:
            self.finish_waits(e)

    def emit(self, block):
        for e, reg in (("pe", block.tensor), ("act", block.scalar), ("dve", block.vector),
                       ("pool", block.gpsimd), ("sp", block.sync)):
            ops = self.q[e]

            def body(E, ops=ops):
                for f in ops:
                    f(E)
            reg(body)


def _chunks(n, c=128):
    return [(i, min(c, n - i)) for i in range(0, n, c)]


class Builder:
    def __init__(self, cfg, debug=()):
        self.cfg = cfg
        self.debug = set(debug)
        c = cfg
        self.D, self.T, self.DFF = c["D"], c["T"], c["DFF"]
        self.KC = self.D // 128
        self.FC = self.DFF // 128
        self.H = self.D // 64
        self.NP = self.H // 2
        self.NKV = max(1, self.H // 8)
        self.NS = c["NS"]
        self.TH = self.T // 2
        self.HALO = 130
        self.W0 = self.TH - self.HALO
        self.NW = self.T - self.W0
        self.nc = bass.Bass("TRN2", target_bir_lowering=False)
        self.ins = {}
        self.outs = {}

    def din(self, name, shape):
        t = self.nc.dram_tensor(name, list(shape), F32, kind="ExternalInput").ap()
        self.ins[name] = tuple(shape)
        return t

    def dout(self, name, shape, dt=F32):
        t = self.nc.dram_tensor(name, list(shape), dt, kind="ExternalOutput").ap()
        self.outs[name] = tuple(shape)
        return t

    def scratch(self, name, shape, dt=F32):
        if name in self.debug:
            return self.dout(name, shape, dt)
        return self.nc.dram_tensor(name, list(shape), dt, kind="Internal").ap()

    def sb(self, name, shape, dt=F32):
        return self.es.enter_context(self.nc.sbuf_tensor("t_" + name, list(shape), dt))

    def linear(self, name, w, K, N, xs, epi, gw, wb, colgroups=None):
        P = self.P
        kch = _chunks(K)
        nk = len(kch)
        nfull = K // 128
        for gi, (n0, gsz) in enumerate(colgroups or _chunks(N, gw)):
            buf = wb[gi % len(wb)]
            bkey = ("wb", buf.name)
            if nfull:
                P.dma("pool", buf[:, 0:nfull, 0:gsz],
                      w[0:nfull * 128, n0:n0 + gsz].rearrange("(kc p) n -> p kc n", p=128),
                      writes=[bkey])
            if nfull < nk:
                k0, ksz = kch[-1]
                P.dma("pool", buf[0:ksz, nfull, 0:gsz], w[k0:k0 + ksz, n0:n0 + gsz], writes=[bkey])
            for (c0, m) in _chunks(gsz):
                for g, (xf, ncols, rk) in enumerate(xs):
                    bk, ps = self.bank()
                    for ki, (k0, ksz) in enumerate(kch):
                        lhsT = buf[0:ksz, ki, c0:c0 + m]
                        rhs = xf(ki, ksz)
                        P.op("pe", lambda E, o=ps[0:m, 0:ncols], l=lhsT, r=rhs, s=(ki == 0), t=(ki == nk - 1):
                             E.matmul(o, l, r, start=s, stop=t),
                             reads=[bkey] + list(rk), writes=[bk])
                    epi((n0 + c0) // 128, n0 + c0, m, g, ps[0:m, 0:ncols], bk)

    def bank(self):
        i = self.P.nbank % len(self.banks)
        self.P.nbank += 1
        return ("bank", i), self.banks[i]

    def act(self, out, in_, func, R, W, bias=None, scale=None, eng="act"):
        kw = {}
        if bias is not None:
            kw["bias"] = bias
        if scale is not None:
            kw["scale"] = scale
        self.P.op("act", lambda E: E.activation(out, in_, func, **kw), reads=R, writes=W)

    def ts(self, eng, out, in0, s1, op0, R, W, s2=None, op1=None):
        if op1 is None:
            self.P.op(eng, lambda E: E.tensor_scalar(out, in0, s1, None, op0), reads=R, writes=W)
        else:
            self.P.op(eng, lambda E: E.tensor_scalar(out, in0, s1, s2, op0, op1), reads=R, writes=W)

    def tt(self, eng, out, a, b, op, R, W):
        self.P.op(eng, lambda E: E.tensor_tensor(out, a, b, op), reads=R, writes=W)

    def stt(self, out, in0, scalar, in1, op0, op1, R, W):
        self.P.op("dve", lambda E: E.scalar_tensor_tensor(out, in0, scalar, in1, op0, op1), reads=R, writes=W)

    def cp(self, eng, out, in_, R, W):
        if eng == "act":
            self.P.op("act", lambda E: E.activation(out, in_, AF.Copy), reads=R, writes=W)
        else:
            self.P.op(eng, lambda E: E.tensor_copy(out, in_), reads=R, writes=W)

    def mm(self, out, lhsT, rhs, R, W, start=True, stop=True):
        self.P.op("pe", lambda E: E.matmul(out, lhsT, rhs, start=start, stop=stop), reads=R, writes=W)

    def dump(self, name, ap, key, shape):
        if ("dbg_" + name) in self.debug:
            dd = self.dout("dbg_" + name, shape)
            self.P.dma("sp", dd, ap, reads=[key])

    def load(self, tile_ap, dram_ap, key, q="sp"):
        self.P.dma(q, tile_ap, dram_ap, writes=[key])


def _fm(v, n):
    return np.ascontiguousarray(np.asarray(v, np.float32).reshape(n, 128).T)


def _consts():
    p = np.arange(128)[:, None]
    f = np.arange(128)[None, :]
    c = {}
    c["c_ident"] = (p == f).astype(np.float32)
    c["c_ones"] = np.ones((128, 128), np.float32)
    blk = ((p // 64) == (f // 64)).astype(np.float32)
    c["c_blk"] = blk
    c["c_blkm"] = blk / 64.0
    su = (p < f).astype(np.float32)
    ui = (p <= f).astype(np.float32)
    c["c_m2"] = np.concatenate([su, ui], axis=1)
    c["c_sl"] = (p > f).astype(np.float32)
    c["c_bd8"] = ((p // 8) == (f // 8)).astype(np.float32)
    lvL, lvU = [], []
    for m in (8, 16, 32, 64):
        ml = (((p // (2 * m)) == (f // (2 * m))) & ((p % (2 * m)) >= m) & ((f % (2 * m)) < m)).astype(np.float32)
        lvL.append(ml); lvU.append(ml.T)
    c["c_lvL"] = np.concatenate(lvL, axis=1)
    c["c_lvU"] = np.concatenate(lvU, axis=1)
    return c


def build(cfg, debug=()):
    B = Builder(cfg, debug)
    nc = B.nc
    D, T, DFF, KC, FC, H, NP, NKV, NS = B.D, B.T, B.DFF, B.KC, B.FC, B.H, B.NP, B.NKV, B.NS
    DL, DA, DG = cfg["DL"], cfg["DA"], cfg["DG"]
    TH = T // 2
    HALO = 258
    W0 = TH - HALO
    NW = T - W0
    KVW = NKV * 64
    KVT = max(1, KVW // 128)
    NC5 = 1 + NS
    C = 128
    NCH = T // C
    NTF = T // 4
    din, dout = B.din, B.dout

    xT = din("xT", [D, T])
    maskrow = din("maskrow", [128, T])
    flagcol = din("flagcol", [128, 1])
    cT = din("cT", [D, NC5])
    xsT = din("xsT", [D, NS])
    shsT = din("shsT", [D, NS])
    st_wkv = din("st_wkv", [NS, H, 64, 64])
    st_conv = din("st_conv", [2, 128, FC, NS, 2])
    ck = din("ck", [NS, 128, KVW])
    cv = din("cv", [NS, 128, KVW])
    cst = {k: din(k, list(v.shape)) for k, v in _consts().items()}
    mod_w = din("mod_w", [2, D, 6 * D])
    modb = din("modb", [2, 128, 6 * KC])
    kv_mod_w = din("kv_mod_w", [D, 2 * D])
    kvmodb = din("kvmodb", [128, 2 * KC])
    vecs = din("vecs", [128, 16, KC])
    vecs2 = din("vecs2", [128, 2, KC])
    convp = din("convp", [128, 2, 4, FC])
    hd = din("hd", [128, 3 + NP])
    sinkT = din("sinkT", [128, H])
    w1 = din("rwkv_w1", [D, DL]); w2 = din("rwkv_w2", [DL, D])
    a1 = din("rwkv_a1", [D, DA]); a2 = din("rwkv_a2", [DA, D])
    g1 = din("rwkv_g1", [D, DG]); g2 = din("rwkv_g2", [DG, D])
    w_r = din("rwkv_w_r", [D, D]); w_k = din("rwkv_w_k", [D, D]); w_v = din("rwkv_w_v", [D, D])
    w_o = din("rwkv_w_o", [D, D])
    w_kv = din("w_kv", [D, 2 * KVW])
    w_q = din("attn_w_q", [D, D]); w_ao = din("attn_w_o", [D, D])
    w_in = din("ffn_w_in", [2, D, 2 * DFF]); w_out = din("ffn_w_out", [2, DFF, D])

    o_y = dout("o_y", [D, T - TH])
    o_ys = dout("o_ys", [D, NS])
    o_wkv = dout("o_wkv", [H, 64, 64])
    o_wkvs = dout("o_wkvs", [NS, H, 64, 64])
    o_shift = dout("o_shift", [128, KC])
    o_shifts = dout("o_shifts", [128, KC, NS])
    o_conv = dout("o_conv", [2, 128, FC, 2])
    o_convs = dout("o_convs", [2, 128, FC, NS, 2])
    o_kwin = dout("o_kwin", [128, KVW])
    o_vwin = dout("o_vwin", [128, KVW])
    o_kwins = dout("o_kwins", [NS, 128, KVW])
    o_vwins = dout("o_vwins", [NS, 128, KVW])

    sc = {n: B.scratch("s_" + n, [D, T]) for n in ("r", "k", "v", "ld", "a", "g")}
    scs = {n: B.scratch("ss_" + n, [D, NS]) for n in ("r", "k", "v", "ld", "a", "g")}
    s_yg = B.scratch("s_yg", [D, T], BF16)
    s_ygs = B.scratch("s_ygs", [D, NS], BF16)

    with contextlib.ExitStack() as es:
        B.es = es
        P = B.P = Prog(nc, es)
        B.banks = [es.enter_context(nc.psum_tensor("ps%d" % i, [128, 512], F32)) for i in range(8)]
        sb = B.sb
        act, ts, tt, stt, cp, mm = B.act, B.ts, B.tt, B.stt, B.cp, B.mm

        K_ = {}
        for k, v in cst.items():
            if k in ("c_bd8", "c_lvL", "c_lvU"):
                continue
            K_[k] = sb(k, list(B.ins[k]))
            P.dma("sp", K_[k][:], v[:, :], writes=[k])
        ident, ones, blk, blkm, m2, msl = (K_[k] for k in ("c_ident", "c_ones", "c_blk", "c_blkm", "c_m2", "c_sl"))
        m2b = sb("m2b", [128, 256], BF16); cp("dve", m2b[:], m2[:], ["c_m2"], ["m2b"])
        mslb = sb("mslb", [128, 128], BF16); cp("dve", mslb[:], msl[:], ["c_sl"], ["mslb"])
        vec = sb("vec", [128, 16, KC]); P.dma("sp", vec[:], vecs[:, :, :], writes=["vec"])
        vec2 = sb("vec2", [128, 2, KC]); P.dma("sp", vec2[:], vecs2[:, :, :], writes=["vec2"])
        cvp = sb("cvp", [128, 2, 4, FC]); P.dma("sp", cvp[:], convp[:, :, :, :], writes=["cvp"])
        hdt = sb("hdt", [128, 3 + NP]); P.dma("sp", hdt[:], hd[:, :], writes=["hdt"])
        flag = sb("flag", [128, 1]); P.dma("sp", flag[:], flagcol[:, :], writes=["flag"])
        modbt = sb("modbt", [128, 2, 6 * KC]); P.dma("sp", modbt[:], modb.rearrange("l p n -> p l n"), writes=["modbt"])
        kvmodbt = sb("kvmodbt", [128, 2 * KC]); P.dma("sp", kvmodbt[:], kvmodb[:, :], writes=["kvmodbt"])
        omm = sb("omm", [128, 6, KC])
        ts("dve", omm[:], vec[:, 4:10, :], -1.0, ALU.mult, ["vec"], ["omm"], 1.0, ALU.add)
        esink = sb("esink", [128, NP])
        act(esink[:], hdt[:, 3:3 + NP], AF.Exp, ["hdt"], ["esink"])
        modv = sb("modv", [128, 2, 6 * KC, NC5])
        kvmv = sb("kvmv", [128, 2 * KC, NC5])
        Gm = sb("Gm", [128, 5, KC, NC5])

        with contextlib.ExitStack() as es2:
            B.es = es2
            c5 = sb("c5", [128, KC, NC5]); P.dma("sp", c5[:], cT.rearrange("(kc p) n -> p kc n", p=128), writes=["c5"])
            c5s = sb("c5s", [128, KC, NC5])
            act(c5s[:], c5[:], AF.Sigmoid, ["c5"], ["c5s"])
            c5b = sb("c5b", [128, KC, NC5], BF16)
            tt("dve", c5b[:], c5[:], c5s[:], ALU.mult, ["c5", "c5s"], ["c5b"])
            wbA = [sb("wbA%d" % i, [128, KC, 512], BF16) for i in range(2)]
            xsA = [(lambda ki, ksz: c5b[0:ksz, ki, :], NC5, ["c5b"])]
            for l in range(2):
                def epiA(ct, n0, m, g, ps, bk, l=l):
                    act(modv[:, l, ct, :], ps, AF.Identity, [bk, "modbt"], ["modv"], bias=modbt[:, l, ct:ct + 1])
                B.linear("modw%d" % l, mod_w[l], D, 6 * D, xsA, epiA, 512, wbA)

            def epiK(ct, n0, m, g, ps, bk):
                act(kvmv[:, ct, :], ps, AF.Identity, [bk, "kvmodbt"], ["kvmv"], bias=kvmodbt[:, ct:ct + 1])
            B.linear("kvmodw", kv_mod_w, D, 2 * D, xsA, epiK, 512, wbA)
            for j in range(NC5):
                for gi, (l, which, vi) in enumerate(((0, 1, 0), (0, 4, 2), (1, 1, 1), (1, 4, 3))):
                    stt(Gm[:, gi, :, j], modv[:, l, which * KC:(which + 1) * KC, j], 1.0, vec[:, vi, :],
                        ALU.add, ALU.mult, ["modv", "vec"], ["Gm"])
                stt(Gm[:, 4, :, j], kvmv[:, KC:2 * KC, j], 1.0, vec2[:, 1, :], ALU.add, ALU.mult,
                    ["kvmv", "vec2"], ["Gm"])
            P.barrier()
        B.es = es
        shiftout = sb("shiftout", [128, KC])
        hlast = sb("hlast", [128, KC], BF16)
        P.op("pool", lambda E: E.memset(hlast[:], 0.0), writes=["hlast"])
        LDC = -float(np.exp(-0.5))

        def front(cols, xsrc, dst, per_col, hprev_src=None, tag="p"):
            grp = _chunks(cols, 512)
            with contextlib.ExitStack() as es3:
                B.es = es3
                hb = sb("hb" + tag, [128, KC, cols + 1], BF16)
                xis = [sb("xi%s%d" % (tag, i), [128, KC, cols], BF16) for i in range(2)]
                xs_ = [sb("xs%s%d" % (tag, i), [128, cols]) for i in range(2)]
                sq = [sb("sq%s%d" % (tag, i), [128, cols]) for i in range(2)]
                rstd = sb("rstd" + tag, [128, cols])
                hf = [sb("hf%s%d" % (tag, i), [128, cols]) for i in range(2)]
                og = [sb("og%s%d" % (tag, i), [128, 512]) for i in range(3)]
                wbF = [sb("wbF%s%d" % (tag, i), [128, KC, 256], BF16) for i in range(2)]
                wbS = [sb("wbS%s%d" % (tag, i), [128, 4, 512], BF16) for i in range(2)]
                lo = sb("lo" + tag, [128, 4, cols], BF16)
                bks = [B.bank() for _ in grp]
                for kc in range(KC):
                    x_ = xs_[kc % 2]; q_ = sq[kc % 2]
                    P.dma("sp", x_[:], xsrc(kc), writes=[x_.name])
                    act(q_[:], x_[:], AF.Square, [x_.name], [q_.name])
                    for gi, (c0, n) in enumerate(grp):
                        mm(bks[gi][1][:, 0:n], ones[:], q_[:, c0:c0 + n], [q_.name, "c_ones"], [bks[gi][0]],
                           start=(kc == 0), stop=(kc == KC - 1))
                for gi, (c0, n) in enumerate(grp):
                    act(rstd[:, c0:c0 + n], bks[gi][1][:, 0:n], AF.Ln, [bks[gi][0]], ["rstd"], scale=1.0 / D, bias=1e-6)
                if ("dbg_" + tag) in B.debug:
                    dd = B.dout("dbg_ln" + tag, [128, cols])
                    P.dma("sp", dd[:, :], rstd[:], reads=["rstd"])
                    dd2 = B.dout("dbg_sq" + tag, [128, cols])
                    P.dma("sp", dd2[:, :], sq[(KC - 1) % 2][:], reads=[sq[(KC - 1) % 2].name])
                act(rstd[:], rstd[:], AF.Exp, ["rstd"], ["rstd"], scale=-0.5)
                if ("dbg_" + tag) in B.debug:
                    dd = B.dout("dbg_rstd" + tag, [128, cols])
                    P.dma("sp", dd[:, :], rstd[:], reads=["rstd"])
                if not per_col:
                    cp("pool", hb[:, :, 0], hlast[:], ["hlast"], ["hb"])
                for kc in range(KC):
                    x_ = xs_[kc % 2]; h_ = hf[kc % 2]
                    P.dma("sp", x_[:], xsrc(kc), writes=[x_.name])
                    tt("dve", h_[:], x_[:], rstd[:], ALU.mult, [x_.name, "rstd"], [h_.name])
                    if kc == KC - 1:
                        B.dump("xn" + tag, h_[:], h_.name, [128, cols])
                        B.dump("xx" + tag, x_[:], x_.name, [128, cols])
                    if per_col:
                        tt("dve", h_[:], h_[:], Gm[:, 0, kc, 1:1 + NS], ALU.mult, [h_.name, "Gm"], [h_.name])
                        tt("dve", h_[:], h_[:], modv[:, 0, 0 * KC + kc, 1:1 + NS], ALU.add, [h_.name, "modv"], [h_.name])
                        cp("pool", shiftouts[:, kc, :], h_[:], [h_.name], ["shiftouts"])
                        cp("pool", hb[:, kc, 1:cols + 1], h_[:], [h_.name], ["hb"])
                    else:
                        act(h_[:], h_[:], AF.Identity, [h_.name, "Gm", "modv"], [h_.name],
                            bias=modv[:, 0, kc, 0:1], scale=Gm[:, 0, kc, 0:1])
                        if kc == KC - 1:
                            B.dump("hh" + tag, h_[:], h_.name, [128, cols])
                        cp("pool", shiftout[:, kc:kc + 1], h_[:, cols - 1:cols], [h_.name], ["shiftout"])
                        tt("dve", hb[:, kc, 1:cols + 1], h_[:], mrow[:, 0:cols], ALU.mult, [h_.name, "mrow"], ["hb"])
                if per_col:
                    hps = sb("hps", [128, KC, NS])
                    P.dma("sp", hps[:], shsT.rearrange("(kc p) n -> p kc n", p=128), writes=["hps"])
                else:
                    cp("pool", hlast[:], hb[:, :, cols], ["hb"], ["hlast"])

                def hprev(kc):
                    if per_col:
                        return hps[:, kc, :]
                    return hb[:, kc, 0:cols]

                def mix(i):
                    xi = xis[i % 2]
                    for kc in range(KC):
                        t_ = sq[kc % 2]
                        ts("pool", t_[:], hprev(kc), vec[:, 4 + i, kc:kc + 1], ALU.mult, ["hb", "hps", "vec"], [t_.name])
                        stt(xi[:, kc, :], hb[:, kc, 1:cols + 1], omm[:, i, kc:kc + 1], t_[:], ALU.mult, ALU.add,
                            ["hb", "omm", t_.name], [xi.name])
                    return [(lambda ki, ksz, c0=c0, n=n, xi=xi: xi[0:ksz, ki, c0:c0 + n], n, [xi.name]) for (c0, n) in grp]
                xsL = [(lambda ki, ksz, c0=c0, n=n: lo[0:ksz, ki, c0:c0 + n], n, ["lo"]) for (c0, n) in grp]
                ogi = [0]

                def store(name, func, bias_v=None, post=None):
                    def epi(ct, n0, m, g, ps, bk):
                        o_ = og[ogi[0] % 3]; ogi[0] += 1
                        c0, n = grp[g]
                        if func is None:
                            cp("act", o_[0:m, 0:n], ps, [bk], [o_.name])
                        else:
                            act(o_[0:m, 0:n], ps, func, [bk, "vec"], [o_.name],
                                bias=(vec[:, bias_v, ct:ct + 1] if bias_v is not None else None))
                        if post is not None:
                            ts("dve", o_[0:m, 0:n], o_[0:m, 0:n], post, ALU.mult, [o_.name], [o_.name])
                        P.dma("sp", dst[name](ct, c0, n), o_[0:m, 0:n], reads=[o_.name])
                    return epi

                def tolo(func):
                    def epi(ct, n0, m, g, ps, bk):
                        c0, n = grp[g]
                        if func is None:
                            cp("act", lo[0:m, ct, c0:c0 + n], ps, [bk], ["lo"])
                        else:
                            act(lo[0:m, ct, c0:c0 + n], ps, func, [bk], ["lo"])
                    return epi
                for i, (nm, w_) in enumerate((("r", w_r), ("ld", None), ("k", w_k), ("v", w_v), ("a", None), ("g", None))):
                    xsX = mix(i)
                    if w_ is not None:
                        B.linear(nm, w_, D, D, xsX, store(nm, None), 256, wbF)
                    elif nm == "ld":
                        B.linear("w1", w1, D, DL, xsX, tolo(AF.Tanh), 256, wbF)
                        B.linear("w2", w2, DL, D, xsL, store("ld", AF.Sigmoid, 10, LDC), 512, wbS)
                    elif nm == "a":
                        B.linear("a1", a1, D, DA, xsX, tolo(None), 256, wbF)
                        B.linear("a2", a2, DA, D, xsL, store("a", AF.Sigmoid, 11), 512, wbS)
                    else:
                        B.linear("g1", g1, D, DG, xsX, tolo(AF.Sigmoid), 256, wbF)
                        B.linear("g2", g2, DG, D, xsL, store("g", None), 512, wbS)
                P.barrier()
            B.es = es

        mrow = sb("mrow", [128, NTF])
        shiftouts = sb("shiftouts", [128, KC, NS])
        for p_ in range(T // NTF):
            P.dma("sp", mrow[:], maskrow[:, p_ * NTF:(p_ + 1) * NTF], writes=["mrow"])
            dstp = {n: (lambda ct, c0, n_, n=n, p_=p_: sc[n][ct * 128:(ct + 1) * 128, p_ * NTF + c0:p_ * NTF + c0 + n_])
                    for n in sc}
            front(NTF, lambda kc, p_=p_: xT[kc * 128:(kc + 1) * 128, p_ * NTF:(p_ + 1) * NTF], dstp, False, tag="p%d" % p_)
        P.dma("sp", o_shift[:, :], shiftout[:], reads=["shiftout"])
        dsts = {n: (lambda ct, c0, n_, n=n: scs[n][ct * 128:(ct + 1) * 128, c0:c0 + n_]) for n in scs}
        front(NS, lambda kc: xsT[kc * 128:(kc + 1) * 128, :], dsts, True, tag="s")
        P.dma("sp", o_shifts[:, :, :], shiftouts[:], reads=["shiftouts"])
        B.dump("gm_end", Gm[:].rearrange("p a k n -> p (a k n)"), "Gm", [128, 5 * KC * NC5])
        B.dump("modv_end", modv[:].rearrange("p a k n -> p (a k n)"), "modv", [128, 2 * 6 * KC * NC5])
        GN_EPS = 64e-5
        omka = sb("omka", [128, KC])
        ts("dve", omka[:], vec[:, 13, :], -1.0, ALU.mult, ["vec"], ["omka"], 1.0, ALU.add)
        onesC = sb("onesC", [128, C]); P.op("pool", lambda E: E.memset(onesC[:], 1.0), writes=["onesC"])
        m2b2 = sb("m2b2", [128, 2, 256], BF16)
        for h_ in range(2):
            cp("dve", m2b2[:, h_, :], m2[:], ["c_m2"], ["m2b2"])
        msl2 = sb("msl2", [128, 2, 128], BF16)
        idb2 = sb("idb2", [128, 2, 128], BF16)
        for h_ in range(2):
            cp("dve", msl2[:, h_, :], msl[:], ["c_sl"], ["msl2"])
            cp("dve", idb2[:, h_, :], ident[:], ["c_ident"], ["idb2"])

        def diag2(t3):
            return t3[:].rearrange("p h (a b) -> p (h a) b", b=64)[:, 0:4:3, :]

        def scan_stage():
            with contextlib.ExitStack() as es4:
                B.es = es4
                GP = min(4, NP)
                NSLOT = 4
                Sm = sb("Sm", [128, NP, 64])
                Sblk = sb("Sblk", [128, NP, 128], BF16)
                bd8_2 = sb("bd8_2", [128, 2, 128], BF16)
                lvL2 = sb("lvL2", [128, 4, 2, 128], BF16); lvU2 = sb("lvU2", [128, 4, 2, 128], BF16)
                ctmp = sb("ctmp", [128, 1152])
                P.dma("sp", ctmp[:, 0:128], cst["c_bd8"][:, :], writes=["ctmp"])
                P.dma("sp", ctmp[:, 128:640], cst["c_lvL"][:, :], writes=["ctmp"])
                P.dma("sp", ctmp[:, 640:1152], cst["c_lvU"][:, :], writes=["ctmp"])
                for h_ in range(2):
                    cp("dve", bd8_2[:, h_, :], ctmp[:, 0:128], ["ctmp"], ["bd8_2"])
                    for l_ in range(4):
                        cp("dve", lvL2[:, l_, h_, :], ctmp[:, 128 + l_ * 128:128 + (l_ + 1) * 128], ["ctmp"], ["lvL2"])
                        cp("dve", lvU2[:, l_, h_, :], ctmp[:, 640 + l_ * 128:640 + (l_ + 1) * 128], ["ctmp"], ["lvU2"])
                lds = {n: [sb("ld_%s%d" % (n, i), [128, GP, C]) for i in range(2)] for n in ("r", "k", "v", "ld", "a", "g")}
                SL = []
                for sl in range(NSLOT):
                    d = {}
                    for n in ("kkr", "sqk", "rn", "kk", "t1", "k2", "bet", "L", "eL", "enL", "ePrev", "eRev", "tmp",
                              "bts", "kts", "ysb", "mu2", "var", "rs", "yn", "rk", "bon"):
                        d[n] = sb("%s_%d" % (n, sl), [128, C])
                    d["gC"] = sb("gC_%d" % sl, [128, 1])
                    d["YY"] = sb("YY_%d" % sl, [128, 2 * C])
                    d["AR"] = sb("AR_%d" % sl, [128, 2 * C], BF16)
                    for n in ("Bbp", "Kbp", "Pm", "Q0", "Q1", "P0", "P1", "T0", "T1", "Vtp", "Btp", "Ktp", "Up",
                              "QA", "PA", "Dq0", "Dq1", "Dp0", "Dp1", "Y1", "Z1", "mt1", "mt2"):
                        d[n] = sb("%s_%d" % (n, sl), [128, 2, 128], BF16)
                    d["QM"] = sb("QM_%d" % sl, [128, 2, 2 * C], BF16)
                    d["KM"] = sb("KM_%d" % sl, [128, 2, 2 * C], BF16)
                    d["RH"] = sb("RH_%d" % sl, [128, 2, 64], BF16)
                    d["yg"] = sb("yg_%d" % sl, [128, C], BF16)
                    for n in ("Bbp", "Kbp", "Vtp", "Btp", "Ktp", "Up"):
                        P.op("pool", lambda E, t=d[n]: E.memset(t[:], 0.0), writes=[d[n].name])
                    SL.append(d)
                P.op("pool", lambda E: E.memset(Sm[:], 0.0), writes=[("Sm", q) for q in range(NP)])
                P.op("pool", lambda E: E.memset(Sblk[:], 0.0), writes=[("S", q) for q in range(NP)])
                cnt = 0
                for j in range(NCH):
                    for q0 in range(0, NP, GP):
                        gb = (cnt % 2); cnt += 1
                        for n in lds:
                            P.dma("sp", lds[n][gb][:], sc[n][q0 * 128:(q0 + GP) * 128, j * C:(j + 1) * C]
                                  .rearrange("(q p) t -> p q t", p=128), writes=[lds[n][gb].name])
                        def unit(qi, j=j, gb=gb, q0=q0):
                            hp = q0 + qi
                            W = SL[hp % NSLOT]
                            N_ = lambda n: W[n].name
                            r_, k_, v_, ld_, a_, g_ = (lds[n][gb][:, qi, :] for n in ("r", "k", "v", "ld", "a", "g"))
                            R6 = {n: lds[n][gb].name for n in lds}
                            pk = ("S", hp)
                            ts("dve", W["kkr"][:], k_, vec[:, 12, hp:hp + 1], ALU.mult, [R6["k"], "vec"], [N_("kkr")])
                            act(W["sqk"][:], W["kkr"][:], AF.Square, [N_("kkr")], [N_("sqk")])
                            bk, ps = B.bank()
                            mm(ps[:, 0:C], blk[:], W["sqk"][:], ["c_blk", N_("sqk")], [bk])
                            ts("dve", W["rn"][:], ps[:, 0:C], 1e-24, ALU.max, [bk], [N_("rn")])
                            act(W["rn"][:], W["rn"][:], AF.Ln, [N_("rn")], [N_("rn")])
                            act(W["rn"][:], W["rn"][:], AF.Exp, [N_("rn")], [N_("rn")], scale=-0.5)
                            tt("dve", W["kk"][:], W["kkr"][:], W["rn"][:], ALU.mult, [N_("kkr"), N_("rn")], [N_("kk")])
                            act(W["t1"][:], a_, AF.Identity, [R6["a"], "vec", "omka"], [N_("t1")],
                                bias=omka[:, hp:hp + 1], scale=vec[:, 13, hp:hp + 1])
                            tt("pool", W["k2"][:], k_, W["t1"][:], ALU.mult, [R6["k"], N_("t1")], [N_("k2")])
                            tt("pool", W["bet"][:], W["kk"][:], a_, ALU.mult, [N_("kk"), R6["a"]], [N_("bet")])
                            P.op("dve", lambda E, o=W["L"][:], d1=ld_: E.tensor_tensor_scan(o, onesC[:], d1, 0.0, ALU.mult, ALU.add),
                                 reads=["onesC", R6["ld"]], writes=[N_("L")])
                            act(W["eL"][:], W["L"][:], AF.Exp, [N_("L")], [N_("eL")])
                            act(W["enL"][:], W["L"][:], AF.Exp, [N_("L")], [N_("enL")], scale=-1.0)
                            tt("pool", W["tmp"][:], W["L"][:], ld_, ALU.subtract, [N_("L"), R6["ld"]], [N_("tmp")])
                            act(W["ePrev"][:], W["tmp"][:], AF.Exp, [N_("tmp")], [N_("ePrev")])
                            act(W["eRev"][:], W["L"][:], AF.Exp, [N_("L")], [N_("eRev")], scale=-1.0, bias=W["L"][:, C - 1:C])
                            act(W["gC"][:], W["L"][:, C - 1:C], AF.Exp, [N_("L")], [N_("gC")])
                            stt(W["AR"][:, 0:C], W["kk"][:], -1.0, W["ePrev"][:], ALU.mult, ALU.mult,
                                [N_("kk"), N_("ePrev")], [N_("AR")])
                            tt("pool", W["AR"][:, C:2 * C], r_, W["eL"][:], ALU.mult, [R6["r"], N_("eL")], [N_("AR")])
                            for h in range(2):
                                hs = slice(h * 64, (h + 1) * 64)
                                tt("dve", W["Bbp"][hs, h, :], W["bet"][hs, :], W["enL"][hs, :], ALU.mult,
                                   [N_("bet"), N_("enL")], [N_("Bbp")])
                                tt("pool", W["Kbp"][hs, h, :], W["k2"][hs, :], W["enL"][hs, :], ALU.mult,
                                   [N_("k2"), N_("enL")], [N_("Kbp")])
                            tt("dve", W["bts"][:], W["bet"][:], W["eRev"][:], ALU.mult, [N_("bet"), N_("eRev")], [N_("bts")])
                            tt("pool", W["kts"][:], W["k2"][:], W["eRev"][:], ALU.mult, [N_("k2"), N_("eRev")], [N_("kts")])
                            if cfg.get("scan_stop", 9) <= 1:
                                return
                            yield
                            bk, ps = B.bank()
                            for ti, (src, rk_) in enumerate(((v_, R6["v"]), (W["bts"][:], N_("bts")), (W["kts"][:], N_("kts")))):
                                P.op("pe", lambda E, o=ps[:, ti * 128:(ti + 1) * 128], i_=src: E.transpose(o, i_, ident[:]),
                                     reads=[rk_, "c_ident"], writes=[bk])
                            for ti, n in enumerate(("Vtp", "Btp", "Ktp")):
                                src3 = ps[:, ti * 128:(ti + 1) * 128].rearrange("p (h b) -> p h b", b=64)
                                if ti == 1:
                                    cp("dve", diag2(W[n]), src3, [bk], [N_(n)])
                                else:
                                    cp("act", diag2(W[n]), src3, [bk], [N_(n)])
                            if cfg.get("scan_stop", 9) <= 2:
                                return
                            yield
                            bkX, psX = B.bank(); bkY, psY = B.bank(); bkZ, psZ = B.bank()
                            for h in range(2):
                                mm(psX[:, h * 256:(h + 1) * 256], W["Bbp"][:, h, :], W["AR"][:], [N_("Bbp"), N_("AR")], [bkX])
                                mm(psY[:, h * 256:(h + 1) * 256], W["Kbp"][:, h, :], W["AR"][:], [N_("Kbp"), N_("AR")], [bkY])
                                mm(psZ[:, h * 128:(h + 1) * 128], W["AR"][:, 0:C], W["Bbp"][:, h, :], [N_("Bbp"), N_("AR")], [bkZ])
                            tt("dve", W["QM"][:].rearrange("p h c -> p (h c)"), psX[:, 0:512], m2b2[:].rearrange("p h c -> p (h c)"),
                               ALU.mult, [bkX, "m2b2"], [N_("QM")])
                            tt("dve", W["KM"][:].rearrange("p h c -> p (h c)"), psY[:, 0:512], m2b2[:].rearrange("p h c -> p (h c)"),
                               ALU.mult, [bkY, "m2b2"], [N_("KM")])
                            tt("dve", W["P0"][:].rearrange("p h c -> p (h c)"), psZ[:, 0:256], msl2[:].rearrange("p h c -> p (h c)"),
                               ALU.mult, [bkZ, "msl2"], [N_("P0")])
                            yield
                            if cfg.get("scan_stop", 9) <= 3:
                                return
                            fl = lambda t: t[:].rearrange("p h c -> p (h c)")
                            QA, PA = W["QA"], W["PA"]
                            cp("pool", QA[:], W["QM"][:, :, 0:C], [N_("QM")], [QA.name])
                            cp("pool", PA[:], W["P0"][:], [N_("P0")], [PA.name])
                            Qc, Pc = W["Q0"], W["P1"]
                            tt("pool", fl(Qc), fl(QA), fl(bd8_2), ALU.mult, [QA.name, "bd8_2"], [Qc.name])
                            tt("pool", fl(Pc), fl(PA), fl(bd8_2), ALU.mult, [PA.name, "bd8_2"], [Pc.name])
                            Dq, Dp = W["Dq0"], W["Dp0"]
                            tt("pool", fl(Dq), fl(Qc), fl(idb2), ALU.add, [Qc.name, "idb2"], [Dq.name])
                            tt("pool", fl(Dp), fl(Pc), fl(idb2), ALU.add, [Pc.name, "idb2"], [Dp.name])
                            dqi = 0
                            sq_bufs = [(W["Q1"], W["P0"]), (W["Q0"], W["P1"])]
                            for s_ in range(2):
                                Qn, Pn = sq_bufs[s_]
                                bkA, psA = B.bank(); bkB, psB = B.bank()
                                for h in range(2):
                                    mm(psA[:, h * 128:(h + 1) * 128], Pc[:, h, :], Qc[:, h, :], [Qc.name, Pc.name], [bkA])
                                    mm(psB[:, h * 128:(h + 1) * 128], Qc[:, h, :], Pc[:, h, :], [Qc.name, Pc.name], [bkB])
                                cp("act", fl(Qn), psA[:, 0:256], [bkA], [Qn.name])
                                cp("dve", fl(Pn), psB[:, 0:256], [bkB], [Pn.name])
                                Dqn, Dpn = W["Dq%d" % (1 - dqi)], W["Dp%d" % (1 - dqi)]
                                bkC, psC = B.bank(); bkD_, psD_ = B.bank()
                                for h in range(2):
                                    mm(psC[:, h * 128:(h + 1) * 128], Pn[:, h, :], Dq[:, h, :], [Pn.name, Dq.name], [bkC])
                                    mm(psD_[:, h * 128:(h + 1) * 128], Qn[:, h, :], Dp[:, h, :], [Qn.name, Dp.name], [bkD_])
                                tt("dve", fl(Dqn), psC[:, 0:256], fl(Dq), ALU.add, [bkC, Dq.name], [Dqn.name])
                                tt("dve", fl(Dpn), psD_[:, 0:256], fl(Dp), ALU.add, [bkD_, Dp.name], [Dpn.name])
                                Qc, Pc, Dq, Dp, dqi = Qn, Pn, Dqn, Dpn, 1 - dqi
                                yield
                            for l_ in range(4):
                                last = (l_ == 3)
                                Dqn, Dpn = W["Dq%d" % (1 - dqi)], W["Dp%d" % (1 - dqi)]
                                bkA, psA = B.bank()
                                for h in range(2):
                                    mm(psA[:, h * 128:(h + 1) * 128], PA[:, h, :], Dq[:, h, :], [PA.name, Dq.name], [bkA])
                                cp("act", fl(W["Y1"]), psA[:, 0:256], [bkA], [N_("Y1")])
                                yield
                                if not last:
                                    bkB, psB = B.bank()
                                    for h in range(2):
                                        mm(psB[:, h * 128:(h + 1) * 128], QA[:, h, :], Dp[:, h, :], [QA.name, Dp.name], [bkB])
                                    cp("act", fl(W["Z1"]), psB[:, 0:256], [bkB], [N_("Z1")])
                                bkC, psC = B.bank()
                                for h in range(2):
                                    mm(psC[:, h * 128:(h + 1) * 128], Dp[:, h, :], W["Y1"][:, h, :], [Dp.name, N_("Y1")], [bkC])
                                tt("dve", fl(W["mt1"]), psC[:, 0:256], lvU2[:, l_, :, :].rearrange("p h c -> p (h c)"), ALU.mult,
                                   [bkC, "lvU2"], [N_("mt1")])
                                tt("pool", fl(Dqn), fl(W["mt1"]), fl(Dq), ALU.add, [N_("mt1"), Dq.name], [Dqn.name])
                                if not last:
                                    bkD_, psD_ = B.bank()
                                    for h in range(2):
                                        mm(psD_[:, h * 128:(h + 1) * 128], Dq[:, h, :], W["Z1"][:, h, :], [Dq.name, N_("Z1")], [bkD_])
                                    tt("dve", fl(W["mt2"]), psD_[:, 0:256], lvL2[:, l_, :, :].rearrange("p h c -> p (h c)"), ALU.mult,
                                       [bkD_, "lvL2"], [N_("mt2")])
                                    tt("pool", fl(Dpn), fl(W["mt2"]), fl(Dp), ALU.add, [N_("mt2"), Dp.name], [Dpn.name])
                                Dq, Dp, dqi = Dqn, Dpn, 1 - dqi
                                yield
                            TT = Dq
                            if cfg.get("scan_stop", 9) <= 4:
                                return
                            yield
                            bkD, psD = B.bank()
                            for h in range(2):
                                hs = slice(h * 64, (h + 1) * 64)
                                mm(psD[:, hs], W["AR"][:, 0:C], Sblk[:, hp, hs], [N_("AR"), pk], [bkD], start=True, stop=False)
                                mm(psD[:, hs], W["KM"][:, h, 0:C], W["Vtp"][:, h, hs], [N_("KM"), N_("Vtp")], [bkD], start=False, stop=True)
                            cp("act", W["RH"][:].rearrange("p h c -> p (h c)"), psD[:, 0:128], [bkD], [N_("RH")])
                            yield
                            bkE, psE = B.bank()
                            for h in range(2):
                                mm(psE[:, h * 64:(h + 1) * 64], TT[:, h, :], W["RH"][:, h, :], [TT.name, N_("RH")], [bkE])
                            cp("dve", diag2(W["Up"]), psE[:, 0:128].rearrange("p (h b) -> p h b", b=64), [bkE], [N_("Up")])
                            yield
                            bkF, psF = B.bank()
                            mm(psF[:, 0:C], Sblk[:, hp, :], W["AR"][:, C:2 * C], [pk, N_("AR")], [bkF], start=True, stop=False)
                            for h in range(2):
                                mm(psF[:, 0:C], W["Up"][:, h, :], W["QM"][:, h, C:2 * C], [N_("Up"), N_("QM")], [bkF], start=False, stop=False)
                                mm(psF[:, 0:C], W["Vtp"][:, h, :], W["KM"][:, h, C:2 * C], [N_("Vtp"), N_("KM")], [bkF],
                                   start=False, stop=(h == 1))
                            bkG, psG = B.bank()
                            for h in range(2):
                                hs = slice(h * 64, (h + 1) * 64)
                                mm(psG[:, 0:64], W["Btp"][:, h, :], W["Up"][:, h, hs], [N_("Btp"), N_("Up")], [bkG],
                                   start=(h == 0), stop=False)
                                mm(psG[:, 0:64], W["Ktp"][:, h, :], W["Vtp"][:, h, hs], [N_("Ktp"), N_("Vtp")], [bkG],
                                   start=False, stop=(h == 1))
                            stt(Sm[:, hp, :], Sm[:, hp, :], W["gC"][:, 0:1], psG[:, 0:64], ALU.mult, ALU.add,
                                [("Sm", hp), N_("gC"), bkG], [("Sm", hp)])
                            for h in range(2):
                                hs = slice(h * 64, (h + 1) * 64)
                                cp("pool", Sblk[hs, hp, hs], Sm[hs, hp, :], [("Sm", hp)], [pk])
                            if cfg.get("scan_stop", 9) <= 5:
                                return
                            yield
                            cp("act", W["YY"][:, 0:C], psF[:, 0:C], [bkF], [N_("YY")])
                            act(W["YY"][:, C:2 * C], W["YY"][:, 0:C], AF.Square, [N_("YY")], [N_("YY")])
                            bkH, psH = B.bank()
                            mm(psH[:, 0:2 * C], blkm[:], W["YY"][:], ["c_blkm", N_("YY")], [bkH])
                            act(W["mu2"][:], psH[:, 0:C], AF.Square, [bkH], [N_("mu2")])
                            tt("dve", W["var"][:], psH[:, C:2 * C], W["mu2"][:], ALU.subtract, [bkH, N_("mu2")], [N_("var")])
                            act(W["rs"][:], W["var"][:], AF.Ln, [N_("var")], [N_("rs")], bias=GN_EPS)
                            act(W["rs"][:], W["rs"][:], AF.Exp, [N_("rs")], [N_("rs")], scale=-0.5)
                            tt("dve", W["yn"][:], W["YY"][:, 0:C], psH[:, 0:C], ALU.subtract, [N_("YY"), bkH], [N_("yn")])
                            tt("pool", W["yn"][:], W["yn"][:], W["rs"][:], ALU.mult, [N_("yn"), N_("rs")], [N_("yn")])
                            act(W["yn"][:], W["yn"][:], AF.Identity, [N_("yn"), "vec", "vec2"], [N_("yn")],
                                bias=vec2[:, 0, hp:hp + 1], scale=vec[:, 15, hp:hp + 1])
                            tt("pool", W["rk"][:], r_, W["k2"][:], ALU.mult, [R6["r"], N_("k2")], [N_("rk")])
                            ts("dve", W["rk"][:], W["rk"][:], vec[:, 14, hp:hp + 1], ALU.mult, [N_("rk"), "vec"], [N_("rk")])
                            yield
                            bkI, psI = B.bank()
                            mm(psI[:, 0:C], blk[:], W["rk"][:], ["c_blk", N_("rk")], [bkI])
                            tt("dve", W["bon"][:], psI[:, 0:C], v_, ALU.mult, [bkI, R6["v"]], [N_("bon")])
                            tt("pool", W["yn"][:], W["yn"][:], W["bon"][:], ALU.add, [N_("yn"), N_("bon")], [N_("yn")])
                            tt("dve", W["yg"][:], W["yn"][:], g_, ALU.mult, [N_("yn"), R6["g"]], [N_("yg")])
                            P.dma("sp", s_yg[hp * 128:(hp + 1) * 128, j * C:(j + 1) * C], W["yg"][:], reads=[N_("yg")])
                        NI = 4
                        for b0_ in range(0, GP, NI):
                            gens = [unit(qi) for qi in range(b0_, min(GP, b0_ + NI))]
                            while gens:
                                for g__ in list(gens):
                                    try:
                                        next(g__)
                                    except StopIteration:
                                        gens.remove(g__)
                Sf = sb("Sf", [128, 128]); So = sb("So", [128, 128])
                P.op("pool", lambda E: E.memset(Sf[:], 0.0), writes=["Sf"])
                for hp in range(NP):
                    for h in range(2):
                        hs = slice(h * 64, (h + 1) * 64)
                        cp("pool", Sf[hs, hs], Sm[hs, hp, :], [("Sm", hp)], ["Sf"])
                    bk, ps = B.bank()
                    P.op("pe", lambda E, o=ps[:, 0:128]: E.transpose(o, Sf[:], ident[:]), reads=["Sf", "c_ident"], writes=[bk])
                    cp("act", So[:], ps[:, 0:128], [bk], ["So"])
                    for h in range(2):
                        hs = slice(h * 64, (h + 1) * 64)
                        P.dma("sp", o_wkv[2 * hp + h, :, :], So[hs, hs], reads=["So"])
                P.barrier()
            B.es = es
        scan_stage()
        NBLK = (NW - 2) // 128
        assert NW == 2 + NBLK * 128
        WG = _chunks(NW, 512)
        s_x1 = B.scratch("s_x1", [D, NW]); s_x2 = B.scratch("s_x2", [D, NW]); s_x3 = B.scratch("s_x3", [D, NW])
        s_act = B.scratch("s_act", [DFF, NW], BF16)
        s_oat = B.scratch("s_oat", [D, NW], BF16)
        convout = sb("convout", [128, 2, FC, 2])
        mwin = sb("mwin", [128, HALO]); P.dma("sp", mwin[:], maskrow[:, W0:W0 + HALO], writes=["mwin"])
        esT = sb("esT", [128, H]); P.dma("sp", esT[:], sinkT[:, :], writes=["esT"])
        act(esT[:], esT[:], AF.Exp, ["esT"], ["esT"])
        kwin_sb = sb("kwin_sb", [128, KVW]); vwin_sb = sb("vwin_sb", [128, KVW])

        def xs_of(tile, key, groups):
            return [(lambda ki, ksz, c0=c0, n=n: tile[0:ksz, ki, c0:c0 + n], n, [key]) for (c0, n) in groups]

        def norm_stats(src, rstd, xa):
            bks = [B.bank() for _ in WG]
            for kc in range(KC):
                x_ = xa[kc % 2]
                P.dma("sp", x_[:], src[kc * 128:(kc + 1) * 128, :], writes=[x_.name])
                act(x_[:], x_[:], AF.Square, [x_.name], [x_.name])
                for gi, (c0, n) in enumerate(WG):
                    mm(bks[gi][1][:, 0:n], ones[:], x_[:, c0:c0 + n], [x_.name, "c_ones"], [bks[gi][0]],
                       start=(kc == 0), stop=(kc == KC - 1))
            for gi, (c0, n) in enumerate(WG):
                act(rstd[:, c0:c0 + n], bks[gi][1][:, 0:n], AF.Ln, [bks[gi][0]], [rstd.name], scale=1.0 / D, bias=1e-6)
            act(rstd[:], rstd[:], AF.Exp, [rstd.name], [rstd.name], scale=-0.5)

        def modulate(src, rstd, hw, gsel, shsel, xa):
            for kc in range(KC):
                x_ = xa[kc % 2]
                P.dma("sp", x_[:], src[kc * 128:(kc + 1) * 128, :], writes=[x_.name])
                tt("dve", x_[:], x_[:], rstd[:], ALU.mult, [x_.name, rstd.name], [x_.name])
                act(hw[:, kc, :], x_[:], AF.Identity, [x_.name, "Gm", "modv", "kvmv"], [hw.name],
                    bias=shsel(kc), scale=gsel(kc))

        def resid_epi(xin, gate_sel, target, og, xg, cg):
            cnt_ = [0]

            def epi(ct, n0, m, g, ps, bk):
                i = cnt_[0] % 2; cnt_[0] += 1
                c0, n = cg[g]
                P.dma("sp", xg[i][:, 0:n], xin[ct * 128:(ct + 1) * 128, c0:c0 + n], writes=[xg[i].name])
                stt(og[i][:, 0:n], ps, gate_sel(ct), xg[i][:, 0:n], ALU.mult, ALU.add,
                    [bk, xg[i].name, "modv"], [og[i].name])
                target(ct, c0, n, og[i], og[i].name)
            return epi

        def to_scratch(dst):
            def tgt(ct, c0, n, o_, key):
                P.dma("sp", dst[ct * 128:(ct + 1) * 128, c0:c0 + n], o_[:, 0:n], reads=[key])
            return tgt

        def ffn(l, xin, target, tg):
            with contextlib.ExitStack() as es5:
                B.es = es5
                rstd = sb("frstd" + tg, [128, NW])
                hw = sb("fhw" + tg, [128, KC, NW], BF16)
                xa = [sb("fxa%s%d" % (tg, i), [128, NW]) for i in range(2)]
                norm_stats(xin, rstd, xa)
                modulate(xin, rstd, hw, lambda kc: Gm[:, 1 + 2 * l, kc, 0:1], lambda kc: modv[:, l, 3 * KC + kc, 0:1], xa)
                wb = [sb("fwb%s%d" % (tg, i), [128, KC, 128], BF16) for i in range(2)]
                gfull = [sb("gfull%s%d" % (tg, i), [128, NW]) for i in range(2)]
                cvt = [sb("cvt%s%d" % (tg, i), [128, NW]) for i in range(2)]
                sg = [sb("sg%s%d" % (tg, i), [128, NW], BF16) for i in range(4)]
                ao = [sb("ao%s%d" % (tg, i), [128, 512], BF16) for i in range(3)]
                for t_ in sg:
                    P.op("pool", lambda E, t_=t_: E.memset(t_[:], 0.0), writes=[t_.name])
                aoc = [0]

                def epi(ct, n0, m, g, ps, bk):
                    c0, n = WG[g]
                    if n0 < DFF:
                        f = ct
                        gf = gfull[f % 2]
                        cp("act", gf[:, c0:c0 + n], ps, [bk], [gf.name])
                        if g == len(WG) - 1:
                            tt("dve", gf[:, 0:HALO], gf[:, 0:HALO], mwin[:], ALU.mult, [gf.name, "mwin"], [gf.name])
                            cp("pool", convout[:, l, f, :], gf[:, NW - 2:NW], [gf.name], ["convout"])
                            c_ = cvt[f % 2]
                            act(c_[:, 2:NW], gf[:, 2:NW], AF.Identity, [gf.name, "cvp"], [c_.name],
                                bias=cvp[:, l, 3, f:f + 1], scale=cvp[:, l, 2, f:f + 1])
                            stt(c_[:, 2:NW], gf[:, 1:NW - 1], cvp[:, l, 1, f:f + 1], c_[:, 2:NW], ALU.mult, ALU.add,
                                [gf.name, c_.name, "cvp"], [c_.name])
                            stt(c_[:, 2:NW], gf[:, 0:NW - 2], cvp[:, l, 0, f:f + 1], c_[:, 2:NW], ALU.mult, ALU.add,
                                [gf.name, c_.name, "cvp"], [c_.name])
                            act(sg[f % 4][:, 2:NW], c_[:, 2:NW], AF.Silu, [c_.name], [sg[f % 4].name])
                    else:
                        f = ct - FC
                        a_ = ao[aoc[0] % 3]; aoc[0] += 1
                        tt("dve", a_[:, 0:n], ps, sg[f % 4][:, c0:c0 + n], ALU.mult, [bk, sg[f % 4].name], [a_.name])
                        P.dma("sp", s_act[f * 128:(f + 1) * 128, c0:c0 + n], a_[:, 0:n], reads=[a_.name])
                cgs = []
                for (n0, gsz) in _chunks(DFF, 128):
                    cgs.append((n0, gsz)); cgs.append((DFF + n0, gsz))
                B.linear("win" + tg, w_in[l], D, 2 * DFF, xs_of(hw, hw.name, WG), epi, 128, wb, colgroups=cgs)
                P.barrier()
            with contextlib.ExitStack() as es5:
                B.es = es5
                CG = _chunks(NW, 384)
                actt = sb("actt" + tg, [128, FC, 384], BF16)
                wbo = [sb("fwo%s%d" % (tg, i), [128, FC, 128], BF16) for i in range(2)]
                og = [sb("fog%s%d" % (tg, i), [128, 384]) for i in range(2)]
                xg = [sb("fxg%s%d" % (tg, i), [128, 384]) for i in range(2)]
                for (c0, n) in CG:
                    P.dma("sp", actt[:, :, 0:n], s_act[:, c0:c0 + n].rearrange("(f p) t -> p f t", p=128), writes=[actt.name])
                    xs1 = [(lambda ki, ksz, n=n: actt[0:ksz, ki, 0:n], n, [actt.name])]
                    B.linear("wout" + tg, w_out[l], DFF, D, xs1,
                             resid_epi(xin, lambda ct: modv[:, l, 5 * KC + ct, 0:1], target, og, xg, [(c0, n)]), 128, wbo)
                P.barrier()
            B.es = es

        def backend():
            with contextlib.ExitStack() as es5:
                B.es = es5
                hw = sb("d1hw", [128, KC, NW], BF16)
                P.dma("sp", hw[:], s_yg[:, W0:T].rearrange("(kc p) t -> p kc t", p=128), writes=["d1hw"])
                wb = [sb("d1wb%d" % i, [128, KC, 256], BF16) for i in range(2)]
                og = [sb("d1og%d" % i, [128, 512]) for i in range(2)]
                xg = [sb("d1xg%d" % i, [128, 512]) for i in range(2)]
                B.linear("wo", w_o, D, D, xs_of(hw, "d1hw", WG),
                         resid_epi(xT[:, W0:T], lambda ct: modv[:, 0, 2 * KC + ct, 0:1], to_scratch(s_x1), og, xg, WG), 256, wb)
                P.barrier()
            B.es = es
            ffn(0, s_x1, to_scratch(s_x2), "0")
            with contextlib.ExitStack() as es5:
                B.es = es5
                rstd = sb("krstd", [128, NW])
                hw = sb("khw", [128, KC, NW], BF16)
                kdup = sb("kdup", [128, NKV, NW], BF16)
                Vt1 = sb("Vt1", [128, NBLK, NKV, 65], BF16)
                P.op("pool", lambda E: E.memset(Vt1[:], 1.0), writes=["Vt1"])
                xa = [sb("kxa%d" % i, [128, NW]) for i in range(2)]
                norm_stats(s_x2, rstd, xa)
                modulate(s_x2, rstd, hw, lambda kc: Gm[:, 4, kc, 0:1], lambda kc: kvmv[:, kc, 0:1], xa)
                wbk = [sb("kwb%d" % i, [128, KC, 128], BF16) for i in range(2)]
                kf = sb("kf", [128, NW]); ksq = sb("ksq", [128, NW]); krs = sb("krs", [128, NW])
                for g in range(NKV):
                    buf = wbk[g % 2]; bkey = ("wb", buf.name)
                    for half in range(2):
                        P.dma("pool", buf[:, :, half * 64:(half + 1) * 64],
                              w_kv[:, g * 64:(g + 1) * 64].rearrange("(kc p) n -> p kc n", p=128), writes=[bkey])
                    for gi, (c0, n) in enumerate(WG):
                        bk, ps = B.bank()
                        for kc in range(KC):
                            mm(ps[:, 0:n], buf[:, kc, :], hw[:, kc, c0:c0 + n], [bkey, hw.name], [bk], start=(kc == 0), stop=(kc == KC - 1))
                        cp("act", kf[:, c0:c0 + n], ps[:, 0:n], [bk], [kf.name])
                    act(ksq[:], kf[:], AF.Square, [kf.name], [ksq.name])
                    for gi, (c0, n) in enumerate(WG):
                        bk, ps = B.bank()
                        mm(ps[:, 0:n], blkm[:], ksq[:, c0:c0 + n], ["c_blkm", ksq.name], [bk])
                        act(krs[:, c0:c0 + n], ps[:, 0:n], AF.Ln, [bk], [krs.name], bias=1e-6)
                    act(krs[:], krs[:], AF.Exp, [krs.name], [krs.name], scale=-0.5)
                    tt("dve", kf[:], kf[:], krs[:], ALU.mult, [kf.name, krs.name], [kf.name])
                    ts("dve", kf[:], kf[:], hdt[:, 0:1], ALU.mult, [kf.name, "hdt"], [kf.name])
                    cp("pool", kdup[:, g, :], kf[:], [kf.name], ["kdup"])
                    bk, ps = B.bank()
                    P.op("pe", lambda E, o=ps[:, 0:128]: E.transpose(o, kf[:, NW - 128:NW], ident[:]), reads=[kf.name, "c_ident"], writes=[bk])
                    cp("act", kwin_sb[:, g * 64:(g + 1) * 64], ps[:, 0:64], [bk], ["kwin_sb"])
                vf = xa[0]

                def epiV(ct, n0, m, g, ps, bk):
                    c0, n = WG[g]
                    cp("act", vf[0:m, c0:c0 + n], ps, [bk], [vf.name])
                    if g == len(WG) - 1:
                        for j in range(NBLK):
                            bk2, ps2 = B.bank()
                            P.op("pe", lambda E, o=ps2[:, 0:m], j=j: E.transpose(o, vf[0:m, 2 + j * 128:2 + (j + 1) * 128], ident[0:m, 0:m]),
                                 reads=[vf.name, "c_ident"], writes=[bk2])
                            nh = m // 64
                            cp("dve", Vt1[:, j, ct * 2:ct * 2 + nh, 0:64], ps2[:, 0:m].rearrange("p (h d) -> p h d", d=64), [bk2], ["Vt1"])
                            if j == NBLK - 1:
                                cp("act", vwin_sb[:, ct * 128:ct * 128 + m], ps2[:, 0:m], [bk2], ["vwin_sb"])
                B.linear("wv", w_kv[:, KVW:2 * KVW], D, KVW, xs_of(hw, hw.name, WG), epiV, 128, wbk)
                P.dma("sp", o_kwin[:, :], kwin_sb[:], reads=["kwin_sb"])
                P.dma("sp", o_vwin[:, :], vwin_sb[:], reads=["vwin_sb"])
                modulate(s_x2, rstd, hw, lambda kc: Gm[:, 2, kc, 0:1], lambda kc: modv[:, 1, 0 * KC + kc, 0:1], xa)
                wbq = wbk
                qf = [kf, kf]
                qsq = ksq; qrs = krs
                qp = sb("qp", [128, 2, NW], BF16)
                P.op("pool", lambda E: E.memset(qp[:], 0.0), writes=["qp"])
                mpo = sb("mpo", [128, 4, 128], BF16)
                for i in range(4):
                    cp("dve", mpo[:, i, :], (mslb[:] if i % 2 == 0 else m2b[:, 128:256]), ["mslb", "m2b"], ["mpo"])
                Eb = [sb("Eb%d" % i, [128, 4, 128], BF16) for i in range(2)]
                otk = [sb("otk%d" % i, [128, 128]) for i in range(2)]
                den = [sb("den%d" % i, [128, 2]) for i in range(2)]
                ofm = [sb("ofm%d" % i, [128, 128], BF16) for i in range(2)]
                zt = sb("zt", [128, 130], BF16)
                P.op("pool", lambda E: E.memset(zt[:], 0.0), writes=["zt"])
                for kc in range(KC):
                    P.dma("sp", s_oat[kc * 128:(kc + 1) * 128, 0:130], zt[:], reads=["zt"])
                cnt_q = [0]

                def epiQ(ct, n0, m, g, ps, bk):
                    c0, n = WG[g]
                    q_ = qf[ct % 2]
                    cp("act", q_[:, c0:c0 + n], ps, [bk], [q_.name])
                    if g < len(WG) - 1:
                        return
                    act(qsq[:], q_[:], AF.Square, [q_.name], [qsq.name])
                    for gi, (c0_, n_) in enumerate(WG):
                        bk2, ps2 = B.bank()
                        mm(ps2[:, 0:n_], blkm[:], qsq[:, c0_:c0_ + n_], ["c_blkm", qsq.name], [bk2])
                        act(qrs[:, c0_:c0_ + n_], ps2[:, 0:n_], AF.Ln, [bk2], [qrs.name], bias=1e-6)
                    act(qrs[:], qrs[:], AF.Exp, [qrs.name], [qrs.name], scale=-0.5)
                    tt("dve", q_[:], q_[:], qrs[:], ALU.mult, [q_.name, qrs.name], [q_.name])
                    for h in range(2):
                        hs = slice(h * 64, (h + 1) * 64)
                        ts("dve", qp[hs, h, :], q_[hs, :], hdt[hs, 1:2], ALU.mult, [q_.name, "hdt"], ["qp"])
                    for qb in range(1, NBLK):
                        i2 = cnt_q[0] % 2; cnt_q[0] += 1
                        qc = slice(2 + qb * 128, 2 + (qb + 1) * 128)
                        bkS, psS = B.bank()
                        for h in range(2):
                            gkv = (2 * ct + h) // (H // NKV)
                            for kbi, kb in enumerate((qb - 1, qb)):
                                kcs = slice(2 + kb * 128, 2 + (kb + 1) * 128)
                                mm(psS[:, (h * 2 + kbi) * 128:(h * 2 + kbi + 1) * 128], kdup[:, gkv, kcs], qp[:, h, qc],
                                   ["kdup", "qp"], [bkS])
                        E_ = Eb[i2]
                        act(E_[:].rearrange("p a b -> p (a b)"), psS[:, 0:512], AF.Exp, [bkS], [E_.name], scale=0.125)
                        tt("dve", E_[:], E_[:], mpo[:], ALU.mult, [E_.name, "mpo"], [E_.name])
                        for kbi, kb in enumerate((qb - 1, qb)):
                            if kb <= 1:
                                ts("pool", E_[:, kbi:4:2, :], E_[:, kbi:4:2, :], flag[:, 0:1], ALU.mult, [E_.name, "flag"], [E_.name])
                        bkO, psO = B.bank()
                        for h in range(2):
                            gkv = (2 * ct + h) // (H // NKV)
                            for kbi, kb in enumerate((qb - 1, qb)):
                                mm(psO[:, h * 65:(h + 1) * 65], E_[:, h * 2 + kbi, :], Vt1[:, kb, gkv, :], [E_.name, "Vt1"], [bkO],
                                   start=(kbi == 0), stop=(kbi == 1))
                        d_ = den[i2]; o_ = otk[i2]
                        tt("dve", d_[:], psO[:, 64:130:65], esT[:, 2 * ct:2 * ct + 2], ALU.add, [bkO, "esT"], [d_.name])
                        P.op("dve", lambda E, d_=d_: E.reciprocal(d_[:], d_[:]), reads=[d_.name], writes=[d_.name])
                        for h in range(2):
                            ts("dve", o_[:, h * 64:(h + 1) * 64], psO[:, h * 65:h * 65 + 64], d_[:, h:h + 1], ALU.mult,
                               [bkO, d_.name], [o_.name])
                        bkT, psT = B.bank()
                        P.op("pe", lambda E, o=psT[:, 0:128], o_=o_: E.transpose(o, o_[:], ident[:]), reads=[o_.name, "c_ident"], writes=[bkT])
                        f_ = ofm[i2]
                        cp("act", f_[:], psT[:, 0:128], [bkT], [f_.name])
                        P.dma("sp", s_oat[ct * 128:(ct + 1) * 128, qc], f_[:], reads=[f_.name])
                B.linear("wq", w_q, D, D, xs_of(hw, hw.name, WG), epiQ, 128, wbq)
                P.barrier()
            B.es = es
            with contextlib.ExitStack() as es5:
                B.es = es5
                hw = sb("d4hw", [128, KC, NW], BF16)
                P.dma("sp", hw[:], s_oat.rearrange("(kc p) t -> p kc t", p=128), writes=["d4hw"])
                wb = [sb("d4wb%d" % i, [128, KC, 256], BF16) for i in range(2)]
                og = [sb("d4og%d" % i, [128, 512]) for i in range(2)]
                xg = [sb("d4xg%d" % i, [128, 512]) for i in range(2)]
                B.linear("wao", w_ao, D, D, xs_of(hw, "d4hw", WG),
                         resid_epi(s_x2, lambda ct: modv[:, 1, 2 * KC + ct, 0:1], to_scratch(s_x3), og, xg, WG), 256, wb)
                P.barrier()
            B.es = es

            def to_y(ct, c0, n, o_, key):
                lo_ = max(c0, HALO)
                if lo_ < c0 + n:
                    P.dma("sp", o_y[ct * 128:(ct + 1) * 128, lo_ - HALO:c0 + n - HALO], o_[:, lo_ - c0:n], reads=[key])
            ffn(1, s_x3, to_y, "1")
            P.dma("sp", o_conv.rearrange("l p f c -> p l f c"), convout[:], reads=["convout"])
        backend()
        def sample_path():
            with contextlib.ExitStack() as es6:
                B.es = es6
                KN = KC * NS
                f3 = lambda t: t[:].rearrange("p k n -> p (k n)")
                T_ = {}
                for n in ("r", "k", "v", "ld", "a", "g"):
                    T_[n] = sb("s_" + n, [128, KC, NS])
                    P.dma("sp", T_[n][:], scs[n].rearrange("(kc p) n -> p kc n", p=128), writes=[T_[n].name])
                W_ = {n: sb("sw_" + n, [128, KC, NS]) for n in ("kkr", "sq", "rn", "kk", "t1", "k2", "bet", "dd", "al", "ys", "yn", "rk", "mu2", "var", "rs", "bon", "ysq")}
                nm = lambda n: (T_[n].name if n in T_ else W_[n].name)
                for j in range(NS):
                    tt("dve", W_["kkr"][:, :, j], T_["k"][:, :, j], vec[:, 12, :], ALU.mult, [nm("k"), "vec"], [nm("kkr")])
                act(f3(W_["sq"]), f3(W_["kkr"]), AF.Square, [nm("kkr")], [nm("sq")])
                bk, ps = B.bank()
                mm(ps[:, 0:KN], blk[:], f3(W_["sq"]), ["c_blk", nm("sq")], [bk])
                ts("dve", f3(W_["rn"]), ps[:, 0:KN], 1e-24, ALU.max, [bk], [nm("rn")])
                act(f3(W_["rn"]), f3(W_["rn"]), AF.Ln, [nm("rn")], [nm("rn")])
                act(f3(W_["rn"]), f3(W_["rn"]), AF.Exp, [nm("rn")], [nm("rn")], scale=-0.5)
                tt("dve", f3(W_["kk"]), f3(W_["kkr"]), f3(W_["rn"]), ALU.mult, [nm("kkr"), nm("rn")], [nm("kk")])
                for j in range(NS):
                    tt("dve", W_["t1"][:, :, j], T_["a"][:, :, j], vec[:, 13, :], ALU.mult, [nm("a"), "vec"], [nm("t1")])
                    tt("dve", W_["t1"][:, :, j], W_["t1"][:, :, j], omka[:], ALU.add, [nm("t1"), "omka"], [nm("t1")])
                tt("dve", f3(W_["k2"]), f3(T_["k"]), f3(W_["t1"]), ALU.mult, [nm("k"), nm("t1")], [nm("k2")])
                tt("dve", f3(W_["bet"]), f3(W_["kk"]), f3(T_["a"]), ALU.mult, [nm("kk"), nm("a")], [nm("bet")])
                act(f3(W_["dd"]), f3(T_["ld"]), AF.Exp, [nm("ld")], [nm("dd")])
                ts("dve", f3(W_["al"]), f3(W_["kk"]), -1.0, ALU.mult, [nm("kk")], [nm("al")])
                Spad = [sb("Spad%d" % i, [128, 128]) for i in range(2)]
                St = [sb("St%d" % i, [128, 128]) for i in range(2)]
                So_ = [sb("Sos%d" % i, [128, 128]) for i in range(2)]
                X1 = [sb("X1_%d" % i, [128, 2]) for i in range(2)]
                X2 = [sb("X2_%d" % i, [128, 2]) for i in range(2)]
                R12 = [sb("R12_%d" % i, [2, 256]) for i in range(2)]
                otm = [sb("otm%d" % i, [128, 128]) for i in range(2)]
                for t_ in Spad:
                    P.op("pool", lambda E, t_=t_: E.memset(t_[:], 0.0), writes=[t_.name])
                it = 0
                for j in range(NS):
                    for hp in range(NP):
                        i2 = it % 2; it += 1
                        sp_, st_, so_, x1, x2, r12, ot = Spad[i2], St[i2], So_[i2], X1[i2], X2[i2], R12[i2], otm[i2]
                        for h in range(2):
                            hs = slice(h * 64, (h + 1) * 64)
                            P.dma("sp", sp_[hs, hs], st_wkv[j, 2 * hp + h, :, :], writes=[sp_.name])
                        bk, ps = B.bank()
                        P.op("pe", lambda E, o=ps[:, 0:128], i_=sp_: E.transpose(o, i_[:], ident[:]), reads=[sp_.name, "c_ident"], writes=[bk])
                        cp("act", st_[:], ps[:, 0:128], [bk], [st_.name])
                        bk, ps = B.bank()
                        mm(ps[:, 0:1], st_[:], W_["al"][:, hp, j:j + 1], [st_.name, nm("al")], [bk])
                        cp("act", x2[:, 0:1], ps[:, 0:1], [bk], [x2.name])
                        cp("pool", x2[:, 1:2], T_["v"][:, hp, j:j + 1], [nm("v")], [x2.name])
                        cp("pool", x1[:, 0:1], W_["bet"][:, hp, j:j + 1], [nm("bet")], [x1.name])
                        cp("pool", x1[:, 1:2], W_["k2"][:, hp, j:j + 1], [nm("k2")], [x1.name])
                        bk, ps = B.bank()
                        P.op("pe", lambda E, o=ps[0:2, 0:128], i_=x1: E.transpose(o, i_[:], ident[:]), reads=[x1.name, "c_ident"], writes=[bk])
                        P.op("pe", lambda E, o=ps[0:2, 128:256], i_=x2: E.transpose(o, i_[:], ident[:]), reads=[x2.name, "c_ident"], writes=[bk])
                        cp("act", r12[:], ps[0:2, 0:256], [bk], [r12.name])
                        bk, ps = B.bank()
                        mm(ps[:, 0:128], r12[:, 0:128], r12[:, 128:256], [r12.name], [bk])
                        tt("dve", ot[:], ps[:, 0:128], blk[:], ALU.mult, [bk, "c_blk"], [ot.name])
                        stt(st_[:], st_[:], W_["dd"][:, hp, j:j + 1], ot[:], ALU.mult, ALU.add, [st_.name, nm("dd"), ot.name], [st_.name])
                        bk, ps = B.bank()
                        mm(ps[:, 0:1], st_[:], T_["r"][:, hp, j:j + 1], [st_.name, nm("r")], [bk])
                        cp("act", W_["ys"][:, hp, j:j + 1], ps[:, 0:1], [bk], [nm("ys")])
                        bk, ps = B.bank()
                        P.op("pe", lambda E, o=ps[:, 0:128], i_=st_: E.transpose(o, i_[:], ident[:]), reads=[st_.name, "c_ident"], writes=[bk])
                        cp("dve", so_[:], ps[:, 0:128], [bk], [so_.name])
                        for h in range(2):
                            hs = slice(h * 64, (h + 1) * 64)
                            P.dma("sp", o_wkvs[j, 2 * hp + h, :, :], so_[hs, hs], reads=[so_.name])
                YY = sb("sYY", [128, 2, KN])
                cp("dve", YY[:, 0, :], f3(W_["ys"]), [nm("ys")], ["sYY"])
                act(YY[:, 1, :], f3(W_["ys"]), AF.Square, [nm("ys")], ["sYY"])
                bk, ps = B.bank()
                mm(ps[:, 0:2 * KN], blkm[:], YY[:].rearrange("p a n -> p (a n)"), ["c_blkm", "sYY"], [bk])
                act(f3(W_["mu2"]), ps[:, 0:KN], AF.Square, [bk], [nm("mu2")])
                tt("dve", f3(W_["var"]), ps[:, KN:2 * KN], f3(W_["mu2"]), ALU.subtract, [bk, nm("mu2")], [nm("var")])
                act(f3(W_["rs"]), f3(W_["var"]), AF.Ln, [nm("var")], [nm("rs")], bias=GN_EPS)
                act(f3(W_["rs"]), f3(W_["rs"]), AF.Exp, [nm("rs")], [nm("rs")], scale=-0.5)
                tt("dve", f3(W_["yn"]), f3(W_["ys"]), ps[:, 0:KN], ALU.subtract, [nm("ys"), bk], [nm("yn")])
                tt("dve", f3(W_["yn"]), f3(W_["yn"]), f3(W_["rs"]), ALU.mult, [nm("yn"), nm("rs")], [nm("yn")])
                tt("dve", f3(W_["rk"]), f3(T_["r"]), f3(W_["k2"]), ALU.mult, [nm("r"), nm("k2")], [nm("rk")])
                for j in range(NS):
                    tt("dve", W_["yn"][:, :, j], W_["yn"][:, :, j], vec[:, 15, :], ALU.mult, [nm("yn"), "vec"], [nm("yn")])
                    tt("dve", W_["yn"][:, :, j], W_["yn"][:, :, j], vec2[:, 0, :], ALU.add, [nm("yn"), "vec2"], [nm("yn")])
                    tt("dve", W_["rk"][:, :, j], W_["rk"][:, :, j], vec[:, 14, :], ALU.mult, [nm("rk"), "vec"], [nm("rk")])
                bk, ps = B.bank()
                mm(ps[:, 0:KN], blk[:], f3(W_["rk"]), ["c_blk", nm("rk")], [bk])
                tt("dve", f3(W_["bon"]), ps[:, 0:KN], f3(T_["v"]), ALU.mult, [bk, nm("v")], [nm("bon")])
                tt("dve", f3(W_["yn"]), f3(W_["yn"]), f3(W_["bon"]), ALU.add, [nm("yn"), nm("bon")], [nm("yn")])
                ygs = sb("ygs", [128, KC, NS], BF16)
                tt("dve", f3(ygs), f3(W_["yn"]), f3(T_["g"]), ALU.mult, [nm("yn"), nm("g")], ["ygs"])

                xs0 = sb("xs0", [128, KC, NS]); P.dma("sp", xs0[:], xsT.rearrange("(kc p) n -> p kc n", p=128), writes=[xs0.name])
                x1s = sb("x1s", [128, KC, NS]); x2s = sb("x2s", [128, KC, NS]); x3s = sb("x3s", [128, KC, NS]); ysf = sb("ysf", [128, KC, NS])
                hS = sb("hS", [128, KC, NS], BF16); h32 = sb("h32", [128, KC, NS]); sqS = sb("sqS", [128, KC, NS])
                rsS = sb("rsS", [128, NS])
                wbs = [sb("swb%d" % i, [128, KC, 512], BF16) for i in range(2)]
                wbo = [sb("swo%d" % i, [128, FC, 128], BF16) for i in range(2)]
                actS = sb("actS", [128, FC, NS], BF16)
                cbuf = sb("cbuf", [128, 2, FC, NS, 2])
                P.dma("sp", cbuf[:].rearrange("p l f n c -> p l (f n c)"), st_conv.rearrange("l p f n c -> p l (f n c)"), writes=["cbuf"])
                cvo = sb("cvo", [128, 2, FC, NS, 2])
                tmpS = [sb("tmpS%d" % i, [128, NS]) for i in range(4)]
                tcnt = [0]

                def xsS(tile, key):
                    return [(lambda ki, ksz: tile[0:ksz, ki, :], NS, [key])]

                def resS(xin, xout, gsel):
                    def epi(ct, n0, m, g, ps, bk):
                        t_ = tmpS[tcnt[0] % 4]; tcnt[0] += 1
                        tt("dve", t_[:], ps, gsel(ct), ALU.mult, [bk, "modv"], [t_.name])
                        tt("dve", xout[:, ct, :], t_[:], xin[:, ct, :], ALU.add, [t_.name, xin.name], [xout.name])
                    return epi

                def normS(x, G3, SH3, keys):
                    act(f3(sqS), f3(x), AF.Square, [x.name], ["sqS"])
                    bk, ps = B.bank()
                    for kc in range(KC):
                        mm(ps[:, 0:NS], ones[:], sqS[:, kc, :], ["c_ones", "sqS"], [bk], start=(kc == 0), stop=(kc == KC - 1))
                    act(rsS[:], ps[:, 0:NS], AF.Ln, [bk], ["rsS"], scale=1.0 / D, bias=1e-6)
                    act(rsS[:], rsS[:], AF.Exp, ["rsS"], ["rsS"], scale=-0.5)
                    for kc in range(KC):
                        tt("dve", h32[:, kc, :], x[:, kc, :], rsS[:], ALU.mult, [x.name, "rsS"], ["h32"])
                    tt("dve", h32[:], h32[:], G3, ALU.mult, ["h32"] + keys, ["h32"])
                    tt("dve", hS[:], h32[:], SH3, ALU.add, ["h32"] + keys, ["hS"])

                def ffnS(l, xin, xout):
                    normS(xin, Gm[:, 1 + 2 * l, :, 1:1 + NS], modv[:, l, 3 * KC:4 * KC, 1:1 + NS], ["Gm", "modv"])
                    sgS = [sb("sgS%d_%d" % (l, i), [128, NS]) for i in range(8)]

                    def epi(ct, n0, m, g, ps, bk):
                        if n0 < DFF:
                            f = ct
                            t_ = tmpS[tcnt[0] % 4]; tcnt[0] += 1
                            cp("act", cvo[:, l, f, :, 1], ps, [bk], ["cvo"])
                            cp("pool", cvo[:, l, f, :, 0], cbuf[:, l, f, :, 1], ["cbuf"], ["cvo"])
                            act(t_[:], ps, AF.Identity, [bk, "cvp"], [t_.name], bias=cvp[:, l, 3, f:f + 1], scale=cvp[:, l, 2, f:f + 1])
                            stt(t_[:], cbuf[:, l, f, :, 1], cvp[:, l, 1, f:f + 1], t_[:], ALU.mult, ALU.add, ["cbuf", "cvp", t_.name], [t_.name])
                            stt(t_[:], cbuf[:, l, f, :, 0], cvp[:, l, 0, f:f + 1], t_[:], ALU.mult, ALU.add, ["cbuf", "cvp", t_.name], [t_.name])
                            act(sgS[f % 8][:], t_[:], AF.Silu, [t_.name], [sgS[f % 8].name])
                        else:
                            f = ct - FC
                            tt("dve", actS[:, f, :], ps, sgS[f % 8][:], ALU.mult, [bk, sgS[f % 8].name], ["actS"])
                    cgs = []
                    for (n0, gsz) in _chunks(DFF, 512):
                        cgs.append((n0, gsz)); cgs.append((DFF + n0, gsz))
                    B.linear("swin%d" % l, w_in[l], D, 2 * DFF, xsS(hS, "hS"), epi, 512, wbs, colgroups=cgs)
                    B.linear("swout%d" % l, w_out[l], DFF, D, xsS(actS, "actS"),
                             resS(xin, xout, lambda ct: modv[:, l, 5 * KC + ct, 1:1 + NS]), 128, wbo)

                B.linear("swo", w_o, D, D, xsS(ygs, "ygs"), resS(xs0, x1s, lambda ct: modv[:, 0, 2 * KC + ct, 1:1 + NS]), 512, wbs)
                ffnS(0, x1s, x2s)
                normS(x2s, Gm[:, 4, :, 1:1 + NS], kvmv[:, 0:KC, 1:1 + NS], ["Gm", "kvmv"])
                knew = sb("knew", [128, NKV, NS]); vnew = sb("vnew", [128, KVT, NS])
                ksqS = sb("ksqS", [128, NS])
                for g in range(NKV):
                    buf = wbs[g % 2]; bkey = ("wb", buf.name)
                    for half in range(2):
                        P.dma("pool", buf[:, :, half * 64:(half + 1) * 64],
                              w_kv[:, g * 64:(g + 1) * 64].rearrange("(kc p) n -> p kc n", p=128), writes=[bkey])
                    bk, ps = B.bank()
                    for kc in range(KC):
                        mm(ps[:, 0:NS], buf[:, kc, 0:128], hS[:, kc, :], [bkey, "hS"], [bk], start=(kc == 0), stop=(kc == KC - 1))
                    cp("act", knew[:, g, :], ps[:, 0:NS], [bk], ["knew"])
                    act(ksqS[:], knew[:, g, :], AF.Square, ["knew"], ["ksqS"])
                    bk, ps = B.bank()
                    mm(ps[:, 0:NS], blkm[:], ksqS[:], ["c_blkm", "ksqS"], [bk])
                    act(ksqS[:], ps[:, 0:NS], AF.Ln, [bk], ["ksqS"], bias=1e-6)
                    act(ksqS[:], ksqS[:], AF.Exp, ["ksqS"], ["ksqS"], scale=-0.5)
                    tt("dve", knew[:, g, :], knew[:, g, :], ksqS[:], ALU.mult, ["knew", "ksqS"], ["knew"])
                    ts("dve", knew[:, g, :], knew[:, g, :], hdt[:, 0:1], ALU.mult, ["knew", "hdt"], ["knew"])

                def epiVs(ct, n0, m, g, ps, bk):
                    cp("act", vnew[0:m, ct, :], ps, [bk], ["vnew"])
                B.linear("swv", w_kv[:, KVW:2 * KVW], D, KVW, xsS(hS, "hS"), epiVs, 128, wbs)
                normS(x2s, Gm[:, 2, :, 1:1 + NS], modv[:, 1, 0:KC, 1:1 + NS], ["Gm", "modv"])
                qS = sb("qS", [128, KC, NS]); qz = sb("qz", [128, 2, KC, NS], BF16)
                P.op("pool", lambda E: E.memset(qz[:], 0.0), writes=["qz"])

                def epiQs(ct, n0, m, g, ps, bk):
                    cp("act", qS[:, ct, :], ps, [bk], ["qS"])
                B.linear("swq", w_q, D, D, xsS(hS, "hS"), epiQs, 512, wbs)
                act(f3(sqS), f3(qS), AF.Square, ["qS"], ["sqS"])
                bk, ps = B.bank()
                mm(ps[:, 0:KN], blkm[:], f3(sqS), ["c_blkm", "sqS"], [bk])
                act(f3(h32), ps[:, 0:KN], AF.Ln, [bk], ["h32"], bias=1e-6)
                act(f3(h32), f3(h32), AF.Exp, ["h32"], ["h32"], scale=-0.5)
                tt("dve", f3(qS), f3(qS), f3(h32), ALU.mult, ["qS", "h32"], ["qS"])
                for h in range(2):
                    hs = slice(h * 64, (h + 1) * 64)
                    ts("dve", qz[hs, h, :, :].rearrange("p k n -> p (k n)"), qS[hs, :, :].rearrange("p k n -> p (k n)"),
                       hdt[hs, 1:2], ALU.mult, ["qS", "hdt"], ["qz"])
                GQ = H // NKV
                PG = GQ // 2
                oatS = sb("oatS", [128, KC, NS], BF16)
                Kc2 = [sb("Kc2_%d" % i, [128, 128]) for i in range(2)]
                Vc2 = [sb("Vc2_%d" % i, [128, 128]) for i in range(2)]
                Vcb = [sb("Vcb_%d" % i, [128, 128], BF16) for i in range(2)]
                KcT = [sb("KcT_%d" % i, [128, 128], BF16) for i in range(2)]
                knb = sb("knb", [128, NKV, NS], BF16)
                cp("dve", knb[:], knew[:], ["knew"], ["knb"])
                Es = [sb("Es_%d" % i, [128, GQ], BF16) for i in range(2)]
                en = [sb("en_%d" % i, [1, GQ], BF16) for i in range(2)]
                vrow = [sb("vrow_%d" % i, [1, 128], BF16) for i in range(2)]
                krow = [sb("krow_%d" % i, [1, 128]) for i in range(2)]
                vrow32 = [sb("vrow32_%d" % i, [1, 128]) for i in range(2)]
                onesb = sb("onesb", [128, 128], BF16); cp("dve", onesb[:], ones[:], ["c_ones"], ["onesb"])
                dn = [sb("dn_%d" % i, [128, GQ]) for i in range(2)]
                ob = [sb("ob_%d" % i, [128, GQ]) for i in range(2)]
                it = 0
                for j in range(NS):
                    P.dma("sp", o_kwins[j, 0:127, :], ck[j, 1:128, :])
                    P.dma("sp", o_vwins[j, 0:127, :], cv[j, 1:128, :])
                    for g in range(NKV):
                        i2 = it % 2; it += 1
                        kc2, vc2, vcb, kct, E_, en_, vr, kr, vr32, dn_, ob_ = (Kc2[i2], Vc2[i2], Vcb[i2], KcT[i2], Es[i2], en[i2],
                                                                            vrow[i2], krow[i2], vrow32[i2], dn[i2], ob[i2])
                        for half in range(2):
                            P.dma("sp", kc2[:, half * 64:(half + 1) * 64], ck[j, :, g * 64:(g + 1) * 64], writes=[kc2.name])
                            P.dma("sp", vc2[:, half * 64:(half + 1) * 64], cv[j, :, g * 64:(g + 1) * 64], writes=[vc2.name])
                        cp("pool", vcb[:], vc2[:], [vc2.name], [vcb.name])
                        bk, ps = B.bank()
                        P.op("pe", lambda E, o=ps[:, 0:128], i_=kc2: E.transpose(o, i_[:], ident[:]), reads=[kc2.name, "c_ident"], writes=[bk])
                        cp("act", kct[:], ps[:, 0:128], [bk], [kct.name])
                        qg = qz[:, :, g * PG:(g + 1) * PG, j]
                        bk, ps = B.bank()
                        mm(ps[:, 0:GQ].rearrange("p (h c) -> p h c", h=2), kct[:], qg, [kct.name, "qz"], [bk])
                        act(E_[:], ps[:, 0:GQ], AF.Exp, [bk], [E_.name], scale=0.125)
                        ts("dve", E_[:], E_[:], msl[:, 0:1], ALU.mult, [E_.name, "c_sl"], [E_.name])
                        bk, ps = B.bank()
                        mm(ps[0:1, 0:GQ].rearrange("p (h c) -> p h c", h=2), knb[:, g, j:j + 1], qg, ["knb", "qz"], [bk])
                        act(en_[:], ps[0:1, 0:GQ], AF.Exp, [bk], [en_.name], scale=0.125)
                        bk, ps = B.bank()
                        P.op("pe", lambda E, o=ps[0:1, 0:128], g=g, j=j: E.transpose(o, knew[:, g, j:j + 1], ident[:]), reads=["knew", "c_ident"], writes=[bk])
                        P.op("pe", lambda E, o=ps[0:1, 128:256], g=g, j=j: E.transpose(o, vnew[:, g // 2, j:j + 1], ident[:]), reads=["vnew", "c_ident"], writes=[bk])
                        cp("act", kr[:], ps[0:1, 0:128], [bk], [kr.name])
                        go = (g % 2) * 64
                        cp("dve", vr32[:, 0:64], ps[0:1, 128 + go:128 + go + 64], [bk], [vr32.name])
                        cp("dve", vr32[:, 64:128], ps[0:1, 128 + go:128 + go + 64], [bk], [vr32.name])
                        cp("pool", vr[:], vr32[:], [vr32.name], [vr.name])
                        P.dma("sp", o_kwins[j, 127:128, g * 64:(g + 1) * 64], kr[:, 0:64], reads=[kr.name])
                        P.dma("sp", o_vwins[j, 127:128, g * 64:(g + 1) * 64], vr32[:, 0:64], reads=[vr32.name])
                        bkN, psN = B.bank()
                        mm(psN[:, 0:GQ], vcb[:], E_[:], [vcb.name, E_.name], [bkN], start=True, stop=False)
                        mm(psN[:, 0:GQ], vr[:], en_[:], [vr.name, en_.name], [bkN], start=False, stop=True)
                        bkD, psD = B.bank()
                        mm(psD[:, 0:GQ], onesb[:], E_[:], ["onesb", E_.name], [bkD], start=True, stop=False)
                        mm(psD[:, 0:GQ], onesb[0:1, :], en_[:], ["onesb", en_.name], [bkD], start=False, stop=True)
                        tt("dve", dn_[:].rearrange("p (h c) -> p h c", h=2), psD[:, 0:GQ].rearrange("p (h c) -> p h c", h=2),
                           esT[:, g * GQ:(g + 1) * GQ].rearrange("p (c h) -> p h c", h=2), ALU.add, [bkD, "esT"], [dn_.name])
                        P.op("dve", lambda E, d_=dn_: E.reciprocal(d_[:], d_[:]), reads=[dn_.name], writes=[dn_.name])
                        tt("dve", ob_[:], psN[:, 0:GQ], dn_[:], ALU.mult, [bkN, dn_.name], [ob_.name])
                        for h in range(2):
                            hs = slice(h * 64, (h + 1) * 64)
                            cp("pool", oatS[hs, g * PG:(g + 1) * PG, j], ob_[hs, h * PG:(h + 1) * PG], [ob_.name], ["oatS"])
                B.linear("swao", w_ao, D, D, xsS(oatS, "oatS"), resS(x2s, x3s, lambda ct: modv[:, 1, 2 * KC + ct, 1:1 + NS]), 512, wbs)
                ffnS(1, x3s, ysf)
                P.dma("sp", o_ys.rearrange("(kc p) n -> p kc n", p=128), ysf[:], reads=[ysf.name])
                P.dma("sp", o_convs.rearrange("l p f n c -> p l (f n c)"), cvo[:].rearrange("p l f n c -> p l (f n c)"), reads=["cvo"])
                P.barrier()
            B.es = es
        sample_path()
        B._st = dict(Gm=Gm, modv=modv, kvmv=kvmv, vec=vec, vec2=vec2, omm=omm, cvp=cvp, hdt=hdt, flag=flag,
                     esink=esink, ident=ident, ones=ones, blk=blk, blkm=blkm, m2b=m2b, mslb=mslb)
        return B, locals()


def finish(B, P, nc):
    P.finish_waits("sp")
    with nc.Block() as block:
        P.emit(block)


def host_inputs(inp, cfg, core):
    D, T, DFF, NS = cfg["D"], cfg["T"], cfg["DFF"], cfg["NS"]
    KC, FC, H = D // 128, DFF // 128, D // 64
    NP = H // 2
    NKV = max(1, H // 8)
    KVW = NKV * 64
    TH = T // 2
    b, hf = core // 2, core % 2
    f32 = np.float32
    m = {}
    xb = np.asarray(inp["x_prompt"][b], f32)
    xT = np.zeros((D, T), f32)
    mask = np.ones((128, T), f32)
    if hf == 1:
        xT[:] = xb.T
    else:
        xT[:, TH:] = xb[:TH].T
        mask[:, :TH] = 0.0
    m["xT"] = xT
    m["maskrow"] = mask
    m["flagcol"] = np.full((128, 1), float(hf), f32)
    ss = slice(core * NS, (core + 1) * NS)
    m["cT"] = np.ascontiguousarray(np.concatenate([inp["c_prompt"][b][None], inp["c_sample"][ss]], 0).T.astype(f32))
    m["xsT"] = np.ascontiguousarray(inp["x_sample"][ss, 0].T.astype(f32))
    m["shsT"] = np.ascontiguousarray(inp["state_shift"][0, ss].T.astype(f32))
    m["st_wkv"] = np.ascontiguousarray(inp["state_wkv"][0, ss].astype(f32))
    m["st_conv"] = np.ascontiguousarray(inp["state_conv"][:, ss].astype(f32).reshape(2, NS, 2, FC, 128).transpose(0, 4, 3, 1, 2))
    m["ck"] = np.ascontiguousarray(inp["cache_k_win"][ss].reshape(NS, 128, KVW).astype(f32))
    m["cv"] = np.ascontiguousarray(inp["cache_v_win"][ss].reshape(NS, 128, KVW).astype(f32))
    m.update(_consts())
    m["mod_w"] = inp["mod_w"]
    m["modb"] = np.stack([_fm(inp["mod_b"][l], 6 * KC) for l in range(2)])
    m["kv_mod_w"] = inp["kv_mod_w"]
    m["kvmodb"] = _fm(inp["kv_mod_b"], 2 * KC)
    vl = [inp["ln1_g"][0], inp["ln1_g"][1], inp["ln2_g"][0], inp["ln2_g"][1]] + \
         [inp["rwkv_mix"][0, i] for i in range(6)] + \
         [inp["rwkv_w0"][0], inp["rwkv_a0"][0], inp["rwkv_k_k"][0], inp["rwkv_k_a"][0],
          inp["rwkv_r_k"][0].reshape(-1), inp["rwkv_lnx_w"][0]]
    m["vecs"] = np.ascontiguousarray(np.stack([_fm(v, KC) for v in vl], 1))
    m["vecs2"] = np.ascontiguousarray(np.stack([_fm(inp["rwkv_lnx_b"][0], KC), _fm(inp["kv_norm_g"], KC)], 1))
    m["convp"] = np.ascontiguousarray(np.stack(
        [np.stack([_fm(inp["ffn_conv_w"][l, j], FC) for j in range(3)] + [_fm(inp["ffn_conv_b"][l], FC)], 1)
         for l in range(2)], 1))
    hd = np.zeros((128, 3 + NP), f32)
    hd[:, 0] = np.tile(inp["k_norm_g"], 2)
    hd[:, 1] = np.tile(inp["attn_q_norm_g"][0], 2)
    hd[:, 3:] = np.repeat(inp["attn_sinks"][0].reshape(NP, 2), 64, axis=1).T
    m["hd"] = hd
    m["sinkT"] = np.tile(inp["attn_sinks"][0][None, :], (128, 1))
    for k in ("rwkv_w1", "rwkv_w2", "rwkv_a1", "rwkv_a2", "rwkv_g1", "rwkv_g2", "rwkv_w_r", "rwkv_w_k",
              "rwkv_w_v", "rwkv_w_o", "attn_w_q", "attn_w_o"):
        m[k] = inp[k][0]
    m["w_kv"] = inp["w_kv"]
    m["ffn_w_in"] = inp["ffn_w_in"]
    m["ffn_w_out"] = inp["ffn_w_out"]
    return {k: np.ascontiguousarray(np.asarray(v, f32)) for k, v in m.items()}


_CACHE = {}


def kernel(**inputs):
    cfg = REAL_CFG
    inp = {k: np.asarray(v) for k, v in inputs.items()}
    if "prog" not in _CACHE:
        B, L = build(cfg)
        finish(B, B.P, B.nc)
        _CACHE["prog"] = B
    B = _CACHE["prog"]
    n = 8
    in_maps = []
    for c in range(n):
        m = host_inputs(inp, cfg, c)
        in_maps.append({k: m[k] for k in B.ins})
    res = run_bass_kernel_spmd(B.nc, in_maps, core_ids=list(range(n)))
    R = res.results
    D, T, DFF, NS = cfg["D"], cfg["T"], cfg["DFF"], cfg["NS"]
    KC, FC, H = D // 128, DFF // 128, D // 64
    NKV = max(1, H // 8)
    TH = T // 2
    NB = 4
    f32 = np.float32
    y_p = np.zeros((NB, T, D), f32); y_s = np.zeros((NB * 8, 1, D), f32)
    wkv_p = np.zeros((1, NB, H, 64, 64), f32); wkv_s = np.zeros((1, NB * 8, H, 64, 64), f32)
    sh_p = np.zeros((1, NB, D), f32); sh_s = np.zeros((1, NB * 8, D), f32)
    cv_p = np.zeros((2, NB, 2, DFF), f32); cv_s = np.zeros((2, NB * 8, 2, DFF), f32)
    kw_p = np.zeros((NB, 128, NKV, 64), f32); kw_s = np.zeros((NB * 8, 128, NKV, 64), f32)
    vw_p = np.zeros((NB, 128, NKV, 64), f32); vw_s = np.zeros((NB * 8, 128, NKV, 64), f32)
    for c in range(n):
        b, hf = c // 2, c % 2
        r = R[c]
        y_p[b, hf * TH:(hf + 1) * TH] = r["o_y"].T
        ss = slice(c * NS, (c + 1) * NS)
        y_s[ss, 0] = r["o_ys"].T
        wkv_s[0, ss] = r["o_wkvs"]
        sh_s[0, ss] = r["o_shifts"].transpose(2, 1, 0).reshape(NS, D)
        cv_s[:, ss] = r["o_convs"].transpose(0, 3, 4, 2, 1).reshape(2, NS, 2, DFF)
        kw_s[ss] = r["o_kwins"].reshape(NS, 128, NKV, 64)
        vw_s[ss] = r["o_vwins"].reshape(NS, 128, NKV, 64)
        if hf == 1:
            wkv_p[0, b] = r["o_wkv"]
            sh_p[0, b] = r["o_shift"].T.reshape(D)
            cv_p[:, b] = r["o_conv"].transpose(0, 3, 2, 1).reshape(2, 2, DFF)
            kw_p[b] = r["o_kwin"].reshape(128, NKV, 64)
            vw_p[b] = r["o_vwin"].reshape(128, NKV, 64)
    return (y_p, y_s, wkv_p, wkv_s, sh_p, sh_s, cv_p, cv_s, kw_p, kw_s, vw_p, vw_s)
```

```python
import contextlib
import numpy as np
import concourse.bass as bass
import concourse.mybir as mybir
from concourse.bass_utils import run_bass_kernel_spmd

F32 = mybir.dt.float32
BF16 = mybir.dt.bfloat16
AF = mybir.ActivationFunctionType
ALU = mybir.AluOpType

REAL_CFG = dict(D=4096, T=2048, DFF=14336, DL=128, DA=128, DG=480, NS=4, WB=128)


class Prog:
    ENG = ("pe", "act", "dve", "pool", "sp")

    def __init__(self, nc, es):
        self.nc = nc
        self.h = dict(pe=nc.tensor, act=nc.scalar, dve=nc.vector, pool=nc.gpsimd, sp=nc.sync)
        self.sem = {e: es.enter_context(nc.semaphore("s_" + e)) for e in self.ENG}
        self.cnt = {e: 0 for e in self.ENG}
        self.q = {e: [] for e in self.ENG}
        self.waited = {e: {} for e in self.ENG}
        self.lastw = {}
        self.readers = {}
        self.semobj = {("c", e): self.sem[e] for e in self.ENG}
        self.dsem = {}
        for e in ("sp", "pool", "act"):
            self.dsem[e] = [es.enter_context(nc.semaphore("d_%s%d" % (e, i))) for i in range(12)]
            for i, s in enumerate(self.dsem[e]):
                self.semobj[("d", e, i)] = s
        self.dval = {k: 0 for k in self.semobj if k[0] == "d"}
        self.drr = {e: 0 for e in ("sp", "pool", "act")}
        self.nbank = 0

    def _deps(self, reads, writes):
        need = {}
        for r in reads:
            lw = self.lastw.get(r)
            if lw:
                need[lw[0]] = max(need.get(lw[0], 0), lw[1])
        for w in writes:
            lw = self.lastw.get(w)
            if lw:
                need[lw[0]] = max(need.get(lw[0], 0), lw[1])
            for k, v in self.readers.get(w, {}).items():
                need[k] = max(need.get(k, 0), v)
        return need

    def _emit_waits(self, eng, need):
        for k, v in need.items():
            if eng == "pe" and k == ("c", "pe"):
                continue
            if self.waited[eng].get(k, 0) < v:
                self.waited[eng][k] = v
                so = self.semobj[k]
                self.q[eng].append(lambda E, so=so, v=v: E.wait_ge(so, v))

    def _mark(self, tok, reads, writes):
        for w in writes:
            self.lastw[w] = tok
            self.readers[w] = {}
        for r in reads:
            self.readers.setdefault(r, {})
            d = self.readers[r]
            d[tok[0]] = max(d.get(tok[0], 0), tok[1])

    def op(self, eng, fn, reads=(), writes=()):
        bk_ = [r for r in reads if isinstance(r, tuple) and r and r[0] == "bank"]
        if bk_:
            writes = list(writes) + bk_
        need = self._deps(reads, writes)
        self._emit_waits(eng, need)
        self.cnt[eng] += 1
        idx = self.cnt[eng]
        so = self.sem[eng]
        self.q[eng].append(lambda E, fn=fn, so=so: fn(E).then_inc(so, 1))
        self.waited[eng][("c", eng)] = max(self.waited[eng].get(("c", eng), 0), 0)
        self._mark((("c", eng), idx), reads, writes)

    def dma(self, eng, out, in_, reads=(), writes=()):
        need = self._deps(reads, writes)
        i = self.drr[eng]
        self.drr[eng] = (i + 1) % len(self.dsem[eng])
        key = ("d", eng, i)
        if self.dval[key] > 0:
            need[key] = max(need.get(key, 0), self.dval[key])
        self._emit_waits(eng, need)
        self.dval[key] += 16
        so = self.semobj[key]
        self.q[eng].append(lambda E, so=so, out=out, in_=in_: E.dma_start(out=out, in_=in_).then_inc(so, 16))
        self._mark((key, self.dval[key]), reads, writes)

    def finish_waits(self, eng="sp"):
        need = {}
        for k, v in self.dval.items():
            if v:
                need[k] = v
        for e in self.ENG:
            if self.cnt[e] and e != eng:
                need[("c", e)] = self.cnt[e]
        self._emit_waits(eng, need)

    def barrier(self):
        for e in self.ENG:
            self.finish_waits(e)

    def emit(self, block):
        for e, reg in (("pe", block.tensor), ("act", block.scalar), ("dve", block.vector),
                       ("pool", block.gpsimd), ("sp", block.sync)):
            ops = self.q[e]

            def body(E, ops=ops):
                for f in ops:
                    f(E)
            reg(body)


def _chunks(n, c=128):
    return [(i, min(c, n - i)) for i in range(0, n, c)]


class Builder:
    def __init__(self, cfg, debug=()):
        self.cfg = cfg
        self.debug = set(debug)
        c = cfg
        self.D, self.T, self.DFF = c["D"], c["T"], c["DFF"]
        self.KC = self.D // 128
        self.FC = self.DFF // 128
        self.H = self.D // 64
        self.NP = self.H // 2
        self.NKV = max(1, self.H // 8)
        self.NS = c["NS"]
        self.TH = self.T // 2
        self.HALO = 130
        self.W0 = self.TH - self.HALO
        self.NW = self.T - self.W0
        self.nc = bass.Bass("TRN2", target_bir_lowering=False)
        self.ins = {}
        self.outs = {}

    def din(self, name, shape):
        t = self.nc.dram_tensor(name, list(shape), F32, kind="ExternalInput").ap()
        self.ins[name] = tuple(shape)
        return t

    def dout(self, name, shape, dt=F32):
        t = self.nc.dram_tensor(name, list(shape), dt, kind="ExternalOutput").ap()
        self.outs[name] = tuple(shape)
        return t

    def scratch(self, name, shape, dt=F32):
        if name in self.debug:
            return self.dout(name, shape, dt)
        return self.nc.dram_tensor(name, list(shape), dt, kind="Internal").ap()

    def sb(self, name, shape, dt=F32):
        return self.es.enter_context(self.nc.sbuf_tensor("t_" + name, list(shape), dt))

    def linear(self, name, w, K, N, xs, epi, gw, wb, colgroups=None):
        P = self.P
        kch = _chunks(K)
        nk = len(kch)
        nfull = K // 128
        for gi, (n0, gsz) in enumerate(colgroups or _chunks(N, gw)):
            buf = wb[gi % len(wb)]
            bkey = ("wb", buf.name)
            if nfull:
                P.dma("pool", buf[:, 0:nfull, 0:gsz],
                      w[0:nfull * 128, n0:n0 + gsz].rearrange("(kc p) n -> p kc n", p=128),
                      writes=[bkey])
            if nfull < nk:
                k0, ksz = kch[-1]
                P.dma("pool", buf[0:ksz, nfull, 0:gsz], w[k0:k0 + ksz, n0:n0 + gsz], writes=[bkey])
            for (c0, m) in _chunks(gsz):
                for g, (xf, ncols, rk) in enumerate(xs):
                    bk, ps = self.bank()
                    for ki, (k0, ksz) in enumerate(kch):
                        lhsT = buf[0:ksz, ki, c0:c0 + m]
                        rhs = xf(ki, ksz)
                        P.op("pe", lambda E, o=ps[0:m, 0:ncols], l=lhsT, r=rhs, s=(ki == 0), t=(ki == nk - 1):
                             E.matmul(o, l, r, start=s, stop=t),
                             reads=[bkey] + list(rk), writes=[bk])
                    epi((n0 + c0) // 128, n0 + c0, m, g, ps[0:m, 0:ncols], bk)

    def bank(self):
        i = self.P.nbank % len(self.banks)
        self.P.nbank += 1
        return ("bank", i), self.banks[i]

    def act(self, out, in_, func, R, W, bias=None, scale=None, eng="act"):
        kw = {}
        if bias is not None:
            kw["bias"] = bias
        if scale is not None:
            kw["scale"] = scale
        self.P.op("act", lambda E: E.activation(out, in_, func, **kw), reads=R, writes=W)

    def ts(self, eng, out, in0, s1, op0, R, W, s2=None, op1=None):
        if op1 is None:
            self.P.op(eng, lambda E: E.tensor_scalar(out, in0, s1, None, op0), reads=R, writes=W)
        else:
            self.P.op(eng, lambda E: E.tensor_scalar(out, in0, s1, s2, op0, op1), reads=R, writes=W)

    def tt(self, eng, out, a, b, op, R, W):
        self.P.op(eng, lambda E: E.tensor_tensor(out, a, b, op), reads=R, writes=W)

    def stt(self, out, in0, scalar, in1, op0, op1, R, W):
        self.P.op("dve", lambda E: E.scalar_tensor_tensor(out, in0, scalar, in1, op0, op1), reads=R, writes=W)

    def cp(self, eng, out, in_, R, W):
        if eng == "act":
            self.P.op("act", lambda E: E.activation(out, in_, AF.Copy), reads=R, writes=W)
        else:
            self.P.op(eng, lambda E: E.tensor_copy(out, in_), reads=R, writes=W)

    def mm(self, out, lhsT, rhs, R, W, start=True, stop=True):
        self.P.op("pe", lambda E: E.matmul(out, lhsT, rhs, start=start, stop=stop), reads=R, writes=W)

    def dump(self, name, ap, key, shape):
        if ("dbg_" + name) in self.debug:
            dd = self.dout("dbg_" + name, shape)
            self.P.dma("sp", dd, ap, reads=[key])

    def load(self, tile_ap, dram_ap, key, q="sp"):
        self.P.dma(q, tile_ap, dram_ap, writes=[key])


def _fm(v, n):
    return np.ascontiguousarray(np.asarray(v, np.float32).reshape(n, 128).T)


def _consts():
    p = np.arange(128)[:, None]
    f = np.arange(128)[None, :]
    c = {}
    c["c_ident"] = (p == f).astype(np.float32)
    c["c_ones"] = np.ones((128, 128), np.float32)
    blk = ((p // 64) == (f // 64)).astype(np.float32)
    c["c_blk"] = blk
    c["c_blkm"] = blk / 64.0
    su = (p < f).astype(np.float32)
    ui = (p <= f).astype(np.float32)
    c["c_m2"] = np.concatenate([su, ui], axis=1)
    c["c_sl"] = (p > f).astype(np.float32)
    c["c_bd8"] = ((p // 8) == (f // 8)).astype(np.float32)
    lvL, lvU = [], []
    for m in (8, 16, 32, 64):
        ml = (((p // (2 * m)) == (f // (2 * m))) & ((p % (2 * m)) >= m) & ((f % (2 * m)) < m)).astype(np.float32)
        lvL.append(ml); lvU.append(ml.T)
    c["c_lvL"] = np.concatenate(lvL, axis=1)
    c["c_lvU"] = np.concatenate(lvU, axis=1)
    return c


def build(cfg, debug=()):
    B = Builder(cfg, debug)
    nc = B.nc
    D, T, DFF, KC, FC, H, NP, NKV, NS = B.D, B.T, B.DFF, B.KC, B.FC, B.H, B.NP, B.NKV, B.NS
    DL, DA, DG = cfg["DL"], cfg["DA"], cfg["DG"]
    TH = T // 2
    HALO = 258
    W0 = TH - HALO
    NW = T - W0
    KVW = NKV * 64
    KVT = max(1, KVW // 128)
    NC5 = 1 + NS
    C = 128
    NCH = T // C
    NTF = T // 4
    din, dout = B.din, B.dout

    xT = din("xT", [D, T])
    maskrow = din("maskrow", [128, T])
    flagcol = din("flagcol", [128, 1])
    cT = din("cT", [D, NC5])
    xsT = din("xsT", [D, NS])
    shsT = din("shsT", [D, NS])
    st_wkv = din("st_wkv", [NS, H, 64, 64])
    st_conv = din("st_conv", [2, 128, FC, NS, 2])
    ck = din("ck", [NS, 128, KVW])
    cv = din("cv", [NS, 128, KVW])
    cst = {k: din(k, list(v.shape)) for k, v in _consts().items()}
    mod_w = din("mod_w", [2, D, 6 * D])
    modb = din("modb", [2, 128, 6 * KC])
    kv_mod_w = din("kv_mod_w", [D, 2 * D])
    kvmodb = din("kvmodb", [128, 2 * KC])
    vecs = din("vecs", [128, 16, KC])
    vecs2 = din("vecs2", [128, 2, KC])
    convp = din("convp", [128, 2, 4, FC])
    hd = din("hd", [128, 3 + NP])
    sinkT = din("sinkT", [128, H])
    w1 = din("rwkv_w1", [D, DL]); w2 = din("rwkv_w2", [DL, D])
    a1 = din("rwkv_a1", [D, DA]); a2 = din("rwkv_a2", [DA, D])
    g1 = din("rwkv_g1", [D, DG]); g2 = din("rwkv_g2", [DG, D])
    w_r = din("rwkv_w_r", [D, D]); w_k = din("rwkv_w_k", [D, D]); w_v = din("rwkv_w_v", [D, D])
    w_o = din("rwkv_w_o", [D, D])
    w_kv = din("w_kv", [D, 2 * KVW])
    w_q = din("attn_w_q", [D, D]); w_ao = din("attn_w_o", [D, D])
    w_in = din("ffn_w_in", [2, D, 2 * DFF]); w_out = din("ffn_w_out", [2, DFF, D])

    o_y = dout("o_y", [D, T - TH])
    o_ys = dout("o_ys", [D, NS])
    o_wkv = dout("o_wkv", [H, 64, 64])
    o_wkvs = dout("o_wkvs", [NS, H, 64, 64])
    o_shift = dout("o_shift", [128, KC])
    o_shifts = dout("o_shifts", [128, KC, NS])
    o_conv = dout("o_conv", [2, 128, FC, 2])
    o_convs = dout("o_convs", [2, 128, FC, NS, 2])
    o_kwin = dout("o_kwin", [128, KVW])
    o_vwin = dout("o_vwin", [128, KVW])
    o_kwins = dout("o_kwins", [NS, 128, KVW])
    o_vwins = dout("o_vwins", [NS, 128, KVW])

    sc = {n: B.scratch("s_" + n, [D, T]) for n in ("r", "k", "v", "ld", "a", "g")}
    scs = {n: B.scratch("ss_" + n, [D, NS]) for n in ("r", "k", "v", "ld", "a", "g")}
    s_yg = B.scratch("s_yg", [D, T], BF16)
    s_ygs = B.scratch("s_ygs", [D, NS], BF16)

    with contextlib.ExitStack() as es:
        B.es = es
        P = B.P = Prog(nc, es)
        B.banks = [es.enter_context(nc.psum_tensor("ps%d" % i, [128, 512], F32)) for i in range(8)]
        sb = B.sb
        act, ts, tt, stt, cp, mm = B.act, B.ts, B.tt, B.stt, B.cp, B.mm

        K_ = {}
        for k, v in cst.items():
            if k in ("c_bd8", "c_lvL", "c_lvU"):
                continue
            K_[k] = sb(k, list(B.ins[k]))
            P.dma("sp", K_[k][:], v[:, :], writes=[k])
        ident, ones, blk, blkm, m2, msl = (K_[k] for k in ("c_ident", "c_ones", "c_blk", "c_blkm", "c_m2", "c_sl"))
        m2b = sb("m2b", [128, 256], BF16); cp("dve", m2b[:], m2[:], ["c_m2"], ["m2b"])
        mslb = sb("mslb", [128, 128], BF16); cp("dve", mslb[:], msl[:], ["c_sl"], ["mslb"])
        vec = sb("vec", [128, 16, KC]); P.dma("sp", vec[:], vecs[:, :, :], writes=["vec"])
        vec2 = sb("vec2", [128, 2, KC]); P.dma("sp", vec2[:], vecs2[:, :, :], writes=["vec2"])
        cvp = sb("cvp", [128, 2, 4, FC]); P.dma("sp", cvp[:], convp[:, :, :, :], writes=["cvp"])
        hdt = sb("hdt", [128, 3 + NP]); P.dma("sp", hdt[:], hd[:, :], writes=["hdt"])
        flag = sb("flag", [128, 1]); P.dma("sp", flag[:], flagcol[:, :], writes=["flag"])
        modbt = sb("modbt", [128, 2, 6 * KC]); P.dma("sp", modbt[:], modb.rearrange("l p n -> p l n"), writes=["modbt"])
        kvmodbt = sb("kvmodbt", [128, 2 * KC]); P.dma("sp", kvmodbt[:], kvmodb[:, :], writes=["kvmodbt"])
        omm = sb("omm", [128, 6, KC])
        ts("dve", omm[:], vec[:, 4:10, :], -1.0, ALU.mult, ["vec"], ["omm"], 1.0, ALU.add)
        esink = sb("esink", [128, NP])
        act(esink[:], hdt[:, 3:3 + NP], AF.Exp, ["hdt"], ["esink"])
        modv = sb("modv", [128, 2, 6 * KC, NC5])
        kvmv = sb("kvmv", [128, 2 * KC, NC5])
        Gm = sb("Gm", [128, 5, KC, NC5])

        with contextlib.ExitStack() as es2:
            B.es = es2
            c5 = sb("c5", [128, KC, NC5]); P.dma("sp", c5[:], cT.rearrange("(kc p) n -> p kc n", p=128), writes=["c5"])
            c5s = sb("c5s", [128, KC, NC5])
            act(c5s[:], c5[:], AF.Sigmoid, ["c5"], ["c5s"])
            c5b = sb("c5b", [128, KC, NC5], BF16)
            tt("dve", c5b[:], c5[:], c5s[:], ALU.mult, ["c5", "c5s"], ["c5b"])
            wbA = [sb("wbA%d" % i, [128, KC, 512], BF16) for i in range(2)]
            xsA = [(lambda ki, ksz: c5b[0:ksz, ki, :], NC5, ["c5b"])]
            for l in range(2):
                def epiA(ct, n0, m, g, ps, bk, l=l):
                    act(modv[:, l, ct, :], ps, AF.Identity, [bk, "modbt"], ["modv"], bias=modbt[:, l, ct:ct + 1])
                B.linear("modw%d" % l, mod_w[l], D, 6 * D, xsA, epiA, 512, wbA)

            def epiK(ct, n0, m, g, ps, bk):
                act(kvmv[:, ct, :], ps, AF.Identity, [bk, "kvmodbt"], ["kvmv"], bias=kvmodbt[:, ct:ct + 1])
            B.linear("kvmodw", kv_mod_w, D, 2 * D, xsA, epiK, 512, wbA)
            for j in range(NC5):
                for gi, (l, which, vi) in enumerate(((0, 1, 0), (0, 4, 2), (1, 1, 1), (1, 4, 3))):
                    stt(Gm[:, gi, :, j], modv[:, l, which * KC:(which + 1) * KC, j], 1.0, vec[:, vi, :],
                        ALU.add, ALU.mult, ["modv", "vec"], ["Gm"])
                stt(Gm[:, 4, :, j], kvmv[:, KC:2 * KC, j], 1.0, vec2[:, 1, :], ALU.add, ALU.mult,
                    ["kvmv", "vec2"], ["Gm"])
            P.barrier()
        B.es = es
        shiftout = sb("shiftout", [128, KC])
        hlast = sb("hlast", [128, KC], BF16)
        P.op("pool", lambda E: E.memset(hlast[:], 0.0), writes=["hlast"])
        LDC = -float(np.exp(-0.5))

        def front(cols, xsrc, dst, per_col, hprev_src=None, tag="p"):
            grp = _chunks(cols, 512)
            with contextlib.ExitStack() as es3:
                B.es = es3
                hb = sb("hb" + tag, [128, KC, cols + 1], BF16)
                xis = [sb("xi%s%d" % (tag, i), [128, KC, cols], BF16) for i in range(2)]
                xs_ = [sb("xs%s%d" % (tag, i), [128, cols]) for i in range(2)]
                sq = [sb("sq%s%d" % (tag, i), [128, cols]) for i in range(2)]
                rstd = sb("rstd" + tag, [128, cols])
                hf = [sb("hf%s%d" % (tag, i), [128, cols]) for i in range(2)]
                og = [sb("og%s%d" % (tag, i), [128, 512]) for i in range(3)]
                wbF = [sb("wbF%s%d" % (tag, i), [128, KC, 256], BF16) for i in range(2)]
                wbS = [sb("wbS%s%d" % (tag, i), [128, 4, 512], BF16) for i in range(2)]
                lo = sb("lo" + tag, [128, 4, cols], BF16)
                bks = [B.bank() for _ in grp]
                for kc in range(KC):
                    x_ = xs_[kc % 2]; q_ = sq[kc % 2]
                    P.dma("sp", x_[:], xsrc(kc), writes=[x_.name])
                    act(q_[:], x_[:], AF.Square, [x_.name], [q_.name])
                    for gi, (c0, n) in enumerate(grp):
                        mm(bks[gi][1][:, 0:n], ones[:], q_[:, c0:c0 + n], [q_.name, "c_ones"], [bks[gi][0]],
                           start=(kc == 0), stop=(kc == KC - 1))
                for gi, (c0, n) in enumerate(grp):
                    act(rstd[:, c0:c0 + n], bks[gi][1][:, 0:n], AF.Ln, [bks[gi][0]], ["rstd"], scale=1.0 / D, bias=1e-6)
                if ("dbg_" + tag) in B.debug:
                    dd = B.dout("dbg_ln" + tag, [128, cols])
                    P.dma("sp", dd[:, :], rstd[:], reads=["rstd"])
                    dd2 = B.dout("dbg_sq" + tag, [128, cols])
                    P.dma("sp", dd2[:, :], sq[(KC - 1) % 2][:], reads=[sq[(KC - 1) % 2].name])
                act(rstd[:], rstd[:], AF.Exp, ["rstd"], ["rstd"], scale=-0.5)
                if ("dbg_" + tag) in B.debug:
                    dd = B.dout("dbg_rstd" + tag, [128, cols])
                    P.dma("sp", dd[:, :], rstd[:], reads=["rstd"])
                if not per_col:
                    cp("pool", hb[:, :, 0], hlast[:], ["hlast"], ["hb"])
                for kc in range(KC):
                    x_ = xs_[kc % 2]; h_ = hf[kc % 2]
                    P.dma("sp", x_[:], xsrc(kc), writes=[x_.name])
                    tt("dve", h_[:], x_[:], rstd[:], ALU.mult, [x_.name, "rstd"], [h_.name])
                    if kc == KC - 1:
                        B.dump("xn" + tag, h_[:], h_.name, [128, cols])
                        B.dump("xx" + tag, x_[:], x_.name, [128, cols])
                    if per_col:
                        tt("dve", h_[:], h_[:], Gm[:, 0, kc, 1:1 + NS], ALU.mult, [h_.name, "Gm"], [h_.name])
                        tt("dve", h_[:], h_[:], modv[:, 0, 0 * KC + kc, 1:1 + NS], ALU.add, [h_.name, "modv"], [h_.name])
                        cp("pool", shiftouts[:, kc, :], h_[:], [h_.name], ["shiftouts"])
                        cp("pool", hb[:, kc, 1:cols + 1], h_[:], [h_.name], ["hb"])
                    else:
                        act(h_[:], h_[:], AF.Identity, [h_.name, "Gm", "modv"], [h_.name],
                            bias=modv[:, 0, kc, 0:1], scale=Gm[:, 0, kc, 0:1])
                        if kc == KC - 1:
                            B.dump("hh" + tag, h_[:], h_.name, [128, cols])
                        cp("pool", shiftout[:, kc:kc + 1], h_[:, cols - 1:cols], [h_.name], ["shiftout"])
                        tt("dve", hb[:, kc, 1:cols + 1], h_[:], mrow[:, 0:cols], ALU.mult, [h_.name, "mrow"], ["hb"])
                if per_col:
                    hps = sb("hps", [128, KC, NS])
                    P.dma("sp", hps[:], shsT.rearrange("(kc p) n -> p kc n", p=128), writes=["hps"])
                else:
                    cp("pool", hlast[:], hb[:, :, cols], ["hb"], ["hlast"])

                def hprev(kc):
                    if per_col:
                        return hps[:, kc, :]
                    return hb[:, kc, 0:cols]

                def mix(i):
                    xi = xis[i % 2]
                    for kc in range(KC):
                        t_ = sq[kc % 2]
                        act(t_[:], hprev(kc), AF.Copy, ["hb", "hps", "vec"], [t_.name], scale=vec[:, 4 + i, kc:kc + 1])
                        stt(xi[:, kc, :], hb[:, kc, 1:cols + 1], omm[:, i, kc:kc + 1], t_[:], ALU.mult, ALU.add,
                            ["hb", "omm", t_.name], [xi.name])
                    return [(lambda ki, ksz, c0=c0, n=n, xi=xi: xi[0:ksz, ki, c0:c0 + n], n, [xi.name]) for (c0, n) in grp]
                xsL = [(lambda ki, ksz, c0=c0, n=n: lo[0:ksz, ki, c0:c0 + n], n, ["lo"]) for (c0, n) in grp]
                ogi = [0]

                def store(name, func, bias_v=None, post=None):
                    def epi(ct, n0, m, g, ps, bk):
                        o_ = og[ogi[0] % 3]; ogi[0] += 1
                        c0, n = grp[g]
                        if func is None:
                            cp("act", o_[0:m, 0:n], ps, [bk], [o_.name])
                        else:
                            act(o_[0:m, 0:n], ps, func, [bk, "vec"], [o_.name],
                                bias=(vec[:, bias_v, ct:ct + 1] if bias_v is not None else None))
                        if post is not None:
                            ts("dve", o_[0:m, 0:n], o_[0:m, 0:n], post, ALU.mult, [o_.name], [o_.name])
                        P.dma("sp", dst[name](ct, c0, n), o_[0:m, 0:n], reads=[o_.name])
                    return epi

                def tolo(func):
                    def epi(ct, n0, m, g, ps, bk):
                        c0, n = grp[g]
                        if func is None:
                            cp("act", lo[0:m, ct, c0:c0 + n], ps, [bk], ["lo"])
                        else:
                            act(lo[0:m, ct, c0:c0 + n], ps, func, [bk], ["lo"])
                    return epi
                for i, (nm, w_) in enumerate((("r", w_r), ("ld", None), ("k", w_k), ("v", w_v), ("a", None), ("g", None))):
                    xsX = mix(i)
                    if w_ is not None:
                        B.linear(nm, w_, D, D, xsX, store(nm, None), 256, wbF)
                    elif nm == "ld":
                        B.linear("w1", w1, D, DL, xsX, tolo(AF.Tanh), 256, wbF)
                        B.linear("w2", w2, DL, D, xsL, store("ld", AF.Sigmoid, 10, LDC), 512, wbS)
                    elif nm == "a":
                        B.linear("a1", a1, D, DA, xsX, tolo(None), 256, wbF)
                        B.linear("a2", a2, DA, D, xsL, store("a", AF.Sigmoid, 11), 512, wbS)
                    else:
                        B.linear("g1", g1, D, DG, xsX, tolo(AF.Sigmoid), 256, wbF)
                        B.linear("g2", g2, DG, D, xsL, store("g", None), 512, wbS)
                P.barrier()
            B.es = es

        mrow = sb("mrow", [128, NTF])
        shiftouts = sb("shiftouts", [128, KC, NS])
        for p_ in range(T // NTF):
            P.dma("sp", mrow[:], maskrow[:, p_ * NTF:(p_ + 1) * NTF], writes=["mrow"])
            dstp = {n: (lambda ct, c0, n_, n=n, p_=p_: sc[n][ct * 128:(ct + 1) * 128, p_ * NTF + c0:p_ * NTF + c0 + n_])
                    for n in sc}
            front(NTF, lambda kc, p_=p_: xT[kc * 128:(kc + 1) * 128, p_ * NTF:(p_ + 1) * NTF], dstp, False, tag="p%d" % p_)
        P.dma("sp", o_shift[:, :], shiftout[:], reads=["shiftout"])
        dsts = {n: (lambda ct, c0, n_, n=n: scs[n][ct * 128:(ct + 1) * 128, c0:c0 + n_]) for n in scs}
        front(NS, lambda kc: xsT[kc * 128:(kc + 1) * 128, :], dsts, True, tag="s")
        P.dma("sp", o_shifts[:, :, :], shiftouts[:], reads=["shiftouts"])
        B.dump("gm_end", Gm[:].rearrange("p a k n -> p (a k n)"), "Gm", [128, 5 * KC * NC5])
        B.dump("modv_end", modv[:].rearrange("p a k n -> p (a k n)"), "modv", [128, 2 * 6 * KC * NC5])
        GN_EPS = 64e-5
        omka = sb("omka", [128, KC])
        ts("dve", omka[:], vec[:, 13, :], -1.0, ALU.mult, ["vec"], ["omka"], 1.0, ALU.add)
        onesC = sb("onesC", [128, C]); P.op("pool", lambda E: E.memset(onesC[:], 1.0), writes=["onesC"])
        m2b2 = sb("m2b2", [128, 2, 256], BF16)
        for h_ in range(2):
            cp("dve", m2b2[:, h_, :], m2[:], ["c_m2"], ["m2b2"])
        msl2 = sb("msl2", [128, 2, 128], BF16)
        idb2 = sb("idb2", [128, 2, 128], BF16)
        for h_ in range(2):
            cp("dve", msl2[:, h_, :], msl[:], ["c_sl"], ["msl2"])
            cp("dve", idb2[:, h_, :], ident[:], ["c_ident"], ["idb2"])

        def diag2(t3):
            return t3[:].rearrange("p h (a b) -> p (h a) b", b=64)[:, 0:4:3, :]

        def scan_stage():
            with contextlib.ExitStack() as es4:
                B.es = es4
                GP = min(4, NP)
                NSLOT = 4
                Sm = sb("Sm", [128, NP, 64])
                Sblk = sb("Sblk", [128, NP, 128], BF16)
                bd8_2 = sb("bd8_2", [128, 2, 128], BF16)
                lvL2 = sb("lvL2", [128, 4, 2, 128], BF16); lvU2 = sb("lvU2", [128, 4, 2, 128], BF16)
                ctmp = sb("ctmp", [128, 1152])
                P.dma("sp", ctmp[:, 0:128], cst["c_bd8"][:, :], writes=["ctmp"])
                P.dma("sp", ctmp[:, 128:640], cst["c_lvL"][:, :], writes=["ctmp"])
                P.dma("sp", ctmp[:, 640:1152], cst["c_lvU"][:, :], writes=["ctmp"])
                for h_ in range(2):
                    cp("dve", bd8_2[:, h_, :], ctmp[:, 0:128], ["ctmp"], ["bd8_2"])
                    for l_ in range(4):
                        cp("dve", lvL2[:, l_, h_, :], ctmp[:, 128 + l_ * 128:128 + (l_ + 1) * 128], ["ctmp"], ["lvL2"])
                        cp("dve", lvU2[:, l_, h_, :], ctmp[:, 640 + l_ * 128:640 + (l_ + 1) * 128], ["ctmp"], ["lvU2"])
                lds = {n: [sb("ld_%s%d" % (n, i), [128, GP, C]) for i in range(2)] for n in ("r", "k", "v", "ld", "a", "g")}
                SL = []
                for sl in range(NSLOT):
                    d = {}
                    for n in ("kkr", "sqk", "rn", "kk", "t1", "k2", "bet", "L", "eL", "enL", "ePrev", "eRev", "tmp",
                              "bts", "kts", "ysb", "mu2", "var", "rs", "yn", "rk", "bon"):
                        d[n] = sb("%s_%d" % (n, sl), [128, C])
                    d["gC"] = sb("gC_%d" % sl, [128, 1])
                    d["YY"] = sb("YY_%d" % sl, [128, 2 * C])
                    d["AR"] = sb("AR_%d" % sl, [128, 2 * C], BF16)
                    for n in ("Bbp", "Kbp", "Pm", "Q0", "Q1", "P0", "P1", "T0", "T1", "Vtp", "Btp", "Ktp", "Up",
                              "QA", "PA", "Dq0", "Dq1", "Dp0", "Dp1", "Y1", "Z1", "mt1", "mt2"):
                        d[n] = sb("%s_%d" % (n, sl), [128, 2, 128], BF16)
                    d["QM"] = sb("QM_%d" % sl, [128, 2, 2 * C], BF16)
                    d["KM"] = sb("KM_%d" % sl, [128, 2, 2 * C], BF16)
                    d["RH"] = sb("RH_%d" % sl, [128, 2, 64], BF16)
                    d["yg"] = sb("yg_%d" % sl, [128, C], BF16)
                    for n in ("Bbp", "Kbp", "Vtp", "Btp", "Ktp", "Up"):
                        P.op("pool", lambda E, t=d[n]: E.memset(t[:], 0.0), writes=[d[n].name])
                    SL.append(d)
                P.op("pool", lambda E: E.memset(Sm[:], 0.0), writes=[("Sm", q) for q in range(NP)])
                P.op("pool", lambda E: E.memset(Sblk[:], 0.0), writes=[("S", q) for q in range(NP)])
                cnt = 0
                for j in range(NCH):
                    for q0 in range(0, NP, GP):
                        gb = (cnt % 2); cnt += 1
                        for n in lds:
                            P.dma("sp", lds[n][gb][:], sc[n][q0 * 128:(q0 + GP) * 128, j * C:(j + 1) * C]
                                  .rearrange("(q p) t -> p q t", p=128), writes=[lds[n][gb].name])
                        def unit(qi, j=j, gb=gb, q0=q0):
                            hp = q0 + qi
                            W = SL[hp % NSLOT]
                            N_ = lambda n: W[n].name
                            r_, k_, v_, ld_, a_, g_ = (lds[n][gb][:, qi, :] for n in ("r", "k", "v", "ld", "a", "g"))
                            R6 = {n: lds[n][gb].name for n in lds}
                            pk = ("S", hp)
                            ts("dve", W["kkr"][:], k_, vec[:, 12, hp:hp + 1], ALU.mult, [R6["k"], "vec"], [N_("kkr")])
                            act(W["sqk"][:], W["kkr"][:], AF.Square, [N_("kkr")], [N_("sqk")])
                            bk, ps = B.bank()
                            mm(ps[:, 0:C], blk[:], W["sqk"][:], ["c_blk", N_("sqk")], [bk])
                            ts("dve", W["rn"][:], ps[:, 0:C], 1e-24, ALU.max, [bk], [N_("rn")])
                            act(W["rn"][:], W["rn"][:], AF.Ln, [N_("rn")], [N_("rn")])
                            act(W["rn"][:], W["rn"][:], AF.Exp, [N_("rn")], [N_("rn")], scale=-0.5)
                            tt("dve", W["kk"][:], W["kkr"][:], W["rn"][:], ALU.mult, [N_("kkr"), N_("rn")], [N_("kk")])
                            act(W["t1"][:], a_, AF.Identity, [R6["a"], "vec", "omka"], [N_("t1")],
                                bias=omka[:, hp:hp + 1], scale=vec[:, 13, hp:hp + 1])
                            tt("pool", W["k2"][:], k_, W["t1"][:], ALU.mult, [R6["k"], N_("t1")], [N_("k2")])
                            tt("pool", W["bet"][:], W["kk"][:], a_, ALU.mult, [N_("kk"), R6["a"]], [N_("bet")])
                            P.op("dve", lambda E, o=W["L"][:], d1=ld_: E.tensor_tensor_scan(o, onesC[:], d1, 0.0, ALU.mult, ALU.add),
                                 reads=["onesC", R6["ld"]], writes=[N_("L")])
                            act(W["eL"][:], W["L"][:], AF.Exp, [N_("L")], [N_("eL")])
                            act(W["enL"][:], W["L"][:], AF.Exp, [N_("L")], [N_("enL")], scale=-1.0)
                            tt("pool", W["tmp"][:], W["L"][:], ld_, ALU.subtract, [N_("L"), R6["ld"]], [N_("tmp")])
                            act(W["ePrev"][:], W["tmp"][:], AF.Exp, [N_("tmp")], [N_("ePrev")])
                            act(W["eRev"][:], W["L"][:], AF.Exp, [N_("L")], [N_("eRev")], scale=-1.0, bias=W["L"][:, C - 1:C])
                            act(W["gC"][:], W["L"][:, C - 1:C], AF.Exp, [N_("L")], [N_("gC")])
                            stt(W["AR"][:, 0:C], W["kk"][:], -1.0, W["ePrev"][:], ALU.mult, ALU.mult,
                                [N_("kk"), N_("ePrev")], [N_("AR")])
                            tt("pool", W["AR"][:, C:2 * C], r_, W["eL"][:], ALU.mult, [R6["r"], N_("eL")], [N_("AR")])
                            for h in range(2):
                                hs = slice(h * 64, (h + 1) * 64)
                                tt("dve", W["Bbp"][hs, h, :], W["bet"][hs, :], W["enL"][hs, :], ALU.mult,
                                   [N_("bet"), N_("enL")], [N_("Bbp")])
                                tt("pool", W["Kbp"][hs, h, :], W["k2"][hs, :], W["enL"][hs, :], ALU.mult,
                                   [N_("k2"), N_("enL")], [N_("Kbp")])
                            tt("dve", W["bts"][:], W["bet"][:], W["eRev"][:], ALU.mult, [N_("bet"), N_("eRev")], [N_("bts")])
                            tt("pool", W["kts"][:], W["k2"][:], W["eRev"][:], ALU.mult, [N_("k2"), N_("eRev")], [N_("kts")])
                            if cfg.get("scan_stop", 9) <= 1:
                                return
                            yield
                            bk, ps = B.bank()
                            for ti, (src, rk_) in enumerate(((v_, R6["v"]), (W["bts"][:], N_("bts")), (W["kts"][:], N_("kts")))):
                                P.op("pe", lambda E, o=ps[:, ti * 128:(ti + 1) * 128], i_=src: E.transpose(o, i_, ident[:]),
                                     reads=[rk_, "c_ident"], writes=[bk])
                            for ti, n in enumerate(("Vtp", "Btp", "Ktp")):
                                src3 = ps[:, ti * 128:(ti + 1) * 128].rearrange("p (h b) -> p h b", b=64)
                                if ti == 1:
                                    cp("dve", diag2(W[n]), src3, [bk], [N_(n)])
                                else:
                                    cp("act", diag2(W[n]), src3, [bk], [N_(n)])
                            if cfg.get("scan_stop", 9) <= 2:
                                return
                            yield
                            bkX, psX = B.bank(); bkY, psY = B.bank(); bkZ, psZ = B.bank()
                            for h in range(2):
                                mm(psX[:, h * 256:(h + 1) * 256], W["Bbp"][:, h, :], W["AR"][:], [N_("Bbp"), N_("AR")], [bkX])
                                mm(psY[:, h * 256:(h + 1) * 256], W["Kbp"][:, h, :], W["AR"][:], [N_("Kbp"), N_("AR")], [bkY])
                                mm(psZ[:, h * 128:(h + 1) * 128], W["AR"][:, 0:C], W["Bbp"][:, h, :], [N_("Bbp"), N_("AR")], [bkZ])
                            tt("dve", W["QM"][:].rearrange("p h c -> p (h c)"), psX[:, 0:512], m2b2[:].rearrange("p h c -> p (h c)"),
                               ALU.mult, [bkX, "m2b2"], [N_("QM")])
                            tt("dve", W["KM"][:].rearrange("p h c -> p (h c)"), psY[:, 0:512], m2b2[:].rearrange("p h c -> p (h c)"),
                               ALU.mult, [bkY, "m2b2"], [N_("KM")])
                            tt("dve", W["P0"][:].rearrange("p h c -> p (h c)"), psZ[:, 0:256], msl2[:].rearrange("p h c -> p (h c)"),
                               ALU.mult, [bkZ, "msl2"], [N_("P0")])
                            yield
                            if cfg.get("scan_stop", 9) <= 3:
                                return
                            fl = lambda t: t[:].rearrange("p h c -> p (h c)")
                            QAh = lambda h: W["QM"][:, h, 0:C]
                            PA = W["P0"]
                            Qc, Pc = W["Q0"], W["P1"]
                            tt("pool", Qc[:], W["QM"][:, :, 0:C], bd8_2[:], ALU.mult, [N_("QM"), "bd8_2"], [Qc.name])
                            tt("pool", fl(Pc), fl(PA), fl(bd8_2), ALU.mult, [PA.name, "bd8_2"], [Pc.name])
                            Dq, Dp = W["Dq0"], W["Dp0"]
                            tt("pool", fl(Dq), fl(Qc), fl(idb2), ALU.add, [Qc.name, "idb2"], [Dq.name])
                            tt("pool", fl(Dp), fl(Pc), fl(idb2), ALU.add, [Pc.name, "idb2"], [Dp.name])
                            dqi = 0
                            sq_bufs = [(W["Q1"], W["PA"]), (W["Q0"], W["P1"])]
                            for s_ in range(2):
                                Qn, Pn = sq_bufs[s_]
                                bkA, psA = B.bank(); bkB, psB = B.bank()
                                for h in range(2):
                                    mm(psA[:, h * 128:(h + 1) * 128], Pc[:, h, :], Qc[:, h, :], [Qc.name, Pc.name], [bkA])
                                    mm(psB[:, h * 128:(h + 1) * 128], Qc[:, h, :], Pc[:, h, :], [Qc.name, Pc.name], [bkB])
                                cp("act", fl(Qn), psA[:, 0:256], [bkA], [Qn.name])
                                cp("dve", fl(Pn), psB[:, 0:256], [bkB], [Pn.name])
                                Dqn, Dpn = W["Dq%d" % (1 - dqi)], W["Dp%d" % (1 - dqi)]
                                bkC, psC = B.bank(); bkD_, psD_ = B.bank()
                                for h in range(2):
                                    mm(psC[:, h * 128:(h + 1) * 128], Pn[:, h, :], Dq[:, h, :], [Pn.name, Dq.name], [bkC], start=True, stop=False)
                                    mm(psC[:, h * 128:(h + 1) * 128], idb2[:, h, :], Dq[:, h, :], ["idb2", Dq.name], [bkC], start=False, stop=True)
                                    mm(psD_[:, h * 128:(h + 1) * 128], Qn[:, h, :], Dp[:, h, :], [Qn.name, Dp.name], [bkD_], start=True, stop=False)
                                    mm(psD_[:, h * 128:(h + 1) * 128], idb2[:, h, :], Dp[:, h, :], ["idb2", Dp.name], [bkD_], start=False, stop=True)
                                cp("act", fl(Dqn), psC[:, 0:256], [bkC], [Dqn.name])
                                cp("dve", fl(Dpn), psD_[:, 0:256], [bkD_], [Dpn.name])
                                Qc, Pc, Dq, Dp, dqi = Qn, Pn, Dqn, Dpn, 1 - dqi
                                yield
                            for l_ in range(4):
                                last = (l_ == 3)
                                Dqn, Dpn = W["Dq%d" % (1 - dqi)], W["Dp%d" % (1 - dqi)]
                                bkA, psA = B.bank()
                                for h in range(2):
                                    mm(psA[:, h * 128:(h + 1) * 128], PA[:, h, :], Dq[:, h, :], [PA.name, Dq.name], [bkA])
                                tt("dve", fl(W["Y1"]), psA[:, 0:256], lvU2[:, l_, :, :].rearrange("p h c -> p (h c)"), ALU.mult,
                                   [bkA, "lvU2"], [N_("Y1")])
                                if not last:
                                    bkB, psB = B.bank()
                                    for h in range(2):
                                        mm(psB[:, h * 128:(h + 1) * 128], QAh(h), Dp[:, h, :], [N_("QM"), Dp.name], [bkB])
                                    tt("dve", fl(W["Z1"]), psB[:, 0:256], lvL2[:, l_, :, :].rearrange("p h c -> p (h c)"), ALU.mult,
                                       [bkB, "lvL2"], [N_("Z1")])
                                yield
                                bkC, psC = B.bank()
                                for h in range(2):
                                    mm(psC[:, h * 128:(h + 1) * 128], Dp[:, h, :], W["Y1"][:, h, :], [Dp.name, N_("Y1")], [bkC], start=True, stop=False)
                                    mm(psC[:, h * 128:(h + 1) * 128], idb2[:, h, :], Dq[:, h, :], ["idb2", Dq.name], [bkC], start=False, stop=True)
                                cp("act", fl(Dqn), psC[:, 0:256], [bkC], [Dqn.name])
                                if not last:
                                    bkD_, psD_ = B.bank()
                                    for h in range(2):
                                        mm(psD_[:, h * 128:(h + 1) * 128], Dq[:, h, :], W["Z1"][:, h, :], [Dq.name, N_("Z1")], [bkD_], start=True, stop=False)
                                        mm(psD_[:, h * 128:(h + 1) * 128], idb2[:, h, :], Dp[:, h, :], ["idb2", Dp.name], [bkD_], start=False, stop=True)
                                    cp("act", fl(Dpn), psD_[:, 0:256], [bkD_], [Dpn.name])
                                Dq, Dp, dqi = Dqn, Dpn, 1 - dqi
                                yield
                            TT = Dq
                            if cfg.get("scan_stop", 9) <= 4:
                                return
                            yield
                            bkD, psD = B.bank()
                            for h in range(2):
                                hs = slice(h * 64, (h + 1) * 64)
                                mm(psD[:, hs], W["AR"][:, 0:C], Sblk[:, hp, hs], [N_("AR"), pk], [bkD], start=True, stop=False)
                                mm(psD[:, hs], W["KM"][:, h, 0:C], W["Vtp"][:, h, hs], [N_("KM"), N_("Vtp")], [bkD], start=False, stop=True)
                            cp("act", W["RH"][:].rearrange("p h c -> p (h c)"), psD[:, 0:128], [bkD], [N_("RH")])
                            yield
                            bkE, psE = B.bank()
                            for h in range(2):
                                mm(psE[:, h * 64:(h + 1) * 64], TT[:, h, :], W["RH"][:, h, :], [TT.name, N_("RH")], [bkE])
                            cp("dve", diag2(W["Up"]), psE[:, 0:128].rearrange("p (h b) -> p h b", b=64), [bkE], [N_("Up")])
                            yield
                            bkF, psF = B.bank()
                            mm(psF[:, 0:C], Sblk[:, hp, :], W["AR"][:, C:2 * C], [pk, N_("AR")], [bkF], start=True, stop=False)
                            for h in range(2):
                                mm(psF[:, 0:C], W["Up"][:, h, :], W["QM"][:, h, C:2 * C], [N_("Up"), N_("QM")], [bkF], start=False, stop=False)
                                mm(psF[:, 0:C], W["Vtp"][:, h, :], W["KM"][:, h, C:2 * C], [N_("Vtp"), N_("KM")], [bkF],
                                   start=False, stop=(h == 1))
                            bkG, psG = B.bank()
                            for h in range(2):
                                hs = slice(h * 64, (h + 1) * 64)
                                mm(psG[:, 0:64], W["Btp"][:, h, :], W["Up"][:, h, hs], [N_("Btp"), N_("Up")], [bkG],
                                   start=(h == 0), stop=False)
                                mm(psG[:, 0:64], W["Ktp"][:, h, :], W["Vtp"][:, h, hs], [N_("Ktp"), N_("Vtp")], [bkG],
                                   start=False, stop=(h == 1))
                            stt(Sm[:, hp, :], Sm[:, hp, :], W["gC"][:, 0:1], psG[:, 0:64], ALU.mult, ALU.add,
                                [("Sm", hp), N_("gC"), bkG], [("Sm", hp)])
                            for h in range(2):
                                hs = slice(h * 64, (h + 1) * 64)
                                cp("pool", Sblk[hs, hp, hs], Sm[hs, hp, :], [("Sm", hp)], [pk])
                            if cfg.get("scan_stop", 9) <= 5:
                                return
                            yield
                            cp("act", W["YY"][:, 0:C], psF[:, 0:C], [bkF], [N_("YY")])
                            act(W["YY"][:, C:2 * C], W["YY"][:, 0:C], AF.Square, [N_("YY")], [N_("YY")])
                            bkH, psH = B.bank()
                            mm(psH[:, 0:2 * C], blkm[:], W["YY"][:], ["c_blkm", N_("YY")], [bkH])
                            act(W["mu2"][:], psH[:, 0:C], AF.Square, [bkH], [N_("mu2")])
                            tt("dve", W["var"][:], psH[:, C:2 * C], W["mu2"][:], ALU.subtract, [bkH, N_("mu2")], [N_("var")])
                            act(W["rs"][:], W["var"][:], AF.Ln, [N_("var")], [N_("rs")], bias=GN_EPS)
                            act(W["rs"][:], W["rs"][:], AF.Exp, [N_("rs")], [N_("rs")], scale=-0.5)
                            tt("dve", W["yn"][:], W["YY"][:, 0:C], psH[:, 0:C], ALU.subtract, [N_("YY"), bkH], [N_("yn")])
                            tt("pool", W["yn"][:], W["yn"][:], W["rs"][:], ALU.mult, [N_("yn"), N_("rs")], [N_("yn")])
                            act(W["yn"][:], W["yn"][:], AF.Identity, [N_("yn"), "vec", "vec2"], [N_("yn")],
                                bias=vec2[:, 0, hp:hp + 1], scale=vec[:, 15, hp:hp + 1])
                            tt("pool", W["rk"][:], r_, W["k2"][:], ALU.mult, [R6["r"], N_("k2")], [N_("rk")])
                            ts("dve", W["rk"][:], W["rk"][:], vec[:, 14, hp:hp + 1], ALU.mult, [N_("rk"), "vec"], [N_("rk")])
                            yield
                            bkI, psI = B.bank()
                            mm(psI[:, 0:C], blk[:], W["rk"][:], ["c_blk", N_("rk")], [bkI])
                            tt("dve", W["bon"][:], psI[:, 0:C], v_, ALU.mult, [bkI, R6["v"]], [N_("bon")])
                            tt("pool", W["yn"][:], W["yn"][:], W["bon"][:], ALU.add, [N_("yn"), N_("bon")], [N_("yn")])
                            tt("dve", W["yg"][:], W["yn"][:], g_, ALU.mult, [N_("yn"), R6["g"]], [N_("yg")])
                            P.dma("sp", s_yg[hp * 128:(hp + 1) * 128, j * C:(j + 1) * C], W["yg"][:], reads=[N_("yg")])
                        NI = 4
                        for b0_ in range(0, GP, NI):
                            gens = [unit(qi) for qi in range(b0_, min(GP, b0_ + NI))]
                            while gens:
                                for g__ in list(gens):
                                    try:
                                        next(g__)
                                    except StopIteration:
                                        gens.remove(g__)
                Sf = sb("Sf", [128, 128]); So = sb("So", [128, 128])
                P.op("pool", lambda E: E.memset(Sf[:], 0.0), writes=["Sf"])
                for hp in range(NP):
                    for h in range(2):
                        hs = slice(h * 64, (h + 1) * 64)
                        cp("pool", Sf[hs, hs], Sm[hs, hp, :], [("Sm", hp)], ["Sf"])
                    bk, ps = B.bank()
                    P.op("pe", lambda E, o=ps[:, 0:128]: E.transpose(o, Sf[:], ident[:]), reads=["Sf", "c_ident"], writes=[bk])
                    cp("act", So[:], ps[:, 0:128], [bk], ["So"])
                    for h in range(2):
                        hs = slice(h * 64, (h + 1) * 64)
                        P.dma("sp", o_wkv[2 * hp + h, :, :], So[hs, hs], reads=["So"])
                P.barrier()
            B.es = es
        scan_stage()
        NBLK = (NW - 2) // 128
        assert NW == 2 + NBLK * 128
        WG = _chunks(NW, 512)
        s_x1 = B.scratch("s_x1", [D, NW]); s_x2 = B.scratch("s_x2", [D, NW]); s_x3 = B.scratch("s_x3", [D, NW])
        s_act = B.scratch("s_act", [DFF, NW], BF16)
        s_oat = B.scratch("s_oat", [D, NW], BF16)
        convout = sb("convout", [128, 2, FC, 2])
        mwin = sb("mwin", [128, HALO]); P.dma("sp", mwin[:], maskrow[:, W0:W0 + HALO], writes=["mwin"])
        esT = sb("esT", [128, H]); P.dma("sp", esT[:], sinkT[:, :], writes=["esT"])
        act(esT[:], esT[:], AF.Exp, ["esT"], ["esT"])
        kwin_sb = sb("kwin_sb", [128, KVW]); vwin_sb = sb("vwin_sb", [128, KVW])

        def xs_of(tile, key, groups):
            return [(lambda ki, ksz, c0=c0, n=n: tile[0:ksz, ki, c0:c0 + n], n, [key]) for (c0, n) in groups]

        def norm_stats(src, rstd, xa):
            bks = [B.bank() for _ in WG]
            for kc in range(KC):
                x_ = xa[kc % 2]
                P.dma("sp", x_[:], src[kc * 128:(kc + 1) * 128, :], writes=[x_.name])
                act(x_[:], x_[:], AF.Square, [x_.name], [x_.name])
                for gi, (c0, n) in enumerate(WG):
                    mm(bks[gi][1][:, 0:n], ones[:], x_[:, c0:c0 + n], [x_.name, "c_ones"], [bks[gi][0]],
                       start=(kc == 0), stop=(kc == KC - 1))
            for gi, (c0, n) in enumerate(WG):
                act(rstd[:, c0:c0 + n], bks[gi][1][:, 0:n], AF.Ln, [bks[gi][0]], [rstd.name], scale=1.0 / D, bias=1e-6)
            act(rstd[:], rstd[:], AF.Exp, [rstd.name], [rstd.name], scale=-0.5)

        def modulate(src, rstd, hw, gsel, shsel, xa):
            for kc in range(KC):
                x_ = xa[kc % 2]
                P.dma("sp", x_[:], src[kc * 128:(kc + 1) * 128, :], writes=[x_.name])
                tt("dve", x_[:], x_[:], rstd[:], ALU.mult, [x_.name, rstd.name], [x_.name])
                act(hw[:, kc, :], x_[:], AF.Identity, [x_.name, "Gm", "modv", "kvmv"], [hw.name],
                    bias=shsel(kc), scale=gsel(kc))

        def resid_epi(xin, gate_sel, target, og, xg, cg):
            cnt_ = [0]

            def epi(ct, n0, m, g, ps, bk):
                i = cnt_[0] % 2; cnt_[0] += 1
                c0, n = cg[g]
                P.dma("sp", xg[i][:, 0:n], xin[ct * 128:(ct + 1) * 128, c0:c0 + n], writes=[xg[i].name])
                stt(og[i][:, 0:n], ps, gate_sel(ct), xg[i][:, 0:n], ALU.mult, ALU.add,
                    [bk, xg[i].name, "modv"], [og[i].name])
                target(ct, c0, n, og[i], og[i].name)
            return epi

        def to_scratch(dst):
            def tgt(ct, c0, n, o_, key):
                P.dma("sp", dst[ct * 128:(ct + 1) * 128, c0:c0 + n], o_[:, 0:n], reads=[key])
            return tgt

        def ffn(l, xin, target, tg):
            with contextlib.ExitStack() as es5:
                B.es = es5
                rstd = sb("frstd" + tg, [128, NW])
                hw = sb("fhw" + tg, [128, KC, NW], BF16)
                xa = [sb("fxa%s%d" % (tg, i), [128, NW]) for i in range(2)]
                norm_stats(xin, rstd, xa)
                modulate(xin, rstd, hw, lambda kc: Gm[:, 1 + 2 * l, kc, 0:1], lambda kc: modv[:, l, 3 * KC + kc, 0:1], xa)
                wb = [sb("fwb%s%d" % (tg, i), [128, KC, 128], BF16) for i in range(2)]
                gfull = [sb("gfull%s%d" % (tg, i), [128, NW]) for i in range(2)]
                cvt = [sb("cvt%s%d" % (tg, i), [128, NW]) for i in range(2)]
                sg = [sb("sg%s%d" % (tg, i), [128, NW], BF16) for i in range(4)]
                ao = [sb("ao%s%d" % (tg, i), [128, 512], BF16) for i in range(3)]
                for t_ in sg:
                    P.op("pool", lambda E, t_=t_: E.memset(t_[:], 0.0), writes=[t_.name])
                aoc = [0]

                def epi(ct, n0, m, g, ps, bk):
                    c0, n = WG[g]
                    if n0 < DFF:
                        f = ct
                        gf = gfull[f % 2]
                        cp("act", gf[:, c0:c0 + n], ps, [bk], [gf.name])
                        if g == len(WG) - 1:
                            tt("dve", gf[:, 0:HALO], gf[:, 0:HALO], mwin[:], ALU.mult, [gf.name, "mwin"], [gf.name])
                            cp("pool", convout[:, l, f, :], gf[:, NW - 2:NW], [gf.name], ["convout"])
                            c_ = cvt[f % 2]
                            act(c_[:, 2:NW], gf[:, 2:NW], AF.Identity, [gf.name, "cvp"], [c_.name],
                                bias=cvp[:, l, 3, f:f + 1], scale=cvp[:, l, 2, f:f + 1])
                            stt(c_[:, 2:NW], gf[:, 1:NW - 1], cvp[:, l, 1, f:f + 1], c_[:, 2:NW], ALU.mult, ALU.add,
                                [gf.name, c_.name, "cvp"], [c_.name])
                            stt(c_[:, 2:NW], gf[:, 0:NW - 2], cvp[:, l, 0, f:f + 1], c_[:, 2:NW], ALU.mult, ALU.add,
                                [gf.name, c_.name, "cvp"], [c_.name])
                            act(sg[f % 4][:, 2:NW], c_[:, 2:NW], AF.Silu, [c_.name], [sg[f % 4].name])
                    else:
                        f = ct - FC
                        a_ = ao[aoc[0] % 3]; aoc[0] += 1
                        tt("dve", a_[:, 0:n], ps, sg[f % 4][:, c0:c0 + n], ALU.mult, [bk, sg[f % 4].name], [a_.name])
                        P.dma("sp", s_act[f * 128:(f + 1) * 128, c0:c0 + n], a_[:, 0:n], reads=[a_.name])
                cgs = []
                for (n0, gsz) in _chunks(DFF, 128):
                    cgs.append((n0, gsz)); cgs.append((DFF + n0, gsz))
                B.linear("win" + tg, w_in[l], D, 2 * DFF, xs_of(hw, hw.name, WG), epi, 128, wb, colgroups=cgs)
                P.barrier()
            with contextlib.ExitStack() as es5:
                B.es = es5
                CG = _chunks(NW, 384)
                actt = sb("actt" + tg, [128, FC, 384], BF16)
                wbo = [sb("fwo%s%d" % (tg, i), [128, FC, 128], BF16) for i in range(2)]
                og = [sb("fog%s%d" % (tg, i), [128, 384]) for i in range(2)]
                xg = [sb("fxg%s%d" % (tg, i), [128, 384]) for i in range(2)]
                for (c0, n) in CG:
                    P.dma("sp", actt[:, :, 0:n], s_act[:, c0:c0 + n].rearrange("(f p) t -> p f t", p=128), writes=[actt.name])
                    xs1 = [(lambda ki, ksz, n=n: actt[0:ksz, ki, 0:n], n, [actt.name])]
                    B.linear("wout" + tg, w_out[l], DFF, D, xs1,
                             resid_epi(xin, lambda ct: modv[:, l, 5 * KC + ct, 0:1], target, og, xg, [(c0, n)]), 128, wbo)
                P.barrier()
            B.es = es

        def backend():
            with contextlib.ExitStack() as es5:
                B.es = es5
                hw = sb("d1hw", [128, KC, NW], BF16)
                P.dma("sp", hw[:], s_yg[:, W0:T].rearrange("(kc p) t -> p kc t", p=128), writes=["d1hw"])
                wb = [sb("d1wb%d" % i, [128, KC, 256], BF16) for i in range(2)]
                og = [sb("d1og%d" % i, [128, 512]) for i in range(2)]
                xg = [sb("d1xg%d" % i, [128, 512]) for i in range(2)]
                B.linear("wo", w_o, D, D, xs_of(hw, "d1hw", WG),
                         resid_epi(xT[:, W0:T], lambda ct: modv[:, 0, 2 * KC + ct, 0:1], to_scratch(s_x1), og, xg, WG), 256, wb)
                P.barrier()
            B.es = es
            ffn(0, s_x1, to_scratch(s_x2), "0")
            with contextlib.ExitStack() as es5:
                B.es = es5
                rstd = sb("krstd", [128, NW])
                hw = sb("khw", [128, KC, NW], BF16)
                kdup = sb("kdup", [128, NKV, NW], BF16)
                Vt1 = sb("Vt1", [128, NBLK, NKV, 65], BF16)
                P.op("pool", lambda E: E.memset(Vt1[:], 1.0), writes=["Vt1"])
                xa = [sb("kxa%d" % i, [128, NW]) for i in range(2)]
                norm_stats(s_x2, rstd, xa)
                modulate(s_x2, rstd, hw, lambda kc: Gm[:, 4, kc, 0:1], lambda kc: kvmv[:, kc, 0:1], xa)
                wbk = [sb("kwb%d" % i, [128, KC, 128], BF16) for i in range(2)]
                kf = sb("kf", [128, NW]); ksq = sb("ksq", [128, NW]); krs = sb("krs", [128, NW])
                for g in range(NKV):
                    buf = wbk[g % 2]; bkey = ("wb", buf.name)
                    for half in range(2):
                        P.dma("pool", buf[:, :, half * 64:(half + 1) * 64],
                              w_kv[:, g * 64:(g + 1) * 64].rearrange("(kc p) n -> p kc n", p=128), writes=[bkey])
                    for gi, (c0, n) in enumerate(WG):
                        bk, ps = B.bank()
                        for kc in range(KC):
                            mm(ps[:, 0:n], buf[:, kc, :], hw[:, kc, c0:c0 + n], [bkey, hw.name], [bk], start=(kc == 0), stop=(kc == KC - 1))
                        cp("act", kf[:, c0:c0 + n], ps[:, 0:n], [bk], [kf.name])
                    act(ksq[:], kf[:], AF.Square, [kf.name], [ksq.name])
                    for gi, (c0, n) in enumerate(WG):
                        bk, ps = B.bank()
                        mm(ps[:, 0:n], blkm[:], ksq[:, c0:c0 + n], ["c_blkm", ksq.name], [bk])
                        act(krs[:, c0:c0 + n], ps[:, 0:n], AF.Ln, [bk], [krs.name], bias=1e-6)
                    act(krs[:], krs[:], AF.Exp, [krs.name], [krs.name], scale=-0.5)
                    tt("dve", kf[:], kf[:], krs[:], ALU.mult, [kf.name, krs.name], [kf.name])
                    ts("dve", kf[:], kf[:], hdt[:, 0:1], ALU.mult, [kf.name, "hdt"], [kf.name])
                    cp("pool", kdup[:, g, :], kf[:], [kf.name], ["kdup"])
                    bk, ps = B.bank()
                    P.op("pe", lambda E, o=ps[:, 0:128]: E.transpose(o, kf[:, NW - 128:NW], ident[:]), reads=[kf.name, "c_ident"], writes=[bk])
                    cp("act", kwin_sb[:, g * 64:(g + 1) * 64], ps[:, 0:64], [bk], ["kwin_sb"])
                vf = xa[0]

                def epiV(ct, n0, m, g, ps, bk):
                    c0, n = WG[g]
                    cp("act", vf[0:m, c0:c0 + n], ps, [bk], [vf.name])
                    if g == len(WG) - 1:
                        for j in range(NBLK):
                            bk2, ps2 = B.bank()
                            P.op("pe", lambda E, o=ps2[:, 0:m], j=j: E.transpose(o, vf[0:m, 2 + j * 128:2 + (j + 1) * 128], ident[0:m, 0:m]),
                                 reads=[vf.name, "c_ident"], writes=[bk2])
                            nh = m // 64
                            cp("dve", Vt1[:, j, ct * 2:ct * 2 + nh, 0:64], ps2[:, 0:m].rearrange("p (h d) -> p h d", d=64), [bk2], ["Vt1"])
                            if j == NBLK - 1:
                                cp("act", vwin_sb[:, ct * 128:ct * 128 + m], ps2[:, 0:m], [bk2], ["vwin_sb"])
                B.linear("wv", w_kv[:, KVW:2 * KVW], D, KVW, xs_of(hw, hw.name, WG), epiV, 128, wbk)
                P.dma("sp", o_kwin[:, :], kwin_sb[:], reads=["kwin_sb"])
                P.dma("sp", o_vwin[:, :], vwin_sb[:], reads=["vwin_sb"])
                modulate(s_x2, rstd, hw, lambda kc: Gm[:, 2, kc, 0:1], lambda kc: modv[:, 1, 0 * KC + kc, 0:1], xa)
                wbq = wbk
                qf = [kf, kf]
                qsq = ksq; qrs = krs
                qp = sb("qp", [128, 2, NW], BF16)
                P.op("pool", lambda E: E.memset(qp[:], 0.0), writes=["qp"])
                mpo = sb("mpo", [128, 4, 128], BF16)
                for i in range(4):
                    cp("dve", mpo[:, i, :], (mslb[:] if i % 2 == 0 else m2b[:, 128:256]), ["mslb", "m2b"], ["mpo"])
                Eb = [sb("Eb%d" % i, [128, 4, 128], BF16) for i in range(2)]
                otk = [sb("otk%d" % i, [128, 128]) for i in range(2)]
                den = [sb("den%d" % i, [128, 2]) for i in range(2)]
                ofm = [sb("ofm%d" % i, [128, 128], BF16) for i in range(2)]
                zt = sb("zt", [128, 130], BF16)
                P.op("pool", lambda E: E.memset(zt[:], 0.0), writes=["zt"])
                for kc in range(KC):
                    P.dma("sp", s_oat[kc * 128:(kc + 1) * 128, 0:130], zt[:], reads=["zt"])
                cnt_q = [0]

                def epiQ(ct, n0, m, g, ps, bk):
                    c0, n = WG[g]
                    q_ = qf[ct % 2]
                    cp("act", q_[:, c0:c0 + n], ps, [bk], [q_.name])
                    if g < len(WG) - 1:
                        return
                    act(qsq[:], q_[:], AF.Square, [q_.name], [qsq.name])
                    for gi, (c0_, n_) in enumerate(WG):
                        bk2, ps2 = B.bank()
                        mm(ps2[:, 0:n_], blkm[:], qsq[:, c0_:c0_ + n_], ["c_blkm", qsq.name], [bk2])
                        act(qrs[:, c0_:c0_ + n_], ps2[:, 0:n_], AF.Ln, [bk2], [qrs.name], bias=1e-6)
                    act(qrs[:], qrs[:], AF.Exp, [qrs.name], [qrs.name], scale=-0.5)
                    tt("dve", q_[:], q_[:], qrs[:], ALU.mult, [q_.name, qrs.name], [q_.name])
                    for h in range(2):
                        hs = slice(h * 64, (h + 1) * 64)
                        ts("dve", qp[hs, h, :], q_[hs, :], hdt[hs, 1:2], ALU.mult, [q_.name, "hdt"], ["qp"])
                    for qb in range(1, NBLK):
                        i2 = cnt_q[0] % 2; cnt_q[0] += 1
                        qc = slice(2 + qb * 128, 2 + (qb + 1) * 128)
                        bkS, psS = B.bank()
                        for h in range(2):
                            gkv = (2 * ct + h) // (H // NKV)
                            for kbi, kb in enumerate((qb - 1, qb)):
                                kcs = slice(2 + kb * 128, 2 + (kb + 1) * 128)
                                mm(psS[:, (h * 2 + kbi) * 128:(h * 2 + kbi + 1) * 128], kdup[:, gkv, kcs], qp[:, h, qc],
                                   ["kdup", "qp"], [bkS])
                        E_ = Eb[i2]
                        act(E_[:].rearrange("p a b -> p (a b)"), psS[:, 0:512], AF.Exp, [bkS], [E_.name], scale=0.125)
                        tt("dve", E_[:], E_[:], mpo[:], ALU.mult, [E_.name, "mpo"], [E_.name])
                        for kbi, kb in enumerate((qb - 1, qb)):
                            if kb <= 1:
                                ts("pool", E_[:, kbi:4:2, :], E_[:, kbi:4:2, :], flag[:, 0:1], ALU.mult, [E_.name, "flag"], [E_.name])
                        bkO, psO = B.bank()
                        for h in range(2):
                            gkv = (2 * ct + h) // (H // NKV)
                            for kbi, kb in enumerate((qb - 1, qb)):
                                mm(psO[:, h * 65:(h + 1) * 65], E_[:, h * 2 + kbi, :], Vt1[:, kb, gkv, :], [E_.name, "Vt1"], [bkO],
                                   start=(kbi == 0), stop=(kbi == 1))
                        d_ = den[i2]; o_ = otk[i2]
                        tt("dve", d_[:], psO[:, 64:130:65], esT[:, 2 * ct:2 * ct + 2], ALU.add, [bkO, "esT"], [d_.name])
                        P.op("dve", lambda E, d_=d_: E.reciprocal(d_[:], d_[:]), reads=[d_.name], writes=[d_.name])
                        for h in range(2):
                            ts("dve", o_[:, h * 64:(h + 1) * 64], psO[:, h * 65:h * 65 + 64], d_[:, h:h + 1], ALU.mult,
                               [bkO, d_.name], [o_.name])
                        bkT, psT = B.bank()
                        P.op("pe", lambda E, o=psT[:, 0:128], o_=o_: E.transpose(o, o_[:], ident[:]), reads=[o_.name, "c_ident"], writes=[bkT])
                        f_ = ofm[i2]
                        cp("act", f_[:], psT[:, 0:128], [bkT], [f_.name])
                        P.dma("sp", s_oat[ct * 128:(ct + 1) * 128, qc], f_[:], reads=[f_.name])
                B.linear("wq", w_q, D, D, xs_of(hw, hw.name, WG), epiQ, 128, wbq)
                P.barrier()
            B.es = es
            with contextlib.ExitStack() as es5:
                B.es = es5
                hw = sb("d4hw", [128, KC, NW], BF16)
                P.dma("sp", hw[:], s_oat.rearrange("(kc p) t -> p kc t", p=128), writes=["d4hw"])
                wb = [sb("d4wb%d" % i, [128, KC, 256], BF16) for i in range(2)]
                og = [sb("d4og%d" % i, [128, 512]) for i in range(2)]
                xg = [sb("d4xg%d" % i, [128, 512]) for i in range(2)]
                B.linear("wao", w_ao, D, D, xs_of(hw, "d4hw", WG),
                         resid_epi(s_x2, lambda ct: modv[:, 1, 2 * KC + ct, 0:1], to_scratch(s_x3), og, xg, WG), 256, wb)
                P.barrier()
            B.es = es

            def to_y(ct, c0, n, o_, key):
                lo_ = max(c0, HALO)
                if lo_ < c0 + n:
                    P.dma("sp", o_y[ct * 128:(ct + 1) * 128, lo_ - HALO:c0 + n - HALO], o_[:, lo_ - c0:n], reads=[key])
            ffn(1, s_x3, to_y, "1")
            P.dma("sp", o_conv.rearrange("l p f c -> p l f c"), convout[:], reads=["convout"])
        backend()
        def sample_path():
            with contextlib.ExitStack() as es6:
                B.es = es6
                KN = KC * NS
                f3 = lambda t: t[:].rearrange("p k n -> p (k n)")
                T_ = {}
                for n in ("r", "k", "v", "ld", "a", "g"):
                    T_[n] = sb("s_" + n, [128, KC, NS])
                    P.dma("sp", T_[n][:], scs[n].rearrange("(kc p) n -> p kc n", p=128), writes=[T_[n].name])
                W_ = {n: sb("sw_" + n, [128, KC, NS]) for n in ("kkr", "sq", "rn", "kk", "t1", "k2", "bet", "dd", "al", "ys", "yn", "rk", "mu2", "var", "rs", "bon", "ysq")}
                nm = lambda n: (T_[n].name if n in T_ else W_[n].name)
                for j in range(NS):
                    tt("dve", W_["kkr"][:, :, j], T_["k"][:, :, j], vec[:, 12, :], ALU.mult, [nm("k"), "vec"], [nm("kkr")])
                act(f3(W_["sq"]), f3(W_["kkr"]), AF.Square, [nm("kkr")], [nm("sq")])
                bk, ps = B.bank()
                mm(ps[:, 0:KN], blk[:], f3(W_["sq"]), ["c_blk", nm("sq")], [bk])
                ts("dve", f3(W_["rn"]), ps[:, 0:KN], 1e-24, ALU.max, [bk], [nm("rn")])
                act(f3(W_["rn"]), f3(W_["rn"]), AF.Ln, [nm("rn")], [nm("rn")])
                act(f3(W_["rn"]), f3(W_["rn"]), AF.Exp, [nm("rn")], [nm("rn")], scale=-0.5)
                tt("dve", f3(W_["kk"]), f3(W_["kkr"]), f3(W_["rn"]), ALU.mult, [nm("kkr"), nm("rn")], [nm("kk")])
                for j in range(NS):
                    tt("dve", W_["t1"][:, :, j], T_["a"][:, :, j], vec[:, 13, :], ALU.mult, [nm("a"), "vec"], [nm("t1")])
                    tt("dve", W_["t1"][:, :, j], W_["t1"][:, :, j], omka[:], ALU.add, [nm("t1"), "omka"], [nm("t1")])
                tt("dve", f3(W_["k2"]), f3(T_["k"]), f3(W_["t1"]), ALU.mult, [nm("k"), nm("t1")], [nm("k2")])
                tt("dve", f3(W_["bet"]), f3(W_["kk"]), f3(T_["a"]), ALU.mult, [nm("kk"), nm("a")], [nm("bet")])
                act(f3(W_["dd"]), f3(T_["ld"]), AF.Exp, [nm("ld")], [nm("dd")])
                ts("dve", f3(W_["al"]), f3(W_["kk"]), -1.0, ALU.mult, [nm("kk")], [nm("al")])
                Spad = [sb("Spad%d" % i, [128, 128]) for i in range(2)]
                St = [sb("St%d" % i, [128, 128]) for i in range(2)]
                So_ = [sb("Sos%d" % i, [128, 128]) for i in range(2)]
                X1 = [sb("X1_%d" % i, [128, 2]) for i in range(2)]
                X2 = [sb("X2_%d" % i, [128, 2]) for i in range(2)]
                R12 = [sb("R12_%d" % i, [2, 256]) for i in range(2)]
                otm = [sb("otm%d" % i, [128, 128]) for i in range(2)]
                for t_ in Spad:
                    P.op("pool", lambda E, t_=t_: E.memset(t_[:], 0.0), writes=[t_.name])
                it = 0
                for j in range(NS):
                    for hp in range(NP):
                        i2 = it % 2; it += 1
                        sp_, st_, so_, x1, x2, r12, ot = Spad[i2], St[i2], So_[i2], X1[i2], X2[i2], R12[i2], otm[i2]
                        for h in range(2):
                            hs = slice(h * 64, (h + 1) * 64)
                            P.dma("sp", sp_[hs, hs], st_wkv[j, 2 * hp + h, :, :], writes=[sp_.name])
                        bk, ps = B.bank()
                        P.op("pe", lambda E, o=ps[:, 0:128], i_=sp_: E.transpose(o, i_[:], ident[:]), reads=[sp_.name, "c_ident"], writes=[bk])
                        cp("act", st_[:], ps[:, 0:128], [bk], [st_.name])
                        bk, ps = B.bank()
                        mm(ps[:, 0:1], st_[:], W_["al"][:, hp, j:j + 1], [st_.name, nm("al")], [bk])
                        cp("act", x2[:, 0:1], ps[:, 0:1], [bk], [x2.name])
                        cp("pool", x2[:, 1:2], T_["v"][:, hp, j:j + 1], [nm("v")], [x2.name])
                        cp("pool", x1[:, 0:1], W_["bet"][:, hp, j:j + 1], [nm("bet")], [x1.name])
                        cp("pool", x1[:, 1:2], W_["k2"][:, hp, j:j + 1], [nm("k2")], [x1.name])
                        bk, ps = B.bank()
                        P.op("pe", lambda E, o=ps[0:2, 0:128], i_=x1: E.transpose(o, i_[:], ident[:]), reads=[x1.name, "c_ident"], writes=[bk])
                        P.op("pe", lambda E, o=ps[0:2, 128:256], i_=x2: E.transpose(o, i_[:], ident[:]), reads=[x2.name, "c_ident"], writes=[bk])
                        cp("act", r12[:], ps[0:2, 0:256], [bk], [r12.name])
                        bk, ps = B.bank()
                        mm(ps[:, 0:128], r12[:, 0:128], r12[:, 128:256], [r12.name], [bk])
                        tt("dve", ot[:], ps[:, 0:128], blk[:], ALU.mult, [bk, "c_blk"], [ot.name])
                        stt(st_[:], st_[:], W_["dd"][:, hp, j:j + 1], ot[:], ALU.mult, ALU.add, [st_.name, nm("dd"), ot.name], [st_.name])
                        bk, ps = B.bank()
                        mm(ps[:, 0:1], st_[:], T_["r"][:, hp, j:j + 1], [st_.name, nm("r")], [bk])
                        cp("act", W_["ys"][:, hp, j:j + 1], ps[:, 0:1], [bk], [nm("ys")])
                        bk, ps = B.bank()
                        P.op("pe", lambda E, o=ps[:, 0:128], i_=st_: E.transpose(o, i_[:], ident[:]), reads=[st_.name, "c_ident"], writes=[bk])
                        cp("dve", so_[:], ps[:, 0:128], [bk], [so_.name])
                        for h in range(2):
                            hs = slice(h * 64, (h + 1) * 64)
                            P.dma("sp", o_wkvs[j, 2 * hp + h, :, :], so_[hs, hs], reads=[so_.name])
                YY = sb("sYY", [128, 2, KN])
                cp("dve", YY[:, 0, :], f3(W_["ys"]), [nm("ys")], ["sYY"])
                act(YY[:, 1, :], f3(W_["ys"]), AF.Square, [nm("ys")], ["sYY"])
                bk, ps = B.bank()
                mm(ps[:, 0:2 * KN], blkm[:], YY[:].rearrange("p a n -> p (a n)"), ["c_blkm", "sYY"], [bk])
                act(f3(W_["mu2"]), ps[:, 0:KN], AF.Square, [bk], [nm("mu2")])
                tt("dve", f3(W_["var"]), ps[:, KN:2 * KN], f3(W_["mu2"]), ALU.subtract, [bk, nm("mu2")], [nm("var")])
                act(f3(W_["rs"]), f3(W_["var"]), AF.Ln, [nm("var")], [nm("rs")], bias=GN_EPS)
                act(f3(W_["rs"]), f3(W_["rs"]), AF.Exp, [nm("rs")], [nm("rs")], scale=-0.5)
                tt("dve", f3(W_["yn"]), f3(W_["ys"]), ps[:, 0:KN], ALU.subtract, [nm("ys"), bk], [nm("yn")])
                tt("dve", f3(W_["yn"]), f3(W_["yn"]), f3(W_["rs"]), ALU.mult, [nm("yn"), nm("rs")], [nm("yn")])
                tt("dve", f3(W_["rk"]), f3(T_["r"]), f3(W_["k2"]), ALU.mult, [nm("r"), nm("k2")], [nm("rk")])
                for j in range(NS):
                    tt("dve", W_["yn"][:, :, j], W_["yn"][:, :, j], vec[:, 15, :], ALU.mult, [nm("yn"), "vec"], [nm("yn")])
                    tt("dve", W_["yn"][:, :, j], W_["yn"][:, :, j], vec2[:, 0, :], ALU.add, [nm("yn"), "vec2"], [nm("yn")])
                    tt("dve", W_["rk"][:, :, j], W_["rk"][:, :, j], vec[:, 14, :], ALU.mult, [nm("rk"), "vec"], [nm("rk")])
                bk, ps = B.bank()
                mm(ps[:, 0:KN], blk[:], f3(W_["rk"]), ["c_blk", nm("rk")], [bk])
                tt("dve", f3(W_["bon"]), ps[:, 0:KN], f3(T_["v"]), ALU.mult, [bk, nm("v")], [nm("bon")])
                tt("dve", f3(W_["yn"]), f3(W_["yn"]), f3(W_["bon"]), ALU.add, [nm("yn"), nm("bon")], [nm("yn")])
                ygs = sb("ygs", [128, KC, NS], BF16)
                tt("dve", f3(ygs), f3(W_["yn"]), f3(T_["g"]), ALU.mult, [nm("yn"), nm("g")], ["ygs"])

                xs0 = sb("xs0", [128, KC, NS]); P.dma("sp", xs0[:], xsT.rearrange("(kc p) n -> p kc n", p=128), writes=[xs0.name])
                x1s = sb("x1s", [128, KC, NS]); x2s = sb("x2s", [128, KC, NS]); x3s = sb("x3s", [128, KC, NS]); ysf = sb("ysf", [128, KC, NS])
                hS = sb("hS", [128, KC, NS], BF16); h32 = sb("h32", [128, KC, NS]); sqS = sb("sqS", [128, KC, NS])
                rsS = sb("rsS", [128, NS])
                wbs = [sb("swb%d" % i, [128, KC, 512], BF16) for i in range(2)]
                wbo = [sb("swo%d" % i, [128, FC, 128], BF16) for i in range(2)]
                actS = sb("actS", [128, FC, NS], BF16)
                cbuf = sb("cbuf", [128, 2, FC, NS, 2])
                P.dma("sp", cbuf[:].rearrange("p l f n c -> p l (f n c)"), st_conv.rearrange("l p f n c -> p l (f n c)"), writes=["cbuf"])
                cvo = sb("cvo", [128, 2, FC, NS, 2])
                tmpS = [sb("tmpS%d" % i, [128, NS]) for i in range(4)]
                tcnt = [0]

                def xsS(tile, key):
                    return [(lambda ki, ksz: tile[0:ksz, ki, :], NS, [key])]

                def resS(xin, xout, gsel):
                    def epi(ct, n0, m, g, ps, bk):
                        t_ = tmpS[tcnt[0] % 4]; tcnt[0] += 1
                        tt("dve", t_[:], ps, gsel(ct), ALU.mult, [bk, "modv"], [t_.name])
                        tt("dve", xout[:, ct, :], t_[:], xin[:, ct, :], ALU.add, [t_.name, xin.name], [xout.name])
                    return epi

                def normS(x, G3, SH3, keys):
                    act(f3(sqS), f3(x), AF.Square, [x.name], ["sqS"])
                    bk, ps = B.bank()
                    for kc in range(KC):
                        mm(ps[:, 0:NS], ones[:], sqS[:, kc, :], ["c_ones", "sqS"], [bk], start=(kc == 0), stop=(kc == KC - 1))
                    act(rsS[:], ps[:, 0:NS], AF.Ln, [bk], ["rsS"], scale=1.0 / D, bias=1e-6)
                    act(rsS[:], rsS[:], AF.Exp, ["rsS"], ["rsS"], scale=-0.5)
                    for kc in range(KC):
                        tt("dve", h32[:, kc, :], x[:, kc, :], rsS[:], ALU.mult, [x.name, "rsS"], ["h32"])
                    tt("dve", h32[:], h32[:], G3, ALU.mult, ["h32"] + keys, ["h32"])
                    tt("dve", hS[:], h32[:], SH3, ALU.add, ["h32"] + keys, ["hS"])

                def ffnS(l, xin, xout):
                    normS(xin, Gm[:, 1 + 2 * l, :, 1:1 + NS], modv[:, l, 3 * KC:4 * KC, 1:1 + NS], ["Gm", "modv"])
                    sgS = [sb("sgS%d_%d" % (l, i), [128, NS]) for i in range(8)]

                    def epi(ct, n0, m, g, ps, bk):
                        if n0 < DFF:
                            f = ct
                            t_ = tmpS[tcnt[0] % 4]; tcnt[0] += 1
                            cp("act", cvo[:, l, f, :, 1], ps, [bk], ["cvo"])
                            cp("pool", cvo[:, l, f, :, 0], cbuf[:, l, f, :, 1], ["cbuf"], ["cvo"])
                            act(t_[:], ps, AF.Identity, [bk, "cvp"], [t_.name], bias=cvp[:, l, 3, f:f + 1], scale=cvp[:, l, 2, f:f + 1])
                            stt(t_[:], cbuf[:, l, f, :, 1], cvp[:, l, 1, f:f + 1], t_[:], ALU.mult, ALU.add, ["cbuf", "cvp", t_.name], [t_.name])
                            stt(t_[:], cbuf[:, l, f, :, 0], cvp[:, l, 0, f:f + 1], t_[:], ALU.mult, ALU.add, ["cbuf", "cvp", t_.name], [t_.name])
                            act(sgS[f % 8][:], t_[:], AF.Silu, [t_.name], [sgS[f % 8].name])
                        else:
                            f = ct - FC
                            tt("dve", actS[:, f, :], ps, sgS[f % 8][:], ALU.mult, [bk, sgS[f % 8].name], ["actS"])
                    cgs = []
                    for (n0, gsz) in _chunks(DFF, 512):
                        cgs.append((n0, gsz)); cgs.append((DFF + n0, gsz))
                    B.linear("swin%d" % l, w_in[l], D, 2 * DFF, xsS(hS, "hS"), epi, 512, wbs, colgroups=cgs)
                    B.linear("swout%d" % l, w_out[l], DFF, D, xsS(actS, "actS"),
                             resS(xin, xout, lambda ct: modv[:, l, 5 * KC + ct, 1:1 + NS]), 128, wbo)

                B.linear("swo", w_o, D, D, xsS(ygs, "ygs"), resS(xs0, x1s, lambda ct: modv[:, 0, 2 * KC + ct, 1:1 + NS]), 512, wbs)
                ffnS(0, x1s, x2s)
                normS(x2s, Gm[:, 4, :, 1:1 + NS], kvmv[:, 0:KC, 1:1 + NS], ["Gm", "kvmv"])
                knew = sb("knew", [128, NKV, NS]); vnew = sb("vnew", [128, KVT, NS])
                ksqS = sb("ksqS", [128, NS])
                for g in range(NKV):
                    buf = wbs[g % 2]; bkey = ("wb", buf.name)
                    for half in range(2):
                        P.dma("pool", buf[:, :, half * 64:(half + 1) * 64],
                              w_kv[:, g * 64:(g + 1) * 64].rearrange("(kc p) n -> p kc n", p=128), writes=[bkey])
                    bk, ps = B.bank()
                    for kc in range(KC):
                        mm(ps[:, 0:NS], buf[:, kc, 0:128], hS[:, kc, :], [bkey, "hS"], [bk], start=(kc == 0), stop=(kc == KC - 1))
                    cp("act", knew[:, g, :], ps[:, 0:NS], [bk], ["knew"])
                    act(ksqS[:], knew[:, g, :], AF.Square, ["knew"], ["ksqS"])
                    bk, ps = B.bank()
                    mm(ps[:, 0:NS], blkm[:], ksqS[:], ["c_blkm", "ksqS"], [bk])
                    act(ksqS[:], ps[:, 0:NS], AF.Ln, [bk], ["ksqS"], bias=1e-6)
                    act(ksqS[:], ksqS[:], AF.Exp, ["ksqS"], ["ksqS"], scale=-0.5)
                    tt("dve", knew[:, g, :], knew[:, g, :], ksqS[:], ALU.mult, ["knew", "ksqS"], ["knew"])
                    ts("dve", knew[:, g, :], knew[:, g, :], hdt[:, 0:1], ALU.mult, ["knew", "hdt"], ["knew"])

                def epiVs(ct, n0, m, g, ps, bk):
                    cp("act", vnew[0:m, ct, :], ps, [bk], ["vnew"])
                B.linear("swv", w_kv[:, KVW:2 * KVW], D, KVW, xsS(hS, "hS"), epiVs, 128, wbs)
                normS(x2s, Gm[:, 2, :, 1:1 + NS], modv[:, 1, 0:KC, 1:1 + NS], ["Gm", "modv"])
                qS = sb("qS", [128, KC, NS]); qz = sb("qz", [128, 2, KC, NS], BF16)
                P.op("pool", lambda E: E.memset(qz[:], 0.0), writes=["qz"])

                def epiQs(ct, n0, m, g, ps, bk):
                    cp("act", qS[:, ct, :], ps, [bk], ["qS"])
                B.linear("swq", w_q, D, D, xsS(hS, "hS"), epiQs, 512, wbs)
                act(f3(sqS), f3(qS), AF.Square, ["qS"], ["sqS"])
                bk, ps = B.bank()
                mm(ps[:, 0:KN], blkm[:], f3(sqS), ["c_blkm", "sqS"], [bk])
                act(f3(h32), ps[:, 0:KN], AF.Ln, [bk], ["h32"], bias=1e-6)
                act(f3(h32), f3(h32), AF.Exp, ["h32"], ["h32"], scale=-0.5)
                tt("dve", f3(qS), f3(qS), f3(h32), ALU.mult, ["qS", "h32"], ["qS"])
                for h in range(2):
                    hs = slice(h * 64, (h + 1) * 64)
                    ts("dve", qz[hs, h, :, :].rearrange("p k n -> p (k n)"), qS[hs, :, :].rearrange("p k n -> p (k n)"),
                       hdt[hs, 1:2], ALU.mult, ["qS", "hdt"], ["qz"])
                GQ = H // NKV
                PG = GQ // 2
                oatS = sb("oatS", [128, KC, NS], BF16)
                Kc2 = [sb("Kc2_%d" % i, [128, 128]) for i in range(2)]
                Vc2 = [sb("Vc2_%d" % i, [128, 128]) for i in range(2)]
                Vcb = [sb("Vcb_%d" % i, [128, 128], BF16) for i in range(2)]
                KcT = [sb("KcT_%d" % i, [128, 128], BF16) for i in range(2)]
                knb = sb("knb", [128, NKV, NS], BF16)
                cp("dve", knb[:], knew[:], ["knew"], ["knb"])
                Es = [sb("Es_%d" % i, [128, GQ], BF16) for i in range(2)]
                en = [sb("en_%d" % i, [1, GQ], BF16) for i in range(2)]
                vrow = [sb("vrow_%d" % i, [1, 128], BF16) for i in range(2)]
                krow = [sb("krow_%d" % i, [1, 128]) for i in range(2)]
                vrow32 = [sb("vrow32_%d" % i, [1, 128]) for i in range(2)]
                onesb = sb("onesb", [128, 128], BF16); cp("dve", onesb[:], ones[:], ["c_ones"], ["onesb"])
                dn = [sb("dn_%d" % i, [128, GQ]) for i in range(2)]
                ob = [sb("ob_%d" % i, [128, GQ]) for i in range(2)]
                it = 0
                for j in range(NS):
                    P.dma("sp", o_kwins[j, 0:127, :], ck[j, 1:128, :])
                    P.dma("sp", o_vwins[j, 0:127, :], cv[j, 1:128, :])
                    for g in range(NKV):
                        i2 = it % 2; it += 1
                        kc2, vc2, vcb, kct, E_, en_, vr, kr, vr32, dn_, ob_ = (Kc2[i2], Vc2[i2], Vcb[i2], KcT[i2], Es[i2], en[i2],
                                                                            vrow[i2], krow[i2], vrow32[i2], dn[i2], ob[i2])
                        for half in range(2):
                            P.dma("sp", kc2[:, half * 64:(half + 1) * 64], ck[j, :, g * 64:(g + 1) * 64], writes=[kc2.name])
                            P.dma("sp", vc2[:, half * 64:(half + 1) * 64], cv[j, :, g * 64:(g + 1) * 64], writes=[vc2.name])
                        cp("pool", vcb[:], vc2[:], [vc2.name], [vcb.name])
                        bk, ps = B.bank()
                        P.op("pe", lambda E, o=ps[:, 0:128], i_=kc2: E.transpose(o, i_[:], ident[:]), reads=[kc2.name, "c_ident"], writes=[bk])
                        cp("act", kct[:], ps[:, 0:128], [bk], [kct.name])
                        qg = qz[:, :, g * PG:(g + 1) * PG, j]
                        bk, ps = B.bank()
                        mm(ps[:, 0:GQ].rearrange("p (h c) -> p h c", h=2), kct[:], qg, [kct.name, "qz"], [bk])
                        act(E_[:], ps[:, 0:GQ], AF.Exp, [bk], [E_.name], scale=0.125)
                        ts("dve", E_[:], E_[:], msl[:, 0:1], ALU.mult, [E_.name, "c_sl"], [E_.name])
                        bk, ps = B.bank()
                        mm(ps[0:1, 0:GQ].rearrange("p (h c) -> p h c", h=2), knb[:, g, j:j + 1], qg, ["knb", "qz"], [bk])
                        act(en_[:], ps[0:1, 0:GQ], AF.Exp, [bk], [en_.name], scale=0.125)
                        bk, ps = B.bank()
                        P.op("pe", lambda E, o=ps[0:1, 0:128], g=g, j=j: E.transpose(o, knew[:, g, j:j + 1], ident[:]), reads=["knew", "c_ident"], writes=[bk])
                        P.op("pe", lambda E, o=ps[0:1, 128:256], g=g, j=j: E.transpose(o, vnew[:, g // 2, j:j + 1], ident[:]), reads=["vnew", "c_ident"], writes=[bk])
                        cp("act", kr[:], ps[0:1, 0:128], [bk], [kr.name])
                        go = (g % 2) * 64
                        cp("dve", vr32[:, 0:64], ps[0:1, 128 + go:128 + go + 64], [bk], [vr32.name])
                        cp("dve", vr32[:, 64:128], ps[0:1, 128 + go:128 + go + 64], [bk], [vr32.name])
                        cp("pool", vr[:], vr32[:], [vr32.name], [vr.name])
                        P.dma("sp", o_kwins[j, 127:128, g * 64:(g + 1) * 64], kr[:, 0:64], reads=[kr.name])
                        P.dma("sp", o_vwins[j, 127:128, g * 64:(g + 1) * 64], vr32[:, 0:64], reads=[vr32.name])
                        bkN, psN = B.bank()
                        mm(psN[:, 0:GQ], vcb[:], E_[:], [vcb.name, E_.name], [bkN], start=True, stop=False)
                        mm(psN[:, 0:GQ], vr[:], en_[:], [vr.name, en_.name], [bkN], start=False, stop=True)
                        bkD, psD = B.bank()
                        mm(psD[:, 0:GQ], onesb[:], E_[:], ["onesb", E_.name], [bkD], start=True, stop=False)
                        mm(psD[:, 0:GQ], onesb[0:1, :], en_[:], ["onesb", en_.name], [bkD], start=False, stop=True)
                        tt("dve", dn_[:].rearrange("p (h c) -> p h c", h=2), psD[:, 0:GQ].rearrange("p (h c) -> p h c", h=2),
                           esT[:, g * GQ:(g + 1) * GQ].rearrange("p (c h) -> p h c", h=2), ALU.add, [bkD, "esT"], [dn_.name])
                        P.op("dve", lambda E, d_=dn_: E.reciprocal(d_[:], d_[:]), reads=[dn_.name], writes=[dn_.name])
                        tt("dve", ob_[:], psN[:, 0:GQ], dn_[:], ALU.mult, [bkN, dn_.name], [ob_.name])
                        for h in range(2):
                            hs = slice(h * 64, (h + 1) * 64)
                            cp("pool", oatS[hs, g * PG:(g + 1) * PG, j], ob_[hs, h * PG:(h + 1) * PG], [ob_.name], ["oatS"])
                B.linear("swao", w_ao, D, D, xsS(oatS, "oatS"), resS(x2s, x3s, lambda ct: modv[:, 1, 2 * KC + ct, 1:1 + NS]), 512, wbs)
                ffnS(1, x3s, ysf)
                P.dma("sp", o_ys.rearrange("(kc p) n -> p kc n", p=128), ysf[:], reads=[ysf.name])
                P.dma("sp", o_convs.rearrange("l p f n c -> p l (f n c)"), cvo[:].rearrange("p l f n c -> p l (f n c)"), reads=["cvo"])
                P.barrier()
            B.es = es
        sample_path()
        B._st = dict(Gm=Gm, modv=modv, kvmv=kvmv, vec=vec, vec2=vec2, omm=omm, cvp=cvp, hdt=hdt, flag=flag,
                     esink=esink, ident=ident, ones=ones, blk=blk, blkm=blkm, m2b=m2b, mslb=mslb)
        return B, locals()


def finish(B, P, nc):
    P.finish_waits("sp")
    with nc.Block() as block:
        P.emit(block)


def host_inputs(inp, cfg, core):
    D, T, DFF, NS = cfg["D"], cfg["T"], cfg["DFF"], cfg["NS"]
    KC, FC, H = D // 128, DFF // 128, D // 64
    NP = H // 2
    NKV = max(1, H // 8)
    KVW = NKV * 64
    TH = T // 2
    b, hf = core // 2, core % 2
    f32 = np.float32
    m = {}
    xb = np.asarray(inp["x_prompt"][b], f32)
    xT = np.zeros((D, T), f32)
    mask = np.ones((128, T), f32)
    if hf == 1:
        xT[:] = xb.T
    else:
        xT[:, TH:] = xb[:TH].T
        mask[:, :TH] = 0.0
    m["xT"] = xT
    m["maskrow"] = mask
    m["flagcol"] = np.full((128, 1), float(hf), f32)
    ss = slice(core * NS, (core + 1) * NS)
    m["cT"] = np.ascontiguousarray(np.concatenate([inp["c_prompt"][b][None], inp["c_sample"][ss]], 0).T.astype(f32))
    m["xsT"] = np.ascontiguousarray(inp["x_sample"][ss, 0].T.astype(f32))
    m["shsT"] = np.ascontiguousarray(inp["state_shift"][0, ss].T.astype(f32))
    m["st_wkv"] = np.ascontiguousarray(inp["state_wkv"][0, ss].astype(f32))
    m["st_conv"] = np.ascontiguousarray(inp["state_conv"][:, ss].astype(f32).reshape(2, NS, 2, FC, 128).transpose(0, 4, 3, 1, 2))
    m["ck"] = np.ascontiguousarray(inp["cache_k_win"][ss].reshape(NS, 128, KVW).astype(f32))
    m["cv"] = np.ascontiguousarray(inp["cache_v_win"][ss].reshape(NS, 128, KVW).astype(f32))
    m.update(_consts())
    m["mod_w"] = inp["mod_w"]
    m["modb"] = np.stack([_fm(inp["mod_b"][l], 6 * KC) for l in range(2)])
    m["kv_mod_w"] = inp["kv_mod_w"]
    m["kvmodb"] = _fm(inp["kv_mod_b"], 2 * KC)
    vl = [inp["ln1_g"][0], inp["ln1_g"][1], inp["ln2_g"][0], inp["ln2_g"][1]] + \
         [inp["rwkv_mix"][0, i] for i in range(6)] + \
         [inp["rwkv_w0"][0], inp["rwkv_a0"][0], inp["rwkv_k_k"][0], inp["rwkv_k_a"][0],
          inp["rwkv_r_k"][0].reshape(-1), inp["rwkv_lnx_w"][0]]
    m["vecs"] = np.ascontiguousarray(np.stack([_fm(v, KC) for v in vl], 1))
    m["vecs2"] = np.ascontiguousarray(np.stack([_fm(inp["rwkv_lnx_b"][0], KC), _fm(inp["kv_norm_g"], KC)], 1))
    m["convp"] = np.ascontiguousarray(np.stack(
        [np.stack([_fm(inp["ffn_conv_w"][l, j], FC) for j in range(3)] + [_fm(inp["ffn_conv_b"][l], FC)], 1)
         for l in range(2)], 1))
    hd = np.zeros((128, 3 + NP), f32)
    hd[:, 0] = np.tile(inp["k_norm_g"], 2)
    hd[:, 1] = np.tile(inp["attn_q_norm_g"][0], 2)
    hd[:, 3:] = np.repeat(inp["attn_sinks"][0].reshape(NP, 2), 64, axis=1).T
    m["hd"] = hd
    m["sinkT"] = np.tile(inp["attn_sinks"][0][None, :], (128, 1))
    for k in ("rwkv_w1", "rwkv_w2", "rwkv_a1", "rwkv_a2", "rwkv_g1", "rwkv_g2", "rwkv_w_r", "rwkv_w_k",
              "rwkv_w_v", "rwkv_w_o", "attn_w_q", "attn_w_o"):
        m[k] = inp[k][0]
    m["w_kv"] = inp["w_kv"]
    m["ffn_w_in"] = inp["ffn_w_in"]
    m["ffn_w_out"] = inp["ffn_w_out"]
    return {k: np.ascontiguousarray(np.asarray(v, f32)) for k, v in m.items()}


_CACHE = {}


def kernel(**inputs):
    cfg = REAL_CFG
    inp = {k: np.asarray(v) for k, v in inputs.items()}
    if "prog" not in _CACHE:
        B, L = build(cfg)
        finish(B, B.P, B.nc)
        _CACHE["prog"] = B
    B = _CACHE["prog"]
    n = 8
    in_maps = []
    for c in range(n):
        m = host_inputs(inp, cfg, c)
        in_maps.append({k: m[k] for k in B.ins})
    res = run_bass_kernel_spmd(B.nc, in_maps, core_ids=list(range(n)))
    R = res.results
    D, T, DFF, NS = cfg["D"], cfg["T"], cfg["DFF"], cfg["NS"]
    KC, FC, H = D // 128, DFF // 128, D // 64
    NKV = max(1, H // 8)
    TH = T // 2
    NB = 4
    f32 = np.float32
    y_p = np.zeros((NB, T, D), f32); y_s = np.zeros((NB * 8, 1, D), f32)
    wkv_p = np.zeros((1, NB, H, 64, 64), f32); wkv_s = np.zeros((1, NB * 8, H, 64, 64), f32)
    sh_p = np.zeros((1, NB, D), f32); sh_s = np.zeros((1, NB * 8, D), f32)
    cv_p = np.zeros((2, NB, 2, DFF), f32); cv_s = np.zeros((2, NB * 8, 2, DFF), f32)
    kw_p = np.zeros((NB, 128, NKV, 64), f32); kw_s = np.zeros((NB * 8, 128, NKV, 64), f32)
    vw_p = np.zeros((NB, 128, NKV, 64), f32); vw_s = np.zeros((NB * 8, 128, NKV, 64), f32)
    for c in range(n):
        b, hf = c // 2, c % 2
        r = R[c]
        y_p[b, hf * TH:(hf + 1) * TH] = r["o_y"].T
        ss = slice(c * NS, (c + 1) * NS)
        y_s[ss, 0] = r["o_ys"].T
        wkv_s[0, ss] = r["o_wkvs"]
        sh_s[0, ss] = r["o_shifts"].transpose(2, 1, 0).reshape(NS, D)
        cv_s[:, ss] = r["o_convs"].transpose(0, 3, 4, 2, 1).reshape(2, NS, 2, DFF)
        kw_s[ss] = r["o_kwins"].reshape(NS, 128, NKV, 64)
        vw_s[ss] = r["o_vwins"].reshape(NS, 128, NKV, 64)
        if hf == 1:
            wkv_p[0, b] = r["o_wkv"]
            sh_p[0, b] = r["o_shift"].T.reshape(D)
            cv_p[:, b] = r["o_conv"].transpose(0, 3, 2, 1).reshape(2, 2, DFF)
            kw_p[b] = r["o_kwin"].reshape(128, NKV, 64)
            vw_p[b] = r["o_vwin"].reshape(128, NKV, 64)
    return (y_p, y_s, wkv_p, wkv_s, sh_p, sh_s, cv_p, cv_s, kw_p, kw_s, vw_p, vw_s)
```

```python
import contextlib
import numpy as np
import concourse.bass as bass
import concourse.mybir as mybir
from concourse.bass_utils import run_bass_kernel_spmd

F32 = mybir.dt.float32
BF16 = mybir.dt.bfloat16
AF = mybir.ActivationFunctionType
ALU = mybir.AluOpType

REAL_CFG = dict(D=4096, T=2048, DFF=14336, DL=128, DA=128, DG=480, NS=4, WB=128)


class Prog:
    ENG = ("pe", "act", "dve", "pool", "sp")

    def __init__(self, nc, es):
        self.nc = nc
        self.h = dict(pe=nc.tensor, act=nc.scalar, dve=nc.vector, pool=nc.gpsimd, sp=nc.sync)
        self.sem = {e: es.enter_context(nc.semaphore("s_" + e)) for e in self.ENG}
        self.cnt = {e: 0 for e in self.ENG}
        self.q = {e: [] for e in self.ENG}
        self.waited = {e: {} for e in self.ENG}
        self.lastw = {}
        self.readers = {}
        self.semobj = {("c", e): self.sem[e] for e in self.ENG}
        self.dsem = {}
        for e in ("sp", "pool", "act"):
            self.dsem[e] = [es.enter_context(nc.semaphore("d_%s%d" % (e, i))) for i in range(12)]
            for i, s in enumerate(self.dsem[e]):
                self.semobj[("d", e, i)] = s
        self.dval = {k: 0 for k in self.semobj if k[0] == "d"}
        self.drr = {e: 0 for e in ("sp", "pool", "act")}
        self.nbank = 0

    def _deps(self, reads, writes):
        need = {}
        for r in reads:
            lw = self.lastw.get(r)
            if lw:
                need[lw[0]] = max(need.get(lw[0], 0), lw[1])
        for w in writes:
            lw = self.lastw.get(w)
            if lw:
                need[lw[0]] = max(need.get(lw[0], 0), lw[1])
            for k, v in self.readers.get(w, {}).items():
                need[k] = max(need.get(k, 0), v)
        return need

    def _emit_waits(self, eng, need):
        for k, v in need.items():
            if eng == "pe" and k == ("c", "pe"):
                continue
            if self.waited[eng].get(k, 0) < v:
                self.waited[eng][k] = v
                so = self.semobj[k]
                self.q[eng].append(lambda E, so=so, v=v: E.wait_ge(so, v))

    def _mark(self, tok, reads, writes):
        for w in writes:
            self.lastw[w] = tok
            self.readers[w] = {}
        for r in reads:
            self.readers.setdefault(r, {})
            d = self.readers[r]
            d[tok[0]] = max(d.get(tok[0], 0), tok[1])

    def op(self, eng, fn, reads=(), writes=()):
        bk_ = [r for r in reads if isinstance(r, tuple) and r and r[0] == "bank"]
        if bk_:
            writes = list(writes) + bk_
        need = self._deps(reads, writes)
        self._emit_waits(eng, need)
        self.cnt[eng] += 1
        idx = self.cnt[eng]
        so = self.sem[eng]
        self.q[eng].append(lambda E, fn=fn, so=so: fn(E).then_inc(so, 1))
        self.waited[eng][("c", eng)] = max(self.waited[eng].get(("c", eng), 0), 0)
        self._mark((("c", eng), idx), reads, writes)

    def dma(self, eng, out, in_, reads=(), writes=()):
        need = self._deps(reads, writes)
        i = self.drr[eng]
        self.drr[eng] = (i + 1) % len(self.dsem[eng])
        key = ("d", eng, i)
        if self.dval[key] > 0:
            need[key] = max(need.get(key, 0), self.dval[key])
        self._emit_waits(eng, need)
        self.dval[key] += 16
        so = self.semobj[key]
        self.q[eng].append(lambda E, so=so, out=out, in_=in_: E.dma_start(out=out, in_=in_).then_inc(so, 16))
        self._mark((key, self.dval[key]), reads, writes)

    def finish_waits(self, eng="sp"):
        need = {}
        for k, v in self.dval.items():
            if v:
                need[k] = v
        for e in self.ENG:
            if self.cnt[e] and e != eng:
                need[("c", e)] = self.cnt[e]
        self._emit_waits(eng, need)

    def barrier(self):
        for e in self.ENG:
            self.finish_waits(e)

    def emit(self, block):
        for e, reg in (("pe", block.tensor), ("act", block.scalar), ("dve", block.vector),
                       ("pool", block.gpsimd), ("sp", block.sync)):
            ops = self.q[e]

            def body(E, ops=ops):
                for f in ops:
                    f(E)
            reg(body)


def _chunks(n, c=128):
    return [(i, min(c, n - i)) for i in range(0, n, c)]


class Builder:
    def __init__(self, cfg, debug=()):
        self.cfg = cfg
        self.debug = set(debug)
        c = cfg
        self.D, self.T, self.DFF = c["D"], c["T"], c["DFF"]
        self.KC = self.D // 128
        self.FC = self.DFF // 128
        self.H = self.D // 64
        self.NP = self.H // 2
        self.NKV = max(1, self.H // 8)
        self.NS = c["NS"]
        self.TH = self.T // 2
        self.HALO = 130
        self.W0 = self.TH - self.HALO
        self.NW = self.T - self.W0
        self.nc = bass.Bass("TRN2", target_bir_lowering=False)
        self.ins = {}
        self.outs = {}

    def din(self, name, shape):
        t = self.nc.dram_tensor(name, list(shape), F32, kind="ExternalInput").ap()
        self.ins[name] = tuple(shape)
        return t

    def dout(self, name, shape, dt=F32):
        t = self.nc.dram_tensor(name, list(shape), dt, kind="ExternalOutput").ap()
        self.outs[name] = tuple(shape)
        return t

    def scratch(self, name, shape, dt=F32):
        if name in self.debug:
            return self.dout(name, shape, dt)
        return self.nc.dram_tensor(name, list(shape), dt, kind="Internal").ap()

    def sb(self, name, shape, dt=F32):
        return self.es.enter_context(self.nc.sbuf_tensor("t_" + name, list(shape), dt))

    def linear(self, name, w, K, N, xs, epi, gw, wb, colgroups=None):
        P = self.P
        kch = _chunks(K)
        nk = len(kch)
        nfull = K // 128
        for gi, (n0, gsz) in enumerate(colgroups or _chunks(N, gw)):
            buf = wb[gi % len(wb)]
            bkey = ("wb", buf.name)
            if nfull:
                P.dma("pool", buf[:, 0:nfull, 0:gsz],
                      w[0:nfull * 128, n0:n0 + gsz].rearrange("(kc p) n -> p kc n", p=128),
                      writes=[bkey])
            if nfull < nk:
                k0, ksz = kch[-1]
                P.dma("pool", buf[0:ksz, nfull, 0:gsz], w[k0:k0 + ksz, n0:n0 + gsz], writes=[bkey])
            for (c0, m) in _chunks(gsz):
                for g, (xf, ncols, rk) in enumerate(xs):
                    bk, ps = self.bank()
                    for ki, (k0, ksz) in enumerate(kch):
                        lhsT = buf[0:ksz, ki, c0:c0 + m]
                        rhs = xf(ki, ksz)
                        P.op("pe", lambda E, o=ps[0:m, 0:ncols], l=lhsT, r=rhs, s=(ki == 0), t=(ki == nk - 1):
                             E.matmul(o, l, r, start=s, stop=t),
                             reads=[bkey] + list(rk), writes=[bk])
                    epi((n0 + c0) // 128, n0 + c0, m, g, ps[0:m, 0:ncols], bk)

    def bank(self):
        i = self.P.nbank % len(self.banks)
        self.P.nbank += 1
        return ("bank", i), self.banks[i]

    def act(self, out, in_, func, R, W, bias=None, scale=None, eng="act"):
        kw = {}
        if bias is not None:
            kw["bias"] = bias
        if scale is not None:
            kw["scale"] = scale
        self.P.op("act", lambda E: E.activation(out, in_, func, **kw), reads=R, writes=W)

    def ts(self, eng, out, in0, s1, op0, R, W, s2=None, op1=None):
        if op1 is None:
            self.P.op(eng, lambda E: E.tensor_scalar(out, in0, s1, None, op0), reads=R, writes=W)
        else:
            self.P.op(eng, lambda E: E.tensor_scalar(out, in0, s1, s2, op0, op1), reads=R, writes=W)

    def tt(self, eng, out, a, b, op, R, W):
        self.P.op(eng, lambda E: E.tensor_tensor(out, a, b, op), reads=R, writes=W)

    def stt(self, out, in0, scalar, in1, op0, op1, R, W):
        self.P.op("dve", lambda E: E.scalar_tensor_tensor(out, in0, scalar, in1, op0, op1), reads=R, writes=W)

    def cp(self, eng, out, in_, R, W):
        if eng == "act":
            self.P.op("act", lambda E: E.activation(out, in_, AF.Copy), reads=R, writes=W)
        else:
            self.P.op(eng, lambda E: E.tensor_copy(out, in_), reads=R, writes=W)

    def mm(self, out, lhsT, rhs, R, W, start=True, stop=True):
        self.P.op("pe", lambda E: E.matmul(out, lhsT, rhs, start=start, stop=stop), reads=R, writes=W)

    def dump(self, name, ap, key, shape):
        if ("dbg_" + name) in self.debug:
            dd = self.dout("dbg_" + name, shape)
            self.P.dma("sp", dd, ap, reads=[key])

    def load(self, tile_ap, dram_ap, key, q="sp"):
        self.P.dma(q, tile_ap, dram_ap, writes=[key])


def _fm(v, n):
    return np.ascontiguousarray(np.asarray(v, np.float32).reshape(n, 128).T)


def _consts():
    p = np.arange(128)[:, None]
    f = np.arange(128)[None, :]
    c = {}
    c["c_ident"] = (p == f).astype(np.float32)
    c["c_ones"] = np.ones((128, 128), np.float32)
    blk = ((p // 64) == (f // 64)).astype(np.float32)
    c["c_blk"] = blk
    c["c_blkm"] = blk / 64.0
    su = (p < f).astype(np.float32)
    ui = (p <= f).astype(np.float32)
    c["c_m2"] = np.concatenate([su, ui], axis=1)
    c["c_sl"] = (p > f).astype(np.float32)
    c["c_bd8"] = ((p // 8) == (f // 8)).astype(np.float32)
    lvL, lvU = [], []
    for m in (8, 16, 32, 64):
        ml = (((p // (2 * m)) == (f // (2 * m))) & ((p % (2 * m)) >= m) & ((f % (2 * m)) < m)).astype(np.float32)
        lvL.append(ml); lvU.append(ml.T)
    c["c_lvL"] = np.concatenate(lvL, axis=1)
    c["c_lvU"] = np.concatenate(lvU, axis=1)
    return c


def build(cfg, debug=()):
    B = Builder(cfg, debug)
    nc = B.nc
    D, T, DFF, KC, FC, H, NP, NKV, NS = B.D, B.T, B.DFF, B.KC, B.FC, B.H, B.NP, B.NKV, B.NS
    DL, DA, DG = cfg["DL"], cfg["DA"], cfg["DG"]
    TH = T // 2
    HALO = 258
    W0 = TH - HALO
    NW = T - W0
    KVW = NKV * 64
    KVT = max(1, KVW // 128)
    NC5 = 1 + NS
    C = 128
    NCH = T // C
    NTF = T // 4
    din, dout = B.din, B.dout

    xT = din("xT", [D, T])
    maskrow = din("maskrow", [128, T])
    flagcol = din("flagcol", [128, 1])
    cT = din("cT", [D, NC5])
    xsT = din("xsT", [D, NS])
    shsT = din("shsT", [D, NS])
    st_wkv = din("st_wkv", [NS, H, 64, 64])
    st_conv = din("st_conv", [2, 128, FC, NS, 2])
    ck = din("ck", [NS, 128, KVW])
    cv = din("cv", [NS, 128, KVW])
    cst = {k: din(k, list(v.shape)) for k, v in _consts().items()}
    mod_w = din("mod_w", [2, D, 6 * D])
    modb = din("modb", [2, 128, 6 * KC])
    kv_mod_w = din("kv_mod_w", [D, 2 * D])
    kvmodb = din("kvmodb", [128, 2 * KC])
    vecs = din("vecs", [128, 16, KC])
    vecs2 = din("vecs2", [128, 2, KC])
    convp = din("convp", [128, 2, 4, FC])
    hd = din("hd", [128, 3 + NP])
    sinkT = din("sinkT", [128, H])
    w1 = din("rwkv_w1", [D, DL]); w2 = din("rwkv_w2", [DL, D])
    a1 = din("rwkv_a1", [D, DA]); a2 = din("rwkv_a2", [DA, D])
    g1 = din("rwkv_g1", [D, DG]); g2 = din("rwkv_g2", [DG, D])
    w_r = din("rwkv_w_r", [D, D]); w_k = din("rwkv_w_k", [D, D]); w_v = din("rwkv_w_v", [D, D])
    w_o = din("rwkv_w_o", [D, D])
    w_kv = din("w_kv", [D, 2 * KVW])
    w_q = din("attn_w_q", [D, D]); w_ao = din("attn_w_o", [D, D])
    w_in = din("ffn_w_in", [2, D, 2 * DFF]); w_out = din("ffn_w_out", [2, DFF, D])

    o_y = dout("o_y", [D, T - TH])
    o_ys = dout("o_ys", [D, NS])
    o_wkv = dout("o_wkv", [H, 64, 64])
    o_wkvs = dout("o_wkvs", [NS, H, 64, 64])
    o_shift = dout("o_shift", [128, KC])
    o_shifts = dout("o_shifts", [128, KC, NS])
    o_conv = dout("o_conv", [2, 128, FC, 2])
    o_convs = dout("o_convs", [2, 128, FC, NS, 2])
    o_kwin = dout("o_kwin", [128, KVW])
    o_vwin = dout("o_vwin", [128, KVW])
    o_kwins = dout("o_kwins", [NS, 128, KVW])
    o_vwins = dout("o_vwins", [NS, 128, KVW])

    sc = {n: B.scratch("s_" + n, [D, T]) for n in ("r", "k", "v", "ld", "a", "g")}
    scs = {n: B.scratch("ss_" + n, [D, NS]) for n in ("r", "k", "v", "ld", "a", "g")}
    s_yg = B.scratch("s_yg", [D, T], BF16)
    s_ygs = B.scratch("s_ygs", [D, NS], BF16)

    with contextlib.ExitStack() as es:
        B.es = es
        P = B.P = Prog(nc, es)
        B.banks = [es.enter_context(nc.psum_tensor("ps%d" % i, [128, 512], F32)) for i in range(8)]
        sb = B.sb
        act, ts, tt, stt, cp, mm = B.act, B.ts, B.tt, B.stt, B.cp, B.mm

        K_ = {}
        for k, v in cst.items():
            if k in ("c_bd8", "c_lvL", "c_lvU"):
                continue
            K_[k] = sb(k, list(B.ins[k]))
            P.dma("sp", K_[k][:], v[:, :], writes=[k])
        ident, ones, blk, blkm, m2, msl = (K_[k] for k in ("c_ident", "c_ones", "c_blk", "c_blkm", "c_m2", "c_sl"))
        m2b = sb("m2b", [128, 256], BF16); cp("dve", m2b[:], m2[:], ["c_m2"], ["m2b"])
        mslb = sb("mslb", [128, 128], BF16); cp("dve", mslb[:], msl[:], ["c_sl"], ["mslb"])
        vec = sb("vec", [128, 16, KC]); P.dma("sp", vec[:], vecs[:, :, :], writes=["vec"])
        vec2 = sb("vec2", [128, 2, KC]); P.dma("sp", vec2[:], vecs2[:, :, :], writes=["vec2"])
        cvp = sb("cvp", [128, 2, 4, FC]); P.dma("sp", cvp[:], convp[:, :, :, :], writes=["cvp"])
        hdt = sb("hdt", [128, 3 + NP]); P.dma("sp", hdt[:], hd[:, :], writes=["hdt"])
        flag = sb("flag", [128, 1]); P.dma("sp", flag[:], flagcol[:, :], writes=["flag"])
        modbt = sb("modbt", [128, 2, 6 * KC]); P.dma("sp", modbt[:], modb.rearrange("l p n -> p l n"), writes=["modbt"])
        kvmodbt = sb("kvmodbt", [128, 2 * KC]); P.dma("sp", kvmodbt[:], kvmodb[:, :], writes=["kvmodbt"])
        omm = sb("omm", [128, 6, KC])
        ts("dve", omm[:], vec[:, 4:10, :], -1.0, ALU.mult, ["vec"], ["omm"], 1.0, ALU.add)
        esink = sb("esink", [128, NP])
        act(esink[:], hdt[:, 3:3 + NP], AF.Exp, ["hdt"], ["esink"])
        modv = sb("modv", [128, 2, 6 * KC, NC5])
        kvmv = sb("kvmv", [128, 2 * KC, NC5])
        Gm = sb("Gm", [128, 5, KC, NC5])

        with contextlib.ExitStack() as es2:
            B.es = es2
            c5 = sb("c5", [128, KC, NC5]); P.dma("sp", c5[:], cT.rearrange("(kc p) n -> p kc n", p=128), writes=["c5"])
            c5s = sb("c5s", [128, KC, NC5])
            act(c5s[:], c5[:], AF.Sigmoid, ["c5"], ["c5s"])
            c5b = sb("c5b", [128, KC, NC5], BF16)
            tt("dve", c5b[:], c5[:], c5s[:], ALU.mult, ["c5", "c5s"], ["c5b"])
            wbA = [sb("wbA%d" % i, [128, KC, 512], BF16) for i in range(2)]
            xsA = [(lambda ki, ksz: c5b[0:ksz, ki, :], NC5, ["c5b"])]
            for l in range(2):
                def epiA(ct, n0, m, g, ps, bk, l=l):
                    act(modv[:, l, ct, :], ps, AF.Identity, [bk, "modbt"], ["modv"], bias=modbt[:, l, ct:ct + 1])
                B.linear("modw%d" % l, mod_w[l], D, 6 * D, xsA, epiA, 512, wbA)

            def epiK(ct, n0, m, g, ps, bk):
                act(kvmv[:, ct, :], ps, AF.Identity, [bk, "kvmodbt"], ["kvmv"], bias=kvmodbt[:, ct:ct + 1])
            B.linear("kvmodw", kv_mod_w, D, 2 * D, xsA, epiK, 512, wbA)
            for j in range(NC5):
                for gi, (l, which, vi) in enumerate(((0, 1, 0), (0, 4, 2), (1, 1, 1), (1, 4, 3))):
                    stt(Gm[:, gi, :, j], modv[:, l, which * KC:(which + 1) * KC, j], 1.0, vec[:, vi, :],
                        ALU.add, ALU.mult, ["modv", "vec"], ["Gm"])
                stt(Gm[:, 4, :, j], kvmv[:, KC:2 * KC, j], 1.0, vec2[:, 1, :], ALU.add, ALU.mult,
                    ["kvmv", "vec2"], ["Gm"])
            P.barrier()
        B.es = es
        shiftout = sb("shiftout", [128, KC])
        hlast = sb("hlast", [128, KC], BF16)
        P.op("pool", lambda E: E.memset(hlast[:], 0.0), writes=["hlast"])
        LDC = -float(np.exp(-0.5))

        def front(cols, xsrc, dst, per_col, hprev_src=None, tag="p"):
            grp = _chunks(cols, 512)
            with contextlib.ExitStack() as es3:
                B.es = es3
                hb = sb("hb" + tag, [128, KC, cols + 1], BF16)
                xis = [sb("xi%s%d" % (tag, i), [128, KC, cols], BF16) for i in range(2)]
                xs_ = [sb("xs%s%d" % (tag, i), [128, cols]) for i in range(2)]
                sq = [sb("sq%s%d" % (tag, i), [128, cols]) for i in range(2)]
                rstd = sb("rstd" + tag, [128, cols])
                hf = [sb("hf%s%d" % (tag, i), [128, cols]) for i in range(2)]
                og = [sb("og%s%d" % (tag, i), [128, 512]) for i in range(3)]
                wbF = [sb("wbF%s%d" % (tag, i), [128, KC, 256], BF16) for i in range(2)]
                wbS = [sb("wbS%s%d" % (tag, i), [128, 4, 512], BF16) for i in range(2)]
                lo = sb("lo" + tag, [128, 4, cols], BF16)
                bks = [B.bank() for _ in grp]
                for kc in range(KC):
                    x_ = xs_[kc % 2]; q_ = sq[kc % 2]
                    P.dma("sp", x_[:], xsrc(kc), writes=[x_.name])
                    act(q_[:], x_[:], AF.Square, [x_.name], [q_.name])
                    for gi, (c0, n) in enumerate(grp):
                        mm(bks[gi][1][:, 0:n], ones[:], q_[:, c0:c0 + n], [q_.name, "c_ones"], [bks[gi][0]],
                           start=(kc == 0), stop=(kc == KC - 1))
                for gi, (c0, n) in enumerate(grp):
                    act(rstd[:, c0:c0 + n], bks[gi][1][:, 0:n], AF.Ln, [bks[gi][0]], ["rstd"], scale=1.0 / D, bias=1e-6)
                if ("dbg_" + tag) in B.debug:
                    dd = B.dout("dbg_ln" + tag, [128, cols])
                    P.dma("sp", dd[:, :], rstd[:], reads=["rstd"])
                    dd2 = B.dout("dbg_sq" + tag, [128, cols])
                    P.dma("sp", dd2[:, :], sq[(KC - 1) % 2][:], reads=[sq[(KC - 1) % 2].name])
                act(rstd[:], rstd[:], AF.Exp, ["rstd"], ["rstd"], scale=-0.5)
                if ("dbg_" + tag) in B.debug:
                    dd = B.dout("dbg_rstd" + tag, [128, cols])
                    P.dma("sp", dd[:, :], rstd[:], reads=["rstd"])
                if not per_col:
                    cp("pool", hb[:, :, 0], hlast[:], ["hlast"], ["hb"])
                for kc in range(KC):
                    x_ = xs_[kc % 2]; h_ = hf[kc % 2]
                    P.dma("sp", x_[:], xsrc(kc), writes=[x_.name])
                    tt("dve", h_[:], x_[:], rstd[:], ALU.mult, [x_.name, "rstd"], [h_.name])
                    if kc == KC - 1:
                        B.dump("xn" + tag, h_[:], h_.name, [128, cols])
                        B.dump("xx" + tag, x_[:], x_.name, [128, cols])
                    if per_col:
                        tt("dve", h_[:], h_[:], Gm[:, 0, kc, 1:1 + NS], ALU.mult, [h_.name, "Gm"], [h_.name])
                        tt("dve", h_[:], h_[:], modv[:, 0, 0 * KC + kc, 1:1 + NS], ALU.add, [h_.name, "modv"], [h_.name])
                        cp("pool", shiftouts[:, kc, :], h_[:], [h_.name], ["shiftouts"])
                        cp("pool", hb[:, kc, 1:cols + 1], h_[:], [h_.name], ["hb"])
                    else:
                        act(h_[:], h_[:], AF.Identity, [h_.name, "Gm", "modv"], [h_.name],
                            bias=modv[:, 0, kc, 0:1], scale=Gm[:, 0, kc, 0:1])
                        if kc == KC - 1:
                            B.dump("hh" + tag, h_[:], h_.name, [128, cols])
                        cp("pool", shiftout[:, kc:kc + 1], h_[:, cols - 1:cols], [h_.name], ["shiftout"])
                        tt("dve", hb[:, kc, 1:cols + 1], h_[:], mrow[:, 0:cols], ALU.mult, [h_.name, "mrow"], ["hb"])
                if per_col:
                    hps = sb("hps", [128, KC, NS])
                    P.dma("sp", hps[:], shsT.rearrange("(kc p) n -> p kc n", p=128), writes=["hps"])
                else:
                    cp("pool", hlast[:], hb[:, :, cols], ["hb"], ["hlast"])

                def hprev(kc):
                    if per_col:
                        return hps[:, kc, :]
                    return hb[:, kc, 0:cols]

                def mix(i):
                    xi = xis[i % 2]
                    for kc in range(KC):
                        t_ = sq[kc % 2]
                        act(t_[:], hprev(kc), AF.Copy, ["hb", "hps", "vec"], [t_.name], scale=vec[:, 4 + i, kc:kc + 1])
                        stt(xi[:, kc, :], hb[:, kc, 1:cols + 1], omm[:, i, kc:kc + 1], t_[:], ALU.mult, ALU.add,
                            ["hb", "omm", t_.name], [xi.name])
                    return [(lambda ki, ksz, c0=c0, n=n, xi=xi: xi[0:ksz, ki, c0:c0 + n], n, [xi.name]) for (c0, n) in grp]
                xsL = [(lambda ki, ksz, c0=c0, n=n: lo[0:ksz, ki, c0:c0 + n], n, ["lo"]) for (c0, n) in grp]
                ogi = [0]

                def store(name, func, bias_v=None, post=None):
                    def epi(ct, n0, m, g, ps, bk):
                        o_ = og[ogi[0] % 3]; ogi[0] += 1
                        c0, n = grp[g]
                        if func is None:
                            cp("act", o_[0:m, 0:n], ps, [bk], [o_.name])
                        else:
                            act(o_[0:m, 0:n], ps, func, [bk, "vec"], [o_.name],
                                bias=(vec[:, bias_v, ct:ct + 1] if bias_v is not None else None))
                        if post is not None:
                            ts("dve", o_[0:m, 0:n], o_[0:m, 0:n], post, ALU.mult, [o_.name], [o_.name])
                        P.dma("sp", dst[name](ct, c0, n), o_[0:m, 0:n], reads=[o_.name])
                    return epi

                def tolo(func):
                    def epi(ct, n0, m, g, ps, bk):
                        c0, n = grp[g]
                        if func is None:
                            cp("act", lo[0:m, ct, c0:c0 + n], ps, [bk], ["lo"])
                        else:
                            act(lo[0:m, ct, c0:c0 + n], ps, func, [bk], ["lo"])
                    return epi
                for i, (nm, w_) in enumerate((("r", w_r), ("ld", None), ("k", w_k), ("v", w_v), ("a", None), ("g", None))):
                    xsX = mix(i)
                    if w_ is not None:
                        B.linear(nm, w_, D, D, xsX, store(nm, None), 256, wbF)
                    elif nm == "ld":
                        B.linear("w1", w1, D, DL, xsX, tolo(AF.Tanh), 256, wbF)
                        B.linear("w2", w2, DL, D, xsL, store("ld", AF.Sigmoid, 10, LDC), 512, wbS)
                    elif nm == "a":
                        B.linear("a1", a1, D, DA, xsX, tolo(None), 256, wbF)
                        B.linear("a2", a2, DA, D, xsL, store("a", AF.Sigmoid, 11), 512, wbS)
                    else:
                        B.linear("g1", g1, D, DG, xsX, tolo(AF.Sigmoid), 256, wbF)
                        B.linear("g2", g2, DG, D, xsL, store("g", None), 512, wbS)
                P.barrier()
            B.es = es

        mrow = sb("mrow", [128, NTF])
        shiftouts = sb("shiftouts", [128, KC, NS])
        for p_ in range(T // NTF):
            P.dma("sp", mrow[:], maskrow[:, p_ * NTF:(p_ + 1) * NTF], writes=["mrow"])
            dstp = {n: (lambda ct, c0, n_, n=n, p_=p_: sc[n][ct * 128:(ct + 1) * 128, p_ * NTF + c0:p_ * NTF + c0 + n_])
                    for n in sc}
            front(NTF, lambda kc, p_=p_: xT[kc * 128:(kc + 1) * 128, p_ * NTF:(p_ + 1) * NTF], dstp, False, tag="p%d" % p_)
        P.dma("sp", o_shift[:, :], shiftout[:], reads=["shiftout"])
        dsts = {n: (lambda ct, c0, n_, n=n: scs[n][ct * 128:(ct + 1) * 128, c0:c0 + n_]) for n in scs}
        front(NS, lambda kc: xsT[kc * 128:(kc + 1) * 128, :], dsts, True, tag="s")
        P.dma("sp", o_shifts[:, :, :], shiftouts[:], reads=["shiftouts"])
        B.dump("gm_end", Gm[:].rearrange("p a k n -> p (a k n)"), "Gm", [128, 5 * KC * NC5])
        B.dump("modv_end", modv[:].rearrange("p a k n -> p (a k n)"), "modv", [128, 2 * 6 * KC * NC5])
        GN_EPS = 64e-5
        omka = sb("omka", [128, KC])
        ts("dve", omka[:], vec[:, 13, :], -1.0, ALU.mult, ["vec"], ["omka"], 1.0, ALU.add)
        onesC = sb("onesC", [128, C]); P.op("pool", lambda E: E.memset(onesC[:], 1.0), writes=["onesC"])
        m2b2 = sb("m2b2", [128, 2, 256], BF16)
        for h_ in range(2):
            cp("dve", m2b2[:, h_, :], m2[:], ["c_m2"], ["m2b2"])
        msl2 = sb("msl2", [128, 2, 128], BF16)
        idb2 = sb("idb2", [128, 2, 128], BF16)
        for h_ in range(2):
            cp("dve", msl2[:, h_, :], msl[:], ["c_sl"], ["msl2"])
            cp("dve", idb2[:, h_, :], ident[:], ["c_ident"], ["idb2"])

        def diag2(t3):
            return t3[:].rearrange("p h (a b) -> p (h a) b", b=64)[:, 0:4:3, :]

        def scan_stage():
            with contextlib.ExitStack() as es4:
                B.es = es4
                GP = min(4, NP)
                NSLOT = 4
                Sm = sb("Sm", [128, NP, 64])
                Sblk = sb("Sblk", [128, NP, 128], BF16)
                bd8_2 = sb("bd8_2", [128, 2, 128], BF16)
                lvL2 = sb("lvL2", [128, 4, 2, 128], BF16); lvU2 = sb("lvU2", [128, 4, 2, 128], BF16)
                ctmp = sb("ctmp", [128, 1152])
                P.dma("sp", ctmp[:, 0:128], cst["c_bd8"][:, :], writes=["ctmp"])
                P.dma("sp", ctmp[:, 128:640], cst["c_lvL"][:, :], writes=["ctmp"])
                P.dma("sp", ctmp[:, 640:1152], cst["c_lvU"][:, :], writes=["ctmp"])
                for h_ in range(2):
                    cp("dve", bd8_2[:, h_, :], ctmp[:, 0:128], ["ctmp"], ["bd8_2"])
                    for l_ in range(4):
                        cp("dve", lvL2[:, l_, h_, :], ctmp[:, 128 + l_ * 128:128 + (l_ + 1) * 128], ["ctmp"], ["lvL2"])
                        cp("dve", lvU2[:, l_, h_, :], ctmp[:, 640 + l_ * 128:640 + (l_ + 1) * 128], ["ctmp"], ["lvU2"])
                lds = {n: [sb("ld_%s%d" % (n, i), [128, GP, C]) for i in range(2)] for n in ("r", "k", "v", "ld", "a", "g")}
                SL = []
                for sl in range(NSLOT):
                    d = {}
                    for n in ("kkr", "sqk", "rn", "kk", "t1", "k2", "bet", "L", "eL", "enL", "ePrev", "eRev", "tmp",
                              "bts", "kts", "ysb", "mu2", "var", "rs", "yn", "rk", "bon"):
                        d[n] = sb("%s_%d" % (n, sl), [128, C])
                    d["gC"] = sb("gC_%d" % sl, [128, 1])
                    d["YY"] = sb("YY_%d" % sl, [128, 2 * C])
                    d["AR"] = sb("AR_%d" % sl, [128, 2 * C], BF16)
                    for n in ("Bbp", "Kbp", "Pm", "Q0", "Q1", "P0", "P1", "T0", "T1", "Vtp", "Btp", "Ktp", "Up",
                              "QA", "PA", "Dq0", "Dq1", "Dp0", "Dp1", "Y1", "Z1", "mt1", "mt2"):
                        d[n] = sb("%s_%d" % (n, sl), [128, 2, 128], BF16)
                    d["QM"] = sb("QM_%d" % sl, [128, 2, 2 * C], BF16)
                    d["KM"] = sb("KM_%d" % sl, [128, 2, 2 * C], BF16)
                    d["RH"] = sb("RH_%d" % sl, [128, 2, 64], BF16)
                    d["yg"] = sb("yg_%d" % sl, [128, C], BF16)
                    for n in ("Bbp", "Kbp", "Vtp", "Btp", "Ktp", "Up"):
                        P.op("pool", lambda E, t=d[n]: E.memset(t[:], 0.0), writes=[d[n].name])
                    SL.append(d)
                P.op("pool", lambda E: E.memset(Sm[:], 0.0), writes=[("Sm", q) for q in range(NP)])
                P.op("pool", lambda E: E.memset(Sblk[:], 0.0), writes=[("S", q) for q in range(NP)])
                cnt = 0
                for j in range(NCH):
                    for q0 in range(0, NP, GP):
                        gb = (cnt % 2); cnt += 1
                        for n in lds:
                            P.dma("sp", lds[n][gb][:], sc[n][q0 * 128:(q0 + GP) * 128, j * C:(j + 1) * C]
                                  .rearrange("(q p) t -> p q t", p=128), writes=[lds[n][gb].name])
                        def unit(qi, j=j, gb=gb, q0=q0):
                            hp = q0 + qi
                            W = SL[hp % NSLOT]
                            N_ = lambda n: W[n].name
                            r_, k_, v_, ld_, a_, g_ = (lds[n][gb][:, qi, :] for n in ("r", "k", "v", "ld", "a", "g"))
                            R6 = {n: lds[n][gb].name for n in lds}
                            pk = ("S", hp)
                            ts("dve", W["kkr"][:], k_, vec[:, 12, hp:hp + 1], ALU.mult, [R6["k"], "vec"], [N_("kkr")])
                            act(W["sqk"][:], W["kkr"][:], AF.Square, [N_("kkr")], [N_("sqk")])
                            bk, ps = B.bank()
                            mm(ps[:, 0:C], blk[:], W["sqk"][:], ["c_blk", N_("sqk")], [bk])
                            ts("dve", W["rn"][:], ps[:, 0:C], 1e-24, ALU.max, [bk], [N_("rn")])
                            act(W["rn"][:], W["rn"][:], AF.Ln, [N_("rn")], [N_("rn")])
                            act(W["rn"][:], W["rn"][:], AF.Exp, [N_("rn")], [N_("rn")], scale=-0.5)
                            tt("dve", W["kk"][:], W["kkr"][:], W["rn"][:], ALU.mult, [N_("kkr"), N_("rn")], [N_("kk")])
                            act(W["t1"][:], a_, AF.Identity, [R6["a"], "vec", "omka"], [N_("t1")],
                                bias=omka[:, hp:hp + 1], scale=vec[:, 13, hp:hp + 1])
                            tt("pool", W["k2"][:], k_, W["t1"][:], ALU.mult, [R6["k"], N_("t1")], [N_("k2")])
                            tt("pool", W["bet"][:], W["kk"][:], a_, ALU.mult, [N_("kk"), R6["a"]], [N_("bet")])
                            P.op("dve", lambda E, o=W["L"][:], d1=ld_: E.tensor_tensor_scan(o, onesC[:], d1, 0.0, ALU.mult, ALU.add),
                                 reads=["onesC", R6["ld"]], writes=[N_("L")])
                            act(W["eL"][:], W["L"][:], AF.Exp, [N_("L")], [N_("eL")])
                            act(W["enL"][:], W["L"][:], AF.Exp, [N_("L")], [N_("enL")], scale=-1.0)
                            tt("pool", W["tmp"][:], W["L"][:], ld_, ALU.subtract, [N_("L"), R6["ld"]], [N_("tmp")])
                            act(W["ePrev"][:], W["tmp"][:], AF.Exp, [N_("tmp")], [N_("ePrev")])
                            act(W["eRev"][:], W["L"][:], AF.Exp, [N_("L")], [N_("eRev")], scale=-1.0, bias=W["L"][:, C - 1:C])
                            act(W["gC"][:], W["L"][:, C - 1:C], AF.Exp, [N_("L")], [N_("gC")])
                            stt(W["AR"][:, 0:C], W["kk"][:], -1.0, W["ePrev"][:], ALU.mult, ALU.mult,
                                [N_("kk"), N_("ePrev")], [N_("AR")])
                            tt("pool", W["AR"][:, C:2 * C], r_, W["eL"][:], ALU.mult, [R6["r"], N_("eL")], [N_("AR")])
                            for h in range(2):
                                hs = slice(h * 64, (h + 1) * 64)
                                tt("dve", W["Bbp"][hs, h, :], W["bet"][hs, :], W["enL"][hs, :], ALU.mult,
                                   [N_("bet"), N_("enL")], [N_("Bbp")])
                                tt("pool", W["Kbp"][hs, h, :], W["k2"][hs, :], W["enL"][hs, :], ALU.mult,
                                   [N_("k2"), N_("enL")], [N_("Kbp")])
                            tt("dve", W["bts"][:], W["bet"][:], W["eRev"][:], ALU.mult, [N_("bet"), N_("eRev")], [N_("bts")])
                            tt("pool", W["kts"][:], W["k2"][:], W["eRev"][:], ALU.mult, [N_("k2"), N_("eRev")], [N_("kts")])
                            if cfg.get("scan_stop", 9) <= 1:
                                return
                            yield
                            bk, ps = B.bank()
                            for ti, (src, rk_) in enumerate(((v_, R6["v"]), (W["bts"][:], N_("bts")), (W["kts"][:], N_("kts")))):
                                P.op("pe", lambda E, o=ps[:, ti * 128:(ti + 1) * 128], i_=src: E.transpose(o, i_, ident[:]),
                                     reads=[rk_, "c_ident"], writes=[bk])
                            for ti, n in enumerate(("Vtp", "Btp", "Ktp")):
                                src3 = ps[:, ti * 128:(ti + 1) * 128].rearrange("p (h b) -> p h b", b=64)
                                if ti == 1:
                                    cp("dve", diag2(W[n]), src3, [bk], [N_(n)])
                                else:
                                    cp("act", diag2(W[n]), src3, [bk], [N_(n)])
                            if cfg.get("scan_stop", 9) <= 2:
                                return
                            yield
                            bkX, psX = B.bank(); bkY, psY = B.bank(); bkZ, psZ = B.bank()
                            for h in range(2):
                                mm(psX[:, h * 256:(h + 1) * 256], W["Bbp"][:, h, :], W["AR"][:], [N_("Bbp"), N_("AR")], [bkX])
                                mm(psY[:, h * 256:(h + 1) * 256], W["Kbp"][:, h, :], W["AR"][:], [N_("Kbp"), N_("AR")], [bkY])
                                mm(psZ[:, h * 128:(h + 1) * 128], W["AR"][:, 0:C], W["Bbp"][:, h, :], [N_("Bbp"), N_("AR")], [bkZ])
                            tt("dve", W["QM"][:].rearrange("p h c -> p (h c)"), psX[:, 0:512], m2b2[:].rearrange("p h c -> p (h c)"),
                               ALU.mult, [bkX, "m2b2"], [N_("QM")])
                            tt("dve", W["KM"][:].rearrange("p h c -> p (h c)"), psY[:, 0:512], m2b2[:].rearrange("p h c -> p (h c)"),
                               ALU.mult, [bkY, "m2b2"], [N_("KM")])
                            tt("dve", W["P0"][:].rearrange("p h c -> p (h c)"), psZ[:, 0:256], msl2[:].rearrange("p h c -> p (h c)"),
                               ALU.mult, [bkZ, "msl2"], [N_("P0")])
                            yield
                            if cfg.get("scan_stop", 9) <= 3:
                                return
                            fl = lambda t: t[:].rearrange("p h c -> p (h c)")
                            QAh = lambda h: W["QM"][:, h, 0:C]
                            PA = W["P0"]
                            Qc, Pc = W["Q0"], W["P1"]
                            tt("pool", Qc[:], W["QM"][:, :, 0:C], bd8_2[:], ALU.mult, [N_("QM"), "bd8_2"], [Qc.name])
                            tt("pool", fl(Pc), fl(PA), fl(bd8_2), ALU.mult, [PA.name, "bd8_2"], [Pc.name])
                            Dq, Dp = W["Dq0"], W["Dp0"]
                            tt("pool", fl(Dq), fl(Qc), fl(idb2), ALU.add, [Qc.name, "idb2"], [Dq.name])
                            tt("pool", fl(Dp), fl(Pc), fl(idb2), ALU.add, [Pc.name, "idb2"], [Dp.name])
                            dqi = 0
                            sq_bufs = [(W["Q1"], W["PA"]), (W["Q0"], W["P1"])]
                            for s_ in range(2):
                                Qn, Pn = sq_bufs[s_]
                                bkA, psA = B.bank(); bkB, psB = B.bank()
                                for h in range(2):
                                    mm(psA[:, h * 128:(h + 1) * 128], Pc[:, h, :], Qc[:, h, :], [Qc.name, Pc.name], [bkA])
                                    mm(psB[:, h * 128:(h + 1) * 128], Qc[:, h, :], Pc[:, h, :], [Qc.name, Pc.name], [bkB])
                                cp("act", fl(Qn), psA[:, 0:256], [bkA], [Qn.name])
                                cp("dve", fl(Pn), psB[:, 0:256], [bkB], [Pn.name])
                                Dqn, Dpn = W["Dq%d" % (1 - dqi)], W["Dp%d" % (1 - dqi)]
                                bkC, psC = B.bank(); bkD_, psD_ = B.bank()
                                for h in range(2):
                                    mm(psC[:, h * 128:(h + 1) * 128], Pn[:, h, :], Dq[:, h, :], [Pn.name, Dq.name], [bkC], start=True, stop=False)
                                    mm(psC[:, h * 128:(h + 1) * 128], idb2[:, h, :], Dq[:, h, :], ["idb2", Dq.name], [bkC], start=False, stop=True)
                                    mm(psD_[:, h * 128:(h + 1) * 128], Qn[:, h, :], Dp[:, h, :], [Qn.name, Dp.name], [bkD_], start=True, stop=False)
                                    mm(psD_[:, h * 128:(h + 1) * 128], idb2[:, h, :], Dp[:, h, :], ["idb2", Dp.name], [bkD_], start=False, stop=True)
                                cp("act", fl(Dqn), psC[:, 0:256], [bkC], [Dqn.name])
                                cp("dve", fl(Dpn), psD_[:, 0:256], [bkD_], [Dpn.name])
                                Qc, Pc, Dq, Dp, dqi = Qn, Pn, Dqn, Dpn, 1 - dqi
                                yield
                            for l_ in range(4):
                                last = (l_ == 3)
                                Dqn, Dpn = W["Dq%d" % (1 - dqi)], W["Dp%d" % (1 - dqi)]
                                bkA, psA = B.bank()
                                for h in range(2):
                                    mm(psA[:, h * 128:(h + 1) * 128], PA[:, h, :], Dq[:, h, :], [PA.name, Dq.name], [bkA])
                                tt("dve", fl(W["Y1"]), psA[:, 0:256], lvU2[:, l_, :, :].rearrange("p h c -> p (h c)"), ALU.mult,
                                   [bkA, "lvU2"], [N_("Y1")])
                                if not last:
                                    bkB, psB = B.bank()
                                    for h in range(2):
                                        mm(psB[:, h * 128:(h + 1) * 128], QAh(h), Dp[:, h, :], [N_("QM"), Dp.name], [bkB])
                                    tt("dve", fl(W["Z1"]), psB[:, 0:256], lvL2[:, l_, :, :].rearrange("p h c -> p (h c)"), ALU.mult,
                                       [bkB, "lvL2"], [N_("Z1")])
                                yield
                                bkC, psC = B.bank()
                                for h in range(2):
                                    mm(psC[:, h * 128:(h + 1) * 128], Dp[:, h, :], W["Y1"][:, h, :], [Dp.name, N_("Y1")], [bkC], start=True, stop=False)
                                    mm(psC[:, h * 128:(h + 1) * 128], idb2[:, h, :], Dq[:, h, :], ["idb2", Dq.name], [bkC], start=False, stop=True)
                                cp("act", fl(Dqn), psC[:, 0:256], [bkC], [Dqn.name])
                                if not last:
                                    bkD_, psD_ = B.bank()
                                    for h in range(2):
                                        mm(psD_[:, h * 128:(h + 1) * 128], Dq[:, h, :], W["Z1"][:, h, :], [Dq.name, N_("Z1")], [bkD_], start=True, stop=False)
                                        mm(psD_[:, h * 128:(h + 1) * 128], idb2[:, h, :], Dp[:, h, :], ["idb2", Dp.name], [bkD_], start=False, stop=True)
                                    cp("act", fl(Dpn), psD_[:, 0:256], [bkD_], [Dpn.name])
                                Dq, Dp, dqi = Dqn, Dpn, 1 - dqi
                                yield
                            TT = Dq
                            if cfg.get("scan_stop", 9) <= 4:
                                return
                            yield
                            bkD, psD = B.bank()
                            for h in range(2):
                                hs = slice(h * 64, (h + 1) * 64)
                                mm(psD[:, hs], W["AR"][:, 0:C], Sblk[:, hp, hs], [N_("AR"), pk], [bkD], start=True, stop=False)
                                mm(psD[:, hs], W["KM"][:, h, 0:C], W["Vtp"][:, h, hs], [N_("KM"), N_("Vtp")], [bkD], start=False, stop=True)
                            cp("act", W["RH"][:].rearrange("p h c -> p (h c)"), psD[:, 0:128], [bkD], [N_("RH")])
                            yield
                            bkE, psE = B.bank()
                            for h in range(2):
                                mm(psE[:, h * 64:(h + 1) * 64], TT[:, h, :], W["RH"][:, h, :], [TT.name, N_("RH")], [bkE])
                            cp("dve", diag2(W["Up"]), psE[:, 0:128].rearrange("p (h b) -> p h b", b=64), [bkE], [N_("Up")])
                            yield
                            bkF, psF = B.bank()
                            mm(psF[:, 0:C], Sblk[:, hp, :], W["AR"][:, C:2 * C], [pk, N_("AR")], [bkF], start=True, stop=False)
                            for h in range(2):
                                mm(psF[:, 0:C], W["Up"][:, h, :], W["QM"][:, h, C:2 * C], [N_("Up"), N_("QM")], [bkF], start=False, stop=False)
                                mm(psF[:, 0:C], W["Vtp"][:, h, :], W["KM"][:, h, C:2 * C], [N_("Vtp"), N_("KM")], [bkF],
                                   start=False, stop=(h == 1))
                            bkG, psG = B.bank()
                            for h in range(2):
                                hs = slice(h * 64, (h + 1) * 64)
                                mm(psG[:, 0:64], W["Btp"][:, h, :], W["Up"][:, h, hs], [N_("Btp"), N_("Up")], [bkG],
                                   start=(h == 0), stop=False)
                                mm(psG[:, 0:64], W["Ktp"][:, h, :], W["Vtp"][:, h, hs], [N_("Ktp"), N_("Vtp")], [bkG],
                                   start=False, stop=(h == 1))
                            stt(Sm[:, hp, :], Sm[:, hp, :], W["gC"][:, 0:1], psG[:, 0:64], ALU.mult, ALU.add,
                                [("Sm", hp), N_("gC"), bkG], [("Sm", hp)])
                            for h in range(2):
                                hs = slice(h * 64, (h + 1) * 64)
                                cp("pool", Sblk[hs, hp, hs], Sm[hs, hp, :], [("Sm", hp)], [pk])
                            if cfg.get("scan_stop", 9) <= 5:
                                return
                            yield
                            cp("act", W["YY"][:, 0:C], psF[:, 0:C], [bkF], [N_("YY")])
                            act(W["YY"][:, C:2 * C], W["YY"][:, 0:C], AF.Square, [N_("YY")], [N_("YY")])
                            bkH, psH = B.bank()
                            mm(psH[:, 0:2 * C], blkm[:], W["YY"][:], ["c_blkm", N_("YY")], [bkH])
                            act(W["mu2"][:], psH[:, 0:C], AF.Square, [bkH], [N_("mu2")])
                            tt("dve", W["var"][:], psH[:, C:2 * C], W["mu2"][:], ALU.subtract, [bkH, N_("mu2")], [N_("var")])
                            act(W["rs"][:], W["var"][:], AF.Ln, [N_("var")], [N_("rs")], bias=GN_EPS)
                            act(W["rs"][:], W["rs"][:], AF.Exp, [N_("rs")], [N_("rs")], scale=-0.5)
                            tt("dve", W["yn"][:], W["YY"][:, 0:C], psH[:, 0:C], ALU.subtract, [N_("YY"), bkH], [N_("yn")])
                            tt("pool", W["yn"][:], W["yn"][:], W["rs"][:], ALU.mult, [N_("yn"), N_("rs")], [N_("yn")])
                            act(W["yn"][:], W["yn"][:], AF.Identity, [N_("yn"), "vec", "vec2"], [N_("yn")],
                                bias=vec2[:, 0, hp:hp + 1], scale=vec[:, 15, hp:hp + 1])
                            tt("pool", W["rk"][:], r_, W["k2"][:], ALU.mult, [R6["r"], N_("k2")], [N_("rk")])
                            ts("dve", W["rk"][:], W["rk"][:], vec[:, 14, hp:hp + 1], ALU.mult, [N_("rk"), "vec"], [N_("rk")])
                            yield
                            bkI, psI = B.bank()
                            mm(psI[:, 0:C], blk[:], W["rk"][:], ["c_blk", N_("rk")], [bkI])
                            tt("dve", W["bon"][:], psI[:, 0:C], v_, ALU.mult, [bkI, R6["v"]], [N_("bon")])
                            tt("pool", W["yn"][:], W["yn"][:], W["bon"][:], ALU.add, [N_("yn"), N_("bon")], [N_("yn")])
                            tt("dve", W["yg"][:], W["yn"][:], g_, ALU.mult, [N_("yn"), R6["g"]], [N_("yg")])
                            P.dma("sp", s_yg[hp * 128:(hp + 1) * 128, j * C:(j + 1) * C], W["yg"][:], reads=[N_("yg")])
                        NI = 4
                        for b0_ in range(0, GP, NI):
                            gens = [unit(qi) for qi in range(b0_, min(GP, b0_ + NI))]
                            while gens:
                                for g__ in list(gens):
                                    try:
                                        next(g__)
                                    except StopIteration:
                                        gens.remove(g__)
                Sf = sb("Sf", [128, 128]); So = sb("So", [128, 128])
                P.op("pool", lambda E: E.memset(Sf[:], 0.0), writes=["Sf"])
                for hp in range(NP):
                    for h in range(2):
                        hs = slice(h * 64, (h + 1) * 64)
                        cp("pool", Sf[hs, hs], Sm[hs, hp, :], [("Sm", hp)], ["Sf"])
                    bk, ps = B.bank()
                    P.op("pe", lambda E, o=ps[:, 0:128]: E.transpose(o, Sf[:], ident[:]), reads=["Sf", "c_ident"], writes=[bk])
                    cp("act", So[:], ps[:, 0:128], [bk], ["So"])
                    for h in range(2):
                        hs = slice(h * 64, (h + 1) * 64)
                        P.dma("sp", o_wkv[2 * hp + h, :, :], So[hs, hs], reads=["So"])
                P.barrier()
            B.es = es
        scan_stage()
        NBLK = (NW - 2) // 128
        assert NW == 2 + NBLK * 128
        WG = _chunks(NW, 512)
        s_x1 = B.scratch("s_x1", [D, NW]); s_x2 = B.scratch("s_x2", [D, NW]); s_x3 = B.scratch("s_x3", [D, NW])
        s_act = B.scratch("s_act", [DFF, NW], BF16)
        s_oat = B.scratch("s_oat", [D, NW], BF16)
        convout = sb("convout", [128, 2, FC, 2])
        mwin = sb("mwin", [128, HALO]); P.dma("sp", mwin[:], maskrow[:, W0:W0 + HALO], writes=["mwin"])
        esT = sb("esT", [128, H]); P.dma("sp", esT[:], sinkT[:, :], writes=["esT"])
        act(esT[:], esT[:], AF.Exp, ["esT"], ["esT"])
        kwin_sb = sb("kwin_sb", [128, KVW]); vwin_sb = sb("vwin_sb", [128, KVW])

        def xs_of(tile, key, groups):
            return [(lambda ki, ksz, c0=c0, n=n: tile[0:ksz, ki, c0:c0 + n], n, [key]) for (c0, n) in groups]

        def norm_stats(src, rstd, xa):
            bks = [B.bank() for _ in WG]
            for kc in range(KC):
                x_ = xa[kc % 2]
                P.dma("sp", x_[:], src[kc * 128:(kc + 1) * 128, :], writes=[x_.name])
                act(x_[:], x_[:], AF.Square, [x_.name], [x_.name])
                for gi, (c0, n) in enumerate(WG):
                    mm(bks[gi][1][:, 0:n], ones[:], x_[:, c0:c0 + n], [x_.name, "c_ones"], [bks[gi][0]],
                       start=(kc == 0), stop=(kc == KC - 1))
            for gi, (c0, n) in enumerate(WG):
                act(rstd[:, c0:c0 + n], bks[gi][1][:, 0:n], AF.Ln, [bks[gi][0]], [rstd.name], scale=1.0 / D, bias=1e-6)
            act(rstd[:], rstd[:], AF.Exp, [rstd.name], [rstd.name], scale=-0.5)

        def modulate(src, rstd, hw, gsel, shsel, xa):
            for kc in range(KC):
                x_ = xa[kc % 2]
                P.dma("sp", x_[:], src[kc * 128:(kc + 1) * 128, :], writes=[x_.name])
                tt("dve", x_[:], x_[:], rstd[:], ALU.mult, [x_.name, rstd.name], [x_.name])
                act(hw[:, kc, :], x_[:], AF.Identity, [x_.name, "Gm", "modv", "kvmv"], [hw.name],
                    bias=shsel(kc), scale=gsel(kc))

        def resid_epi(xin, gate_sel, target, og, xg, cg):
            cnt_ = [0]

            def epi(ct, n0, m, g, ps, bk):
                i = cnt_[0] % 2; cnt_[0] += 1
                c0, n = cg[g]
                P.dma("sp", xg[i][:, 0:n], xin[ct * 128:(ct + 1) * 128, c0:c0 + n], writes=[xg[i].name])
                stt(og[i][:, 0:n], ps, gate_sel(ct), xg[i][:, 0:n], ALU.mult, ALU.add,
                    [bk, xg[i].name, "modv"], [og[i].name])
                target(ct, c0, n, og[i], og[i].name)
            return epi

        def to_scratch(dst):
            def tgt(ct, c0, n, o_, key):
                P.dma("sp", dst[ct * 128:(ct + 1) * 128, c0:c0 + n], o_[:, 0:n], reads=[key])
            return tgt

        def ffn(l, xin, target, tg):
            with contextlib.ExitStack() as es5:
                B.es = es5
                rstd = sb("frstd" + tg, [128, NW])
                hw = sb("fhw" + tg, [128, KC, NW], BF16)
                xa = [sb("fxa%s%d" % (tg, i), [128, NW]) for i in range(2)]
                norm_stats(xin, rstd, xa)
                modulate(xin, rstd, hw, lambda kc: Gm[:, 1 + 2 * l, kc, 0:1], lambda kc: modv[:, l, 3 * KC + kc, 0:1], xa)
                wb = [sb("fwb%s%d" % (tg, i), [128, KC, 128], BF16) for i in range(2)]
                gfull = [sb("gfull%s%d" % (tg, i), [128, NW]) for i in range(2)]
                cvt = [sb("cvt%s%d" % (tg, i), [128, NW]) for i in range(2)]
                sg = [sb("sg%s%d" % (tg, i), [128, NW], BF16) for i in range(4)]
                ao = [sb("ao%s%d" % (tg, i), [128, 512], BF16) for i in range(3)]
                for t_ in sg:
                    P.op("pool", lambda E, t_=t_: E.memset(t_[:], 0.0), writes=[t_.name])
                aoc = [0]

                def epi(ct, n0, m, g, ps, bk):
                    c0, n = WG[g]
                    if n0 < DFF:
                        f = ct
                        gf = gfull[f % 2]
                        cp("act", gf[:, c0:c0 + n], ps, [bk], [gf.name])
                        if g == len(WG) - 1:
                            tt("dve", gf[:, 0:HALO], gf[:, 0:HALO], mwin[:], ALU.mult, [gf.name, "mwin"], [gf.name])
                            cp("pool", convout[:, l, f, :], gf[:, NW - 2:NW], [gf.name], ["convout"])
                            c_ = cvt[f % 2]
                            act(c_[:, 2:NW], gf[:, 2:NW], AF.Identity, [gf.name, "cvp"], [c_.name],
                                bias=cvp[:, l, 3, f:f + 1], scale=cvp[:, l, 2, f:f + 1])
                            stt(c_[:, 2:NW], gf[:, 1:NW - 1], cvp[:, l, 1, f:f + 1], c_[:, 2:NW], ALU.mult, ALU.add,
                                [gf.name, c_.name, "cvp"], [c_.name])
                            stt(c_[:, 2:NW], gf[:, 0:NW - 2], cvp[:, l, 0, f:f + 1], c_[:, 2:NW], ALU.mult, ALU.add,
                                [gf.name, c_.name, "cvp"], [c_.name])
                            act(sg[f % 4][:, 2:NW], c_[:, 2:NW], AF.Silu, [c_.name], [sg[f % 4].name])
                    else:
                        f = ct - FC
                        a_ = ao[aoc[0] % 3]; aoc[0] += 1
                        tt("dve", a_[:, 0:n], ps, sg[f % 4][:, c0:c0 + n], ALU.mult, [bk, sg[f % 4].name], [a_.name])
                        P.dma("sp", s_act[f * 128:(f + 1) * 128, c0:c0 + n], a_[:, 0:n], reads=[a_.name])
                cgs = []
                for (n0, gsz) in _chunks(DFF, 128):
                    cgs.append((n0, gsz)); cgs.append((DFF + n0, gsz))
                B.linear("win" + tg, w_in[l], D, 2 * DFF, xs_of(hw, hw.name, WG), epi, 128, wb, colgroups=cgs)
                P.barrier()
            with contextlib.ExitStack() as es5:
                B.es = es5
                CW = -(-NW // 3)
                CW += CW % 2
                CG = _chunks(NW, CW)
                actt = sb("actt" + tg, [128, FC, CW], BF16)
                wbo = [sb("fwo%s%d" % (tg, i), [128, FC, 128], BF16) for i in range(2)]
                og = [sb("fog%s%d" % (tg, i), [128, CW]) for i in range(2)]
                xg = [sb("fxg%s%d" % (tg, i), [128, CW]) for i in range(2)]
                for (c0, n) in CG:
                    P.dma("sp", actt[:, :, 0:n], s_act[:, c0:c0 + n].rearrange("(f p) t -> p f t", p=128), writes=[actt.name])
                    xs1 = [(lambda ki, ksz, n=n: actt[0:ksz, ki, 0:n], n, [actt.name])]
                    B.linear("wout" + tg, w_out[l], DFF, D, xs1,
                             resid_epi(xin, lambda ct: modv[:, l, 5 * KC + ct, 0:1], target, og, xg, [(c0, n)]), 128, wbo)
                P.barrier()
            B.es = es

        def backend():
            with contextlib.ExitStack() as es5:
                B.es = es5
                hw = sb("d1hw", [128, KC, NW], BF16)
                P.dma("sp", hw[:], s_yg[:, W0:T].rearrange("(kc p) t -> p kc t", p=128), writes=["d1hw"])
                wb = [sb("d1wb%d" % i, [128, KC, 256], BF16) for i in range(2)]
                og = [sb("d1og%d" % i, [128, 512]) for i in range(2)]
                xg = [sb("d1xg%d" % i, [128, 512]) for i in range(2)]
                B.linear("wo", w_o, D, D, xs_of(hw, "d1hw", WG),
                         resid_epi(xT[:, W0:T], lambda ct: modv[:, 0, 2 * KC + ct, 0:1], to_scratch(s_x1), og, xg, WG), 256, wb)
                P.barrier()
            B.es = es
            ffn(0, s_x1, to_scratch(s_x2), "0")
            with contextlib.ExitStack() as es5:
                B.es = es5
                rstd = sb("krstd", [128, NW])
                hw = sb("khw", [128, KC, NW], BF16)
                kdup = sb("kdup", [128, NKV, NW], BF16)
                Vt1 = sb("Vt1", [128, NBLK, NKV, 65], BF16)
                P.op("pool", lambda E: E.memset(Vt1[:], 1.0), writes=["Vt1"])
                xa = [sb("kxa%d" % i, [128, NW]) for i in range(2)]
                norm_stats(s_x2, rstd, xa)
                modulate(s_x2, rstd, hw, lambda kc: Gm[:, 4, kc, 0:1], lambda kc: kvmv[:, kc, 0:1], xa)
                wbk = [sb("kwb%d" % i, [128, KC, 128], BF16) for i in range(2)]
                kf = sb("kf", [128, NW]); ksq = sb("ksq", [128, NW]); krs = sb("krs", [128, NW])
                for g in range(NKV):
                    buf = wbk[g % 2]; bkey = ("wb", buf.name)
                    for half in range(2):
                        P.dma("pool", buf[:, :, half * 64:(half + 1) * 64],
                              w_kv[:, g * 64:(g + 1) * 64].rearrange("(kc p) n -> p kc n", p=128), writes=[bkey])
                    for gi, (c0, n) in enumerate(WG):
                        bk, ps = B.bank()
                        for kc in range(KC):
                            mm(ps[:, 0:n], buf[:, kc, :], hw[:, kc, c0:c0 + n], [bkey, hw.name], [bk], start=(kc == 0), stop=(kc == KC - 1))
                        cp("act", kf[:, c0:c0 + n], ps[:, 0:n], [bk], [kf.name])
                    act(ksq[:], kf[:], AF.Square, [kf.name], [ksq.name])
                    for gi, (c0, n) in enumerate(WG):
                        bk, ps = B.bank()
                        mm(ps[:, 0:n], blkm[:], ksq[:, c0:c0 + n], ["c_blkm", ksq.name], [bk])
                        act(krs[:, c0:c0 + n], ps[:, 0:n], AF.Ln, [bk], [krs.name], bias=1e-6)
                    act(krs[:], krs[:], AF.Exp, [krs.name], [krs.name], scale=-0.5)
                    tt("dve", kf[:], kf[:], krs[:], ALU.mult, [kf.name, krs.name], [kf.name])
                    ts("dve", kf[:], kf[:], hdt[:, 0:1], ALU.mult, [kf.name, "hdt"], [kf.name])
                    cp("pool", kdup[:, g, :], kf[:], [kf.name], ["kdup"])
                    bk, ps = B.bank()
                    P.op("pe", lambda E, o=ps[:, 0:128]: E.transpose(o, kf[:, NW - 128:NW], ident[:]), reads=[kf.name, "c_ident"], writes=[bk])
                    cp("act", kwin_sb[:, g * 64:(g + 1) * 64], ps[:, 0:64], [bk], ["kwin_sb"])
                vf = xa[0]

                def epiV(ct, n0, m, g, ps, bk):
                    c0, n = WG[g]
                    cp("act", vf[0:m, c0:c0 + n], ps, [bk], [vf.name])
                    if g == len(WG) - 1:
                        for j in range(NBLK):
                            bk2, ps2 = B.bank()
                            P.op("pe", lambda E, o=ps2[:, 0:m], j=j: E.transpose(o, vf[0:m, 2 + j * 128:2 + (j + 1) * 128], ident[0:m, 0:m]),
                                 reads=[vf.name, "c_ident"], writes=[bk2])
                            nh = m // 64
                            cp("dve", Vt1[:, j, ct * 2:ct * 2 + nh, 0:64], ps2[:, 0:m].rearrange("p (h d) -> p h d", d=64), [bk2], ["Vt1"])
                            if j == NBLK - 1:
                                cp("act", vwin_sb[:, ct * 128:ct * 128 + m], ps2[:, 0:m], [bk2], ["vwin_sb"])
                B.linear("wv", w_kv[:, KVW:2 * KVW], D, KVW, xs_of(hw, hw.name, WG), epiV, 128, wbk)
                P.dma("sp", o_kwin[:, :], kwin_sb[:], reads=["kwin_sb"])
                P.dma("sp", o_vwin[:, :], vwin_sb[:], reads=["vwin_sb"])
                modulate(s_x2, rstd, hw, lambda kc: Gm[:, 2, kc, 0:1], lambda kc: modv[:, 1, 0 * KC + kc, 0:1], xa)
                wbq = wbk
                qf = [kf, kf]
                qsq = ksq; qrs = krs
                qp = sb("qp", [128, 2, NW], BF16)
                P.op("pool", lambda E: E.memset(qp[:], 0.0), writes=["qp"])
                mpo = sb("mpo", [128, 4, 128], BF16)
                for i in range(4):
                    cp("dve", mpo[:, i, :], (mslb[:] if i % 2 == 0 else m2b[:, 128:256]), ["mslb", "m2b"], ["mpo"])
                Eb = [sb("Eb%d" % i, [128, 4, 128], BF16) for i in range(2)]
                otk = [sb("otk%d" % i, [128, 128]) for i in range(2)]
                den = [sb("den%d" % i, [128, 2]) for i in range(2)]
                ofm = [sb("ofm%d" % i, [128, 128], BF16) for i in range(2)]
                zt = sb("zt", [128, 130], BF16)
                P.op("pool", lambda E: E.memset(zt[:], 0.0), writes=["zt"])
                for kc in range(KC):
                    P.dma("sp", s_oat[kc * 128:(kc + 1) * 128, 0:130], zt[:], reads=["zt"])
                cnt_q = [0]

                def epiQ(ct, n0, m, g, ps, bk):
                    c0, n = WG[g]
                    q_ = qf[ct % 2]
                    cp("act", q_[:, c0:c0 + n], ps, [bk], [q_.name])
                    if g < len(WG) - 1:
                        return
                    act(qsq[:], q_[:], AF.Square, [q_.name], [qsq.name])
                    for gi, (c0_, n_) in enumerate(WG):
                        bk2, ps2 = B.bank()
                        mm(ps2[:, 0:n_], blkm[:], qsq[:, c0_:c0_ + n_], ["c_blkm", qsq.name], [bk2])
                        act(qrs[:, c0_:c0_ + n_], ps2[:, 0:n_], AF.Ln, [bk2], [qrs.name], bias=1e-6)
                    act(qrs[:], qrs[:], AF.Exp, [qrs.name], [qrs.name], scale=-0.5)
                    tt("dve", q_[:], q_[:], qrs[:], ALU.mult, [q_.name, qrs.name], [q_.name])
                    for h in range(2):
                        hs = slice(h * 64, (h + 1) * 64)
                        ts("dve", qp[hs, h, :], q_[hs, :], hdt[hs, 1:2], ALU.mult, [q_.name, "hdt"], ["qp"])
                    for qb in range(1, NBLK):
                        i2 = cnt_q[0] % 2; cnt_q[0] += 1
                        qc = slice(2 + qb * 128, 2 + (qb + 1) * 128)
                        bkS, psS = B.bank()
                        for h in range(2):
                            gkv = (2 * ct + h) // (H // NKV)
                            for kbi, kb in enumerate((qb - 1, qb)):
                                kcs = slice(2 + kb * 128, 2 + (kb + 1) * 128)
                                mm(psS[:, (h * 2 + kbi) * 128:(h * 2 + kbi + 1) * 128], kdup[:, gkv, kcs], qp[:, h, qc],
                                   ["kdup", "qp"], [bkS])
                        E_ = Eb[i2]
                        act(E_[:].rearrange("p a b -> p (a b)"), psS[:, 0:512], AF.Exp, [bkS], [E_.name], scale=0.125)
                        tt("dve", E_[:], E_[:], mpo[:], ALU.mult, [E_.name, "mpo"], [E_.name])
                        for kbi, kb in enumerate((qb - 1, qb)):
                            if kb <= 1:
                                ts("pool", E_[:, kbi:4:2, :], E_[:, kbi:4:2, :], flag[:, 0:1], ALU.mult, [E_.name, "flag"], [E_.name])
                        bkO, psO = B.bank()
                        for h in range(2):
                            gkv = (2 * ct + h) // (H // NKV)
                            for kbi, kb in enumerate((qb - 1, qb)):
                                mm(psO[:, h * 65:(h + 1) * 65], E_[:, h * 2 + kbi, :], Vt1[:, kb, gkv, :], [E_.name, "Vt1"], [bkO],
                                   start=(kbi == 0), stop=(kbi == 1))
                        d_ = den[i2]; o_ = otk[i2]
                        tt("dve", d_[:], psO[:, 64:130:65], esT[:, 2 * ct:2 * ct + 2], ALU.add, [bkO, "esT"], [d_.name])
                        P.op("dve", lambda E, d_=d_: E.reciprocal(d_[:], d_[:]), reads=[d_.name], writes=[d_.name])
                        for h in range(2):
                            ts("dve", o_[:, h * 64:(h + 1) * 64], psO[:, h * 65:h * 65 + 64], d_[:, h:h + 1], ALU.mult,
                               [bkO, d_.name], [o_.name])
                        bkT, psT = B.bank()
                        P.op("pe", lambda E, o=psT[:, 0:128], o_=o_: E.transpose(o, o_[:], ident[:]), reads=[o_.name, "c_ident"], writes=[bkT])
                        f_ = ofm[i2]
                        cp("act", f_[:], psT[:, 0:128], [bkT], [f_.name])
                        P.dma("sp", s_oat[ct * 128:(ct + 1) * 128, qc], f_[:], reads=[f_.name])
                B.linear("wq", w_q, D, D, xs_of(hw, hw.name, WG), epiQ, 128, wbq)
                P.barrier()
            B.es = es
            with contextlib.ExitStack() as es5:
                B.es = es5
                hw = sb("d4hw", [128, KC, NW], BF16)
                P.dma("sp", hw[:], s_oat.rearrange("(kc p) t -> p kc t", p=128), writes=["d4hw"])
                wb = [sb("d4wb%d" % i, [128, KC, 256], BF16) for i in range(2)]
                og = [sb("d4og%d" % i, [128, 512]) for i in range(2)]
                xg = [sb("d4xg%d" % i, [128, 512]) for i in range(2)]
                B.linear("wao", w_ao, D, D, xs_of(hw, "d4hw", WG),
                         resid_epi(s_x2, lambda ct: modv[:, 1, 2 * KC + ct, 0:1], to_scratch(s_x3), og, xg, WG), 256, wb)
                P.barrier()
            B.es = es

            def to_y(ct, c0, n, o_, key):
                lo_ = max(c0, HALO)
                if lo_ < c0 + n:
                    P.dma("sp", o_y[ct * 128:(ct + 1) * 128, lo_ - HALO:c0 + n - HALO], o_[:, lo_ - c0:n], reads=[key])
            ffn(1, s_x3, to_y, "1")
            P.dma("sp", o_conv.rearrange("l p f c -> p l f c"), convout[:], reads=["convout"])
        backend()
        def sample_path():
            with contextlib.ExitStack() as es6:
                B.es = es6
                KN = KC * NS
                f3 = lambda t: t[:].rearrange("p k n -> p (k n)")
                T_ = {}
                for n in ("r", "k", "v", "ld", "a", "g"):
                    T_[n] = sb("s_" + n, [128, KC, NS])
                    P.dma("sp", T_[n][:], scs[n].rearrange("(kc p) n -> p kc n", p=128), writes=[T_[n].name])
                W_ = {n: sb("sw_" + n, [128, KC, NS]) for n in ("kkr", "sq", "rn", "kk", "t1", "k2", "bet", "dd", "al", "ys", "yn", "rk", "mu2", "var", "rs", "bon", "ysq")}
                nm = lambda n: (T_[n].name if n in T_ else W_[n].name)
                for j in range(NS):
                    tt("dve", W_["kkr"][:, :, j], T_["k"][:, :, j], vec[:, 12, :], ALU.mult, [nm("k"), "vec"], [nm("kkr")])
                act(f3(W_["sq"]), f3(W_["kkr"]), AF.Square, [nm("kkr")], [nm("sq")])
                bk, ps = B.bank()
                mm(ps[:, 0:KN], blk[:], f3(W_["sq"]), ["c_blk", nm("sq")], [bk])
                ts("dve", f3(W_["rn"]), ps[:, 0:KN], 1e-24, ALU.max, [bk], [nm("rn")])
                act(f3(W_["rn"]), f3(W_["rn"]), AF.Ln, [nm("rn")], [nm("rn")])
                act(f3(W_["rn"]), f3(W_["rn"]), AF.Exp, [nm("rn")], [nm("rn")], scale=-0.5)
                tt("dve", f3(W_["kk"]), f3(W_["kkr"]), f3(W_["rn"]), ALU.mult, [nm("kkr"), nm("rn")], [nm("kk")])
                for j in range(NS):
                    tt("dve", W_["t1"][:, :, j], T_["a"][:, :, j], vec[:, 13, :], ALU.mult, [nm("a"), "vec"], [nm("t1")])
                    tt("dve", W_["t1"][:, :, j], W_["t1"][:, :, j], omka[:], ALU.add, [nm("t1"), "omka"], [nm("t1")])
                tt("dve", f3(W_["k2"]), f3(T_["k"]), f3(W_["t1"]), ALU.mult, [nm("k"), nm("t1")], [nm("k2")])
                tt("dve", f3(W_["bet"]), f3(W_["kk"]), f3(T_["a"]), ALU.mult, [nm("kk"), nm("a")], [nm("bet")])
                act(f3(W_["dd"]), f3(T_["ld"]), AF.Exp, [nm("ld")], [nm("dd")])
                ts("dve", f3(W_["al"]), f3(W_["kk"]), -1.0, ALU.mult, [nm("kk")], [nm("al")])
                Spad = [sb("Spad%d" % i, [128, 128]) for i in range(2)]
                St = [sb("St%d" % i, [128, 128]) for i in range(2)]
                So_ = [sb("Sos%d" % i, [128, 128]) for i in range(2)]
                X1 = [sb("X1_%d" % i, [128, 2]) for i in range(2)]
                X2 = [sb("X2_%d" % i, [128, 2]) for i in range(2)]
                R12 = [sb("R12_%d" % i, [2, 256]) for i in range(2)]
                otm = [sb("otm%d" % i, [128, 128]) for i in range(2)]
                for t_ in Spad:
                    P.op("pool", lambda E, t_=t_: E.memset(t_[:], 0.0), writes=[t_.name])
                it = 0
                for j in range(NS):
                    for hp in range(NP):
                        i2 = it % 2; it += 1
                        sp_, st_, so_, x1, x2, r12, ot = Spad[i2], St[i2], So_[i2], X1[i2], X2[i2], R12[i2], otm[i2]
                        for h in range(2):
                            hs = slice(h * 64, (h + 1) * 64)
                            P.dma("sp", sp_[hs, hs], st_wkv[j, 2 * hp + h, :, :], writes=[sp_.name])
                        bk, ps = B.bank()
                        P.op("pe", lambda E, o=ps[:, 0:128], i_=sp_: E.transpose(o, i_[:], ident[:]), reads=[sp_.name, "c_ident"], writes=[bk])
                        cp("act", st_[:], ps[:, 0:128], [bk], [st_.name])
                        bk, ps = B.bank()
                        mm(ps[:, 0:1], st_[:], W_["al"][:, hp, j:j + 1], [st_.name, nm("al")], [bk])
                        cp("act", x2[:, 0:1], ps[:, 0:1], [bk], [x2.name])
                        cp("pool", x2[:, 1:2], T_["v"][:, hp, j:j + 1], [nm("v")], [x2.name])
                        cp("pool", x1[:, 0:1], W_["bet"][:, hp, j:j + 1], [nm("bet")], [x1.name])
                        cp("pool", x1[:, 1:2], W_["k2"][:, hp, j:j + 1], [nm("k2")], [x1.name])
                        bk, ps = B.bank()
                        P.op("pe", lambda E, o=ps[0:2, 0:128], i_=x1: E.transpose(o, i_[:], ident[:]), reads=[x1.name, "c_ident"], writes=[bk])
                        P.op("pe", lambda E, o=ps[0:2, 128:256], i_=x2: E.transpose(o, i_[:], ident[:]), reads=[x2.name, "c_ident"], writes=[bk])
                        cp("act", r12[:], ps[0:2, 0:256], [bk], [r12.name])
                        bk, ps = B.bank()
                        mm(ps[:, 0:128], r12[:, 0:128], r12[:, 128:256], [r12.name], [bk])
                        tt("dve", ot[:], ps[:, 0:128], blk[:], ALU.mult, [bk, "c_blk"], [ot.name])
                        stt(st_[:], st_[:], W_["dd"][:, hp, j:j + 1], ot[:], ALU.mult, ALU.add, [st_.name, nm("dd"), ot.name], [st_.name])
                        bk, ps = B.bank()
                        mm(ps[:, 0:1], st_[:], T_["r"][:, hp, j:j + 1], [st_.name, nm("r")], [bk])
                        cp("act", W_["ys"][:, hp, j:j + 1], ps[:, 0:1], [bk], [nm("ys")])
                        bk, ps = B.bank()
                        P.op("pe", lambda E, o=ps[:, 0:128], i_=st_: E.transpose(o, i_[:], ident[:]), reads=[st_.name, "c_ident"], writes=[bk])
                        cp("dve", so_[:], ps[:, 0:128], [bk], [so_.name])
                        for h in range(2):
                            hs = slice(h * 64, (h + 1) * 64)
                            P.dma("sp", o_wkvs[j, 2 * hp + h, :, :], so_[hs, hs], reads=[so_.name])
                YY = sb("sYY", [128, 2, KN])
                cp("dve", YY[:, 0, :], f3(W_["ys"]), [nm("ys")], ["sYY"])
                act(YY[:, 1, :], f3(W_["ys"]), AF.Square, [nm("ys")], ["sYY"])
                bk, ps = B.bank()
                mm(ps[:, 0:2 * KN], blkm[:], YY[:].rearrange("p a n -> p (a n)"), ["c_blkm", "sYY"], [bk])
                act(f3(W_["mu2"]), ps[:, 0:KN], AF.Square, [bk], [nm("mu2")])
                tt("dve", f3(W_["var"]), ps[:, KN:2 * KN], f3(W_["mu2"]), ALU.subtract, [bk, nm("mu2")], [nm("var")])
                act(f3(W_["rs"]), f3(W_["var"]), AF.Ln, [nm("var")], [nm("rs")], bias=GN_EPS)
                act(f3(W_["rs"]), f3(W_["rs"]), AF.Exp, [nm("rs")], [nm("rs")], scale=-0.5)
                tt("dve", f3(W_["yn"]), f3(W_["ys"]), ps[:, 0:KN], ALU.subtract, [nm("ys"), bk], [nm("yn")])
                tt("dve", f3(W_["yn"]), f3(W_["yn"]), f3(W_["rs"]), ALU.mult, [nm("yn"), nm("rs")], [nm("yn")])
                tt("dve", f3(W_["rk"]), f3(T_["r"]), f3(W_["k2"]), ALU.mult, [nm("r"), nm("k2")], [nm("rk")])
                for j in range(NS):
                    tt("dve", W_["yn"][:, :, j], W_["yn"][:, :, j], vec[:, 15, :], ALU.mult, [nm("yn"), "vec"], [nm("yn")])
                    tt("dve", W_["yn"][:, :, j], W_["yn"][:, :, j], vec2[:, 0, :], ALU.add, [nm("yn"), "vec2"], [nm("yn")])
                    tt("dve", W_["rk"][:, :, j], W_["rk"][:, :, j], vec[:, 14, :], ALU.mult, [nm("rk"), "vec"], [nm("rk")])
                bk, ps = B.bank()
                mm(ps[:, 0:KN], blk[:], f3(W_["rk"]), ["c_blk", nm("rk")], [bk])
                tt("dve", f3(W_["bon"]), ps[:, 0:KN], f3(T_["v"]), ALU.mult, [bk, nm("v")], [nm("bon")])
                tt("dve", f3(W_["yn"]), f3(W_["yn"]), f3(W_["bon"]), ALU.add, [nm("yn"), nm("bon")], [nm("yn")])
                ygs = sb("ygs", [128, KC, NS], BF16)
                tt("dve", f3(ygs), f3(W_["yn"]), f3(T_["g"]), ALU.mult, [nm("yn"), nm("g")], ["ygs"])

                xs0 = sb("xs0", [128, KC, NS]); P.dma("sp", xs0[:], xsT.rearrange("(kc p) n -> p kc n", p=128), writes=[xs0.name])
                x1s = sb("x1s", [128, KC, NS]); x2s = sb("x2s", [128, KC, NS]); x3s = sb("x3s", [128, KC, NS]); ysf = sb("ysf", [128, KC, NS])
                hS = sb("hS", [128, KC, NS], BF16); h32 = sb("h32", [128, KC, NS]); sqS = sb("sqS", [128, KC, NS])
                rsS = sb("rsS", [128, NS])
                wbs = [sb("swb%d" % i, [128, KC, 512], BF16) for i in range(2)]
                wbo = [sb("swo%d" % i, [128, FC, 128], BF16) for i in range(2)]
                actS = sb("actS", [128, FC, NS], BF16)
                cbuf = sb("cbuf", [128, 2, FC, NS, 2])
                P.dma("sp", cbuf[:].rearrange("p l f n c -> p l (f n c)"), st_conv.rearrange("l p f n c -> p l (f n c)"), writes=["cbuf"])
                cvo = sb("cvo", [128, 2, FC, NS, 2])
                tmpS = [sb("tmpS%d" % i, [128, NS]) for i in range(4)]
                tcnt = [0]

                def xsS(tile, key):
                    return [(lambda ki, ksz: tile[0:ksz, ki, :], NS, [key])]

                def resS(xin, xout, gsel):
                    def epi(ct, n0, m, g, ps, bk):
                        t_ = tmpS[tcnt[0] % 4]; tcnt[0] += 1
                        tt("dve", t_[:], ps, gsel(ct), ALU.mult, [bk, "modv"], [t_.name])
                        tt("dve", xout[:, ct, :], t_[:], xin[:, ct, :], ALU.add, [t_.name, xin.name], [xout.name])
                    return epi

                def normS(x, G3, SH3, keys):
                    act(f3(sqS), f3(x), AF.Square, [x.name], ["sqS"])
                    bk, ps = B.bank()
                    for kc in range(KC):
                        mm(ps[:, 0:NS], ones[:], sqS[:, kc, :], ["c_ones", "sqS"], [bk], start=(kc == 0), stop=(kc == KC - 1))
                    act(rsS[:], ps[:, 0:NS], AF.Ln, [bk], ["rsS"], scale=1.0 / D, bias=1e-6)
                    act(rsS[:], rsS[:], AF.Exp, ["rsS"], ["rsS"], scale=-0.5)
                    for kc in range(KC):
                        tt("dve", h32[:, kc, :], x[:, kc, :], rsS[:], ALU.mult, [x.name, "rsS"], ["h32"])
                    tt("dve", h32[:], h32[:], G3, ALU.mult, ["h32"] + keys, ["h32"])
                    tt("dve", hS[:], h32[:], SH3, ALU.add, ["h32"] + keys, ["hS"])

                def ffnS(l, xin, xout):
                    normS(xin, Gm[:, 1 + 2 * l, :, 1:1 + NS], modv[:, l, 3 * KC:4 * KC, 1:1 + NS], ["Gm", "modv"])
                    sgS = [sb("sgS%d_%d" % (l, i), [128, NS]) for i in range(8)]

                    def epi(ct, n0, m, g, ps, bk):
                        if n0 < DFF:
                            f = ct
                            t_ = tmpS[tcnt[0] % 4]; tcnt[0] += 1
                            cp("act", cvo[:, l, f, :, 1], ps, [bk], ["cvo"])
                            cp("pool", cvo[:, l, f, :, 0], cbuf[:, l, f, :, 1], ["cbuf"], ["cvo"])
                            act(t_[:], ps, AF.Identity, [bk, "cvp"], [t_.name], bias=cvp[:, l, 3, f:f + 1], scale=cvp[:, l, 2, f:f + 1])
                            stt(t_[:], cbuf[:, l, f, :, 1], cvp[:, l, 1, f:f + 1], t_[:], ALU.mult, ALU.add, ["cbuf", "cvp", t_.name], [t_.name])
                            stt(t_[:], cbuf[:, l, f, :, 0], cvp[:, l, 0, f:f + 1], t_[:], ALU.mult, ALU.add, ["cbuf", "cvp", t_.name], [t_.name])
                            act(sgS[f % 8][:], t_[:], AF.Silu, [t_.name], [sgS[f % 8].name])
                        else:
                            f = ct - FC
                            tt("dve", actS[:, f, :], ps, sgS[f % 8][:], ALU.mult, [bk, sgS[f % 8].name], ["actS"])
                    cgs = []
                    for (n0, gsz) in _chunks(DFF, 512):
                        cgs.append((n0, gsz)); cgs.append((DFF + n0, gsz))
                    B.linear("swin%d" % l, w_in[l], D, 2 * DFF, xsS(hS, "hS"), epi, 512, wbs, colgroups=cgs)
                    B.linear("swout%d" % l, w_out[l], DFF, D, xsS(actS, "actS"),
                             resS(xin, xout, lambda ct: modv[:, l, 5 * KC + ct, 1:1 + NS]), 128, wbo)

                B.linear("swo", w_o, D, D, xsS(ygs, "ygs"), resS(xs0, x1s, lambda ct: modv[:, 0, 2 * KC + ct, 1:1 + NS]), 512, wbs)
                ffnS(0, x1s, x2s)
                normS(x2s, Gm[:, 4, :, 1:1 + NS], kvmv[:, 0:KC, 1:1 + NS], ["Gm", "kvmv"])
                knew = sb("knew", [128, NKV, NS]); vnew = sb("vnew", [128, KVT, NS])
                ksqS = sb("ksqS", [128, NS])
                for g in range(NKV):
                    buf = wbs[g % 2]; bkey = ("wb", buf.name)
                    for half in range(2):
                        P.dma("pool", buf[:, :, half * 64:(half + 1) * 64],
                              w_kv[:, g * 64:(g + 1) * 64].rearrange("(kc p) n -> p kc n", p=128), writes=[bkey])
                    bk, ps = B.bank()
                    for kc in range(KC):
                        mm(ps[:, 0:NS], buf[:, kc, 0:128], hS[:, kc, :], [bkey, "hS"], [bk], start=(kc == 0), stop=(kc == KC - 1))
                    cp("act", knew[:, g, :], ps[:, 0:NS], [bk], ["knew"])
                    act(ksqS[:], knew[:, g, :], AF.Square, ["knew"], ["ksqS"])
                    bk, ps = B.bank()
                    mm(ps[:, 0:NS], blkm[:], ksqS[:], ["c_blkm", "ksqS"], [bk])
                    act(ksqS[:], ps[:, 0:NS], AF.Ln, [bk], ["ksqS"], bias=1e-6)
                    act(ksqS[:], ksqS[:], AF.Exp, ["ksqS"], ["ksqS"], scale=-0.5)
                    tt("dve", knew[:, g, :], knew[:, g, :], ksqS[:], ALU.mult, ["knew", "ksqS"], ["knew"])
                    ts("dve", knew[:, g, :], knew[:, g, :], hdt[:, 0:1], ALU.mult, ["knew", "hdt"], ["knew"])

                def epiVs(ct, n0, m, g, ps, bk):
                    cp("act", vnew[0:m, ct, :], ps, [bk], ["vnew"])
                B.linear("swv", w_kv[:, KVW:2 * KVW], D, KVW, xsS(hS, "hS"), epiVs, 128, wbs)
                normS(x2s, Gm[:, 2, :, 1:1 + NS], modv[:, 1, 0:KC, 1:1 + NS], ["Gm", "modv"])
                qS = sb("qS", [128, KC, NS]); qz = sb("qz", [128, 2, KC, NS], BF16)
                P.op("pool", lambda E: E.memset(qz[:], 0.0), writes=["qz"])

                def epiQs(ct, n0, m, g, ps, bk):
                    cp("act", qS[:, ct, :], ps, [bk], ["qS"])
                B.linear("swq", w_q, D, D, xsS(hS, "hS"), epiQs, 512, wbs)
                act(f3(sqS), f3(qS), AF.Square, ["qS"], ["sqS"])
                bk, ps = B.bank()
                mm(ps[:, 0:KN], blkm[:], f3(sqS), ["c_blkm", "sqS"], [bk])
                act(f3(h32), ps[:, 0:KN], AF.Ln, [bk], ["h32"], bias=1e-6)
                act(f3(h32), f3(h32), AF.Exp, ["h32"], ["h32"], scale=-0.5)
                tt("dve", f3(qS), f3(qS), f3(h32), ALU.mult, ["qS", "h32"], ["qS"])
                for h in range(2):
                    hs = slice(h * 64, (h + 1) * 64)
                    ts("dve", qz[hs, h, :, :].rearrange("p k n -> p (k n)"), qS[hs, :, :].rearrange("p k n -> p (k n)"),
                       hdt[hs, 1:2], ALU.mult, ["qS", "hdt"], ["qz"])
                GQ = H // NKV
                PG = GQ // 2
                oatS = sb("oatS", [128, KC, NS], BF16)
                Kc2 = [sb("Kc2_%d" % i, [128, 128]) for i in range(2)]
                Vc2 = [sb("Vc2_%d" % i, [128, 128]) for i in range(2)]
                Vcb = [sb("Vcb_%d" % i, [128, 128], BF16) for i in range(2)]
                KcT = [sb("KcT_%d" % i, [128, 128], BF16) for i in range(2)]
                knb = sb("knb", [128, NKV, NS], BF16)
                cp("dve", knb[:], knew[:], ["knew"], ["knb"])
                Es = [sb("Es_%d" % i, [128, GQ], BF16) for i in range(2)]
                en = [sb("en_%d" % i, [1, GQ], BF16) for i in range(2)]
                vrow = [sb("vrow_%d" % i, [1, 128], BF16) for i in range(2)]
                krow = [sb("krow_%d" % i, [1, 128]) for i in range(2)]
                vrow32 = [sb("vrow32_%d" % i, [1, 128]) for i in range(2)]
                onesb = sb("onesb", [128, 128], BF16); cp("dve", onesb[:], ones[:], ["c_ones"], ["onesb"])
                dn = [sb("dn_%d" % i, [128, GQ]) for i in range(2)]
                ob = [sb("ob_%d" % i, [128, GQ]) for i in range(2)]
                it = 0
                for j in range(NS):
                    P.dma("sp", o_kwins[j, 0:127, :], ck[j, 1:128, :])
                    P.dma("sp", o_vwins[j, 0:127, :], cv[j, 1:128, :])
                    for g in range(NKV):
                        i2 = it % 2; it += 1
                        kc2, vc2, vcb, kct, E_, en_, vr, kr, vr32, dn_, ob_ = (Kc2[i2], Vc2[i2], Vcb[i2], KcT[i2], Es[i2], en[i2],
                                                                            vrow[i2], krow[i2], vrow32[i2], dn[i2], ob[i2])
                        for half in range(2):
                            P.dma("sp", kc2[:, half * 64:(half + 1) * 64], ck[j, :, g * 64:(g + 1) * 64], writes=[kc2.name])
                            P.dma("sp", vc2[:, half * 64:(half + 1) * 64], cv[j, :, g * 64:(g + 1) * 64], writes=[vc2.name])
                        cp("pool", vcb[:], vc2[:], [vc2.name], [vcb.name])
                        bk, ps = B.bank()
                        P.op("pe", lambda E, o=ps[:, 0:128], i_=kc2: E.transpose(o, i_[:], ident[:]), reads=[kc2.name, "c_ident"], writes=[bk])
                        cp("act", kct[:], ps[:, 0:128], [bk], [kct.name])
                        qg = qz[:, :, g * PG:(g + 1) * PG, j]
                        bk, ps = B.bank()
                        mm(ps[:, 0:GQ].rearrange("p (h c) -> p h c", h=2), kct[:], qg, [kct.name, "qz"], [bk])
                        act(E_[:], ps[:, 0:GQ], AF.Exp, [bk], [E_.name], scale=0.125)
                        ts("dve", E_[:], E_[:], msl[:, 0:1], ALU.mult, [E_.name, "c_sl"], [E_.name])
                        bk, ps = B.bank()
                        mm(ps[0:1, 0:GQ].rearrange("p (h c) -> p h c", h=2), knb[:, g, j:j + 1], qg, ["knb", "qz"], [bk])
                        act(en_[:], ps[0:1, 0:GQ], AF.Exp, [bk], [en_.name], scale=0.125)
                        bk, ps = B.bank()
                        P.op("pe", lambda E, o=ps[0:1, 0:128], g=g, j=j: E.transpose(o, knew[:, g, j:j + 1], ident[:]), reads=["knew", "c_ident"], writes=[bk])
                        P.op("pe", lambda E, o=ps[0:1, 128:256], g=g, j=j: E.transpose(o, vnew[:, g // 2, j:j + 1], ident[:]), reads=["vnew", "c_ident"], writes=[bk])
                        cp("act", kr[:], ps[0:1, 0:128], [bk], [kr.name])
                        go = (g % 2) * 64
                        cp("dve", vr32[:, 0:64], ps[0:1, 128 + go:128 + go + 64], [bk], [vr32.name])
                        cp("dve", vr32[:, 64:128], ps[0:1, 128 + go:128 + go + 64], [bk], [vr32.name])
                        cp("pool", vr[:], vr32[:], [vr32.name], [vr.name])
                        P.dma("sp", o_kwins[j, 127:128, g * 64:(g + 1) * 64], kr[:, 0:64], reads=[kr.name])
                        P.dma("sp", o_vwins[j, 127:128, g * 64:(g + 1) * 64], vr32[:, 0:64], reads=[vr32.name])
                        bkN, psN = B.bank()
                        mm(psN[:, 0:GQ], vcb[:], E_[:], [vcb.name, E_.name], [bkN], start=True, stop=False)
                        mm(psN[:, 0:GQ], vr[:], en_[:], [vr.name, en_.name], [bkN], start=False, stop=True)
                        bkD, psD = B.bank()
                        mm(psD[:, 0:GQ], onesb[:], E_[:], ["onesb", E_.name], [bkD], start=True, stop=False)
                        mm(psD[:, 0:GQ], onesb[0:1, :], en_[:], ["onesb", en_.name], [bkD], start=False, stop=True)
                        tt("dve", dn_[:].rearrange("p (h c) -> p h c", h=2), psD[:, 0:GQ].rearrange("p (h c) -> p h c", h=2),
                           esT[:, g * GQ:(g + 1) * GQ].rearrange("p (c h) -> p h c", h=2), ALU.add, [bkD, "esT"], [dn_.name])
                        P.op("dve", lambda E, d_=dn_: E.reciprocal(d_[:], d_[:]), reads=[dn_.name], writes=[dn_.name])
                        tt("dve", ob_[:], psN[:, 0:GQ], dn_[:], ALU.mult, [bkN, dn_.name], [ob_.name])
                        for h in range(2):
                            hs = slice(h * 64, (h + 1) * 64)
                            cp("pool", oatS[hs, g * PG:(g + 1) * PG, j], ob_[hs, h * PG:(h + 1) * PG], [ob_.name], ["oatS"])
                B.linear("swao", w_ao, D, D, xsS(oatS, "oatS"), resS(x2s, x3s, lambda ct: modv[:, 1, 2 * KC + ct, 1:1 + NS]), 512, wbs)
                ffnS(1, x3s, ysf)
                P.dma("sp", o_ys.rearrange("(kc p) n -> p kc n", p=128), ysf[:], reads=[ysf.name])
                P.dma("sp", o_convs.rearrange("l p f n c -> p l (f n c)"), cvo[:].rearrange("p l f n c -> p l (f n c)"), reads=["cvo"])
                P.barrier()
            B.es = es
        sample_path()
        B._st = dict(Gm=Gm, modv=modv, kvmv=kvmv, vec=vec, vec2=vec2, omm=omm, cvp=cvp, hdt=hdt, flag=flag,
                     esink=esink, ident=ident, ones=ones, blk=blk, blkm=blkm, m2b=m2b, mslb=mslb)
        return B, locals()


def finish(B, P, nc):
    P.finish_waits("sp")
    with nc.Block() as block:
        P.emit(block)


def host_inputs(inp, cfg, core):
    D, T, DFF, NS = cfg["D"], cfg["T"], cfg["DFF"], cfg["NS"]
    KC, FC, H = D // 128, DFF // 128, D // 64
    NP = H // 2
    NKV = max(1, H // 8)
    KVW = NKV * 64
    TH = T // 2
    b, hf = core // 2, core % 2
    f32 = np.float32
    m = {}
    xb = np.asarray(inp["x_prompt"][b], f32)
    xT = np.zeros((D, T), f32)
    mask = np.ones((128, T), f32)
    if hf == 1:
        xT[:] = xb.T
    else:
        xT[:, TH:] = xb[:TH].T
        mask[:, :TH] = 0.0
    m["xT"] = xT
    m["maskrow"] = mask
    m["flagcol"] = np.full((128, 1), float(hf), f32)
    ss = slice(core * NS, (core + 1) * NS)
    m["cT"] = np.ascontiguousarray(np.concatenate([inp["c_prompt"][b][None], inp["c_sample"][ss]], 0).T.astype(f32))
    m["xsT"] = np.ascontiguousarray(inp["x_sample"][ss, 0].T.astype(f32))
    m["shsT"] = np.ascontiguousarray(inp["state_shift"][0, ss].T.astype(f32))
    m["st_wkv"] = np.ascontiguousarray(inp["state_wkv"][0, ss].astype(f32))
    m["st_conv"] = np.ascontiguousarray(inp["state_conv"][:, ss].astype(f32).reshape(2, NS, 2, FC, 128).transpose(0, 4, 3, 1, 2))
    m["ck"] = np.ascontiguousarray(inp["cache_k_win"][ss].reshape(NS, 128, KVW).astype(f32))
    m["cv"] = np.ascontiguousarray(inp["cache_v_win"][ss].reshape(NS, 128, KVW).astype(f32))
    m.update(_consts())
    m["mod_w"] = inp["mod_w"]
    m["modb"] = np.stack([_fm(inp["mod_b"][l], 6 * KC) for l in range(2)])
    m["kv_mod_w"] = inp["kv_mod_w"]
    m["kvmodb"] = _fm(inp["kv_mod_b"], 2 * KC)
    vl = [inp["ln1_g"][0], inp["ln1_g"][1], inp["ln2_g"][0], inp["ln2_g"][1]] + \
         [inp["rwkv_mix"][0, i] for i in range(6)] + \
         [inp["rwkv_w0"][0], inp["rwkv_a0"][0], inp["rwkv_k_k"][0], inp["rwkv_k_a"][0],
          inp["rwkv_r_k"][0].reshape(-1), inp["rwkv_lnx_w"][0]]
    m["vecs"] = np.ascontiguousarray(np.stack([_fm(v, KC) for v in vl], 1))
    m["vecs2"] = np.ascontiguousarray(np.stack([_fm(inp["rwkv_lnx_b"][0], KC), _fm(inp["kv_norm_g"], KC)], 1))
    m["convp"] = np.ascontiguousarray(np.stack(
        [np.stack([_fm(inp["ffn_conv_w"][l, j], FC) for j in range(3)] + [_fm(inp["ffn_conv_b"][l], FC)], 1)
         for l in range(2)], 1))
    hd = np.zeros((128, 3 + NP), f32)
    hd[:, 0] = np.tile(inp["k_norm_g"], 2)
    hd[:, 1] = np.tile(inp["attn_q_norm_g"][0], 2)
    hd[:, 3:] = np.repeat(inp["attn_sinks"][0].reshape(NP, 2), 64, axis=1).T
    m["hd"] = hd
    m["sinkT"] = np.tile(inp["attn_sinks"][0][None, :], (128, 1))
    for k in ("rwkv_w1", "rwkv_w2", "rwkv_a1", "rwkv_a2", "rwkv_g1", "rwkv_g2", "rwkv_w_r", "rwkv_w_k",
              "rwkv_w_v", "rwkv_w_o", "attn_w_q", "attn_w_o"):
        m[k] = inp[k][0]
    m["w_kv"] = inp["w_kv"]
    m["ffn_w_in"] = inp["ffn_w_in"]
    m["ffn_w_out"] = inp["ffn_w_out"]
    return {k: np.ascontiguousarray(np.asarray(v, f32)) for k, v in m.items()}


_CACHE = {}


def kernel(**inputs):
    cfg = REAL_CFG
    inp = {k: np.asarray(v) for k, v in inputs.items()}
    if "prog" not in _CACHE:
        B, L = build(cfg)
        finish(B, B.P, B.nc)
        _CACHE["prog"] = B
    B = _CACHE["prog"]
    n = 8
    in_maps = []
    for c in range(n):
        m = host_inputs(inp, cfg, c)
        in_maps.append({k: m[k] for k in B.ins})
    res = run_bass_kernel_spmd(B.nc, in_maps, core_ids=list(range(n)))
    R = res.results
    D, T, DFF, NS = cfg["D"], cfg["T"], cfg["DFF"], cfg["NS"]
    KC, FC, H = D // 128, DFF // 128, D // 64
    NKV = max(1, H // 8)
    TH = T // 2
    NB = 4
    f32 = np.float32
    y_p = np.zeros((NB, T, D), f32); y_s = np.zeros((NB * 8, 1, D), f32)
    wkv_p = np.zeros((1, NB, H, 64, 64), f32); wkv_s = np.zeros((1, NB * 8, H, 64, 64), f32)
    sh_p = np.zeros((1, NB, D), f32); sh_s = np.zeros((1, NB * 8, D), f32)
    cv_p = np.zeros((2, NB, 2, DFF), f32); cv_s = np.zeros((2, NB * 8, 2, DFF), f32)
    kw_p = np.zeros((NB, 128, NKV, 64), f32); kw_s = np.zeros((NB * 8, 128, NKV, 64), f32)
    vw_p = np.zeros((NB, 128, NKV, 64), f32); vw_s = np.zeros((NB * 8, 128, NKV, 64), f32)
    for c in range(n):
        b, hf = c // 2, c % 2
        r = R[c]
        y_p[b, hf * TH:(hf + 1) * TH] = r["o_y"].T
        ss = slice(c * NS, (c + 1) * NS)
        y_s[ss, 0] = r["o_ys"].T
        wkv_s[0, ss] = r["o_wkvs"]
        sh_s[0, ss] = r["o_shifts"].transpose(2, 1, 0).reshape(NS, D)
        cv_s[:, ss] = r["o_convs"].transpose(0, 3, 4, 2, 1).reshape(2, NS, 2, DFF)
        kw_s[ss] = r["o_kwins"].reshape(NS, 128, NKV, 64)
        vw_s[ss] = r["o_vwins"].reshape(NS, 128, NKV, 64)
        if hf == 1:
            wkv_p[0, b] = r["o_wkv"]
            sh_p[0, b] = r["o_shift"].T.reshape(D)
            cv_p[:, b] = r["o_conv"].transpose(0, 3, 2, 1).reshape(2, 2, DFF)
            kw_p[b] = r["o_kwin"].reshape(128, NKV, 64)
            vw_p[b] = r["o_vwin"].reshape(128, NKV, 64)
    return (y_p, y_s, wkv_p, wkv_s, sh_p, sh_s, cv_p, cv_s, kw_p, kw_s, vw_p, vw_s)
```

```python
import contextlib
import numpy as np
import concourse.bass as bass
import concourse.mybir as mybir
from concourse.bass_utils import run_bass_kernel_spmd

F32 = mybir.dt.float32
BF16 = mybir.dt.bfloat16
AF = mybir.ActivationFunctionType
ALU = mybir.AluOpType

REAL_CFG = dict(D=4096, T=2048, DFF=14336, DL=128, DA=128, DG=480, NS=4, WB=128)


class Prog:
    ENG = ("pe", "act", "dve", "pool", "sp")

    def __init__(self, nc, es):
        self.nc = nc
        self.h = dict(pe=nc.tensor, act=nc.scalar, dve=nc.vector, pool=nc.gpsimd, sp=nc.sync)
        self.sem = {e: es.enter_context(nc.semaphore("s_" + e)) for e in self.ENG}
        self.cnt = {e: 0 for e in self.ENG}
        self.q = {e: [] for e in self.ENG}
        self.waited = {e: {} for e in self.ENG}
        self.lastw = {}
        self.readers = {}
        self.semobj = {("c", e): self.sem[e] for e in self.ENG}
        self.dsem = {}
        for e in ("sp", "pool", "act"):
            self.dsem[e] = [es.enter_context(nc.semaphore("d_%s%d" % (e, i))) for i in range(12)]
            for i, s in enumerate(self.dsem[e]):
                self.semobj[("d", e, i)] = s
        self.dval = {k: 0 for k in self.semobj if k[0] == "d"}
        self.drr = {e: 0 for e in ("sp", "pool", "act")}
        self.nbank = 0

    def _deps(self, reads, writes):
        need = {}
        for r in reads:
            lw = self.lastw.get(r)
            if lw:
                need[lw[0]] = max(need.get(lw[0], 0), lw[1])
        for w in writes:
            lw = self.lastw.get(w)
            if lw:
                need[lw[0]] = max(need.get(lw[0], 0), lw[1])
            for k, v in self.readers.get(w, {}).items():
                need[k] = max(need.get(k, 0), v)
        return need

    def _emit_waits(self, eng, need):
        for k, v in need.items():
            if eng == "pe" and k == ("c", "pe"):
                continue
            if self.waited[eng].get(k, 0) < v:
                self.waited[eng][k] = v
                so = self.semobj[k]
                self.q[eng].append(lambda E, so=so, v=v: E.wait_ge(so, v))

    def _mark(self, tok, reads, writes):
        for w in writes:
            self.lastw[w] = tok
            self.readers[w] = {}
        for r in reads:
            self.readers.setdefault(r, {})
            d = self.readers[r]
            d[tok[0]] = max(d.get(tok[0], 0), tok[1])

    def op(self, eng, fn, reads=(), writes=()):
        bk_ = [r for r in reads if isinstance(r, tuple) and r and r[0] == "bank"]
        if bk_:
            writes = list(writes) + bk_
        need = self._deps(reads, writes)
        self._emit_waits(eng, need)
        self.cnt[eng] += 1
        idx = self.cnt[eng]
        so = self.sem[eng]
        self.q[eng].append(lambda E, fn=fn, so=so: fn(E).then_inc(so, 1))
        self.waited[eng][("c", eng)] = max(self.waited[eng].get(("c", eng), 0), 0)
        self._mark((("c", eng), idx), reads, writes)

    def dma(self, eng, out, in_, reads=(), writes=()):
        need = self._deps(reads, writes)
        i = self.drr[eng]
        self.drr[eng] = (i + 1) % len(self.dsem[eng])
        key = ("d", eng, i)
        if self.dval[key] > 0:
            need[key] = max(need.get(key, 0), self.dval[key])
        self._emit_waits(eng, need)
        self.dval[key] += 16
        so = self.semobj[key]
        self.q[eng].append(lambda E, so=so, out=out, in_=in_: E.dma_start(out=out, in_=in_).then_inc(so, 16))
        self._mark((key, self.dval[key]), reads, writes)

    def finish_waits(self, eng="sp"):
        need = {}
        for k, v in self.dval.items():
            if v:
                need[k] = v
        for e in self.ENG:
            if self.cnt[e] and e != eng:
                need[("c", e)] = self.cnt[e]
        self._emit_waits(eng, need)

    def barrier(self):
        for e in self.ENG:
            self.finish_waits(e)

    def emit(self, block):
        for e, reg in (("pe", block.tensor), ("act", block.scalar), ("dve", block.vector),
                       ("pool", block.gpsimd), ("sp", block.sync)):
            ops = self.q[e]

            def body(E, ops=ops):
                for f in ops:
                    f(E)
            reg(body)


def _chunks(n, c=128):
    return [(i, min(c, n - i)) for i in range(0, n, c)]


class Builder:
    def __init__(self, cfg, debug=()):
        self.cfg = cfg
        self.debug = set(debug)
        c = cfg
        self.D, self.T, self.DFF = c["D"], c["T"], c["DFF"]
        self.KC = self.D // 128
        self.FC = self.DFF // 128
        self.H = self.D // 64
        self.NP = self.H // 2
        self.NKV = max(1, self.H // 8)
        self.NS = c["NS"]
        self.TH = self.T // 2
        self.HALO = 130
        self.W0 = self.TH - self.HALO
        self.NW = self.T - self.W0
        self.nc = bass.Bass("TRN2", target_bir_lowering=False)
        self.ins = {}
        self.outs = {}

    def din(self, name, shape):
        t = self.nc.dram_tensor(name, list(shape), F32, kind="ExternalInput").ap()
        self.ins[name] = tuple(shape)
        return t

    def dout(self, name, shape, dt=F32):
        t = self.nc.dram_tensor(name, list(shape), dt, kind="ExternalOutput").ap()
        self.outs[name] = tuple(shape)
        return t

    def scratch(self, name, shape, dt=F32):
        if name in self.debug:
            return self.dout(name, shape, dt)
        return self.nc.dram_tensor(name, list(shape), dt, kind="Internal").ap()

    def sb(self, name, shape, dt=F32):
        return self.es.enter_context(self.nc.sbuf_tensor("t_" + name, list(shape), dt))

    def linear(self, name, w, K, N, xs, epi, gw, wb, colgroups=None):
        P = self.P
        kch = _chunks(K)
        nk = len(kch)
        nfull = K // 128
        for gi, (n0, gsz) in enumerate(colgroups or _chunks(N, gw)):
            buf = wb[gi % len(wb)]
            bkey = ("wb", buf.name)
            if nfull:
                P.dma("pool", buf[:, 0:nfull, 0:gsz],
                      w[0:nfull * 128, n0:n0 + gsz].rearrange("(kc p) n -> p kc n", p=128),
                      writes=[bkey])
            if nfull < nk:
                k0, ksz = kch[-1]
                P.dma("pool", buf[0:ksz, nfull, 0:gsz], w[k0:k0 + ksz, n0:n0 + gsz], writes=[bkey])
            for (c0, m) in _chunks(gsz):
                for g, (xf, ncols, rk) in enumerate(xs):
                    bk, ps = self.bank()
                    for ki, (k0, ksz) in enumerate(kch):
                        lhsT = buf[0:ksz, ki, c0:c0 + m]
                        rhs = xf(ki, ksz)
                        P.op("pe", lambda E, o=ps[0:m, 0:ncols], l=lhsT, r=rhs, s=(ki == 0), t=(ki == nk - 1):
                             E.matmul(o, l, r, start=s, stop=t),
                             reads=[bkey] + list(rk), writes=[bk])
                    epi((n0 + c0) // 128, n0 + c0, m, g, ps[0:m, 0:ncols], bk)

    def bank(self):
        i = self.P.nbank % len(self.banks)
        self.P.nbank += 1
        return ("bank", i), self.banks[i]

    def act(self, out, in_, func, R, W, bias=None, scale=None, eng="act"):
        kw = {}
        if bias is not None:
            kw["bias"] = bias
        if scale is not None:
            kw["scale"] = scale
        self.P.op("act", lambda E: E.activation(out, in_, func, **kw), reads=R, writes=W)

    def ts(self, eng, out, in0, s1, op0, R, W, s2=None, op1=None):
        if op1 is None:
            self.P.op(eng, lambda E: E.tensor_scalar(out, in0, s1, None, op0), reads=R, writes=W)
        else:
            self.P.op(eng, lambda E: E.tensor_scalar(out, in0, s1, s2, op0, op1), reads=R, writes=W)

    def tt(self, eng, out, a, b, op, R, W):
        self.P.op(eng, lambda E: E.tensor_tensor(out, a, b, op), reads=R, writes=W)

    def stt(self, out, in0, scalar, in1, op0, op1, R, W):
        self.P.op("dve", lambda E: E.scalar_tensor_tensor(out, in0, scalar, in1, op0, op1), reads=R, writes=W)

    def cp(self, eng, out, in_, R, W):
        if eng == "act":
            self.P.op("act", lambda E: E.activation(out, in_, AF.Copy), reads=R, writes=W)
        else:
            self.P.op(eng, lambda E: E.tensor_copy(out, in_), reads=R, writes=W)

    def mm(self, out, lhsT, rhs, R, W, start=True, stop=True):
        self.P.op("pe", lambda E: E.matmul(out, lhsT, rhs, start=start, stop=stop), reads=R, writes=W)

    def dump(self, name, ap, key, shape):
        if ("dbg_" + name) in self.debug:
            dd = self.dout("dbg_" + name, shape)
            self.P.dma("sp", dd, ap, reads=[key])

    def load(self, tile_ap, dram_ap, key, q="sp"):
        self.P.dma(q, tile_ap, dram_ap, writes=[key])


def _fm(v, n):
    return np.ascontiguousarray(np.asarray(v, np.float32).reshape(n, 128).T)


def _consts():
    p = np.arange(128)[:, None]
    f = np.arange(128)[None, :]
    c = {}
    c["c_ident"] = (p == f).astype(np.float32)
    c["c_ones"] = np.ones((128, 128), np.float32)
    blk = ((p // 64) == (f // 64)).astype(np.float32)
    c["c_blk"] = blk
    c["c_blkm"] = blk / 64.0
    su = (p < f).astype(np.float32)
    ui = (p <= f).astype(np.float32)
    c["c_m2"] = np.concatenate([su, ui], axis=1)
    c["c_sl"] = (p > f).astype(np.float32)
    c["c_bd8"] = ((p // 8) == (f // 8)).astype(np.float32)
    lvL, lvU = [], []
    for m in (8, 16, 32, 64):
        ml = (((p // (2 * m)) == (f // (2 * m))) & ((p % (2 * m)) >= m) & ((f % (2 * m)) < m)).astype(np.float32)
        lvL.append(ml); lvU.append(ml.T)
    c["c_lvL"] = np.concatenate(lvL, axis=1)
    c["c_lvU"] = np.concatenate(lvU, axis=1)
    return c


def build(cfg, debug=()):
    B = Builder(cfg, debug)
    nc = B.nc
    D, T, DFF, KC, FC, H, NP, NKV, NS = B.D, B.T, B.DFF, B.KC, B.FC, B.H, B.NP, B.NKV, B.NS
    DL, DA, DG = cfg["DL"], cfg["DA"], cfg["DG"]
    TH = T // 2
    HALO = 258
    W0 = TH - HALO
    NW = T - W0
    KVW = NKV * 64
    KVT = max(1, KVW // 128)
    NC5 = 1 + NS
    C = 128
    NCH = T // C
    NTF = T // 4
    din, dout = B.din, B.dout

    xT = din("xT", [D, T])
    maskrow = din("maskrow", [128, T])
    flagcol = din("flagcol", [128, 1])
    cT = din("cT", [D, NC5])
    xsT = din("xsT", [D, NS])
    shsT = din("shsT", [D, NS])
    st_wkv = din("st_wkv", [NS, H, 64, 64])
    st_conv = din("st_conv", [2, 128, FC, NS, 2])
    ck = din("ck", [NS, 128, KVW])
    cv = din("cv", [NS, 128, KVW])
    cst = {k: din(k, list(v.shape)) for k, v in _consts().items()}
    mod_w = din("mod_w", [2, D, 6 * D])
    modb = din("modb", [2, 128, 6 * KC])
    kv_mod_w = din("kv_mod_w", [D, 2 * D])
    kvmodb = din("kvmodb", [128, 2 * KC])
    vecs = din("vecs", [128, 16, KC])
    vecs2 = din("vecs2", [128, 2, KC])
    convp = din("convp", [128, 2, 4, FC])
    hd = din("hd", [128, 3 + NP])
    sinkT = din("sinkT", [128, H])
    w1 = din("rwkv_w1", [D, DL]); w2 = din("rwkv_w2", [DL, D])
    a1 = din("rwkv_a1", [D, DA]); a2 = din("rwkv_a2", [DA, D])
    g1 = din("rwkv_g1", [D, DG]); g2 = din("rwkv_g2", [DG, D])
    w_r = din("rwkv_w_r", [D, D]); w_k = din("rwkv_w_k", [D, D]); w_v = din("rwkv_w_v", [D, D])
    w_o = din("rwkv_w_o", [D, D])
    w_kv = din("w_kv", [D, 2 * KVW])
    w_q = din("attn_w_q", [D, D]); w_ao = din("attn_w_o", [D, D])
    w_in = din("ffn_w_in", [2, D, 2 * DFF]); w_out = din("ffn_w_out", [2, DFF, D])

    o_y = dout("o_y", [D, T - TH])
    o_ys = dout("o_ys", [D, NS])
    o_wkv = dout("o_wkv", [H, 64, 64])
    o_wkvs = dout("o_wkvs", [NS, H, 64, 64])
    o_shift = dout("o_shift", [128, KC])
    o_shifts = dout("o_shifts", [128, KC, NS])
    o_conv = dout("o_conv", [2, 128, FC, 2])
    o_convs = dout("o_convs", [2, 128, FC, NS, 2])
    o_kwin = dout("o_kwin", [128, KVW])
    o_vwin = dout("o_vwin", [128, KVW])
    o_kwins = dout("o_kwins", [NS, 128, KVW])
    o_vwins = dout("o_vwins", [NS, 128, KVW])

    sc = {n: B.scratch("s_" + n, [D, T]) for n in ("r", "k", "v", "ld", "a", "g")}
    scs = {n: B.scratch("ss_" + n, [D, NS]) for n in ("r", "k", "v", "ld", "a", "g")}
    s_yg = B.scratch("s_yg", [D, T], BF16)
    s_ygs = B.scratch("s_ygs", [D, NS], BF16)

    with contextlib.ExitStack() as es:
        B.es = es
        P = B.P = Prog(nc, es)
        B.banks = [es.enter_context(nc.psum_tensor("ps%d" % i, [128, 512], F32)) for i in range(8)]
        sb = B.sb
        act, ts, tt, stt, cp, mm = B.act, B.ts, B.tt, B.stt, B.cp, B.mm

        K_ = {}
        for k, v in cst.items():
            if k in ("c_bd8", "c_lvL", "c_lvU"):
                continue
            K_[k] = sb(k, list(B.ins[k]))
            P.dma("sp", K_[k][:], v[:, :], writes=[k])
        ident, ones, blk, blkm, m2, msl = (K_[k] for k in ("c_ident", "c_ones", "c_blk", "c_blkm", "c_m2", "c_sl"))
        m2b = sb("m2b", [128, 256], BF16); cp("dve", m2b[:], m2[:], ["c_m2"], ["m2b"])
        mslb = sb("mslb", [128, 128], BF16); cp("dve", mslb[:], msl[:], ["c_sl"], ["mslb"])
        vec = sb("vec", [128, 16, KC]); P.dma("sp", vec[:], vecs[:, :, :], writes=["vec"])
        vec2 = sb("vec2", [128, 2, KC]); P.dma("sp", vec2[:], vecs2[:, :, :], writes=["vec2"])
        cvp = sb("cvp", [128, 2, 4, FC]); P.dma("sp", cvp[:], convp[:, :, :, :], writes=["cvp"])
        hdt = sb("hdt", [128, 3 + NP]); P.dma("sp", hdt[:], hd[:, :], writes=["hdt"])
        flag = sb("flag", [128, 1]); P.dma("sp", flag[:], flagcol[:, :], writes=["flag"])
        modbt = sb("modbt", [128, 2, 6 * KC]); P.dma("sp", modbt[:], modb.rearrange("l p n -> p l n"), writes=["modbt"])
        kvmodbt = sb("kvmodbt", [128, 2 * KC]); P.dma("sp", kvmodbt[:], kvmodb[:, :], writes=["kvmodbt"])
        omm = sb("omm", [128, 6, KC])
        ts("dve", omm[:], vec[:, 4:10, :], -1.0, ALU.mult, ["vec"], ["omm"], 1.0, ALU.add)
        esink = sb("esink", [128, NP])
        act(esink[:], hdt[:, 3:3 + NP], AF.Exp, ["hdt"], ["esink"])
        modv = sb("modv", [128, 2, 6 * KC, NC5])
        kvmv = sb("kvmv", [128, 2 * KC, NC5])
        Gm = sb("Gm", [128, 5, KC, NC5])

        with contextlib.ExitStack() as es2:
            B.es = es2
            c5 = sb("c5", [128, KC, NC5]); P.dma("sp", c5[:], cT.rearrange("(kc p) n -> p kc n", p=128), writes=["c5"])
            c5s = sb("c5s", [128, KC, NC5])
            act(c5s[:], c5[:], AF.Sigmoid, ["c5"], ["c5s"])
            c5b = sb("c5b", [128, KC, NC5], BF16)
            tt("dve", c5b[:], c5[:], c5s[:], ALU.mult, ["c5", "c5s"], ["c5b"])
            wbA = [sb("wbA%d" % i, [128, KC, 512], BF16) for i in range(2)]
            xsA = [(lambda ki, ksz: c5b[0:ksz, ki, :], NC5, ["c5b"])]
            for l in range(2):
                def epiA(ct, n0, m, g, ps, bk, l=l):
                    act(modv[:, l, ct, :], ps, AF.Identity, [bk, "modbt"], ["modv"], bias=modbt[:, l, ct:ct + 1])
                B.linear("modw%d" % l, mod_w[l], D, 6 * D, xsA, epiA, 512, wbA)

            def epiK(ct, n0, m, g, ps, bk):
                act(kvmv[:, ct, :], ps, AF.Identity, [bk, "kvmodbt"], ["kvmv"], bias=kvmodbt[:, ct:ct + 1])
            B.linear("kvmodw", kv_mod_w, D, 2 * D, xsA, epiK, 512, wbA)
            for j in range(NC5):
                for gi, (l, which, vi) in enumerate(((0, 1, 0), (0, 4, 2), (1, 1, 1), (1, 4, 3))):
                    stt(Gm[:, gi, :, j], modv[:, l, which * KC:(which + 1) * KC, j], 1.0, vec[:, vi, :],
                        ALU.add, ALU.mult, ["modv", "vec"], ["Gm"])
                stt(Gm[:, 4, :, j], kvmv[:, KC:2 * KC, j], 1.0, vec2[:, 1, :], ALU.add, ALU.mult,
                    ["kvmv", "vec2"], ["Gm"])
            P.barrier()
        B.es = es
        shiftout = sb("shiftout", [128, KC])
        hlast = sb("hlast", [128, KC], BF16)
        P.op("pool", lambda E: E.memset(hlast[:], 0.0), writes=["hlast"])
        LDC = -float(np.exp(-0.5))

        def front(cols, xsrc, dst, per_col, hprev_src=None, tag="p"):
            grp = _chunks(cols, 512)
            with contextlib.ExitStack() as es3:
                B.es = es3
                hb = sb("hb" + tag, [128, KC, cols + 1], BF16)
                xis = [sb("xi%s%d" % (tag, i), [128, KC, cols], BF16) for i in range(2)]
                xs_ = [sb("xs%s%d" % (tag, i), [128, cols]) for i in range(2)]
                sq = [sb("sq%s%d" % (tag, i), [128, cols]) for i in range(2)]
                rstd = sb("rstd" + tag, [128, cols])
                hf = [sb("hf%s%d" % (tag, i), [128, cols]) for i in range(2)]
                og = [sb("og%s%d" % (tag, i), [128, 512]) for i in range(3)]
                wbF = [sb("wbF%s%d" % (tag, i), [128, KC, 256], BF16) for i in range(2)]
                wbS = [sb("wbS%s%d" % (tag, i), [128, 4, 512], BF16) for i in range(2)]
                lo = sb("lo" + tag, [128, 4, cols], BF16)
                bks = [B.bank() for _ in grp]
                for kc in range(KC):
                    x_ = xs_[kc % 2]; q_ = sq[kc % 2]
                    P.dma("sp", x_[:], xsrc(kc), writes=[x_.name])
                    act(q_[:], x_[:], AF.Square, [x_.name], [q_.name])
                    for gi, (c0, n) in enumerate(grp):
                        mm(bks[gi][1][:, 0:n], ones[:], q_[:, c0:c0 + n], [q_.name, "c_ones"], [bks[gi][0]],
                           start=(kc == 0), stop=(kc == KC - 1))
                for gi, (c0, n) in enumerate(grp):
                    act(rstd[:, c0:c0 + n], bks[gi][1][:, 0:n], AF.Ln, [bks[gi][0]], ["rstd"], scale=1.0 / D, bias=1e-6)
                if ("dbg_" + tag) in B.debug:
                    dd = B.dout("dbg_ln" + tag, [128, cols])
                    P.dma("sp", dd[:, :], rstd[:], reads=["rstd"])
                    dd2 = B.dout("dbg_sq" + tag, [128, cols])
                    P.dma("sp", dd2[:, :], sq[(KC - 1) % 2][:], reads=[sq[(KC - 1) % 2].name])
                act(rstd[:], rstd[:], AF.Exp, ["rstd"], ["rstd"], scale=-0.5)
                if ("dbg_" + tag) in B.debug:
                    dd = B.dout("dbg_rstd" + tag, [128, cols])
                    P.dma("sp", dd[:, :], rstd[:], reads=["rstd"])
                if not per_col:
                    cp("pool", hb[:, :, 0], hlast[:], ["hlast"], ["hb"])
                for kc in range(KC):
                    x_ = xs_[kc % 2]; h_ = hf[kc % 2]
                    P.dma("sp", x_[:], xsrc(kc), writes=[x_.name])
                    tt("dve", h_[:], x_[:], rstd[:], ALU.mult, [x_.name, "rstd"], [h_.name])
                    if kc == KC - 1:
                        B.dump("xn" + tag, h_[:], h_.name, [128, cols])
                        B.dump("xx" + tag, x_[:], x_.name, [128, cols])
                    if per_col:
                        tt("dve", h_[:], h_[:], Gm[:, 0, kc, 1:1 + NS], ALU.mult, [h_.name, "Gm"], [h_.name])
                        tt("dve", h_[:], h_[:], modv[:, 0, 0 * KC + kc, 1:1 + NS], ALU.add, [h_.name, "modv"], [h_.name])
                        cp("pool", shiftouts[:, kc, :], h_[:], [h_.name], ["shiftouts"])
                        cp("pool", hb[:, kc, 1:cols + 1], h_[:], [h_.name], ["hb"])
                    else:
                        act(h_[:], h_[:], AF.Identity, [h_.name, "Gm", "modv"], [h_.name],
                            bias=modv[:, 0, kc, 0:1], scale=Gm[:, 0, kc, 0:1])
                        if kc == KC - 1:
                            B.dump("hh" + tag, h_[:], h_.name, [128, cols])
                        cp("pool", shiftout[:, kc:kc + 1], h_[:, cols - 1:cols], [h_.name], ["shiftout"])
                        tt("dve", hb[:, kc, 1:cols + 1], h_[:], mrow[:, 0:cols], ALU.mult, [h_.name, "mrow"], ["hb"])
                if per_col:
                    hps = sb("hps", [128, KC, NS])
                    P.dma("sp", hps[:], shsT.rearrange("(kc p) n -> p kc n", p=128), writes=["hps"])
                else:
                    cp("pool", hlast[:], hb[:, :, cols], ["hb"], ["hlast"])

                def hprev(kc):
                    if per_col:
                        return hps[:, kc, :]
                    return hb[:, kc, 0:cols]

                def mix(i):
                    xi = xis[i % 2]
                    for kc in range(KC):
                        t_ = sq[kc % 2]
                        act(t_[:], hprev(kc), AF.Copy, ["hb", "hps", "vec"], [t_.name], scale=vec[:, 4 + i, kc:kc + 1])
                        stt(xi[:, kc, :], hb[:, kc, 1:cols + 1], omm[:, i, kc:kc + 1], t_[:], ALU.mult, ALU.add,
                            ["hb", "omm", t_.name], [xi.name])
                    return [(lambda ki, ksz, c0=c0, n=n, xi=xi: xi[0:ksz, ki, c0:c0 + n], n, [xi.name]) for (c0, n) in grp]
                xsL = [(lambda ki, ksz, c0=c0, n=n: lo[0:ksz, ki, c0:c0 + n], n, ["lo"]) for (c0, n) in grp]
                ogi = [0]

                def store(name, func, bias_v=None, post=None):
                    def epi(ct, n0, m, g, ps, bk):
                        o_ = og[ogi[0] % 3]; ogi[0] += 1
                        c0, n = grp[g]
                        if func is None:
                            cp("act", o_[0:m, 0:n], ps, [bk], [o_.name])
                        else:
                            act(o_[0:m, 0:n], ps, func, [bk, "vec"], [o_.name],
                                bias=(vec[:, bias_v, ct:ct + 1] if bias_v is not None else None))
                        if post is not None:
                            ts("dve", o_[0:m, 0:n], o_[0:m, 0:n], post, ALU.mult, [o_.name], [o_.name])
                        P.dma("sp", dst[name](ct, c0, n), o_[0:m, 0:n], reads=[o_.name])
                    return epi

                def tolo(func):
                    def epi(ct, n0, m, g, ps, bk):
                        c0, n = grp[g]
                        if func is None:
                            cp("act", lo[0:m, ct, c0:c0 + n], ps, [bk], ["lo"])
                        else:
                            act(lo[0:m, ct, c0:c0 + n], ps, func, [bk], ["lo"])
                    return epi
                for i, (nm, w_) in enumerate((("r", w_r), ("ld", None), ("k", w_k), ("v", w_v), ("a", None), ("g", None))):
                    xsX = mix(i)
                    if w_ is not None:
                        B.linear(nm, w_, D, D, xsX, store(nm, None), 256, wbF)
                    elif nm == "ld":
                        B.linear("w1", w1, D, DL, xsX, tolo(AF.Tanh), 256, wbF)
                        B.linear("w2", w2, DL, D, xsL, store("ld", AF.Sigmoid, 10, LDC), 512, wbS)
                    elif nm == "a":
                        B.linear("a1", a1, D, DA, xsX, tolo(None), 256, wbF)
                        B.linear("a2", a2, DA, D, xsL, store("a", AF.Sigmoid, 11), 512, wbS)
                    else:
                        B.linear("g1", g1, D, DG, xsX, tolo(AF.Sigmoid), 256, wbF)
                        B.linear("g2", g2, DG, D, xsL, store("g", None), 512, wbS)
                P.barrier()
            B.es = es

        mrow = sb("mrow", [128, NTF])
        shiftouts = sb("shiftouts", [128, KC, NS])
        for p_ in range(T // NTF):
            P.dma("sp", mrow[:], maskrow[:, p_ * NTF:(p_ + 1) * NTF], writes=["mrow"])
            dstp = {n: (lambda ct, c0, n_, n=n, p_=p_: sc[n][ct * 128:(ct + 1) * 128, p_ * NTF + c0:p_ * NTF + c0 + n_])
                    for n in sc}
            front(NTF, lambda kc, p_=p_: xT[kc * 128:(kc + 1) * 128, p_ * NTF:(p_ + 1) * NTF], dstp, False, tag="p%d" % p_)
        P.dma("sp", o_shift[:, :], shiftout[:], reads=["shiftout"])
        dsts = {n: (lambda ct, c0, n_, n=n: scs[n][ct * 128:(ct + 1) * 128, c0:c0 + n_]) for n in scs}
        front(NS, lambda kc: xsT[kc * 128:(kc + 1) * 128, :], dsts, True, tag="s")
        P.dma("sp", o_shifts[:, :, :], shiftouts[:], reads=["shiftouts"])
        B.dump("gm_end", Gm[:].rearrange("p a k n -> p (a k n)"), "Gm", [128, 5 * KC * NC5])
        B.dump("modv_end", modv[:].rearrange("p a k n -> p (a k n)"), "modv", [128, 2 * 6 * KC * NC5])
        GN_EPS = 64e-5
        omka = sb("omka", [128, KC])
        ts("dve", omka[:], vec[:, 13, :], -1.0, ALU.mult, ["vec"], ["omka"], 1.0, ALU.add)
        onesC = sb("onesC", [128, C]); P.op("pool", lambda E: E.memset(onesC[:], 1.0), writes=["onesC"])
        m2b2 = sb("m2b2", [128, 2, 256], BF16)
        for h_ in range(2):
            cp("dve", m2b2[:, h_, :], m2[:], ["c_m2"], ["m2b2"])
        msl2 = sb("msl2", [128, 2, 128], BF16)
        idb2 = sb("idb2", [128, 2, 128], BF16)
        for h_ in range(2):
            cp("dve", msl2[:, h_, :], msl[:], ["c_sl"], ["msl2"])
            cp("dve", idb2[:, h_, :], ident[:], ["c_ident"], ["idb2"])

        def diag2(t3):
            return t3[:].rearrange("p h (a b) -> p (h a) b", b=64)[:, 0:4:3, :]

        def scan_stage():
            with contextlib.ExitStack() as es4:
                B.es = es4
                GP = min(4, NP)
                NSLOT = 4
                Sm = sb("Sm", [128, NP, 64])
                Sblk = sb("Sblk", [128, NP, 128], BF16)
                bd8_2 = sb("bd8_2", [128, 2, 128], BF16)
                lvL2 = sb("lvL2", [128, 4, 2, 128], BF16); lvU2 = sb("lvU2", [128, 4, 2, 128], BF16)
                ctmp = sb("ctmp", [128, 1152])
                P.dma("sp", ctmp[:, 0:128], cst["c_bd8"][:, :], writes=["ctmp"])
                P.dma("sp", ctmp[:, 128:640], cst["c_lvL"][:, :], writes=["ctmp"])
                P.dma("sp", ctmp[:, 640:1152], cst["c_lvU"][:, :], writes=["ctmp"])
                for h_ in range(2):
                    cp("dve", bd8_2[:, h_, :], ctmp[:, 0:128], ["ctmp"], ["bd8_2"])
                    for l_ in range(4):
                        cp("dve", lvL2[:, l_, h_, :], ctmp[:, 128 + l_ * 128:128 + (l_ + 1) * 128], ["ctmp"], ["lvL2"])
                        cp("dve", lvU2[:, l_, h_, :], ctmp[:, 640 + l_ * 128:640 + (l_ + 1) * 128], ["ctmp"], ["lvU2"])
                lds = {n: [sb("ld_%s%d" % (n, i), [128, GP, C]) for i in range(2)] for n in ("r", "k", "v", "ld", "a", "g")}
                SL = []
                for sl in range(NSLOT):
                    d = {}
                    for n in ("kkr", "sqk", "rn", "kk", "t1", "k2", "bet", "L", "eL", "enL", "ePrev", "eRev", "tmp",
                              "bts", "kts", "ysb", "mu2", "var", "rs", "yn", "rk", "bon"):
                        d[n] = sb("%s_%d" % (n, sl), [128, C])
                    d["gC"] = sb("gC_%d" % sl, [128, 1])
                    d["YY"] = sb("YY_%d" % sl, [128, 2 * C])
                    d["AR"] = sb("AR_%d" % sl, [128, 2 * C], BF16)
                    for n in ("Bbp", "Kbp", "Pm", "Q0", "Q1", "P0", "P1", "T0", "T1", "Vtp", "Btp", "Ktp", "Up",
                              "QA", "PA", "Dq0", "Dq1", "Dp0", "Dp1", "Y1", "Z1", "mt1", "mt2"):
                        d[n] = sb("%s_%d" % (n, sl), [128, 2, 128], BF16)
                    d["QM"] = sb("QM_%d" % sl, [128, 2, 2 * C], BF16)
                    d["KM"] = sb("KM_%d" % sl, [128, 2, 2 * C], BF16)
                    d["RH"] = sb("RH_%d" % sl, [128, 2, 64], BF16)
                    d["yg"] = sb("yg_%d" % sl, [128, C], BF16)
                    for n in ("Bbp", "Kbp", "Vtp", "Btp", "Ktp", "Up"):
                        P.op("pool", lambda E, t=d[n]: E.memset(t[:], 0.0), writes=[d[n].name])
                    SL.append(d)
                P.op("pool", lambda E: E.memset(Sm[:], 0.0), writes=[("Sm", q) for q in range(NP)])
                P.op("pool", lambda E: E.memset(Sblk[:], 0.0), writes=[("S", q) for q in range(NP)])
                groups_ = [(j, q0) for j in range(NCH) for q0 in range(0, NP, GP)]

                def load_group(gi_):
                    j_, q0_ = groups_[gi_]
                    for n in lds:
                        P.dma("sp", lds[n][gi_ % 2][:], sc[n][q0_ * 128:(q0_ + GP) * 128, j_ * C:(j_ + 1) * C]
                              .rearrange("(q p) t -> p q t", p=128), writes=[lds[n][gi_ % 2].name])
                load_group(0)
                for gi_, (j, q0) in enumerate(groups_):
                    if True:
                        gb = gi_ % 2
                        if gi_ + 1 < len(groups_):
                            load_group(gi_ + 1)
                        def unit(qi, j=j, gb=gb, q0=q0):
                            hp = q0 + qi
                            W = SL[hp % NSLOT]
                            N_ = lambda n: W[n].name
                            r_, k_, v_, ld_, a_, g_ = (lds[n][gb][:, qi, :] for n in ("r", "k", "v", "ld", "a", "g"))
                            R6 = {n: lds[n][gb].name for n in lds}
                            pk = ("S", hp)
                            ts("dve", W["kkr"][:], k_, vec[:, 12, hp:hp + 1], ALU.mult, [R6["k"], "vec"], [N_("kkr")])
                            act(W["sqk"][:], W["kkr"][:], AF.Square, [N_("kkr")], [N_("sqk")])
                            bk, ps = B.bank()
                            mm(ps[:, 0:C], blk[:], W["sqk"][:], ["c_blk", N_("sqk")], [bk])
                            ts("dve", W["rn"][:], ps[:, 0:C], 1e-24, ALU.max, [bk], [N_("rn")])
                            act(W["rn"][:], W["rn"][:], AF.Ln, [N_("rn")], [N_("rn")])
                            act(W["rn"][:], W["rn"][:], AF.Exp, [N_("rn")], [N_("rn")], scale=-0.5)
                            tt("dve", W["kk"][:], W["kkr"][:], W["rn"][:], ALU.mult, [N_("kkr"), N_("rn")], [N_("kk")])
                            act(W["t1"][:], a_, AF.Identity, [R6["a"], "vec", "omka"], [N_("t1")],
                                bias=omka[:, hp:hp + 1], scale=vec[:, 13, hp:hp + 1])
                            tt("pool", W["k2"][:], k_, W["t1"][:], ALU.mult, [R6["k"], N_("t1")], [N_("k2")])
                            tt("pool", W["bet"][:], W["kk"][:], a_, ALU.mult, [N_("kk"), R6["a"]], [N_("bet")])
                            P.op("dve", lambda E, o=W["L"][:], d1=ld_: E.tensor_tensor_scan(o, onesC[:], d1, 0.0, ALU.mult, ALU.add),
                                 reads=["onesC", R6["ld"]], writes=[N_("L")])
                            act(W["eL"][:], W["L"][:], AF.Exp, [N_("L")], [N_("eL")])
                            act(W["enL"][:], W["L"][:], AF.Exp, [N_("L")], [N_("enL")], scale=-1.0)
                            tt("pool", W["tmp"][:], W["L"][:], ld_, ALU.subtract, [N_("L"), R6["ld"]], [N_("tmp")])
                            act(W["ePrev"][:], W["tmp"][:], AF.Exp, [N_("tmp")], [N_("ePrev")])
                            act(W["eRev"][:], W["L"][:], AF.Exp, [N_("L")], [N_("eRev")], scale=-1.0, bias=W["L"][:, C - 1:C])
                            act(W["gC"][:], W["L"][:, C - 1:C], AF.Exp, [N_("L")], [N_("gC")])
                            stt(W["AR"][:, 0:C], W["kk"][:], -1.0, W["ePrev"][:], ALU.mult, ALU.mult,
                                [N_("kk"), N_("ePrev")], [N_("AR")])
                            tt("pool", W["AR"][:, C:2 * C], r_, W["eL"][:], ALU.mult, [R6["r"], N_("eL")], [N_("AR")])
                            for h in range(2):
                                hs = slice(h * 64, (h + 1) * 64)
                                tt("dve", W["Bbp"][hs, h, :], W["bet"][hs, :], W["enL"][hs, :], ALU.mult,
                                   [N_("bet"), N_("enL")], [N_("Bbp")])
                                tt("pool", W["Kbp"][hs, h, :], W["k2"][hs, :], W["enL"][hs, :], ALU.mult,
                                   [N_("k2"), N_("enL")], [N_("Kbp")])
                            tt("dve", W["bts"][:], W["bet"][:], W["eRev"][:], ALU.mult, [N_("bet"), N_("eRev")], [N_("bts")])
                            tt("pool", W["kts"][:], W["k2"][:], W["eRev"][:], ALU.mult, [N_("k2"), N_("eRev")], [N_("kts")])
                            if cfg.get("scan_stop", 9) <= 1:
                                return
                            yield
                            bk, ps = B.bank()
                            for ti, (src, rk_) in enumerate(((v_, R6["v"]), (W["bts"][:], N_("bts")), (W["kts"][:], N_("kts")))):
                                P.op("pe", lambda E, o=ps[:, ti * 128:(ti + 1) * 128], i_=src: E.transpose(o, i_, ident[:]),
                                     reads=[rk_, "c_ident"], writes=[bk])
                            for ti, n in enumerate(("Vtp", "Btp", "Ktp")):
                                src3 = ps[:, ti * 128:(ti + 1) * 128].rearrange("p (h b) -> p h b", b=64)
                                if ti == 1:
                                    cp("dve", diag2(W[n]), src3, [bk], [N_(n)])
                                else:
                                    cp("act", diag2(W[n]), src3, [bk], [N_(n)])
                            if cfg.get("scan_stop", 9) <= 2:
                                return
                            yield
                            bkX, psX = B.bank(); bkY, psY = B.bank(); bkZ, psZ = B.bank()
                            for h in range(2):
                                mm(psX[:, h * 256:(h + 1) * 256], W["Bbp"][:, h, :], W["AR"][:], [N_("Bbp"), N_("AR")], [bkX])
                                mm(psY[:, h * 256:(h + 1) * 256], W["Kbp"][:, h, :], W["AR"][:], [N_("Kbp"), N_("AR")], [bkY])
                                mm(psZ[:, h * 128:(h + 1) * 128], W["AR"][:, 0:C], W["Bbp"][:, h, :], [N_("Bbp"), N_("AR")], [bkZ])
                            tt("dve", W["QM"][:].rearrange("p h c -> p (h c)"), psX[:, 0:512], m2b2[:].rearrange("p h c -> p (h c)"),
                               ALU.mult, [bkX, "m2b2"], [N_("QM")])
                            tt("dve", W["KM"][:].rearrange("p h c -> p (h c)"), psY[:, 0:512], m2b2[:].rearrange("p h c -> p (h c)"),
                               ALU.mult, [bkY, "m2b2"], [N_("KM")])
                            tt("dve", W["P0"][:].rearrange("p h c -> p (h c)"), psZ[:, 0:256], msl2[:].rearrange("p h c -> p (h c)"),
                               ALU.mult, [bkZ, "msl2"], [N_("P0")])
                            yield
                            if cfg.get("scan_stop", 9) <= 3:
                                return
                            fl = lambda t: t[:].rearrange("p h c -> p (h c)")
                            QAh = lambda h: W["QM"][:, h, 0:C]
                            PA = W["P0"]
                            Qc, Pc = W["Q0"], W["P1"]
                            tt("pool", Qc[:], W["QM"][:, :, 0:C], bd8_2[:], ALU.mult, [N_("QM"), "bd8_2"], [Qc.name])
                            tt("pool", fl(Pc), fl(PA), fl(bd8_2), ALU.mult, [PA.name, "bd8_2"], [Pc.name])
                            Dq, Dp = W["Dq0"], W["Dp0"]
                            tt("pool", fl(Dq), fl(Qc), fl(idb2), ALU.add, [Qc.name, "idb2"], [Dq.name])
                            tt("pool", fl(Dp), fl(Pc), fl(idb2), ALU.add, [Pc.name, "idb2"], [Dp.name])
                            dqi = 0
                            sq_bufs = [(W["Q1"], W["PA"]), (W["Q0"], W["P1"])]
                            for s_ in range(2):
                                Qn, Pn = sq_bufs[s_]
                                bkA, psA = B.bank(); bkB, psB = B.bank()
                                for h in range(2):
                                    mm(psA[:, h * 128:(h + 1) * 128], Pc[:, h, :], Qc[:, h, :], [Qc.name, Pc.name], [bkA])
                                    mm(psB[:, h * 128:(h + 1) * 128], Qc[:, h, :], Pc[:, h, :], [Qc.name, Pc.name], [bkB])
                                cp("act", fl(Qn), psA[:, 0:256], [bkA], [Qn.name])
                                cp("dve", fl(Pn), psB[:, 0:256], [bkB], [Pn.name])
                                Dqn, Dpn = W["Dq%d" % (1 - dqi)], W["Dp%d" % (1 - dqi)]
                                bkC, psC = B.bank(); bkD_, psD_ = B.bank()
                                for h in range(2):
                                    mm(psC[:, h * 128:(h + 1) * 128], Pn[:, h, :], Dq[:, h, :], [Pn.name, Dq.name], [bkC], start=True, stop=False)
                                    mm(psC[:, h * 128:(h + 1) * 128], idb2[:, h, :], Dq[:, h, :], ["idb2", Dq.name], [bkC], start=False, stop=True)
                                    mm(psD_[:, h * 128:(h + 1) * 128], Qn[:, h, :], Dp[:, h, :], [Qn.name, Dp.name], [bkD_], start=True, stop=False)
                                    mm(psD_[:, h * 128:(h + 1) * 128], idb2[:, h, :], Dp[:, h, :], ["idb2", Dp.name], [bkD_], start=False, stop=True)
                                cp("act", fl(Dqn), psC[:, 0:256], [bkC], [Dqn.name])
                                cp("dve", fl(Dpn), psD_[:, 0:256], [bkD_], [Dpn.name])
                                Qc, Pc, Dq, Dp, dqi = Qn, Pn, Dqn, Dpn, 1 - dqi
                                yield
                            for l_ in range(4):
                                last = (l_ == 3)
                                Dqn, Dpn = W["Dq%d" % (1 - dqi)], W["Dp%d" % (1 - dqi)]
                                bkA, psA = B.bank()
                                for h in range(2):
                                    mm(psA[:, h * 128:(h + 1) * 128], PA[:, h, :], Dq[:, h, :], [PA.name, Dq.name], [bkA])
                                tt("dve", fl(W["Y1"]), psA[:, 0:256], lvU2[:, l_, :, :].rearrange("p h c -> p (h c)"), ALU.mult,
                                   [bkA, "lvU2"], [N_("Y1")])
                                if not last:
                                    bkB, psB = B.bank()
                                    for h in range(2):
                                        mm(psB[:, h * 128:(h + 1) * 128], QAh(h), Dp[:, h, :], [N_("QM"), Dp.name], [bkB])
                                    tt("dve", fl(W["Z1"]), psB[:, 0:256], lvL2[:, l_, :, :].rearrange("p h c -> p (h c)"), ALU.mult,
                                       [bkB, "lvL2"], [N_("Z1")])
                                yield
                                bkC, psC = B.bank()
                                for h in range(2):
                                    mm(psC[:, h * 128:(h + 1) * 128], Dp[:, h, :], W["Y1"][:, h, :], [Dp.name, N_("Y1")], [bkC], start=True, stop=False)
                                    mm(psC[:, h * 128:(h + 1) * 128], idb2[:, h, :], Dq[:, h, :], ["idb2", Dq.name], [bkC], start=False, stop=True)
                                cp("act", fl(Dqn), psC[:, 0:256], [bkC], [Dqn.name])
                                if not last:
                                    bkD_, psD_ = B.bank()
                                    for h in range(2):
                                        mm(psD_[:, h * 128:(h + 1) * 128], Dq[:, h, :], W["Z1"][:, h, :], [Dq.name, N_("Z1")], [bkD_], start=True, stop=False)
                                        mm(psD_[:, h * 128:(h + 1) * 128], idb2[:, h, :], Dp[:, h, :], ["idb2", Dp.name], [bkD_], start=False, stop=True)
                                    cp("act", fl(Dpn), psD_[:, 0:256], [bkD_], [Dpn.name])
                                Dq, Dp, dqi = Dqn, Dpn, 1 - dqi
                                yield
                            TT = Dq
                            if cfg.get("scan_stop", 9) <= 4:
                                return
                            yield
                            bkD, psD = B.bank()
                            for h in range(2):
                                hs = slice(h * 64, (h + 1) * 64)
                                mm(psD[:, hs], W["AR"][:, 0:C], Sblk[:, hp, hs], [N_("AR"), pk], [bkD], start=True, stop=False)
                                mm(psD[:, hs], W["KM"][:, h, 0:C], W["Vtp"][:, h, hs], [N_("KM"), N_("Vtp")], [bkD], start=False, stop=True)
                            cp("act", W["RH"][:].rearrange("p h c -> p (h c)"), psD[:, 0:128], [bkD], [N_("RH")])
                            yield
                            bkE, psE = B.bank()
                            for h in range(2):
                                mm(psE[:, h * 64:(h + 1) * 64], TT[:, h, :], W["RH"][:, h, :], [TT.name, N_("RH")], [bkE])
                            cp("dve", diag2(W["Up"]), psE[:, 0:128].rearrange("p (h b) -> p h b", b=64), [bkE], [N_("Up")])
                            yield
                            bkF, psF = B.bank()
                            mm(psF[:, 0:C], Sblk[:, hp, :], W["AR"][:, C:2 * C], [pk, N_("AR")], [bkF], start=True, stop=False)
                            for h in range(2):
                                mm(psF[:, 0:C], W["Up"][:, h, :], W["QM"][:, h, C:2 * C], [N_("Up"), N_("QM")], [bkF], start=False, stop=False)
                                mm(psF[:, 0:C], W["Vtp"][:, h, :], W["KM"][:, h, C:2 * C], [N_("Vtp"), N_("KM")], [bkF],
                                   start=False, stop=(h == 1))
                            bkG, psG = B.bank()
                            for h in range(2):
                                hs = slice(h * 64, (h + 1) * 64)
                                mm(psG[:, 0:64], W["Btp"][:, h, :], W["Up"][:, h, hs], [N_("Btp"), N_("Up")], [bkG],
                                   start=(h == 0), stop=False)
                                mm(psG[:, 0:64], W["Ktp"][:, h, :], W["Vtp"][:, h, hs], [N_("Ktp"), N_("Vtp")], [bkG],
                                   start=False, stop=(h == 1))
                            stt(Sm[:, hp, :], Sm[:, hp, :], W["gC"][:, 0:1], psG[:, 0:64], ALU.mult, ALU.add,
                                [("Sm", hp), N_("gC"), bkG], [("Sm", hp)])
                            for h in range(2):
                                hs = slice(h * 64, (h + 1) * 64)
                                cp("pool", Sblk[hs, hp, hs], Sm[hs, hp, :], [("Sm", hp)], [pk])
                            if cfg.get("scan_stop", 9) <= 5:
                                return
                            yield
                            cp("act", W["YY"][:, 0:C], psF[:, 0:C], [bkF], [N_("YY")])
                            act(W["YY"][:, C:2 * C], W["YY"][:, 0:C], AF.Square, [N_("YY")], [N_("YY")])
                            bkH, psH = B.bank()
                            mm(psH[:, 0:2 * C], blkm[:], W["YY"][:], ["c_blkm", N_("YY")], [bkH])
                            act(W["mu2"][:], psH[:, 0:C], AF.Square, [bkH], [N_("mu2")])
                            tt("dve", W["var"][:], psH[:, C:2 * C], W["mu2"][:], ALU.subtract, [bkH, N_("mu2")], [N_("var")])
                            act(W["rs"][:], W["var"][:], AF.Ln, [N_("var")], [N_("rs")], bias=GN_EPS)
                            act(W["rs"][:], W["rs"][:], AF.Exp, [N_("rs")], [N_("rs")], scale=-0.5)
                            tt("dve", W["yn"][:], W["YY"][:, 0:C], psH[:, 0:C], ALU.subtract, [N_("YY"), bkH], [N_("yn")])
                            tt("pool", W["yn"][:], W["yn"][:], W["rs"][:], ALU.mult, [N_("yn"), N_("rs")], [N_("yn")])
                            act(W["yn"][:], W["yn"][:], AF.Identity, [N_("yn"), "vec", "vec2"], [N_("yn")],
                                bias=vec2[:, 0, hp:hp + 1], scale=vec[:, 15, hp:hp + 1])
                            tt("pool", W["rk"][:], r_, W["k2"][:], ALU.mult, [R6["r"], N_("k2")], [N_("rk")])
                            ts("dve", W["rk"][:], W["rk"][:], vec[:, 14, hp:hp + 1], ALU.mult, [N_("rk"), "vec"], [N_("rk")])
                            yield
                            bkI, psI = B.bank()
                            mm(psI[:, 0:C], blk[:], W["rk"][:], ["c_blk", N_("rk")], [bkI])
                            tt("dve", W["bon"][:], psI[:, 0:C], v_, ALU.mult, [bkI, R6["v"]], [N_("bon")])
                            tt("pool", W["yn"][:], W["yn"][:], W["bon"][:], ALU.add, [N_("yn"), N_("bon")], [N_("yn")])
                            tt("dve", W["yg"][:], W["yn"][:], g_, ALU.mult, [N_("yn"), R6["g"]], [N_("yg")])
                            P.dma("sp", s_yg[hp * 128:(hp + 1) * 128, j * C:(j + 1) * C], W["yg"][:], reads=[N_("yg")])
                        NI = 4
                        for b0_ in range(0, GP, NI):
                            gens = [unit(qi) for qi in range(b0_, min(GP, b0_ + NI))]
                            while gens:
                                for g__ in list(gens):
                                    try:
                                        next(g__)
                                    except StopIteration:
                                        gens.remove(g__)
                Sf = sb("Sf", [128, 128]); So = sb("So", [128, 128])
                P.op("pool", lambda E: E.memset(Sf[:], 0.0), writes=["Sf"])
                for hp in range(NP):
                    for h in range(2):
                        hs = slice(h * 64, (h + 1) * 64)
                        cp("pool", Sf[hs, hs], Sm[hs, hp, :], [("Sm", hp)], ["Sf"])
                    bk, ps = B.bank()
                    P.op("pe", lambda E, o=ps[:, 0:128]: E.transpose(o, Sf[:], ident[:]), reads=["Sf", "c_ident"], writes=[bk])
                    cp("act", So[:], ps[:, 0:128], [bk], ["So"])
                    for h in range(2):
                        hs = slice(h * 64, (h + 1) * 64)
                        P.dma("sp", o_wkv[2 * hp + h, :, :], So[hs, hs], reads=["So"])
                P.barrier()
            B.es = es
        scan_stage()
        NBLK = (NW - 2) // 128
        assert NW == 2 + NBLK * 128
        WG = _chunks(NW, 512)
        s_x1 = B.scratch("s_x1", [D, NW]); s_x2 = B.scratch("s_x2", [D, NW]); s_x3 = B.scratch("s_x3", [D, NW])
        s_act = B.scratch("s_act", [DFF, NW], BF16)
        s_oat = B.scratch("s_oat", [D, NW], BF16)
        convout = sb("convout", [128, 2, FC, 2])
        mwin = sb("mwin", [128, HALO]); P.dma("sp", mwin[:], maskrow[:, W0:W0 + HALO], writes=["mwin"])
        esT = sb("esT", [128, H]); P.dma("sp", esT[:], sinkT[:, :], writes=["esT"])
        act(esT[:], esT[:], AF.Exp, ["esT"], ["esT"])
        kwin_sb = sb("kwin_sb", [128, KVW]); vwin_sb = sb("vwin_sb", [128, KVW])

        def xs_of(tile, key, groups):
            return [(lambda ki, ksz, c0=c0, n=n: tile[0:ksz, ki, c0:c0 + n], n, [key]) for (c0, n) in groups]

        def norm_stats(src, rstd, xa):
            bks = [B.bank() for _ in WG]
            for kc in range(KC):
                x_ = xa[kc % 2]
                P.dma("sp", x_[:], src[kc * 128:(kc + 1) * 128, :], writes=[x_.name])
                act(x_[:], x_[:], AF.Square, [x_.name], [x_.name])
                for gi, (c0, n) in enumerate(WG):
                    mm(bks[gi][1][:, 0:n], ones[:], x_[:, c0:c0 + n], [x_.name, "c_ones"], [bks[gi][0]],
                       start=(kc == 0), stop=(kc == KC - 1))
            for gi, (c0, n) in enumerate(WG):
                act(rstd[:, c0:c0 + n], bks[gi][1][:, 0:n], AF.Ln, [bks[gi][0]], [rstd.name], scale=1.0 / D, bias=1e-6)
            act(rstd[:], rstd[:], AF.Exp, [rstd.name], [rstd.name], scale=-0.5)

        def modulate(src, rstd, hw, gsel, shsel, xa):
            for kc in range(KC):
                x_ = xa[kc % 2]
                P.dma("sp", x_[:], src[kc * 128:(kc + 1) * 128, :], writes=[x_.name])
                tt("dve", x_[:], x_[:], rstd[:], ALU.mult, [x_.name, rstd.name], [x_.name])
                act(hw[:, kc, :], x_[:], AF.Identity, [x_.name, "Gm", "modv", "kvmv"], [hw.name],
                    bias=shsel(kc), scale=gsel(kc))

        def resid_epi(xin, gate_sel, target, og, xg, cg):
            cnt_ = [0]

            def epi(ct, n0, m, g, ps, bk):
                i = cnt_[0] % 2; cnt_[0] += 1
                c0, n = cg[g]
                P.dma("sp", xg[i][:, 0:n], xin[ct * 128:(ct + 1) * 128, c0:c0 + n], writes=[xg[i].name])
                stt(og[i][:, 0:n], ps, gate_sel(ct), xg[i][:, 0:n], ALU.mult, ALU.add,
                    [bk, xg[i].name, "modv"], [og[i].name])
                target(ct, c0, n, og[i], og[i].name)
            return epi

        def to_scratch(dst):
            def tgt(ct, c0, n, o_, key):
                P.dma("sp", dst[ct * 128:(ct + 1) * 128, c0:c0 + n], o_[:, 0:n], reads=[key])
            return tgt

        def ffn(l, xin, target, tg):
            with contextlib.ExitStack() as es5:
                B.es = es5
                rstd = sb("frstd" + tg, [128, NW])
                hw = sb("fhw" + tg, [128, KC, NW], BF16)
                xa = [sb("fxa%s%d" % (tg, i), [128, NW]) for i in range(2)]
                norm_stats(xin, rstd, xa)
                modulate(xin, rstd, hw, lambda kc: Gm[:, 1 + 2 * l, kc, 0:1], lambda kc: modv[:, l, 3 * KC + kc, 0:1], xa)
                wb = [sb("fwb%s%d" % (tg, i), [128, KC, 128], BF16) for i in range(2)]
                gfull = [sb("gfull%s%d" % (tg, i), [128, NW]) for i in range(2)]
                cvt = [sb("cvt%s%d" % (tg, i), [128, NW]) for i in range(2)]
                sg = [sb("sg%s%d" % (tg, i), [128, NW], BF16) for i in range(4)]
                ao = [sb("ao%s%d" % (tg, i), [128, 512], BF16) for i in range(3)]
                for t_ in sg:
                    P.op("pool", lambda E, t_=t_: E.memset(t_[:], 0.0), writes=[t_.name])
                aoc = [0]

                def epi(ct, n0, m, g, ps, bk):
                    c0, n = WG[g]
                    if n0 < DFF:
                        f = ct
                        gf = gfull[f % 2]
                        cp("act", gf[:, c0:c0 + n], ps, [bk], [gf.name])
                        if g == len(WG) - 1:
                            tt("dve", gf[:, 0:HALO], gf[:, 0:HALO], mwin[:], ALU.mult, [gf.name, "mwin"], [gf.name])
                            cp("pool", convout[:, l, f, :], gf[:, NW - 2:NW], [gf.name], ["convout"])
                            c_ = cvt[f % 2]
                            act(c_[:, 2:NW], gf[:, 2:NW], AF.Identity, [gf.name, "cvp"], [c_.name],
                                bias=cvp[:, l, 3, f:f + 1], scale=cvp[:, l, 2, f:f + 1])
                            stt(c_[:, 2:NW], gf[:, 1:NW - 1], cvp[:, l, 1, f:f + 1], c_[:, 2:NW], ALU.mult, ALU.add,
                                [gf.name, c_.name, "cvp"], [c_.name])
                            stt(c_[:, 2:NW], gf[:, 0:NW - 2], cvp[:, l, 0, f:f + 1], c_[:, 2:NW], ALU.mult, ALU.add,
                                [gf.name, c_.name, "cvp"], [c_.name])
                            act(sg[f % 4][:, 2:NW], c_[:, 2:NW], AF.Silu, [c_.name], [sg[f % 4].name])
                    else:
                        f = ct - FC
                        a_ = ao[aoc[0] % 3]; aoc[0] += 1
                        tt("dve", a_[:, 0:n], ps, sg[f % 4][:, c0:c0 + n], ALU.mult, [bk, sg[f % 4].name], [a_.name])
                        P.dma("sp", s_act[f * 128:(f + 1) * 128, c0:c0 + n], a_[:, 0:n], reads=[a_.name])
                cgs = []
                for (n0, gsz) in _chunks(DFF, 128):
                    cgs.append((n0, gsz)); cgs.append((DFF + n0, gsz))
                B.linear("win" + tg, w_in[l], D, 2 * DFF, xs_of(hw, hw.name, WG), epi, 128, wb, colgroups=cgs)
                P.barrier()
            with contextlib.ExitStack() as es5:
                B.es = es5
                CW = -(-NW // 3)
                CW += CW % 2
                CG = _chunks(NW, CW)
                actt = sb("actt" + tg, [128, FC, CW], BF16)
                wbo = [sb("fwo%s%d" % (tg, i), [128, FC, 128], BF16) for i in range(2)]
                og = [sb("fog%s%d" % (tg, i), [128, CW]) for i in range(2)]
                xg = [sb("fxg%s%d" % (tg, i), [128, CW]) for i in range(2)]
                for (c0, n) in CG:
                    P.dma("sp", actt[:, :, 0:n], s_act[:, c0:c0 + n].rearrange("(f p) t -> p f t", p=128), writes=[actt.name])
                    xs1 = [(lambda ki, ksz, n=n: actt[0:ksz, ki, 0:n], n, [actt.name])]
                    B.linear("wout" + tg, w_out[l], DFF, D, xs1,
                             resid_epi(xin, lambda ct: modv[:, l, 5 * KC + ct, 0:1], target, og, xg, [(c0, n)]), 128, wbo)
                P.barrier()
            B.es = es

        def backend():
            with contextlib.ExitStack() as es5:
                B.es = es5
                hw = sb("d1hw", [128, KC, NW], BF16)
                P.dma("sp", hw[:], s_yg[:, W0:T].rearrange("(kc p) t -> p kc t", p=128), writes=["d1hw"])
                wb = [sb("d1wb%d" % i, [128, KC, 256], BF16) for i in range(2)]
                og = [sb("d1og%d" % i, [128, 512]) for i in range(2)]
                xg = [sb("d1xg%d" % i, [128, 512]) for i in range(2)]
                B.linear("wo", w_o, D, D, xs_of(hw, "d1hw", WG),
                         resid_epi(xT[:, W0:T], lambda ct: modv[:, 0, 2 * KC + ct, 0:1], to_scratch(s_x1), og, xg, WG), 256, wb)
                P.barrier()
            B.es = es
            ffn(0, s_x1, to_scratch(s_x2), "0")
            with contextlib.ExitStack() as es5:
                B.es = es5
                rstd = sb("krstd", [128, NW])
                hw = sb("khw", [128, KC, NW], BF16)
                kdup = sb("kdup", [128, NKV, NW], BF16)
                Vt1 = sb("Vt1", [128, NBLK, NKV, 65], BF16)
                P.op("pool", lambda E: E.memset(Vt1[:], 1.0), writes=["Vt1"])
                xa = [sb("kxa%d" % i, [128, NW]) for i in range(2)]
                norm_stats(s_x2, rstd, xa)
                modulate(s_x2, rstd, hw, lambda kc: Gm[:, 4, kc, 0:1], lambda kc: kvmv[:, kc, 0:1], xa)
                wbk = [sb("kwb%d" % i, [128, KC, 128], BF16) for i in range(2)]
                kf = sb("kf", [128, NW]); ksq = sb("ksq", [128, NW]); krs = sb("krs", [128, NW])
                for g in range(NKV):
                    buf = wbk[g % 2]; bkey = ("wb", buf.name)
                    for half in range(2):
                        P.dma("pool", buf[:, :, half * 64:(half + 1) * 64],
                              w_kv[:, g * 64:(g + 1) * 64].rearrange("(kc p) n -> p kc n", p=128), writes=[bkey])
                    for gi, (c0, n) in enumerate(WG):
                        bk, ps = B.bank()
                        for kc in range(KC):
                            mm(ps[:, 0:n], buf[:, kc, :], hw[:, kc, c0:c0 + n], [bkey, hw.name], [bk], start=(kc == 0), stop=(kc == KC - 1))
                        cp("act", kf[:, c0:c0 + n], ps[:, 0:n], [bk], [kf.name])
                    act(ksq[:], kf[:], AF.Square, [kf.name], [ksq.name])
                    for gi, (c0, n) in enumerate(WG):
                        bk, ps = B.bank()
                        mm(ps[:, 0:n], blkm[:], ksq[:, c0:c0 + n], ["c_blkm", ksq.name], [bk])
                        act(krs[:, c0:c0 + n], ps[:, 0:n], AF.Ln, [bk], [krs.name], bias=1e-6)
                    act(krs[:], krs[:], AF.Exp, [krs.name], [krs.name], scale=-0.5)
                    tt("dve", kf[:], kf[:], krs[:], ALU.mult, [kf.name, krs.name], [kf.name])
                    ts("dve", kf[:], kf[:], hdt[:, 0:1], ALU.mult, [kf.name, "hdt"], [kf.name])
                    cp("pool", kdup[:, g, :], kf[:], [kf.name], ["kdup"])
                    bk, ps = B.bank()
                    P.op("pe", lambda E, o=ps[:, 0:128]: E.transpose(o, kf[:, NW - 128:NW], ident[:]), reads=[kf.name, "c_ident"], writes=[bk])
                    cp("act", kwin_sb[:, g * 64:(g + 1) * 64], ps[:, 0:64], [bk], ["kwin_sb"])
                vf = xa[0]

                def epiV(ct, n0, m, g, ps, bk):
                    c0, n = WG[g]
                    cp("act", vf[0:m, c0:c0 + n], ps, [bk], [vf.name])
                    if g == len(WG) - 1:
                        for j in range(NBLK):
                            bk2, ps2 = B.bank()
                            P.op("pe", lambda E, o=ps2[:, 0:m], j=j: E.transpose(o, vf[0:m, 2 + j * 128:2 + (j + 1) * 128], ident[0:m, 0:m]),
                                 reads=[vf.name, "c_ident"], writes=[bk2])
                            nh = m // 64
                            cp("dve", Vt1[:, j, ct * 2:ct * 2 + nh, 0:64], ps2[:, 0:m].rearrange("p (h d) -> p h d", d=64), [bk2], ["Vt1"])
                            if j == NBLK - 1:
                                cp("act", vwin_sb[:, ct * 128:ct * 128 + m], ps2[:, 0:m], [bk2], ["vwin_sb"])
                B.linear("wv", w_kv[:, KVW:2 * KVW], D, KVW, xs_of(hw, hw.name, WG), epiV, 128, wbk)
                P.dma("sp", o_kwin[:, :], kwin_sb[:], reads=["kwin_sb"])
                P.dma("sp", o_vwin[:, :], vwin_sb[:], reads=["vwin_sb"])
                modulate(s_x2, rstd, hw, lambda kc: Gm[:, 2, kc, 0:1], lambda kc: modv[:, 1, 0 * KC + kc, 0:1], xa)
                wbq = wbk
                qf = [kf, kf]
                qsq = ksq; qrs = krs
                qp = sb("qp", [128, 2, NW], BF16)
                P.op("pool", lambda E: E.memset(qp[:], 0.0), writes=["qp"])
                mpo = sb("mpo", [128, 4, 128], BF16)
                for i in range(4):
                    cp("dve", mpo[:, i, :], (mslb[:] if i % 2 == 0 else m2b[:, 128:256]), ["mslb", "m2b"], ["mpo"])
                Eb = [sb("Eb%d" % i, [128, 4, 128], BF16) for i in range(2)]
                otk = [sb("otk%d" % i, [128, 128]) for i in range(2)]
                den = [sb("den%d" % i, [128, 2]) for i in range(2)]
                ofm = [sb("ofm%d" % i, [128, 128], BF16) for i in range(2)]
                zt = sb("zt", [128, 130], BF16)
                P.op("pool", lambda E: E.memset(zt[:], 0.0), writes=["zt"])
                for kc in range(KC):
                    P.dma("sp", s_oat[kc * 128:(kc + 1) * 128, 0:130], zt[:], reads=["zt"])
                cnt_q = [0]

                def epiQ(ct, n0, m, g, ps, bk):
                    c0, n = WG[g]
                    q_ = qf[ct % 2]
                    cp("act", q_[:, c0:c0 + n], ps, [bk], [q_.name])
                    if g < len(WG) - 1:
                        return
                    act(qsq[:], q_[:], AF.Square, [q_.name], [qsq.name])
                    for gi, (c0_, n_) in enumerate(WG):
                        bk2, ps2 = B.bank()
                        mm(ps2[:, 0:n_], blkm[:], qsq[:, c0_:c0_ + n_], ["c_blkm", qsq.name], [bk2])
                        act(qrs[:, c0_:c0_ + n_], ps2[:, 0:n_], AF.Ln, [bk2], [qrs.name], bias=1e-6)
                    act(qrs[:], qrs[:], AF.Exp, [qrs.name], [qrs.name], scale=-0.5)
                    tt("dve", q_[:], q_[:], qrs[:], ALU.mult, [q_.name, qrs.name], [q_.name])
                    for h in range(2):
                        hs = slice(h * 64, (h + 1) * 64)
                        ts("dve", qp[hs, h, :], q_[hs, :], hdt[hs, 1:2], ALU.mult, [q_.name, "hdt"], ["qp"])
                    for qb in range(1, NBLK):
                        i2 = cnt_q[0] % 2; cnt_q[0] += 1
                        qc = slice(2 + qb * 128, 2 + (qb + 1) * 128)
                        bkS, psS = B.bank()
                        for h in range(2):
                            gkv = (2 * ct + h) // (H // NKV)
                            for kbi, kb in enumerate((qb - 1, qb)):
                                kcs = slice(2 + kb * 128, 2 + (kb + 1) * 128)
                                mm(psS[:, (h * 2 + kbi) * 128:(h * 2 + kbi + 1) * 128], kdup[:, gkv, kcs], qp[:, h, qc],
                                   ["kdup", "qp"], [bkS])
                        E_ = Eb[i2]
                        act(E_[:].rearrange("p a b -> p (a b)"), psS[:, 0:512], AF.Exp, [bkS], [E_.name], scale=0.125)
                        tt("dve", E_[:], E_[:], mpo[:], ALU.mult, [E_.name, "mpo"], [E_.name])
                        for kbi, kb in enumerate((qb - 1, qb)):
                            if kb <= 1:
                                ts("pool", E_[:, kbi:4:2, :], E_[:, kbi:4:2, :], flag[:, 0:1], ALU.mult, [E_.name, "flag"], [E_.name])
                        bkO, psO = B.bank()
                        for h in range(2):
                            gkv = (2 * ct + h) // (H // NKV)
                            for kbi, kb in enumerate((qb - 1, qb)):
                                mm(psO[:, h * 65:(h + 1) * 65], E_[:, h * 2 + kbi, :], Vt1[:, kb, gkv, :], [E_.name, "Vt1"], [bkO],
                                   start=(kbi == 0), stop=(kbi == 1))
                        d_ = den[i2]; o_ = otk[i2]
                        tt("dve", d_[:], psO[:, 64:130:65], esT[:, 2 * ct:2 * ct + 2], ALU.add, [bkO, "esT"], [d_.name])
                        P.op("dve", lambda E, d_=d_: E.reciprocal(d_[:], d_[:]), reads=[d_.name], writes=[d_.name])
                        for h in range(2):
                            ts("dve", o_[:, h * 64:(h + 1) * 64], psO[:, h * 65:h * 65 + 64], d_[:, h:h + 1], ALU.mult,
                               [bkO, d_.name], [o_.name])
                        bkT, psT = B.bank()
                        P.op("pe", lambda E, o=psT[:, 0:128], o_=o_: E.transpose(o, o_[:], ident[:]), reads=[o_.name, "c_ident"], writes=[bkT])
                        f_ = ofm[i2]
                        cp("act", f_[:], psT[:, 0:128], [bkT], [f_.name])
                        P.dma("sp", s_oat[ct * 128:(ct + 1) * 128, qc], f_[:], reads=[f_.name])
                B.linear("wq", w_q, D, D, xs_of(hw, hw.name, WG), epiQ, 128, wbq)
                P.barrier()
            B.es = es
            with contextlib.ExitStack() as es5:
                B.es = es5
                hw = sb("d4hw", [128, KC, NW], BF16)
                P.dma("sp", hw[:], s_oat.rearrange("(kc p) t -> p kc t", p=128), writes=["d4hw"])
                wb = [sb("d4wb%d" % i, [128, KC, 256], BF16) for i in range(2)]
                og = [sb("d4og%d" % i, [128, 512]) for i in range(2)]
                xg = [sb("d4xg%d" % i, [128, 512]) for i in range(2)]
                B.linear("wao", w_ao, D, D, xs_of(hw, "d4hw", WG),
                         resid_epi(s_x2, lambda ct: modv[:, 1, 2 * KC + ct, 0:1], to_scratch(s_x3), og, xg, WG), 256, wb)
                P.barrier()
            B.es = es

            def to_y(ct, c0, n, o_, key):
                lo_ = max(c0, HALO)
                if lo_ < c0 + n:
                    P.dma("sp", o_y[ct * 128:(ct + 1) * 128, lo_ - HALO:c0 + n - HALO], o_[:, lo_ - c0:n], reads=[key])
            ffn(1, s_x3, to_y, "1")
            P.dma("sp", o_conv.rearrange("l p f c -> p l f c"), convout[:], reads=["convout"])
        backend()
        def sample_path():
            with contextlib.ExitStack() as es6:
                B.es = es6
                KN = KC * NS
                f3 = lambda t: t[:].rearrange("p k n -> p (k n)")
                T_ = {}
                for n in ("r", "k", "v", "ld", "a", "g"):
                    T_[n] = sb("s_" + n, [128, KC, NS])
                    P.dma("sp", T_[n][:], scs[n].rearrange("(kc p) n -> p kc n", p=128), writes=[T_[n].name])
                W_ = {n: sb("sw_" + n, [128, KC, NS]) for n in ("kkr", "sq", "rn", "kk", "t1", "k2", "bet", "dd", "al", "ys", "yn", "rk", "mu2", "var", "rs", "bon", "ysq")}
                nm = lambda n: (T_[n].name if n in T_ else W_[n].name)
                for j in range(NS):
                    tt("dve", W_["kkr"][:, :, j], T_["k"][:, :, j], vec[:, 12, :], ALU.mult, [nm("k"), "vec"], [nm("kkr")])
                act(f3(W_["sq"]), f3(W_["kkr"]), AF.Square, [nm("kkr")], [nm("sq")])
                bk, ps = B.bank()
                mm(ps[:, 0:KN], blk[:], f3(W_["sq"]), ["c_blk", nm("sq")], [bk])
                ts("dve", f3(W_["rn"]), ps[:, 0:KN], 1e-24, ALU.max, [bk], [nm("rn")])
                act(f3(W_["rn"]), f3(W_["rn"]), AF.Ln, [nm("rn")], [nm("rn")])
                act(f3(W_["rn"]), f3(W_["rn"]), AF.Exp, [nm("rn")], [nm("rn")], scale=-0.5)
                tt("dve", f3(W_["kk"]), f3(W_["kkr"]), f3(W_["rn"]), ALU.mult, [nm("kkr"), nm("rn")], [nm("kk")])
                for j in range(NS):
                    tt("dve", W_["t1"][:, :, j], T_["a"][:, :, j], vec[:, 13, :], ALU.mult, [nm("a"), "vec"], [nm("t1")])
                    tt("dve", W_["t1"][:, :, j], W_["t1"][:, :, j], omka[:], ALU.add, [nm("t1"), "omka"], [nm("t1")])
                tt("dve", f3(W_["k2"]), f3(T_["k"]), f3(W_["t1"]), ALU.mult, [nm("k"), nm("t1")], [nm("k2")])
                tt("dve", f3(W_["bet"]), f3(W_["kk"]), f3(T_["a"]), ALU.mult, [nm("kk"), nm("a")], [nm("bet")])
                act(f3(W_["dd"]), f3(T_["ld"]), AF.Exp, [nm("ld")], [nm("dd")])
                ts("dve", f3(W_["al"]), f3(W_["kk"]), -1.0, ALU.mult, [nm("kk")], [nm("al")])
                Spad = [sb("Spad%d" % i, [128, 128]) for i in range(2)]
                St = [sb("St%d" % i, [128, 128]) for i in range(2)]
                So_ = [sb("Sos%d" % i, [128, 128]) for i in range(2)]
                X1 = [sb("X1_%d" % i, [128, 2]) for i in range(2)]
                X2 = [sb("X2_%d" % i, [128, 2]) for i in range(2)]
                R12 = [sb("R12_%d" % i, [2, 256]) for i in range(2)]
                otm = [sb("otm%d" % i, [128, 128]) for i in range(2)]
                for t_ in Spad:
                    P.op("pool", lambda E, t_=t_: E.memset(t_[:], 0.0), writes=[t_.name])
                it = 0
                for j in range(NS):
                    for hp in range(NP):
                        i2 = it % 2; it += 1
                        sp_, st_, so_, x1, x2, r12, ot = Spad[i2], St[i2], So_[i2], X1[i2], X2[i2], R12[i2], otm[i2]
                        for h in range(2):
                            hs = slice(h * 64, (h + 1) * 64)
                            P.dma("sp", sp_[hs, hs], st_wkv[j, 2 * hp + h, :, :], writes=[sp_.name])
                        bk, ps = B.bank()
                        P.op("pe", lambda E, o=ps[:, 0:128], i_=sp_: E.transpose(o, i_[:], ident[:]), reads=[sp_.name, "c_ident"], writes=[bk])
                        cp("act", st_[:], ps[:, 0:128], [bk], [st_.name])
                        bk, ps = B.bank()
                        mm(ps[:, 0:1], st_[:], W_["al"][:, hp, j:j + 1], [st_.name, nm("al")], [bk])
                        cp("act", x2[:, 0:1], ps[:, 0:1], [bk], [x2.name])
                        cp("pool", x2[:, 1:2], T_["v"][:, hp, j:j + 1], [nm("v")], [x2.name])
                        cp("pool", x1[:, 0:1], W_["bet"][:, hp, j:j + 1], [nm("bet")], [x1.name])
                        cp("pool", x1[:, 1:2], W_["k2"][:, hp, j:j + 1], [nm("k2")], [x1.name])
                        bk, ps = B.bank()
                        P.op("pe", lambda E, o=ps[0:2, 0:128], i_=x1: E.transpose(o, i_[:], ident[:]), reads=[x1.name, "c_ident"], writes=[bk])
                        P.op("pe", lambda E, o=ps[0:2, 128:256], i_=x2: E.transpose(o, i_[:], ident[:]), reads=[x2.name, "c_ident"], writes=[bk])
                        cp("act", r12[:], ps[0:2, 0:256], [bk], [r12.name])
                        bk, ps = B.bank()
                        mm(ps[:, 0:128], r12[:, 0:128], r12[:, 128:256], [r12.name], [bk])
                        tt("dve", ot[:], ps[:, 0:128], blk[:], ALU.mult, [bk, "c_blk"], [ot.name])
                        stt(st_[:], st_[:], W_["dd"][:, hp, j:j + 1], ot[:], ALU.mult, ALU.add, [st_.name, nm("dd"), ot.name], [st_.name])
                        bk, ps = B.bank()
                        mm(ps[:, 0:1], st_[:], T_["r"][:, hp, j:j + 1], [st_.name, nm("r")], [bk])
                        cp("act", W_["ys"][:, hp, j:j + 1], ps[:, 0:1], [bk], [nm("ys")])
                        bk, ps = B.bank()
                        P.op("pe", lambda E, o=ps[:, 0:128], i_=st_: E.transpose(o, i_[:], ident[:]), reads=[st_.name, "c_ident"], writes=[bk])
                        cp("dve", so_[:], ps[:, 0:128], [bk], [so_.name])
                        for h in range(2):
                            hs = slice(h * 64, (h + 1) * 64)
                            P.dma("sp", o_wkvs[j, 2 * hp + h, :, :], so_[hs, hs], reads=[so_.name])
                YY = sb("sYY", [128, 2, KN])
                cp("dve", YY[:, 0, :], f3(W_["ys"]), [nm("ys")], ["sYY"])
                act(YY[:, 1, :], f3(W_["ys"]), AF.Square, [nm("ys")], ["sYY"])
                bk, ps = B.bank()
                mm(ps[:, 0:2 * KN], blkm[:], YY[:].rearrange("p a n -> p (a n)"), ["c_blkm", "sYY"], [bk])
                act(f3(W_["mu2"]), ps[:, 0:KN], AF.Square, [bk], [nm("mu2")])
                tt("dve", f3(W_["var"]), ps[:, KN:2 * KN], f3(W_["mu2"]), ALU.subtract, [bk, nm("mu2")], [nm("var")])
                act(f3(W_["rs"]), f3(W_["var"]), AF.Ln, [nm("var")], [nm("rs")], bias=GN_EPS)
                act(f3(W_["rs"]), f3(W_["rs"]), AF.Exp, [nm("rs")], [nm("rs")], scale=-0.5)
                tt("dve", f3(W_["yn"]), f3(W_["ys"]), ps[:, 0:KN], ALU.subtract, [nm("ys"), bk], [nm("yn")])
                tt("dve", f3(W_["yn"]), f3(W_["yn"]), f3(W_["rs"]), ALU.mult, [nm("yn"), nm("rs")], [nm("yn")])
                tt("dve", f3(W_["rk"]), f3(T_["r"]), f3(W_["k2"]), ALU.mult, [nm("r"), nm("k2")], [nm("rk")])
                for j in range(NS):
                    tt("dve", W_["yn"][:, :, j], W_["yn"][:, :, j], vec[:, 15, :], ALU.mult, [nm("yn"), "vec"], [nm("yn")])
                    tt("dve", W_["yn"][:, :, j], W_["yn"][:, :, j], vec2[:, 0, :], ALU.add, [nm("yn"), "vec2"], [nm("yn")])
                    tt("dve", W_["rk"][:, :, j], W_["rk"][:, :, j], vec[:, 14, :], ALU.mult, [nm("rk"), "vec"], [nm("rk")])
                bk, ps = B.bank()
                mm(ps[:, 0:KN], blk[:], f3(W_["rk"]), ["c_blk", nm("rk")], [bk])
                tt("dve", f3(W_["bon"]), ps[:, 0:KN], f3(T_["v"]), ALU.mult, [bk, nm("v")], [nm("bon")])
                tt("dve", f3(W_["yn"]), f3(W_["yn"]), f3(W_["bon"]), ALU.add, [nm("yn"), nm("bon")], [nm("yn")])
                ygs = sb("ygs", [128, KC, NS], BF16)
                tt("dve", f3(ygs), f3(W_["yn"]), f3(T_["g"]), ALU.mult, [nm("yn"), nm("g")], ["ygs"])

                xs0 = sb("xs0", [128, KC, NS]); P.dma("sp", xs0[:], xsT.rearrange("(kc p) n -> p kc n", p=128), writes=[xs0.name])
                x1s = sb("x1s", [128, KC, NS]); x2s = sb("x2s", [128, KC, NS]); x3s = sb("x3s", [128, KC, NS]); ysf = sb("ysf", [128, KC, NS])
                hS = sb("hS", [128, KC, NS], BF16); h32 = sb("h32", [128, KC, NS]); sqS = sb("sqS", [128, KC, NS])
                rsS = sb("rsS", [128, NS])
                wbs = [sb("swb%d" % i, [128, KC, 512], BF16) for i in range(2)]
                wbo = [sb("swo%d" % i, [128, FC, 128], BF16) for i in range(2)]
                actS = sb("actS", [128, FC, NS], BF16)
                cbuf = sb("cbuf", [128, 2, FC, NS, 2])
                P.dma("sp", cbuf[:].rearrange("p l f n c -> p l (f n c)"), st_conv.rearrange("l p f n c -> p l (f n c)"), writes=["cbuf"])
                cvo = sb("cvo", [128, 2, FC, NS, 2])
                tmpS = [sb("tmpS%d" % i, [128, NS]) for i in range(4)]
                tcnt = [0]

                def xsS(tile, key):
                    return [(lambda ki, ksz: tile[0:ksz, ki, :], NS, [key])]

                def resS(xin, xout, gsel):
                    def epi(ct, n0, m, g, ps, bk):
                        t_ = tmpS[tcnt[0] % 4]; tcnt[0] += 1
                        tt("dve", t_[:], ps, gsel(ct), ALU.mult, [bk, "modv"], [t_.name])
                        tt("dve", xout[:, ct, :], t_[:], xin[:, ct, :], ALU.add, [t_.name, xin.name], [xout.name])
                    return epi

                def normS(x, G3, SH3, keys):
                    act(f3(sqS), f3(x), AF.Square, [x.name], ["sqS"])
                    bk, ps = B.bank()
                    for kc in range(KC):
                        mm(ps[:, 0:NS], ones[:], sqS[:, kc, :], ["c_ones", "sqS"], [bk], start=(kc == 0), stop=(kc == KC - 1))
                    act(rsS[:], ps[:, 0:NS], AF.Ln, [bk], ["rsS"], scale=1.0 / D, bias=1e-6)
                    act(rsS[:], rsS[:], AF.Exp, ["rsS"], ["rsS"], scale=-0.5)
                    for kc in range(KC):
                        tt("dve", h32[:, kc, :], x[:, kc, :], rsS[:], ALU.mult, [x.name, "rsS"], ["h32"])
                    tt("dve", h32[:], h32[:], G3, ALU.mult, ["h32"] + keys, ["h32"])
                    tt("dve", hS[:], h32[:], SH3, ALU.add, ["h32"] + keys, ["hS"])

                def ffnS(l, xin, xout):
                    normS(xin, Gm[:, 1 + 2 * l, :, 1:1 + NS], modv[:, l, 3 * KC:4 * KC, 1:1 + NS], ["Gm", "modv"])
                    sgS = [sb("sgS%d_%d" % (l, i), [128, NS]) for i in range(8)]

                    def epi(ct, n0, m, g, ps, bk):
                        if n0 < DFF:
                            f = ct
                            t_ = tmpS[tcnt[0] % 4]; tcnt[0] += 1
                            cp("act", cvo[:, l, f, :, 1], ps, [bk], ["cvo"])
                            cp("pool", cvo[:, l, f, :, 0], cbuf[:, l, f, :, 1], ["cbuf"], ["cvo"])
                            act(t_[:], ps, AF.Identity, [bk, "cvp"], [t_.name], bias=cvp[:, l, 3, f:f + 1], scale=cvp[:, l, 2, f:f + 1])
                            stt(t_[:], cbuf[:, l, f, :, 1], cvp[:, l, 1, f:f + 1], t_[:], ALU.mult, ALU.add, ["cbuf", "cvp", t_.name], [t_.name])
                            stt(t_[:], cbuf[:, l, f, :, 0], cvp[:, l, 0, f:f + 1], t_[:], ALU.mult, ALU.add, ["cbuf", "cvp", t_.name], [t_.name])
                            act(sgS[f % 8][:], t_[:], AF.Silu, [t_.name], [sgS[f % 8].name])
                        else:
                            f = ct - FC
                            tt("dve", actS[:, f, :], ps, sgS[f % 8][:], ALU.mult, [bk, sgS[f % 8].name], ["actS"])
                    cgs = []
                    for (n0, gsz) in _chunks(DFF, 512):
                        cgs.append((n0, gsz)); cgs.append((DFF + n0, gsz))
                    B.linear("swin%d" % l, w_in[l], D, 2 * DFF, xsS(hS, "hS"), epi, 512, wbs, colgroups=cgs)
                    B.linear("swout%d" % l, w_out[l], DFF, D, xsS(actS, "actS"),
                             resS(xin, xout, lambda ct: modv[:, l, 5 * KC + ct, 1:1 + NS]), 128, wbo)

                B.linear("swo", w_o, D, D, xsS(ygs, "ygs"), resS(xs0, x1s, lambda ct: modv[:, 0, 2 * KC + ct, 1:1 + NS]), 512, wbs)
                ffnS(0, x1s, x2s)
                normS(x2s, Gm[:, 4, :, 1:1 + NS], kvmv[:, 0:KC, 1:1 + NS], ["Gm", "kvmv"])
                knew = sb("knew", [128, NKV, NS]); vnew = sb("vnew", [128, KVT, NS])
                ksqS = sb("ksqS", [128, NS])
                for g in range(NKV):
                    buf = wbs[g % 2]; bkey = ("wb", buf.name)
                    for half in range(2):
                        P.dma("pool", buf[:, :, half * 64:(half + 1) * 64],
                              w_kv[:, g * 64:(g + 1) * 64].rearrange("(kc p) n -> p kc n", p=128), writes=[bkey])
                    bk, ps = B.bank()
                    for kc in range(KC):
                        mm(ps[:, 0:NS], buf[:, kc, 0:128], hS[:, kc, :], [bkey, "hS"], [bk], start=(kc == 0), stop=(kc == KC - 1))
                    cp("act", knew[:, g, :], ps[:, 0:NS], [bk], ["knew"])
                    act(ksqS[:], knew[:, g, :], AF.Square, ["knew"], ["ksqS"])
                    bk, ps = B.bank()
                    mm(ps[:, 0:NS], blkm[:], ksqS[:], ["c_blkm", "ksqS"], [bk])
                    act(ksqS[:], ps[:, 0:NS], AF.Ln, [bk], ["ksqS"], bias=1e-6)
                    act(ksqS[:], ksqS[:], AF.Exp, ["ksqS"], ["ksqS"], scale=-0.5)
                    tt("dve", knew[:, g, :], knew[:, g, :], ksqS[:], ALU.mult, ["knew", "ksqS"], ["knew"])
                    ts("dve", knew[:, g, :], knew[:, g, :], hdt[:, 0:1], ALU.mult, ["knew", "hdt"], ["knew"])

                def epiVs(ct, n0, m, g, ps, bk):
                    cp("act", vnew[0:m, ct, :], ps, [bk], ["vnew"])
                B.linear("swv", w_kv[:, KVW:2 * KVW], D, KVW, xsS(hS, "hS"), epiVs, 128, wbs)
                normS(x2s, Gm[:, 2, :, 1:1 + NS], modv[:, 1, 0:KC, 1:1 + NS], ["Gm", "modv"])
                qS = sb("qS", [128, KC, NS]); qz = sb("qz", [128, 2, KC, NS], BF16)
                P.op("pool", lambda E: E.memset(qz[:], 0.0), writes=["qz"])

                def epiQs(ct, n0, m, g, ps, bk):
                    cp("act", qS[:, ct, :], ps, [bk], ["qS"])
                B.linear("swq", w_q, D, D, xsS(hS, "hS"), epiQs, 512, wbs)
                act(f3(sqS), f3(qS), AF.Square, ["qS"], ["sqS"])
                bk, ps = B.bank()
                mm(ps[:, 0:KN], blkm[:], f3(sqS), ["c_blkm", "sqS"], [bk])
                act(f3(h32), ps[:, 0:KN], AF.Ln, [bk], ["h32"], bias=1e-6)
                act(f3(h32), f3(h32), AF.Exp, ["h32"], ["h32"], scale=-0.5)
                tt("dve", f3(qS), f3(qS), f3(h32), ALU.mult, ["qS", "h32"], ["qS"])
                for h in range(2):
                    hs = slice(h * 64, (h + 1) * 64)
                    ts("dve", qz[hs, h, :, :].rearrange("p k n -> p (k n)"), qS[hs, :, :].rearrange("p k n -> p (k n)"),
                       hdt[hs, 1:2], ALU.mult, ["qS", "hdt"], ["qz"])
                GQ = H // NKV
                PG = GQ // 2
                oatS = sb("oatS", [128, KC, NS], BF16)
                Kc2 = [sb("Kc2_%d" % i, [128, 128]) for i in range(2)]
                Vc2 = [sb("Vc2_%d" % i, [128, 128]) for i in range(2)]
                Vcb = [sb("Vcb_%d" % i, [128, 128], BF16) for i in range(2)]
                KcT = [sb("KcT_%d" % i, [128, 128], BF16) for i in range(2)]
                knb = sb("knb", [128, NKV, NS], BF16)
                cp("dve", knb[:], knew[:], ["knew"], ["knb"])
                Es = [sb("Es_%d" % i, [128, GQ], BF16) for i in range(2)]
                en = [sb("en_%d" % i, [1, GQ], BF16) for i in range(2)]
                vrow = [sb("vrow_%d" % i, [1, 128], BF16) for i in range(2)]
                krow = [sb("krow_%d" % i, [1, 128]) for i in range(2)]
                vrow32 = [sb("vrow32_%d" % i, [1, 128]) for i in range(2)]
                onesb = sb("onesb", [128, 128], BF16); cp("dve", onesb[:], ones[:], ["c_ones"], ["onesb"])
                dn = [sb("dn_%d" % i, [128, GQ]) for i in range(2)]
                ob = [sb("ob_%d" % i, [128, GQ]) for i in range(2)]
                it = 0
                for j in range(NS):
                    P.dma("sp", o_kwins[j, 0:127, :], ck[j, 1:128, :])
                    P.dma("sp", o_vwins[j, 0:127, :], cv[j, 1:128, :])
                    for g in range(NKV):
                        i2 = it % 2; it += 1
                        kc2, vc2, vcb, kct, E_, en_, vr, kr, vr32, dn_, ob_ = (Kc2[i2], Vc2[i2], Vcb[i2], KcT[i2], Es[i2], en[i2],
                                                                            vrow[i2], krow[i2], vrow32[i2], dn[i2], ob[i2])
                        for half in range(2):
                            P.dma("sp", kc2[:, half * 64:(half + 1) * 64], ck[j, :, g * 64:(g + 1) * 64], writes=[kc2.name])
                            P.dma("sp", vc2[:, half * 64:(half + 1) * 64], cv[j, :, g * 64:(g + 1) * 64], writes=[vc2.name])
                        cp("pool", vcb[:], vc2[:], [vc2.name], [vcb.name])
                        bk, ps = B.bank()
                        P.op("pe", lambda E, o=ps[:, 0:128], i_=kc2: E.transpose(o, i_[:], ident[:]), reads=[kc2.name, "c_ident"], writes=[bk])
                        cp("act", kct[:], ps[:, 0:128], [bk], [kct.name])
                        qg = qz[:, :, g * PG:(g + 1) * PG, j]
                        bk, ps = B.bank()
                        mm(ps[:, 0:GQ].rearrange("p (h c) -> p h c", h=2), kct[:], qg, [kct.name, "qz"], [bk])
                        act(E_[:], ps[:, 0:GQ], AF.Exp, [bk], [E_.name], scale=0.125)
                        ts("dve", E_[:], E_[:], msl[:, 0:1], ALU.mult, [E_.name, "c_sl"], [E_.name])
                        bk, ps = B.bank()
                        mm(ps[0:1, 0:GQ].rearrange("p (h c) -> p h c", h=2), knb[:, g, j:j + 1], qg, ["knb", "qz"], [bk])
                        act(en_[:], ps[0:1, 0:GQ], AF.Exp, [bk], [en_.name], scale=0.125)
                        bk, ps = B.bank()
                        P.op("pe", lambda E, o=ps[0:1, 0:128], g=g, j=j: E.transpose(o, knew[:, g, j:j + 1], ident[:]), reads=["knew", "c_ident"], writes=[bk])
                        P.op("pe", lambda E, o=ps[0:1, 128:256], g=g, j=j: E.transpose(o, vnew[:, g // 2, j:j + 1], ident[:]), reads=["vnew", "c_ident"], writes=[bk])
                        cp("act", kr[:], ps[0:1, 0:128], [bk], [kr.name])
                        go = (g % 2) * 64
                        cp("dve", vr32[:, 0:64], ps[0:1, 128 + go:128 + go + 64], [bk], [vr32.name])
                        cp("dve", vr32[:, 64:128], ps[0:1, 128 + go:128 + go + 64], [bk], [vr32.name])
                        cp("pool", vr[:], vr32[:], [vr32.name], [vr.name])
                        P.dma("sp", o_kwins[j, 127:128, g * 64:(g + 1) * 64], kr[:, 0:64], reads=[kr.name])
                        P.dma("sp", o_vwins[j, 127:128, g * 64:(g + 1) * 64], vr32[:, 0:64], reads=[vr32.name])
                        bkN, psN = B.bank()
                        mm(psN[:, 0:GQ], vcb[:], E_[:], [vcb.name, E_.name], [bkN], start=True, stop=False)
                        mm(psN[:, 0:GQ], vr[:], en_[:], [vr.name, en_.name], [bkN], start=False, stop=True)
                        bkD, psD = B.bank()
                        mm(psD[:, 0:GQ], onesb[:], E_[:], ["onesb", E_.name], [bkD], start=True, stop=False)
                        mm(psD[:, 0:GQ], onesb[0:1, :], en_[:], ["onesb", en_.name], [bkD], start=False, stop=True)
                        tt("dve", dn_[:].rearrange("p (h c) -> p h c", h=2), psD[:, 0:GQ].rearrange("p (h c) -> p h c", h=2),
                           esT[:, g * GQ:(g + 1) * GQ].rearrange("p (c h) -> p h c", h=2), ALU.add, [bkD, "esT"], [dn_.name])
                        P.op("dve", lambda E, d_=dn_: E.reciprocal(d_[:], d_[:]), reads=[dn_.name], writes=[dn_.name])
                        tt("dve", ob_[:], psN[:, 0:GQ], dn_[:], ALU.mult, [bkN, dn_.name], [ob_.name])
                        for h in range(2):
                            hs = slice(h * 64, (h + 1) * 64)
                            cp("pool", oatS[hs, g * PG:(g + 1) * PG, j], ob_[hs, h * PG:(h + 1) * PG], [ob_.name], ["oatS"])
                B.linear("swao", w_ao, D, D, xsS(oatS, "oatS"), resS(x2s, x3s, lambda ct: modv[:, 1, 2 * KC + ct, 1:1 + NS]), 512, wbs)
                ffnS(1, x3s, ysf)
                P.dma("sp", o_ys.rearrange("(kc p) n -> p kc n", p=128), ysf[:], reads=[ysf.name])
                P.dma("sp", o_convs.rearrange("l p f n c -> p l (f n c)"), cvo[:].rearrange("p l f n c -> p l (f n c)"), reads=["cvo"])
                P.barrier()
            B.es = es
        sample_path()
        B._st = dict(Gm=Gm, modv=modv, kvmv=kvmv, vec=vec, vec2=vec2, omm=omm, cvp=cvp, hdt=hdt, flag=flag,
                     esink=esink, ident=ident, ones=ones, blk=blk, blkm=blkm, m2b=m2b, mslb=mslb)
        return B, locals()


def finish(B, P, nc):
    P.finish_waits("sp")
    with nc.Block() as block:
        P.emit(block)


def host_inputs(inp, cfg, core):
    D, T, DFF, NS = cfg["D"], cfg["T"], cfg["DFF"], cfg["NS"]
    KC, FC, H = D // 128, DFF // 128, D // 64
    NP = H // 2
    NKV = max(1, H // 8)
    KVW = NKV * 64
    TH = T // 2
    b, hf = core // 2, core % 2
    f32 = np.float32
    m = {}
    xb = np.asarray(inp["x_prompt"][b], f32)
    xT = np.zeros((D, T), f32)
    mask = np.ones((128, T), f32)
    if hf == 1:
        xT[:] = xb.T
    else:
        xT[:, TH:] = xb[:TH].T
        mask[:, :TH] = 0.0
    m["xT"] = xT
    m["maskrow"] = mask
    m["flagcol"] = np.full((128, 1), float(hf), f32)
    ss = slice(core * NS, (core + 1) * NS)
    m["cT"] = np.ascontiguousarray(np.concatenate([inp["c_prompt"][b][None], inp["c_sample"][ss]], 0).T.astype(f32))
    m["xsT"] = np.ascontiguousarray(inp["x_sample"][ss, 0].T.astype(f32))
    m["shsT"] = np.ascontiguousarray(inp["state_shift"][0, ss].T.astype(f32))
    m["st_wkv"] = np.ascontiguousarray(inp["state_wkv"][0, ss].astype(f32))
    m["st_conv"] = np.ascontiguousarray(inp["state_conv"][:, ss].astype(f32).reshape(2, NS, 2, FC, 128).transpose(0, 4, 3, 1, 2))
    m["ck"] = np.ascontiguousarray(inp["cache_k_win"][ss].reshape(NS, 128, KVW).astype(f32))
    m["cv"] = np.ascontiguousarray(inp["cache_v_win"][ss].reshape(NS, 128, KVW).astype(f32))
    m.update(_consts())
    m["mod_w"] = inp["mod_w"]
    m["modb"] = np.stack([_fm(inp["mod_b"][l], 6 * KC) for l in range(2)])
    m["kv_mod_w"] = inp["kv_mod_w"]
    m["kvmodb"] = _fm(inp["kv_mod_b"], 2 * KC)
    vl = [inp["ln1_g"][0], inp["ln1_g"][1], inp["ln2_g"][0], inp["ln2_g"][1]] + \
         [inp["rwkv_mix"][0, i] for i in range(6)] + \
         [inp["rwkv_w0"][0], inp["rwkv_a0"][0], inp["rwkv_k_k"][0], inp["rwkv_k_a"][0],
          inp["rwkv_r_k"][0].reshape(-1), inp["rwkv_lnx_w"][0]]
    m["vecs"] = np.ascontiguousarray(np.stack([_fm(v, KC) for v in vl], 1))
    m["vecs2"] = np.ascontiguousarray(np.stack([_fm(inp["rwkv_lnx_b"][0], KC), _fm(inp["kv_norm_g"], KC)], 1))
    m["convp"] = np.ascontiguousarray(np.stack(
        [np.stack([_fm(inp["ffn_conv_w"][l, j], FC) for j in range(3)] + [_fm(inp["ffn_conv_b"][l], FC)], 1)
         for l in range(2)], 1))
    hd = np.zeros((128, 3 + NP), f32)
    hd[:, 0] = np.tile(inp["k_norm_g"], 2)
    hd[:, 1] = np.tile(inp["attn_q_norm_g"][0], 2)
    hd[:, 3:] = np.repeat(inp["attn_sinks"][0].reshape(NP, 2), 64, axis=1).T
    m["hd"] = hd
    m["sinkT"] = np.tile(inp["attn_sinks"][0][None, :], (128, 1))
    for k in ("rwkv_w1", "rwkv_w2", "rwkv_a1", "rwkv_a2", "rwkv_g1", "rwkv_g2", "rwkv_w_r", "rwkv_w_k",
              "rwkv_w_v", "rwkv_w_o", "attn_w_q", "attn_w_o"):
        m[k] = inp[k][0]
    m["w_kv"] = inp["w_kv"]
    m["ffn_w_in"] = inp["ffn_w_in"]
    m["ffn_w_out"] = inp["ffn_w_out"]
    return {k: np.ascontiguousarray(np.asarray(v, f32)) for k, v in m.items()}


_CACHE = {}


def kernel(**inputs):
    cfg = REAL_CFG
    inp = {k: np.asarray(v) for k, v in inputs.items()}
    if "prog" not in _CACHE:
        B, L = build(cfg)
        finish(B, B.P, B.nc)
        _CACHE["prog"] = B
    B = _CACHE["prog"]
    n = 8
    in_maps = []
    for c in range(n):
        m = host_inputs(inp, cfg, c)
        in_maps.append({k: m[k] for k in B.ins})
    res = run_bass_kernel_spmd(B.nc, in_maps, core_ids=list(range(n)))
    R = res.results
    D, T, DFF, NS = cfg["D"], cfg["T"], cfg["DFF"], cfg["NS"]
    KC, FC, H = D // 128, DFF // 128, D // 64
    NKV = max(1, H // 8)
    TH = T // 2
    NB = 4
    f32 = np.float32
    y_p = np.zeros((NB, T, D), f32); y_s = np.zeros((NB * 8, 1, D), f32)
    wkv_p = np.zeros((1, NB, H, 64, 64), f32); wkv_s = np.zeros((1, NB * 8, H, 64, 64), f32)
    sh_p = np.zeros((1, NB, D), f32); sh_s = np.zeros((1, NB * 8, D), f32)
    cv_p = np.zeros((2, NB, 2, DFF), f32); cv_s = np.zeros((2, NB * 8, 2, DFF), f32)
    kw_p = np.zeros((NB, 128, NKV, 64), f32); kw_s = np.zeros((NB * 8, 128, NKV, 64), f32)
    vw_p = np.zeros((NB, 128, NKV, 64), f32); vw_s = np.zeros((NB * 8, 128, NKV, 64), f32)
    for c in range(n):
        b, hf = c // 2, c % 2
        r = R[c]
        y_p[b, hf * TH:(hf + 1) * TH] = r["o_y"].T
        ss = slice(c * NS, (c + 1) * NS)
        y_s[ss, 0] = r["o_ys"].T
        wkv_s[0, ss] = r["o_wkvs"]
        sh_s[0, ss] = r["o_shifts"].transpose(2, 1, 0).reshape(NS, D)
        cv_s[:, ss] = r["o_convs"].transpose(0, 3, 4, 2, 1).reshape(2, NS, 2, DFF)
        kw_s[ss] = r["o_kwins"].reshape(NS, 128, NKV, 64)
        vw_s[ss] = r["o_vwins"].reshape(NS, 128, NKV, 64)
        if hf == 1:
            wkv_p[0, b] = r["o_wkv"]
            sh_p[0, b] = r["o_shift"].T.reshape(D)
            cv_p[:, b] = r["o_conv"].transpose(0, 3, 2, 1).reshape(2, 2, DFF)
            kw_p[b] = r["o_kwin"].reshape(128, NKV, 64)
            vw_p[b] = r["o_vwin"].reshape(128, NKV, 64)
    return (y_p, y_s, wkv_p, wkv_s, sh_p, sh_s, cv_p, cv_s, kw_p, kw_s, vw_p, vw_s)
```
